# Optimizing a Trainium2 kernel written in Bass

```python
import math
import jax, jax.numpy as jnp
from jax import lax
import numpy as np

D_MODEL = 1024
BATCH = 2
SEQ = 16384
DEPTH = 2

N_META = 16
NORM_EPS = 1e-6

ATTN_HEAD_DIM = 64
ATTN_HEADS = D_MODEL // ATTN_HEAD_DIM
ATTN_KV_HEADS = ATTN_HEADS // 8
ATTN_GROUPS = ATTN_HEADS // ATTN_KV_HEADS
ATTN_WIDTH = ATTN_HEADS * ATTN_HEAD_DIM
ATTN_KV_WIDTH = ATTN_KV_HEADS * ATTN_HEAD_DIM
ATTN_IN = 2 * ATTN_WIDTH + 2 * ATTN_KV_WIDTH
WINDOW = 128
ATTN_BLOCK = 128

DN_HEAD_DIM_K = 128
DN_HEAD_DIM_V = 128
DN_K_HEADS = D_MODEL // DN_HEAD_DIM_K
DN_V_HEADS = 2 * DN_K_HEADS
DN_KEY_WIDTH = DN_K_HEADS * DN_HEAD_DIM_K
DN_VALUE_WIDTH = DN_V_HEADS * DN_HEAD_DIM_V
DN_CONV = 4
DN_CHUNK = 64
DN_CONV_WIDTH = 2 * DN_KEY_WIDTH + DN_VALUE_WIDTH
DN_IN = DN_CONV_WIDTH + DN_VALUE_WIDTH + 2 * DN_V_HEADS

N_ATTN_LAYERS = (DEPTH + 1) // 2
N_DN_LAYERS = DEPTH // 2

kernel_name = "hybrid_swa_sink_alibi_gated_deltanet_meta"


def rms_norm(x, w):
    xf = x.astype(jnp.float32)
    y = xf * lax.rsqrt(jnp.mean(xf * xf, axis=-1, keepdims=True) + NORM_EPS)
    return (y * w.astype(jnp.float32)).astype(x.dtype)


def l2_norm(x):
    xf = x.astype(jnp.float32)
    return xf * lax.rsqrt(jnp.sum(xf * xf, axis=-1, keepdims=True) + NORM_EPS)


def alibi_slopes(n_heads):
    return jnp.asarray(np.exp2(-8.0 * np.arange(1, n_heads + 1) / n_heads), dtype=jnp.float32)


def banded_sink_attention(q, k, v, sinks):
    B, L = q.shape[:2]
    pad = ATTN_BLOCK - N_META
    Lp = L + pad
    nb = Lp // ATTN_BLOCK
    padt = lambda t: jnp.pad(t, ((0, 0), (pad, 0), (0, 0), (0, 0)))
    qb = padt(q).reshape(B, nb, ATTN_BLOCK, ATTN_KV_HEADS, ATTN_GROUPS, ATTN_HEAD_DIM)
    kb = padt(k).reshape(B, nb, ATTN_BLOCK, ATTN_KV_HEADS, ATTN_HEAD_DIM)
    vb = padt(v).reshape(B, nb, ATTN_BLOCK, ATTN_KV_HEADS, ATTN_HEAD_DIM)
    prev = lambda t: jnp.pad(t, ((0, 0), (1, 0), (0, 0), (0, 0), (0, 0)))[:, :-1]
    k_band = jnp.concatenate([prev(kb), kb], axis=2)
    v_band = jnp.concatenate([prev(vb), vb], axis=2)
    k_meta = k[:, :N_META]
    v_meta = v[:, :N_META]

    scale = ATTN_HEAD_DIM ** -0.5
    s_band = jnp.einsum('bnqhgd,bnkhd->bnhgqk', qb, k_band,
                        preferred_element_type=jnp.float32) * scale
    s_meta = jnp.einsum('bnqhgd,bmhd->bnhgqm', qb, k_meta,
                        preferred_element_type=jnp.float32) * scale

    pos_q = jnp.arange(Lp, dtype=jnp.int32).reshape(nb, ATTN_BLOCK) - pad
    pos_kb = jnp.concatenate([pos_q - ATTN_BLOCK, pos_q], axis=-1)
    pos_meta = jnp.arange(N_META, dtype=jnp.int32)
    dist_band = pos_q[:, :, None] - pos_kb[:, None, :]
    valid_band = (pos_kb[:, None, :] >= N_META) & (dist_band >= 0) & (dist_band < WINDOW)
    dist_meta = pos_q[:, :, None] - pos_meta[None, None, :]
    valid_meta = dist_meta >= 0

    slopes = alibi_slopes(ATTN_HEADS).reshape(1, 1, ATTN_KV_HEADS, ATTN_GROUPS, 1, 1)
    clipdist = lambda d: jnp.minimum(d, WINDOW).astype(jnp.float32)[None, :, None, None]
    s_band = jnp.where(valid_band[None, :, None, None], s_band - slopes * clipdist(dist_band), -jnp.inf)
    s_meta = jnp.where(valid_meta[None, :, None, None], s_meta - slopes * clipdist(dist_meta), -jnp.inf)

    sink = jnp.broadcast_to(
        sinks.astype(jnp.float32).reshape(1, 1, ATTN_KV_HEADS, ATTN_GROUPS, 1, 1),
        s_band.shape[:-1] + (1,))
    p = jax.nn.softmax(jnp.concatenate([s_band, s_meta, sink], axis=-1), axis=-1)
    p_band = p[..., :2 * ATTN_BLOCK].astype(v.dtype)
    p_meta = p[..., 2 * ATTN_BLOCK:2 * ATTN_BLOCK + N_META].astype(v.dtype)
    o = (jnp.einsum('bnhgqk,bnkhd->bnqhgd', p_band, v_band)
         + jnp.einsum('bnhgqm,bmhd->bnqhgd', p_meta, v_meta))
    return o.reshape(B, Lp, ATTN_WIDTH)[:, pad:]


def attention_mixer(h, norm_w, w_in, q_norm_w, k_norm_w, sinks, w_out):
    B, L, _ = h.shape
    u = rms_norm(h, norm_w) @ w_in
    q, k, v, gate = jnp.split(
        u, [ATTN_WIDTH, ATTN_WIDTH + ATTN_KV_WIDTH, ATTN_WIDTH + 2 * ATTN_KV_WIDTH], axis=-1)
    q = rms_norm(q.reshape(B, L, ATTN_HEADS, ATTN_HEAD_DIM), q_norm_w)
    k = rms_norm(k.reshape(B, L, ATTN_KV_HEADS, ATTN_HEAD_DIM), k_norm_w)
    v = v.reshape(B, L, ATTN_KV_HEADS, ATTN_HEAD_DIM)
    o = banded_sink_attention(q, k, v, sinks)
    return (o * jax.nn.silu(gate)) @ w_out


def causal_depthwise_conv(x, w):
    K, C = w.shape
    return lax.conv_general_dilated(
        x, w[:, None, :], window_strides=(1,), padding=[(K - 1, 0)],
        dimension_numbers=('NWC', 'WIO', 'NWC'), feature_group_count=C)


def chunked_gated_delta_rule(q, k, v, beta, g):
    B, L, H, _ = q.shape
    pad = DN_CHUNK - N_META
    Lc = L + pad
    n = Lc // DN_CHUNK

    def to_chunks(t):
        t = jnp.pad(t.astype(jnp.float32), [(0, 0), (pad, 0)] + [(0, 0)] * (t.ndim - 2))
        t = t.reshape((B, n, DN_CHUNK) + t.shape[2:])
        return jnp.swapaxes(jnp.moveaxis(t, 1, 0), 2, 3)

    xs = (to_chunks(q), to_chunks(k), to_chunks(v), to_chunks(beta), to_chunks(g))
    causal = jnp.tril(jnp.ones((DN_CHUNK, DN_CHUNK), dtype=bool))
    strict = jnp.tril(jnp.ones((DN_CHUNK, DN_CHUNK), dtype=bool), -1)
    eye = jnp.eye(DN_CHUNK, dtype=jnp.float32)

    def step(S, inp):
        qc, kc, vc, bc, gc = inp
        gcum = jnp.cumsum(gc, axis=-1)
        decay = jnp.exp(jnp.where(causal, gcum[..., :, None] - gcum[..., None, :], -jnp.inf))
        kb = kc * bc[..., None]
        m = jnp.where(strict, jnp.einsum('bhcd,bhsd->bhcs', kb, kc) * decay, 0.0)
        rhs = jnp.concatenate([vc * bc[..., None], kb * jnp.exp(gcum)[..., None]], axis=-1)
        sol = lax.linalg.triangular_solve(m + eye, rhs, left_side=True, lower=True,
                                          unit_diagonal=True)
        u, w = sol[..., :DN_HEAD_DIM_V], sol[..., DN_HEAD_DIM_V:]
        v_new = u - jnp.einsum('bhcd,bhdv->bhcv', w, S)
        attn = jnp.einsum('bhcd,bhsd->bhcs', qc, kc) * decay
        o = (jnp.einsum('bhcd,bhdv->bhcv', qc * jnp.exp(gcum)[..., None], S)
             + jnp.einsum('bhcs,bhsv->bhcv', attn, v_new))
        g_last = gcum[..., -1]
        k_state = kc * jnp.exp(g_last[..., None] - gcum)[..., None]
        S = S * jnp.exp(g_last)[..., None, None] + jnp.einsum('bhcd,bhcv->bhdv', k_state, v_new)
        return S, o

    S0 = jnp.zeros((B, H, DN_HEAD_DIM_K, DN_HEAD_DIM_V), jnp.float32)
    _, o = lax.scan(step, S0, xs)
    o = jnp.moveaxis(jnp.swapaxes(o, 2, 3), 0, 1).reshape(B, Lc, H, DN_HEAD_DIM_V)
    return o[:, pad:]


def deltanet_mixer(h, norm_w, w_in, conv_w, a_log, dt_bias, o_norm_w, w_out):
    B, L, _ = h.shape
    u = rms_norm(h, norm_w) @ w_in
    qkv, z, b, a = jnp.split(
        u, [DN_CONV_WIDTH, DN_CONV_WIDTH + DN_VALUE_WIDTH,
            DN_CONV_WIDTH + DN_VALUE_WIDTH + DN_V_HEADS], axis=-1)
    qkv = jax.nn.silu(causal_depthwise_conv(qkv, conv_w))
    q, k, v = jnp.split(qkv, [DN_KEY_WIDTH, 2 * DN_KEY_WIDTH], axis=-1)
    rep = DN_V_HEADS // DN_K_HEADS
    q = jnp.repeat(l2_norm(q.reshape(B, L, DN_K_HEADS, DN_HEAD_DIM_K)), rep, axis=2)
    k = jnp.repeat(l2_norm(k.reshape(B, L, DN_K_HEADS, DN_HEAD_DIM_K)), rep, axis=2)
    q = q * (DN_HEAD_DIM_K ** -0.5)
    v = v.reshape(B, L, DN_V_HEADS, DN_HEAD_DIM_V)
    beta = jax.nn.sigmoid(b.astype(jnp.float32))
    g = -jnp.exp(a_log.astype(jnp.float32)) * jax.nn.softplus(
        a.astype(jnp.float32) + dt_bias.astype(jnp.float32))
    o = chunked_gated_delta_rule(q, k, v, beta, g).astype(h.dtype)
    o = rms_norm(o, o_norm_w) * jax.nn.silu(z.reshape(B, L, DN_V_HEADS, DN_HEAD_DIM_V))
    return o.reshape(B, L, DN_VALUE_WIDTH) @ w_out


def setup_inputs(seed: int = 0) -> dict:
    key = jax.random.key(seed)
    ks = jax.random.split(key, 20)
    f32 = jnp.float32
    nA, nB = N_ATTN_LAYERS, N_DN_LAYERS
    out_scale = 0.5
    dt = jnp.exp(jax.random.uniform(ks[13], (nB, DN_V_HEADS), f32,
                                    math.log(1e-3), math.log(1e-1)))
    return {
        "x": jax.random.normal(ks[0], (BATCH, SEQ, D_MODEL), f32),
        "meta_tokens": jax.random.normal(ks[1], (N_META, D_MODEL), f32),
        "attn_norm_w": 1.0 + 0.02 * jax.random.normal(ks[2], (nA, D_MODEL), f32),
        "attn_w_in": jax.random.normal(ks[3], (nA, D_MODEL, ATTN_IN), f32) * D_MODEL ** -0.5,
        "attn_q_norm_w": 1.0 + 0.02 * jax.random.normal(ks[4], (nA, ATTN_HEAD_DIM), f32),
        "attn_k_norm_w": 1.0 + 0.02 * jax.random.normal(ks[5], (nA, ATTN_HEAD_DIM), f32),
        "attn_sinks": 0.5 * jax.random.normal(ks[6], (nA, ATTN_HEADS), f32),
        "attn_w_out": jax.random.normal(ks[7], (nA, ATTN_WIDTH, D_MODEL), f32)
                      * ATTN_WIDTH ** -0.5 * out_scale,
        "dn_norm_w": 1.0 + 0.02 * jax.random.normal(ks[8], (nB, D_MODEL), f32),
        "dn_w_in": jax.random.normal(ks[9], (nB, D_MODEL, DN_IN), f32) * D_MODEL ** -0.5,
        "dn_conv_w": jax.random.normal(ks[10], (nB, DN_CONV, DN_CONV_WIDTH), f32) * DN_CONV ** -0.5,
        "dn_a_log": jnp.log(jax.random.uniform(ks[11], (nB, DN_V_HEADS), f32, 1.0, 16.0)),
        "dn_dt_bias": dt + jnp.log(-jnp.expm1(-dt)),
        "dn_o_norm_w": 1.0 + 0.02 * jax.random.normal(ks[12], (nB, DN_HEAD_DIM_V), f32),
        "dn_w_out": jax.random.normal(ks[14], (nB, DN_VALUE_WIDTH, D_MODEL), f32)
                    * DN_VALUE_WIDTH ** -0.5 * out_scale,
    }


def reference(x, meta_tokens, attn_norm_w, attn_w_in, attn_q_norm_w, attn_k_norm_w,
              attn_sinks, attn_w_out, dn_norm_w, dn_w_in, dn_conv_w, dn_a_log,
              dn_dt_bias, dn_o_norm_w, dn_w_out):
    B = x.shape[0]
    meta = jnp.broadcast_to(meta_tokens.astype(x.dtype)[None], (B, N_META, x.shape[-1]))
    h = jnp.concatenate([meta, x], axis=1)
    for i in range(DEPTH):
        j = i // 2
        if i % 2 == 0:
            h = h + attention_mixer(h, attn_norm_w[j], attn_w_in[j], attn_q_norm_w[j],
                                    attn_k_norm_w[j], attn_sinks[j], attn_w_out[j])
        else:
            h = h + deltanet_mixer(h, dn_norm_w[j], dn_w_in[j], dn_conv_w[j], dn_a_log[j],
                                   dn_dt_bias[j], dn_o_norm_w[j], dn_w_out[j])
    return h[:, N_META:]
```

```python
import numpy as np
import ml_dtypes
import concourse.bass as bass
import concourse.mybir as mybir
from concourse.bass_utils import run_bass_kernel_spmd

F32 = mybir.dt.float32
BF16 = mybir.dt.bfloat16
AF = mybir.ActivationFunctionType
ALU = mybir.AluOpType
AX = mybir.AxisListType

NPBF = ml_dtypes.bfloat16


class Prog:
    COMPUTE = ("pe", "act", "dve", "pool")

    def __init__(self, nc, n_dma_sems=32):
        self.nc = nc
        self.ops = []
        self.last_w = {}
        self.readers = {}
        self.n_dma_sems = n_dma_sems
        self.dma_count = 0
        self.dma_last = [None] * n_dma_sems
        self.dma_uses = [0] * n_dma_sems

    def add(self, eng, fn, reads=(), writes=(), dma=False):
        oid = len(self.ops)
        deps = set()
        for r in reads:
            if r in self.last_w:
                deps.add(self.last_w[r])
        for w in writes:
            if w in self.last_w:
                deps.add(self.last_w[w])
            deps |= self.readers.get(w, set())
        op = dict(eng=eng, fn=fn, deps=deps, dma=dma, signal=False, sigval=None)
        if dma:
            k = self.dma_count % self.n_dma_sems
            self.dma_count += 1
            if self.dma_last[k] is not None:
                deps.add(self.dma_last[k])
            self.dma_last[k] = oid
            self.dma_uses[k] += 1
            op["dsem"] = k
            op["dval"] = 16 * self.dma_uses[k]
        deps.discard(oid)
        self.ops.append(op)
        for r in reads:
            self.readers.setdefault(r, set()).add(oid)
        for w in writes:
            self.last_w[w] = oid
            self.readers[w] = set()
        return oid

    def emit(self):
        nc = self.nc
        ops = self.ops
        for op in ops:
            for d in op["deps"]:
                D = ops[d]
                if D["dma"]:
                    continue
                if D["eng"] == op["eng"] == "pe" and not op["dma"]:
                    continue
                D["signal"] = True
        cnt = {e: 0 for e in self.COMPUTE}
        for op in ops:
            if op["dma"]:
                continue
            if op["signal"]:
                cnt[op["eng"]] += 1
                op["sigval"] = cnt[op["eng"]]
        by_eng = {e: [] for e in ("pe", "act", "dve", "pool", "sp")}
        for op in ops:
            by_eng[op["eng"]].append(op)
        import contextlib
        with contextlib.ExitStack() as st:
            csem = {e: st.enter_context(nc.semaphore("cs_" + e)) for e in self.COMPUTE}
            dsem = [st.enter_context(nc.semaphore("ds_%d" % k)) for k in range(self.n_dma_sems)]
            block = st.enter_context(nc.Block())

            def run(engname, eng):
                waited = {}
                for op in by_eng[engname]:
                    targets = []
                    for d in sorted(op["deps"]):
                        D = ops[d]
                        if D["dma"]:
                            targets.append((("d", D["dsem"]), dsem[D["dsem"]], D["dval"]))
                        else:
                            if D["eng"] == engname == "pe" and not op["dma"]:
                                continue
                            targets.append((("c", D["eng"]), csem[D["eng"]], D["sigval"]))
                    best = {}
                    for key, sem, val in targets:
                        if val > waited.get(key, 0) and val > best.get(key, (None, 0))[1]:
                            best[key] = (sem, val)
                    for key, (sem, val) in best.items():
                        eng.wait_ge(sem, val)
                        waited[key] = val
                    ins = op["fn"](eng)
                    if op["dma"]:
                        ins.then_inc(dsem[op["dsem"]], 16)
                    elif op["signal"]:
                        ins.then_inc(csem[engname], 1)
                if engname == "sp":
                    for k in range(self.n_dma_sems):
                        if self.dma_uses[k]:
                            eng.wait_ge(dsem[k], 16 * self.dma_uses[k])

            @block.tensor
            def _(e):
                run("pe", e)

            @block.scalar
            def _(e):
                run("act", e)

            @block.vector
            def _(e):
                run("dve", e)

            @block.gpsimd
            def _(e):
                run("pool", e)

            @block.sync
            def _(e):
                run("sp", e)


def build_phase_c(ntok=4096):
    nc = bass.Bass("TRN2", target_bir_lowering=False)
    ogT = nc.dram_tensor("ogT", [2048, ntok], BF16, kind="ExternalInput").ap()
    h1 = nc.dram_tensor("h1", [ntok, 1024], F32, kind="ExternalInput").ap()
    wout = nc.dram_tensor("wout", [2048, 1024], F32, kind="ExternalInput").ap()
    y = nc.dram_tensor("y", [ntok, 1024], F32, kind="ExternalOutput").ap()
    import contextlib
    with contextlib.ExitStack() as st:
        P = Prog(nc)
        emit_phase_c(nc, st, P, ogT, h1, wout, y, ntok)
        P.emit()
    return nc


def emit_phase_c(nc, st, P, ogT, h1, wout, y, ntok):
    TT = 512
    nt = ntok // TT
    w_sb = st.enter_context(nc.sbuf_tensor("c_w", [128, 16, 1024], BF16))
    wst = [st.enter_context(nc.sbuf_tensor("c_wst%d" % i, [128, 2, 1024], F32)) for i in range(2)]
    og_sb = [st.enter_context(nc.sbuf_tensor("c_og%d" % i, [128, 16, TT], BF16)) for i in range(2)]
    h_sb = [st.enter_context(nc.sbuf_tensor("c_h%d" % i, [128, 1024], F32)) for i in range(3)]
    y_sb = [st.enter_context(nc.sbuf_tensor("c_y%d" % i, [128, 1024], F32)) for i in range(3)]
    ps = [st.enter_context(nc.psum_tensor("c_ps%d" % i, [128, 512], F32)) for i in range(4)]
    woutv = wout.rearrange("(c p) n -> p c n", p=128)
    for j in range(8):
        s = j % 2
        P.add("sp", lambda e, j=j, s=s: e.dma_start(out=wst[s][:], in_=woutv[:, 2 * j:2 * j + 2, :]),
              writes=[("c_wst", s)], dma=True)
        eng = "act" if j % 2 == 0 else "dve"
        if eng == "act":
            P.add("act", lambda e, j=j, s=s: e.copy(out=w_sb[:, 2 * j:2 * j + 2, :], in_=wst[s][:]),
                  reads=[("c_wst", s)], writes=[("c_w", j)])
        else:
            P.add("dve", lambda e, j=j, s=s: e.tensor_copy(out=w_sb[:, 2 * j:2 * j + 2, :], in_=wst[s][:]),
                  reads=[("c_wst", s)], writes=[("c_w", j)])
    ogv = ogT.rearrange("(c p) t -> p c t", p=128)
    blk = 0
    for t in range(nt):
        so = t % 2
        P.add("sp", lambda e, t=t, so=so: e.dma_start(out=og_sb[so][:], in_=ogv[:, :, t * TT:(t + 1) * TT]),
              writes=[("c_og", so)], dma=True)
        for b in range(TT // 128):
            r0 = t * TT + b * 128
            sh = blk % 3
            P.add("sp", lambda e, r0=r0, sh=sh: e.dma_start(out=h_sb[sh][:], in_=h1[r0:r0 + 128, :]),
                  writes=[("c_h", sh)], dma=True)
            for g in range(2):
                pb = (blk * 2 + g) % 4
                for c in range(16):
                    P.add("pe", lambda e, pb=pb, so=so, b=b, c=c, g=g: e.matmul(
                        ps[pb][:], lhsT=og_sb[so][:, c, b * 128:(b + 1) * 128],
                        rhs=w_sb[:, c, g * 512:(g + 1) * 512], start=(c == 0), stop=(c == 15)),
                        reads=[("c_og", so), ("c_w", c // 2)], writes=[("c_ps", pb)])
                P.add("dve", lambda e, pb=pb, sh=sh, g=g: e.tensor_tensor(
                    out=y_sb[sh][:, g * 512:(g + 1) * 512], in0=ps[pb][:],
                    in1=h_sb[sh][:, g * 512:(g + 1) * 512], op=ALU.add),
                    reads=[("c_ps", pb), ("c_h", sh)], writes=[("c_y", sh, g)])
            P.add("sp", lambda e, r0=r0, sh=sh: e.dma_start(out=y[r0:r0 + 128, :], in_=y_sb[sh][:]),
                  reads=[("c_y", sh, 0), ("c_y", sh, 1)], writes=[("c_yout", blk)], dma=True)
            blk += 1


EPS = 1e-6


def make_identity(nc, st, P, name):
    identf = st.enter_context(nc.sbuf_tensor(name + "_f", [128, 128], F32))
    ident = st.enter_context(nc.sbuf_tensor(name, [128, 128], BF16))
    P.add("pool", lambda e: e.memset(identf[:], 1.0), writes=[name + "_f"])
    P.add("pool", lambda e: e.affine_select(out=identf[:], in_=identf[:], pattern=[[-1, 128]],
                                             compare_op=ALU.is_equal, fill=0.0, base=0,
                                             channel_multiplier=1),
          reads=[name + "_f"], writes=[name + "_f"])
    P.add("dve", lambda e: e.tensor_copy(out=ident[:], in_=identf[:]), reads=[name + "_f"], writes=[name])
    return ident, identf


def build_phase_a(nb=33):
    nc = bass.Bass("TRN2", target_bir_lowering=False)
    d = {}
    d["xa"] = nc.dram_tensor("xa", [nb * 128, 1024], F32, kind="ExternalInput").ap()
    d["xm"] = nc.dram_tensor("xm", [128, 1024], F32, kind="ExternalInput").ap()
    d["w_in"] = nc.dram_tensor("w_in", [1024, 2304], F32, kind="ExternalInput").ap()
    d["nw"] = nc.dram_tensor("nw", [128, 8], F32, kind="ExternalInput").ap()
    d["wq"] = nc.dram_tensor("wq", [128, 128], F32, kind="ExternalInput").ap()
    d["wk"] = nc.dram_tensor("wk", [128, 128], F32, kind="ExternalInput").ap()
    d["sinks"] = nc.dram_tensor("sinks", [128, 16], F32, kind="ExternalInput").ap()
    d["w_out"] = nc.dram_tensor("w_out", [1024, 1024], F32, kind="ExternalInput").ap()
    d["tabs"] = nc.dram_tensor("tabs", [128, 4, 2048], F32, kind="ExternalInput").ap()
    d["mtabs"] = nc.dram_tensor("mtabs", [16, 3, 2048], F32, kind="ExternalInput").ap()
    d["h1"] = nc.dram_tensor("h1", [(nb - 1) * 128, 1024], F32, kind="ExternalOutput").ap()
    d["xn1T"] = nc.dram_tensor("xn1T", [1024, 64 + (nb - 1) * 128], BF16, kind="ExternalOutput").ap()
    import contextlib
    with contextlib.ExitStack() as st:
        P = Prog(nc)
        emit_phase_a(nc, st, P, d, nb)
        P.emit()
    return nc


def emit_phase_a(nc, st, P, d, nb):
    sb = lambda name, shape, dt: st.enter_context(nc.sbuf_tensor(name, shape, dt))
    ident, identf = make_identity(nc, st, P, "a_ident")
    w_sb = sb("a_w", [128, 8, 2304], BF16)
    wo_sb = sb("a_wo", [128, 8, 1024], BF16)
    wst = [sb("a_wst%d" % i, [128, 2304], F32) for i in range(2)]
    nw = sb("a_nw", [128, 8], F32)
    wqk = sb("a_wqk", [128, 128], F32)
    wk_t = sb("a_wk", [128, 128], F32)
    esink = sb("a_esink", [128, 16], F32)
    tabs = sb("a_tabs", [128, 4, 2048], F32)
    mtabs = sb("a_mtabs", [16, 3, 2048], F32)
    x_sb = [sb("a_x%d" % i, [128, 1024], F32) for i in range(2)]
    junk = sb("a_junk", [128, 1024], F32)
    st_sb = sb("a_stat", [128, 8], F32)
    xn = sb("a_xn", [128, 1024], BF16)
    xnT = sb("a_xnT", [128, 8, 128], BF16)
    qss = sb("a_qss", [128, 16], F32)
    qr = sb("a_qr", [128, 16], F32)
    kss = sb("a_kss", [128, 4], F32)
    qn = sb("a_qn", [128, 1024], BF16)
    kn = sb("a_kn", [128, 128], F32)
    ksq = sb("a_ksq", [128, 128], F32)
    kq = sb("a_kq", [128, 128], BF16)
    gs = sb("a_gs", [128, 1024], BF16)
    QT = sb("a_QT", [128, 1024], BF16)
    KT = [sb("a_KT%d" % i, [128, 128], BF16) for i in range(2)]
    KTm = sb("a_KTm", [128, 128], BF16)
    vE = [sb("a_vE%d" % i, [128, 2, 65], BF16) for i in range(2)]
    vEm = sb("a_vEm", [128, 2, 65], BF16)
    E_sb = [sb("a_E%d" % i, [128, 512], F32) for i in range(2)]
    PTp = sb("a_PTp", [128, 16, 128], BF16)
    PTc = sb("a_PTc", [128, 16, 128], BF16)
    PTm = sb("a_PTm", [16, 16, 128], BF16)
    den = sb("a_den", [128, 16], F32)
    rden = sb("a_rden", [128, 16], F32)
    ogp = sb("a_ogp", [128, 1024], F32)
    og = sb("a_og", [128, 1024], BF16)
    ogT = sb("a_ogT", [128, 8, 128], BF16)
    h1 = [sb("a_h1%d" % i, [128, 1024], F32) for i in range(2)]
    xn1 = sb("a_xn1", [128, 1024], BF16)
    xn1T = [sb("a_xn1T%d" % i, [128, 8, 128], BF16) for i in range(2)]

    ps_t = st.enter_context(nc.psum_tensor("a_pst", [128, 8, 128], BF16))
    ps_u = [st.enter_context(nc.psum_tensor("a_psu%d" % i, [128, 512], F32)) for i in range(2)]
    ps_s = [st.enter_context(nc.psum_tensor("a_pss%d" % i, [128, 512], F32)) for i in range(2)]
    ps_o = st.enter_context(nc.psum_tensor("a_pso", [128, 3, 512], F32))

    P.add("sp", lambda e: e.dma_start(out=nw[:], in_=d["nw"]), writes=["a_nw"], dma=True)
    P.add("sp", lambda e: e.dma_start(out=wqk[:], in_=d["wq"]), writes=["a_wqk"], dma=True)
    P.add("sp", lambda e: e.dma_start(out=wk_t[:], in_=d["wk"]), writes=["a_wk"], dma=True)
    P.add("sp", lambda e: e.dma_start(out=esink[:], in_=d["sinks"]), writes=["a_esink"], dma=True)
    P.add("dve", lambda e: e.scalar_tensor_tensor(out=wqk[:], in0=wqk[:], scalar=8.0, in1=wk_t[:],
                                                  op0=ALU.mult, op1=ALU.mult),
          reads=["a_wqk", "a_wk"], writes=["a_wqk"])
    P.add("act", lambda e: e.activation(out=esink[:], in_=esink[:], func=AF.Exp), reads=["a_esink"], writes=["a_esink"])
    for i in range(2):
        P.add("pool", lambda e, i=i: e.memset(vE[i][:], 1.0), writes=[("a_vE", i)])
    P.add("pool", lambda e: e.memset(vEm[:], 1.0), writes=["a_vEm"])
    w_inv = d["w_in"].rearrange("(c p) n -> p c n", p=128)
    for c in range(8):
        s = c % 2
        P.add("sp", lambda e, c=c, s=s: e.dma_start(out=wst[s][:], in_=w_inv[:, c, :]), writes=[("a_wst", s)], dma=True)
        if c % 2 == 0:
            P.add("dve", lambda e, c=c, s=s: e.tensor_scalar(out=w_sb[:, c, :], in0=wst[s][:], scalar1=nw[:, c:c + 1],
                                                           scalar2=None, op0=ALU.mult),
                  reads=[("a_wst", s), "a_nw"], writes=[("a_w", c)])
        else:
            P.add("act", lambda e, c=c, s=s: e.activation(out=w_sb[:, c, :], in_=wst[s][:], func=AF.Copy, scale=nw[:, c:c + 1]),
                  reads=[("a_wst", s), "a_nw"], writes=[("a_w", c)])
    w_outv = d["w_out"].rearrange("(c p) n -> p c n", p=128)
    for c in range(4):
        s = c % 2
        P.add("sp", lambda e, c=c, s=s: e.dma_start(out=wst[s][:, 0:2048].rearrange("p (a n) -> p a n", a=2),
                                                   in_=w_outv[:, 2 * c:2 * c + 2, :]), writes=[("a_wst", s)], dma=True)
        P.add("dve" if c % 2 == 0 else "pool",
              lambda e, c=c, s=s: e.tensor_copy(out=wo_sb[:, 2 * c:2 * c + 2, :],
                                                in_=wst[s][:, 0:2048].rearrange("p (a n) -> p a n", a=2)),
              reads=[("a_wst", s)], writes=[("a_wo", c)])
    for j in range(4):
        P.add("sp", lambda e, j=j: e.dma_start(out=tabs[:, j, :], in_=d["tabs"][:, j, :]), writes=[("a_tabs", j)], dma=True)
    P.add("sp", lambda e: e.dma_start(out=mtabs[:], in_=d["mtabs"]), writes=["a_mtabs"], dma=True)
    W_ALL = [("a_w", c) for c in range(8)]
    WO_ALL = [("a_wo", c) for c in range(4)]

    def rstd_from_ss(ss_ap, ln_ap, out_ap, bias, key_in, key_out):
        P.add("act", lambda e: e.activation(out=ln_ap, in_=ss_ap, func=AF.Ln, bias=bias, scale=1.0),
              reads=[key_in], writes=[key_out + "_ln"])
        P.add("act", lambda e: e.activation(out=out_ap, in_=ln_ap, func=AF.Exp, scale=-0.5),
              reads=[key_out + "_ln"], writes=[key_out])

    def transposes8(src, src_key, dstT, dst_key, evac_eng):
        for c in range(8):
            P.add("pe", lambda e, c=c: e.transpose(out=ps_t[:, c, :], in_=src[:, c * 128:(c + 1) * 128], identity=ident[:]),
                  reads=[src_key, "a_ident"], writes=["a_pst"])
        if evac_eng == "act":
            P.add("act", lambda e: e.copy(out=dstT, in_=ps_t[:]), reads=["a_pst"], writes=[dst_key])
        else:
            P.add(evac_eng, lambda e: e.tensor_copy(out=dstT, in_=ps_t[:]), reads=["a_pst"], writes=[dst_key])

    def front(src_ap, xs, KT_dst, KT_key, vE_dst, vE_key, full):
        P.add("sp", lambda e: e.dma_start(out=x_sb[xs][:], in_=src_ap), writes=[("a_x", xs)], dma=True)
        P.add("act", lambda e: e.activation(out=junk[:], in_=x_sb[xs][:], func=AF.Square, accum_out=st_sb[:, 0:1]),
              reads=[("a_x", xs)], writes=["a_junk", "a_ss"])
        rstd_from_ss(st_sb[:, 0:1], st_sb[:, 1:2], st_sb[:, 2:3], 1024 * EPS, "a_ss", "a_rstd")
        P.add("dve", lambda e: e.tensor_scalar(out=xn[:], in0=x_sb[xs][:], scalar1=st_sb[:, 2:3], scalar2=32.0,
                                               op0=ALU.mult, op1=ALU.mult),
              reads=[("a_x", xs), "a_rstd"], writes=["a_xn"])
        transposes8(xn, "a_xn", xnT[:], "a_xnT", "act")
        groups = [(0, 512), (512, 512), (1024, 256), (1280, 512), (1792, 512)] if full else [(1024, 256)]
        for gi, (c0, cw) in enumerate(groups):
            pb = gi % 2
            for c in range(8):
                P.add("pe", lambda e, c=c, c0=c0, cw=cw, pb=pb: e.matmul(
                    ps_u[pb][:, 0:cw], lhsT=xnT[:, c, :], rhs=w_sb[:, c, c0:c0 + cw], start=(c == 0), stop=(c == 7)),
                    reads=["a_xnT"] + W_ALL, writes=[("a_psu", pb)])
            if c0 < 1024:
                h0 = c0 // 64
                P.add("act", lambda e, pb=pb: e.activation(out=junk[:, 0:512], in_=ps_u[pb][:], func=AF.Square),
                      reads=[("a_psu", pb)], writes=["a_junk"])
                P.add("dve", lambda e, h0=h0: e.tensor_reduce(out=qss[:, h0:h0 + 8], in_=junk[:, 0:512].rearrange("p (a b) -> p a b", a=8),
                                                            axis=AX.X, op=ALU.add),
                      reads=["a_junk"], writes=[("a_qss", h0)])
                rstd_from_ss(qss[:, h0:h0 + 8], qr[:, h0:h0 + 8], qr[:, h0:h0 + 8], 64 * EPS, ("a_qss", h0), "a_qr%d" % h0)
                P.add("dve", lambda e, pb=pb, h0=h0, c0=c0: e.tensor_tensor(
                    out=qn[:, c0:c0 + 512].rearrange("p (a b) -> p a b", a=8),
                    in0=ps_u[pb][:].rearrange("p (a b) -> p a b", a=8),
                    in1=qr[:, h0:h0 + 8].unsqueeze(2).to_broadcast([128, 8, 64]), op=ALU.mult),
                    reads=[("a_psu", pb), "a_qr%d" % h0], writes=[("a_qn", h0)])
            elif c0 == 1024:
                P.add("act", lambda e, pb=pb: e.activation(out=ksq[:], in_=ps_u[pb][:, 0:128], func=AF.Square),
                      reads=[("a_psu", pb)], writes=["a_junk2"])
                P.add("dve", lambda e: e.tensor_reduce(out=kss[:, 0:2], in_=ksq[:].rearrange("p (a b) -> p a b", a=2),
                                                       axis=AX.X, op=ALU.add),
                      reads=["a_junk2"], writes=["a_kss"])
                rstd_from_ss(kss[:, 0:2], kss[:, 2:4], kss[:, 2:4], 64 * EPS, "a_kss", "a_kr")
                P.add("dve", lambda e, pb=pb: e.tensor_tensor(
                    out=kn[:].rearrange("p (a b) -> p a b", a=2), in0=ps_u[pb][:, 0:128].rearrange("p (a b) -> p a b", a=2),
                    in1=kss[:, 2:4].unsqueeze(2).to_broadcast([128, 2, 64]), op=ALU.mult),
                    reads=[("a_psu", pb), "a_kr"], writes=["a_kn"])
                P.add("dve", lambda e: e.tensor_tensor(out=kq[:], in0=kn[:], in1=wqk[:], op=ALU.mult),
                      reads=["a_kn", "a_wqk"], writes=["a_kq"])
                P.add("act", lambda e, pb=pb: e.copy(out=vE_dst[:, :, 0:64], in_=ps_u[pb][:, 128:256].rearrange("p (a b) -> p a b", a=2)),
                      reads=[("a_psu", pb)], writes=[vE_key])
            else:
                g0 = c0 - 1280
                P.add("act", lambda e, pb=pb, g0=g0: e.activation(out=gs[:, g0:g0 + 512], in_=ps_u[pb][:], func=AF.Silu),
                      reads=[("a_psu", pb)], writes=[("a_gs", g0)])
        if full:
            for c in range(8):
                P.add("pe", lambda e, c=c: e.transpose(out=ps_t[:, c, :], in_=qn[:, c * 128:(c + 1) * 128], identity=ident[:]),
                      reads=[("a_qn", 0), ("a_qn", 8), "a_ident"], writes=["a_pst"])
            P.add("dve", lambda e: e.tensor_copy(out=QT[:].rearrange("p (a b) -> p a b", a=8), in_=ps_t[:]),
                  reads=["a_pst"], writes=["a_QT"])
        P.add("pe", lambda e: e.transpose(out=ps_t[:, 0, :], in_=kq[:], identity=ident[:]),
              reads=["a_kq", "a_ident"], writes=["a_pst"])
        P.add("act", lambda e: e.copy(out=KT_dst[:], in_=ps_t[:, 0, :]), reads=["a_pst"], writes=[KT_key])

    front(d["xm"], 0, KTm, "a_KTm", vEm, "a_vEm", full=False)

    for i in range(nb):
        cur = i % 2
        prv = 1 - cur
        front(d["xa"][i * 128:(i + 1) * 128, :], i % 2, KT[cur], ("a_KT", cur), vE[cur], ("a_vE", cur), full=True)
        if i == 0:
            chunks = [("cur", 0), ("meta", 0)]
        elif i == 1:
            chunks = [("prev", 1), ("cur", 3), ("meta", 1)]
        else:
            chunks = [("prev", 2), ("cur", 3), ("meta", 2)]
        n_e = 0
        for kind, tix in chunks:
            for g in range(2):
                for hf in range(2):
                    pb = n_e % 2
                    hh = g * 8 + hf * 4
                    if kind == "meta":
                        lhsT, lk, M = KTm[g * 64:(g + 1) * 64, 0:16], "a_KTm", 16
                    elif kind == "cur":
                        lhsT, lk, M = KT[cur][g * 64:(g + 1) * 64, :], ("a_KT", cur), 128
                    else:
                        lhsT, lk, M = KT[prv][g * 64:(g + 1) * 64, :], ("a_KT", prv), 128
                    P.add("pe", lambda e, pb=pb, lhsT=lhsT, g=g, hf=hf, M=M: e.matmul(
                        ps_s[pb][0:M, :], lhsT=lhsT, rhs=QT[g * 64:(g + 1) * 64, hf * 512:(hf + 1) * 512], start=True, stop=True),
                        reads=[lk, "a_QT"], writes=[("a_pss", pb)])
                    P.add("act", lambda e, pb=pb, M=M: e.activation(out=E_sb[pb][0:M, :], in_=ps_s[pb][0:M, :], func=AF.Exp),
                          reads=[("a_pss", pb)], writes=[("a_E", pb)])
                    if kind == "meta":
                        tab = mtabs[:, tix, hh * 128:(hh + 4) * 128]
                        dst = PTm[:, hh:hh + 4, :]
                        dk = ("a_PTm", hh)
                        tk = "a_mtabs"
                    else:
                        tab = tabs[:, tix, hh * 128:(hh + 4) * 128]
                        dst = (PTc if kind == "cur" else PTp)[:, hh:hh + 4, :]
                        dk = ("a_PTc" if kind == "cur" else "a_PTp", hh)
                        tk = ("a_tabs", tix)
                    P.add("dve" if n_e % 2 == 0 else "pool",
                          lambda e, pb=pb, M=M, tab=tab, dst=dst: e.tensor_tensor(
                              out=dst.rearrange("p a b -> p (a b)"), in0=E_sb[pb][0:M, :], in1=tab, op=ALU.mult),
                          reads=[("a_E", pb), tk], writes=[dk])
                    n_e += 1
        for h in range(16):
            g = h // 8
            hh = (h // 4) * 4
            bank, off = h // 7, (h % 7) * 65
            for ci, (kind, tix) in enumerate(chunks):
                if kind == "meta":
                    lhsT, lk = PTm[:, h, :], ("a_PTm", hh)
                    rhs, rk = vEm[0:16, g, :], "a_vEm"
                elif kind == "cur":
                    lhsT, lk = PTc[:, h, :], ("a_PTc", hh)
                    rhs, rk = vE[cur][:, g, :], ("a_vE", cur)
                else:
                    lhsT, lk = PTp[:, h, :], ("a_PTp", hh)
                    rhs, rk = vE[prv][:, g, :], ("a_vE", prv)
                P.add("pe", lambda e, bank=bank, off=off, lhsT=lhsT, rhs=rhs, ci=ci, n=len(chunks): e.matmul(
                    ps_o[:, bank, off:off + 65], lhsT=lhsT, rhs=rhs, start=(ci == 0), stop=(ci == n - 1)),
                    reads=[lk, rk], writes=[("a_pso", bank)])
        for bank, (h0, nh) in enumerate([(0, 7), (7, 7), (14, 2)]):
            ov = ps_o[:, bank, 0:nh * 65].rearrange("p (a b) -> p a b", b=65)
            P.add("dve", lambda e, ov=ov, h0=h0, nh=nh: e.tensor_tensor(
                out=den[:, h0:h0 + nh].unsqueeze(2), in0=ov[:, :, 64:65], in1=esink[:, h0:h0 + nh].unsqueeze(2), op=ALU.add),
                reads=[("a_pso", bank), "a_esink"], writes=[("a_den", bank)])
            P.add("dve", lambda e, h0=h0, nh=nh: e.reciprocal(out=rden[:, h0:h0 + nh], in_=den[:, h0:h0 + nh]),
                  reads=[("a_den", bank)], writes=[("a_rden", bank)])
            P.add("dve", lambda e, ov=ov, h0=h0, nh=nh: e.tensor_tensor(
                out=ogp[:, h0 * 64:(h0 + nh) * 64].rearrange("p (a b) -> p a b", b=64), in0=ov[:, :, 0:64],
                in1=rden[:, h0:h0 + nh].unsqueeze(2).to_broadcast([128, nh, 64]), op=ALU.mult),
                reads=[("a_pso", bank), ("a_rden", bank)], writes=[("a_ogp", bank)])
        P.add("pool", lambda e: e.tensor_tensor(out=og[:], in0=ogp[:], in1=gs[:], op=ALU.mult),
              reads=[("a_ogp", 0), ("a_ogp", 1), ("a_ogp", 2), ("a_gs", 0), ("a_gs", 512)], writes=["a_og"])
        transposes8(og, "a_og", ogT[:], "a_ogT", "act")
        hs = i % 2
        for g2 in range(2):
            pb = g2
            for c in range(8):
                P.add("pe", lambda e, c=c, g2=g2, pb=pb: e.matmul(
                    ps_u[pb][:], lhsT=ogT[:, c, :], rhs=wo_sb[:, c, g2 * 512:(g2 + 1) * 512], start=(c == 0), stop=(c == 7)),
                    reads=["a_ogT"] + WO_ALL, writes=[("a_psu", pb)])
            P.add("dve", lambda e, g2=g2, pb=pb, hs=hs, xs=i % 2: e.tensor_tensor(
                out=h1[hs][:, g2 * 512:(g2 + 1) * 512], in0=ps_u[pb][:], in1=x_sb[xs][:, g2 * 512:(g2 + 1) * 512], op=ALU.add),
                reads=[("a_psu", pb), ("a_x", i % 2)], writes=[("a_h1", hs, g2)])
        H1K = [("a_h1", hs, 0), ("a_h1", hs, 1)]
        if i >= 1:
            P.add("sp", lambda e, i=i, hs=hs: e.dma_start(out=d["h1"][(i - 1) * 128:i * 128, :], in_=h1[hs][:]),
                  reads=H1K, writes=[("a_h1out", i)], dma=True)
        P.add("act", lambda e, hs=hs: e.activation(out=junk[:], in_=h1[hs][:], func=AF.Square, accum_out=st_sb[:, 4:5]),
              reads=H1K, writes=["a_junk", "a_ss1"])
        rstd_from_ss(st_sb[:, 4:5], st_sb[:, 5:6], st_sb[:, 6:7], 1024 * EPS, "a_ss1", "a_rstd1")
        P.add("dve", lambda e, hs=hs: e.tensor_scalar(out=xn1[:], in0=h1[hs][:], scalar1=st_sb[:, 6:7], scalar2=32.0,
                                                      op0=ALU.mult, op1=ALU.mult),
              reads=H1K + ["a_rstd1"], writes=["a_xn1"])
        transposes8(xn1, "a_xn1", xn1T[hs][:], ("a_xn1T", hs), "act")
        xv = d["xn1T"].rearrange("(c p) t -> p c t", p=128)
        if i == 0:
            P.add("sp", lambda e, hs=hs: e.dma_start(out=xv[:, :, 0:64], in_=xn1T[hs][:, :, 64:128]),
                  reads=[("a_xn1T", hs)], writes=[("a_xn1out", i)], dma=True)
        else:
            P.add("sp", lambda e, hs=hs, i=i: e.dma_start(out=xv[:, :, 64 + (i - 1) * 128:64 + i * 128], in_=xn1T[hs][:]),
                  reads=[("a_xn1T", hs)], writes=[("a_xn1out", i)], dma=True)


def attn_tables(seg0):
    m = np.exp2(-8.0 * np.arange(1, 17) / 16.0)[None, :, None]
    j = np.arange(128)[:, None, None].astype(np.float64)
    i = np.arange(128)[None, None, :].astype(np.float64)
    t_prev = np.where(j > i, np.exp(-m * (128 + i - j)), 0.0)
    t_cur = np.where(j <= i, np.exp(-m * (i - j)), 0.0)
    jm = np.arange(16)[:, None, None].astype(np.float64)
    t_meta = np.exp(-m * 128.0) * np.ones((16, 16, 128))
    if seg0:
        t_cur0 = np.zeros_like(t_cur)
        t_prev1 = np.zeros_like(t_prev)
        pq = i - 112.0
        t_meta0 = np.where(pq >= jm, np.exp(-m * np.maximum(pq - jm, 0.0)), 0.0) * np.ones((16, 16, 128))
        t_meta1 = np.exp(-m * np.minimum(16.0 + i - jm, 128.0))
    else:
        t_cur0, t_prev1, t_meta0, t_meta1 = t_cur, t_prev, t_meta, t_meta
    tabs = np.stack([t_cur0, t_prev1, t_prev, t_cur], axis=1).reshape(128, 4, 2048).astype(np.float32)
    mtabs = np.stack([t_meta0, t_meta1, t_meta], axis=1).reshape(16, 3, 2048).astype(np.float32)
    return np.ascontiguousarray(tabs), np.ascontiguousarray(mtabs)


def attn_weight_layout(attn_norm_w, attn_w_in, attn_q_norm_w, attn_k_norm_w, attn_sinks, attn_w_out):
    w_in = np.asarray(attn_w_in[0], dtype=np.float32)
    perm = []
    for c in range(8):
        for half in range(2):
            h = c + 8 * half
            perm.extend(range(h * 64, (h + 1) * 64))
    perm = np.array(perm + list(range(1024, 2304)))
    out = {}
    out["w_in"] = np.ascontiguousarray(w_in[:, perm])
    out["nw"] = np.ascontiguousarray(np.asarray(attn_norm_w[0], np.float32).reshape(8, 128).T)
    out["wq"] = np.ascontiguousarray(np.broadcast_to(np.tile(np.asarray(attn_q_norm_w[0], np.float32), 2)[None, :], (128, 128)))
    out["wk"] = np.ascontiguousarray(np.broadcast_to(np.tile(np.asarray(attn_k_norm_w[0], np.float32), 2)[None, :], (128, 128)))
    out["sinks"] = np.ascontiguousarray(np.broadcast_to(np.asarray(attn_sinks[0], np.float32)[None, :], (128, 16)))
    out["w_out"] = np.ascontiguousarray(np.asarray(attn_w_out[0], np.float32))
    return out


NEG = -30000.0
DBG_STOP = 99
CB_U, CB_MUI, CB_MUS, CB_MLS, CB_BD, CB_B1L, CB_B1U, CB_B2L, CB_ONES, CB_I = 0, 64, 320, 576, 832, 1088, 1344, 1600, 1856, 1984
CB_COLS = 2048


def dn_consts():
    a = np.arange(64)[:, None]
    b = np.arange(64)[None, :]
    rep4 = lambda m: np.tile(m[:, None, :], (1, 4, 1)).reshape(64, 256)
    U = (a <= b).astype(np.float32)
    mui = np.where(a <= b, 0.0, NEG)
    mus = np.where(a < b, 0.0, NEG)
    mls = np.where(b < a, 0.0, NEG)
    bd = (a // 16 == b // 16).astype(np.float32)
    b1l = ((a // 32 == b // 32) & (a // 16 == b // 16 + 1)).astype(np.float32)
    b2l = ((a >= 32) & (b < 32)).astype(np.float32)
    c = np.zeros((64, CB_COLS), np.float32)
    c[:, CB_U:CB_U + 64] = U
    c[:, CB_MUI:CB_MUI + 256] = rep4(mui)
    c[:, CB_MUS:CB_MUS + 256] = rep4(mus)
    c[:, CB_MLS:CB_MLS + 256] = rep4(mls)
    c[:, CB_BD:CB_BD + 256] = rep4(bd)
    c[:, CB_B1L:CB_B1L + 256] = rep4(b1l)
    c[:, CB_B1U:CB_B1U + 256] = rep4(b1l.T)
    c[:, CB_B2L:CB_B2L + 256] = rep4(b2l)
    c[:, CB_ONES:CB_ONES + 128] = 1.0
    c[:, CB_I:CB_I + 64] = np.eye(64)
    return c


def build_phase_b(nchunks=257):
    nc = bass.Bass("TRN2", target_bir_lowering=False)
    TT = 64 * nchunks
    d = {}
    d["xnT"] = nc.dram_tensor("xnT", [1024, TT], BF16, kind="ExternalInput").ap()
    d["wB"] = nc.dram_tensor("wB", [1024, 1544], F32, kind="ExternalInput").ap()
    d["nwB"] = nc.dram_tensor("nwB", [128, 8], F32, kind="ExternalInput").ap()
    d["convw"] = nc.dram_tensor("convw", [128, 8, 4], F32, kind="ExternalInput").ap()
    d["onw"] = nc.dram_tensor("onw", [128, 1], F32, kind="ExternalInput").ap()
    d["alog"] = nc.dram_tensor("alog", [64, 4], F32, kind="ExternalInput").ap()
    d["dtb"] = nc.dram_tensor("dtb", [64, 4], F32, kind="ExternalInput").ap()
    d["cstB"] = nc.dram_tensor("cstB", [64, CB_COLS], F32, kind="ExternalInput").ap()
    d["ogT"] = nc.dram_tensor("ogT", [512, TT - 64], BF16, kind="ExternalOutput").ap()
    import contextlib
    with contextlib.ExitStack() as st:
        P = Prog(nc)
        emit_phase_b(nc, st, P, d, nchunks)
        P.emit()
    return nc


def emit_phase_b(nc, st, P, d, nchunks):
    sb = lambda name, shape, dt: st.enter_context(nc.sbuf_tensor(name, shape, dt))
    V = lambda fn, r, w: P.add("dve", fn, reads=r, writes=w)
    A = lambda fn, r, w: P.add("act", fn, reads=r, writes=w)
    G = lambda fn, r, w: P.add("pool", fn, reads=r, writes=w)
    T = lambda fn, r, w: P.add("pe", fn, reads=r, writes=w)
    D = lambda fn, r, w: P.add("sp", fn, reads=r, writes=w, dma=True)
    TM = 512
    ident, identf = make_identity(nc, st, P, "b_ident")
    w_sb = sb("b_w", [128, 8, 1544], BF16)
    wst = [sb("b_wst%d" % i, [128, 1544], F32) for i in range(2)]
    nw = sb("b_nw", [128, 8], F32)
    cw = sb("b_cw", [128, 8, 4], F32)
    onw = sb("b_onw", [128, 1], F32)
    negA = sb("b_negA", [64, 4], F32)
    dtb = sb("b_dtb", [64, 4], F32)
    cst = sb("b_cst", [64, CB_COLS], F32)
    ones_bf = sb("b_ones", [128, 128], BF16)
    xt = [sb("b_xt%d" % i, [128, 8, TM], BF16) for i in range(2)]
    u_sb = sb("b_u", [128, 8, TM + 3], F32)
    acc = [sb("b_acc%d" % i, [128, TM], F32) for i in range(2)]
    csil = sb("b_csil", [128, 4, TM], F32)
    ctmp = sb("b_ctmp", [128, TM], F32)
    sq = sb("b_sq", [128, TM], BF16)
    rs = sb("b_rs", [128, TM], F32)
    qkT = sb("b_qkT", [128, 4, TM], BF16)
    vT = sb("b_vT", [128, 4, TM], BF16)
    zs = sb("b_zs", [128, 4, TM], F32)
    o_sb = sb("b_o", [128, 4, TM], F32)
    ogt = sb("b_ogt", [128, 4, TM], BF16)
    S32 = sb("b_S32", [128, 4, 128], F32)
    Sb = sb("b_Sb", [128, 4, 128], BF16)
    ba = sb("b_ba", [64, 8, 8], F32)
    e1 = sb("b_e1", [64, 8, 8], F32)
    lnb = sb("b_lnb", [64, 8, 4], F32)
    beta = sb("b_beta", [64, 8, 4], F32)
    gg = sb("b_g", [64, 8, 4], F32)
    gb = sb("b_gb", [64, 4, 128], F32)
    lb = sb("b_lb", [64, 4, 64], F32)
    gc = sb("b_gc", [64, 4], F32)
    gcl = sb("b_gcl", [64, 4], F32)
    beg = sb("b_beg", [64, 4], F32)
    ekl = sb("b_ekl", [64, 4], F32)
    egl = sb("b_egl", [128, 4], F32)
    args = sb("b_args", [64, 3, 256], F32)
    Eall = sb("b_Eall", [64, 3, 256], F32)
    Eg = sb("b_Eg", [128, 256], F32)
    Nm = sb("b_N", [64, 256], F32)
    Mm = sb("b_M", [64, 256], F32)
    attnT = sb("b_attnT", [64, 256], BF16)
    L = [sb("b_L%d" % i, [64, 256], F32) for i in range(4)]
    Uu = [sb("b_U%d" % i, [64, 256], F32) for i in range(3)]
    O1 = sb("b_O1", [64, 256], F32)
    N1 = sb("b_N1", [64, 256], F32)
    O2 = sb("b_O2", [64, 256], F32)
    PU = [sb("b_PU%d" % i, [64, 256], F32) for i in range(2)]
    PL = [sb("b_PL%d" % i, [64, 256], F32) for i in range(2)]
    Y = [sb("b_Y%d" % i, [64, 256], F32) for i in range(2)]
    T32U = sb("b_T32U", [64, 256], F32)
    T32L = sb("b_T32L", [64, 256], F32)
    TTb = sb("b_TTb", [64, 256], BF16)
    wTn = sb("b_wTn", [128, 256], BF16)
    vb = sb("b_vb", [64, 4, 128], BF16)
    kbg = sb("b_kbg", [64, 4, 128], BF16)
    kst = sb("b_kst", [64, 4, 128], BF16)
    qg = sb("b_qg", [128, 256], BF16)
    vnb = sb("b_vnb", [64, 4, 128], BF16)
    kcp = sb("b_kcp", [128, 2, 64], BF16)

    psA = st.enter_context(nc.psum_tensor("b_psA", [128, 512], F32))
    psB = st.enter_context(nc.psum_tensor("b_psB", [128, 512], F32))
    psG = st.enter_context(nc.psum_tensor("b_psG", [128, 512], F32))
    psM = st.enter_context(nc.psum_tensor("b_psM", [128, 512], F32))
    psI = st.enter_context(nc.psum_tensor("b_psI", [128, 512], F32))
    psT = st.enter_context(nc.psum_tensor("b_psT", [128, 1024], BF16))
    psV = st.enter_context(nc.psum_tensor("b_psV", [128, 512], F32))
    psS = st.enter_context(nc.psum_tensor("b_psS", [128, 512], F32))

    cU = cst[:, CB_U:CB_U + 64]
    cI = cst[:, CB_I:CB_I + 64]
    cOnes = cst[:, CB_ONES:CB_ONES + 128]
    c4 = lambda o: cst[:, o:o + 256]

    for name, t_, src in [("b_nw", nw, "nwB"), ("b_cw", cw, "convw"), ("b_onw", onw, "onw"), ("b_negA", negA, "alog"),
                          ("b_dtb", dtb, "dtb"), ("b_cst", cst, "cstB")]:
        D(lambda e, t_=t_, src=src: e.dma_start(out=t_[:], in_=d[src]), [], [name])
    A(lambda e: e.activation(out=negA[:], in_=negA[:], func=AF.Exp), ["b_negA"], ["b_negA"])
    V(lambda e: e.tensor_scalar(out=negA[:], in0=negA[:], scalar1=-1.0, scalar2=None, op0=ALU.mult), ["b_negA"], ["b_negA"])
    G(lambda e: e.memset(ones_bf[:], 1.0), [], ["b_ones"])
    G(lambda e: e.memset(S32[:], 0.0), [], ["b_S32"])
    G(lambda e: e.memset(Sb[:], 0.0), [], ["b_Sb"])
    G(lambda e: e.memset(u_sb[:], 0.0), [], ["b_u"])
    wv = d["wB"].rearrange("(c p) n -> p c n", p=128)
    for c in range(8):
        s = c % 2
        D(lambda e, c=c, s=s: e.dma_start(out=wst[s][:], in_=wv[:, c, :]), [], [("b_wst", s)])
        if c % 2 == 0:
            V(lambda e, c=c, s=s: e.tensor_scalar(out=w_sb[:, c, :], in0=wst[s][:], scalar1=nw[:, c:c + 1], scalar2=None, op0=ALU.mult),
              [("b_wst", s), "b_nw"], ["b_w"])
        else:
            A(lambda e, c=c, s=s: e.activation(out=w_sb[:, c, :], in_=wst[s][:], func=AF.Copy, scale=nw[:, c:c + 1]),
              [("b_wst", s), "b_nw"], ["b_w"])

    def rsqrt_act(out_ap, in_ap, scale, bias_ln, bias_exp, rkeys, wkey):
        A(lambda e: e.activation(out=out_ap, in_=in_ap, func=AF.Ln, bias=bias_ln, scale=scale), rkeys, [wkey])
        A(lambda e: e.activation(out=out_ap, in_=out_ap, func=AF.Exp, scale=-0.5, bias=bias_exp), [wkey], [wkey])

    def tile(ti, t0, TW, need_o):
        xs = ti % 2
        nck = TW // 64
        xv = d["xnT"].rearrange("(c p) t -> p c t", p=128)
        D(lambda e: e.dma_start(out=xt[xs][:, :, 0:TW], in_=xv[:, :, t0:t0 + TW]), [], [("b_xt", xs)])
        for m in range(8):
            ps = psA if m % 2 == 0 else psB
            pk = "b_psA" if m % 2 == 0 else "b_psB"
            pkw = [pk]
            for c in range(8):
                T(lambda e, c=c, m=m, ps=ps: e.matmul(ps[:, 0:TW], lhsT=w_sb[:, c, m * 128:(m + 1) * 128], rhs=xt[xs][:, c, 0:TW],
                                                      start=(c == 0), stop=(c == 7)), ["b_w", ("b_xt", xs)], pkw)
            A(lambda e, m=m, ps=ps: e.copy(out=u_sb[:, m, 3:3 + TW], in_=ps[:, 0:TW]), [pk], [("b_u", m)])
            ac = acc[m % 2]
            ak = ("b_acc", m % 2)
            eng = V if m % 2 == 0 else G
            eng(lambda e, m=m, ac=ac: e.tensor_scalar(out=ac[:, 0:TW], in0=u_sb[:, m, 3:3 + TW], scalar1=cw[:, m, 3:4], scalar2=None,
                                                     op0=ALU.mult), [("b_u", m), "b_cw"], [ak])
            for j in (2, 1, 0):
                if m % 2 == 0:
                    V(lambda e, m=m, ac=ac, j=j: e.scalar_tensor_tensor(out=ac[:, 0:TW], in0=u_sb[:, m, j:j + TW], scalar=cw[:, m, j:j + 1],
                                                                       in1=ac[:, 0:TW], op0=ALU.mult, op1=ALU.add),
                      [("b_u", m), "b_cw", ak], [ak])
                else:
                    G(lambda e, m=m, j=j: e.tensor_scalar(out=ctmp[:, 0:TW], in0=u_sb[:, m, j:j + TW], scalar1=cw[:, m, j:j + 1], scalar2=None,
                                                          op0=ALU.mult), [("b_u", m), "b_cw"], ["b_ctmp"])
                    G(lambda e, ac=ac: e.tensor_tensor(out=ac[:, 0:TW], in0=ac[:, 0:TW], in1=ctmp[:, 0:TW], op=ALU.add), ["b_ctmp", ak], [ak])
            if m < 4:
                A(lambda e, m=m, ac=ac: e.activation(out=csil[:, m, 0:TW], in_=ac[:, 0:TW], func=AF.Silu), [ak], [("b_csil", m)])
            else:
                A(lambda e, m=m, ac=ac: e.activation(out=vT[:, m - 4, 0:TW], in_=ac[:, 0:TW], func=AF.Silu), [ak], [("b_vT", m - 4)])
            eng(lambda e, m=m: e.tensor_copy(out=u_sb[:, m, 0:3], in_=u_sb[:, m, TW:TW + 3]), [("b_u", m)], [("b_u", m)])
        if DBG_STOP <= 1:
            return
        for m in range(4):
            A(lambda e, m=m: e.activation(out=sq[:, 0:TW], in_=csil[:, m, 0:TW], func=AF.Square), [("b_csil", m)], ["b_sq"])
            T(lambda e: e.matmul(psA[:, 0:TW], lhsT=ones_bf[:], rhs=sq[:, 0:TW], start=True, stop=True), ["b_sq", "b_ones"], ["b_psA"])
            rsqrt_act(rs[:, 0:TW], psA[:, 0:TW], 1.0, EPS, (-0.5 * float(np.log(128.0))) if m < 2 else 0.0, ["b_psA"], "b_rs")
            V(lambda e, m=m: e.tensor_tensor(out=qkT[:, m, 0:TW], in0=csil[:, m, 0:TW], in1=rs[:, 0:TW], op=ALU.mult),
              [("b_csil", m), "b_rs"], [("b_qkT", m)])
        if DBG_STOP <= 2:
            return
        if need_o:
            for h in range(4):
                ps = psA if h % 2 == 0 else psB
                pk = "b_psA" if h % 2 == 0 else "b_psB"
                pkw = [pk]
                for c in range(8):
                    T(lambda e, c=c, h=h, ps=ps: e.matmul(ps[:, 0:TW], lhsT=w_sb[:, c, 1024 + h * 128:1024 + (h + 1) * 128],
                                                          rhs=xt[xs][:, c, 0:TW], start=(c == 0), stop=(c == 7)),
                      ["b_w", ("b_xt", xs)], pkw)
                A(lambda e, h=h, ps=ps: e.activation(out=zs[:, h, 0:TW], in_=ps[:, 0:TW], func=AF.Silu), [pk], [("b_zs", h)])
                G(lambda e, h=h: e.tensor_scalar(out=zs[:, h, 0:TW], in0=zs[:, h, 0:TW], scalar1=onw[:, 0:1], scalar2=None, op0=ALU.mult),
                  [("b_zs", h), "b_onw"], [("b_zs", h)])
        if DBG_STOP <= 3:
            return
        bav = psM[0:64, 64:128].rearrange("p (a b) -> p a b", b=8)
        for ck in range(nck):
            for c in range(8):
                T(lambda e, c=c, ck=ck: e.matmul(bav[:, ck, :], lhsT=xt[xs][:, c, ck * 64:(ck + 1) * 64], rhs=w_sb[:, c, 1536:1544],
                                                 start=(c == 0), stop=(c == 7)), ["b_w", ("b_xt", xs)], ["b_psM"])
        V(lambda e: e.tensor_copy(out=ba[:, 0:nck, :], in_=bav[:, 0:nck, :]), ["b_psM"], ["b_ba"])
        V(lambda e: e.tensor_tensor(out=ba[:, 0:nck, 4:8], in0=ba[:, 0:nck, 4:8], in1=dtb[:].unsqueeze(1).to_broadcast([64, nck, 4]), op=ALU.add),
          ["b_ba", "b_dtb"], ["b_ba"])
        A(lambda e: e.activation(out=e1[:, 0:nck, 0:4], in_=ba[:, 0:nck, 0:4], func=AF.Exp, scale=-1.0), ["b_ba"], ["b_e1"])
        A(lambda e: e.activation(out=e1[:, 0:nck, 4:8], in_=ba[:, 0:nck, 4:8], func=AF.Exp), ["b_ba", "b_e1"], ["b_e1"])
        A(lambda e: e.activation(out=e1[:, 0:nck, :], in_=e1[:, 0:nck, :], func=AF.Ln, bias=1.0, scale=1.0), ["b_e1"], ["b_e1"])
        V(lambda e: e.tensor_scalar(out=lnb[:, 0:nck, :], in0=e1[:, 0:nck, 0:4], scalar1=-1.0, scalar2=None, op0=ALU.mult), ["b_e1"], ["b_lnb"])
        A(lambda e: e.activation(out=beta[:, 0:nck, :], in_=lnb[:, 0:nck, :], func=AF.Exp), ["b_lnb"], ["b_beta"])
        V(lambda e: e.tensor_tensor(out=gg[:, 0:nck, :], in0=e1[:, 0:nck, 4:8], in1=negA[:].unsqueeze(1).to_broadcast([64, nck, 4]), op=ALU.mult),
          ["b_e1", "b_negA"], ["b_g"])
        if DBG_STOP <= 4:
            return
        for ck in range(nck):
            chunk(ti, xs, ck, need_o)
        if need_o:
            for h in range(4):
                A(lambda e, h=h: e.activation(out=sq[:, 0:TW], in_=o_sb[:, h, 0:TW], func=AF.Square), [("b_o", h)], ["b_sq"])
                T(lambda e: e.matmul(psA[:, 0:TW], lhsT=ones_bf[:], rhs=sq[:, 0:TW], start=True, stop=True), ["b_sq", "b_ones"], ["b_psA"])
                rsqrt_act(rs[:, 0:TW], psA[:, 0:TW], 1.0 / 128.0, EPS, 0.0, ["b_psA"], "b_rs")
                V(lambda e, h=h: e.tensor_tensor(out=o_sb[:, h, 0:TW], in0=o_sb[:, h, 0:TW], in1=rs[:, 0:TW], op=ALU.mult),
                  [("b_o", h), "b_rs"], [("b_o", h)])
                G(lambda e, h=h: e.tensor_tensor(out=ogt[:, h, 0:TW], in0=o_sb[:, h, 0:TW], in1=zs[:, h, 0:TW], op=ALU.mult),
                  [("b_o", h), ("b_zs", h)], [("b_ogt", h)])
            ov = d["ogT"].rearrange("(h p) t -> p h t", p=128)
            D(lambda e: e.dma_start(out=ov[:, :, t0 - 64:t0 - 64 + TW], in_=ogt[:, :, 0:TW]), [("b_ogt", h) for h in range(4)], [("b_ogout", ti)])

    def chunk(ti, xs, ck, need_o):
        c0 = ck * 64
        g_ck = gg[:, ck, :]
        G(lambda e: e.tensor_copy(out=gb[:], in_=g_ck.unsqueeze(2).to_broadcast([64, 4, 128])), ["b_g"], ["b_gb"])
        G(lambda e: e.tensor_copy(out=lb[:], in_=lnb[:, ck, :].unsqueeze(2).to_broadcast([64, 4, 64])), ["b_lnb"], ["b_lb"])
        Gp = psG[:, 0:256]
        GBp = psG[0:64, 256:512]
        for h in range(4):
            T(lambda e, h=h: e.matmul(Gp[:, h * 64:(h + 1) * 64], lhsT=gb[:, h, :], rhs=cU, start=True, stop=True),
              ["b_gb", "b_cst"], ["b_psG"])
        for h in range(4):
            T(lambda e, h=h: e.matmul(GBp[:, h * 64:(h + 1) * 64], lhsT=gb[:, h, 0:64], rhs=cU, start=True, stop=False),
              ["b_gb", "b_cst"], ["b_psG"])
            T(lambda e, h=h: e.matmul(GBp[:, h * 64:(h + 1) * 64], lhsT=lb[:, h, :], rhs=cI, start=False, stop=True),
              ["b_lb", "b_cst"], ["b_psG"])
        gcol = psM[0:64, 0:4]
        glast = psM[:, 4:8]
        T(lambda e: e.matmul(gcol, lhsT=cU, rhs=g_ck, start=True, stop=True), ["b_g", "b_cst"], ["b_psM"])
        T(lambda e: e.matmul(glast, lhsT=cOnes, rhs=g_ck, start=True, stop=True), ["b_g", "b_cst"], ["b_psM"])
        V(lambda e: e.tensor_copy(out=gc[:], in_=gcol), ["b_psM"], ["b_gc"])
        V(lambda e: e.tensor_tensor(out=gcl[:], in0=gc[:], in1=lnb[:, ck, :], op=ALU.add), ["b_gc", "b_lnb"], ["b_gcl"])
        V(lambda e: e.tensor_tensor(out=ekl[:], in0=glast[0:64, :], in1=gc[:], op=ALU.subtract), ["b_psM", "b_gc"], ["b_ekl"])
        A(lambda e: e.activation(out=ekl[:], in_=ekl[:], func=AF.Exp), ["b_ekl"], ["b_ekl"])
        A(lambda e: e.activation(out=beg[:], in_=gcl[:], func=AF.Exp), ["b_gcl"], ["b_beg"])
        A(lambda e: e.activation(out=egl[:], in_=glast, func=AF.Exp), ["b_psM"], ["b_egl"])
        if DBG_STOP <= 5:
            return
        bc = lambda t_: t_[:].unsqueeze(2).to_broadcast([64, 4, 64])
        a3 = lambda i: args[:, i, :].rearrange("p (a b) -> p a b", a=4)
        p3 = lambda ap: ap.rearrange("p (a b) -> p a b", a=4)
        V(lambda e: e.tensor_tensor(out=a3(0), in0=p3(Gp[0:64, :]), in1=bc(gc), op=ALU.subtract), ["b_psG", "b_gc"], [("b_args", 0)])
        V(lambda e: e.tensor_tensor(out=a3(1), in0=p3(GBp), in1=bc(gc), op=ALU.subtract), ["b_psG", "b_gc"], [("b_args", 1)])
        V(lambda e: e.tensor_tensor(out=a3(2), in0=p3(Gp[0:64, :]), in1=bc(gcl), op=ALU.subtract), ["b_psG", "b_gcl"], [("b_args", 2)])
        G(lambda e: e.tensor_tensor(out=args[:, 0, :], in0=args[:, 0, :], in1=c4(CB_MUI), op=ALU.add), [("b_args", 0), "b_cst"], [("b_args", 0)])
        G(lambda e: e.tensor_tensor(out=args[:, 1, :], in0=args[:, 1, :], in1=c4(CB_MUS), op=ALU.add), [("b_args", 1), "b_cst"], [("b_args", 1)])
        V(lambda e: e.scalar_tensor_tensor(out=args[:, 2, :], in0=args[:, 2, :], scalar=-1.0, in1=c4(CB_MLS), op0=ALU.mult, op1=ALU.add),
          [("b_args", 2), "b_cst"], [("b_args", 2)])
        A(lambda e: e.activation(out=Eall[:], in_=args[:], func=AF.Exp), [("b_args", 0), ("b_args", 1), ("b_args", 2)], ["b_Eall"])
        if need_o:
            A(lambda e: e.activation(out=Eg[:], in_=Gp, func=AF.Exp), ["b_psG"], ["b_Eg"])
        if DBG_STOP <= 6:
            return
        KQ = psM[0:64, 128:384].rearrange("p (a b c) -> p a b c", a=2, b=2)
        G(lambda e: e.tensor_copy(out=kcp[:], in_=qkT[:, 2:4, c0:c0 + 64]), [("b_qkT", 2), ("b_qkT", 3)], ["b_kcp"])
        for hk in range(2):
            kch = qkT[:, 2 + hk, c0:c0 + 64]
            T(lambda e, hk=hk, kch=kch: e.matmul(KQ[:, 0, hk, :], lhsT=kch, rhs=kcp[:, hk, :], start=True, stop=True),
              [("b_qkT", 2 + hk), "b_kcp"], ["b_psM"])
            if need_o:
                T(lambda e, hk=hk, kch=kch: e.matmul(KQ[:, 1, hk, :], lhsT=kch, rhs=qkT[:, hk, c0:c0 + 64], start=True, stop=True),
                  [("b_qkT", 2 + hk), ("b_qkT", hk)], ["b_psM"])
        if DBG_STOP == 65:
            return
        pair = lambda ap: ap.unsqueeze(2).to_broadcast([64, 2, 2, 64])
        o4 = lambda t_: t_.rearrange("p (a b c) -> p a b c", a=2, b=2)
        for j in range(2):
            V(lambda e, j=j: e.tensor_tensor(out=o4(Nm[:])[:, :, j, :], in0=KQ[:, 0, :, :], in1=o4(Eall[:, 1, :])[:, :, j, :], op=ALU.mult),
              ["b_psM", "b_Eall"], ["b_N"])
            V(lambda e, j=j: e.tensor_tensor(out=o4(Mm[:])[:, :, j, :], in0=KQ[:, 0, :, :], in1=o4(Eall[:, 2, :])[:, :, j, :], op=ALU.mult),
              ["b_psM", "b_Eall"], ["b_M"])
            if need_o:
                V(lambda e, j=j: e.tensor_tensor(out=o4(attnT[:])[:, :, j, :], in0=KQ[:, 1, :, :], in1=o4(Eall[:, 0, :])[:, :, j, :], op=ALU.mult),
                  ["b_psM", "b_Eall"], ["b_attnT"])
        if DBG_STOP <= 7:
            return
        G(lambda e: e.tensor_tensor(out=L[0][:], in0=Mm[:], in1=c4(CB_BD), op=ALU.mult), ["b_M", "b_cst"], [("b_L", 0)])
        G(lambda e: e.tensor_tensor(out=Uu[0][:], in0=Nm[:], in1=c4(CB_BD), op=ALU.mult), ["b_N", "b_cst"], [("b_U", 0)])
        G(lambda e: e.tensor_tensor(out=O1[:], in0=Mm[:], in1=c4(CB_B1L), op=ALU.mult), ["b_M", "b_cst"], ["b_O1"])
        G(lambda e: e.tensor_tensor(out=N1[:], in0=Nm[:], in1=c4(CB_B1U), op=ALU.mult), ["b_N", "b_cst"], ["b_N1"])
        G(lambda e: e.tensor_tensor(out=O2[:], in0=Mm[:], in1=c4(CB_B2L), op=ALU.mult), ["b_M", "b_cst"], ["b_O2"])
        I4 = cI.unsqueeze(1).to_broadcast([64, 4, 64])
        V(lambda e: e.tensor_tensor(out=p3(PU[0][:]), in0=I4, in1=p3(Uu[0][:]), op=ALU.subtract), ["b_cst", ("b_U", 0)], [("b_PU", 0)])
        V(lambda e: e.tensor_tensor(out=p3(PL[0][:]), in0=I4, in1=p3(L[0][:]), op=ALU.subtract), ["b_cst", ("b_L", 0)], [("b_PL", 0)])
        if DBG_STOP == 71:
            return
        slot = [0]

        def mm4(lhs, lk, rhs, rk):
            s = slot[0] % 2
            slot[0] += 1
            pv = psI[0:64, s * 256:(s + 1) * 256]
            for h in range(4):
                T(lambda e, h=h: e.matmul(pv[:, h * 64:(h + 1) * 64], lhsT=lhs[:, h * 64:(h + 1) * 64], rhs=rhs[:, h * 64:(h + 1) * 64],
                                          start=True, stop=True), [lk, rk], ["b_psI"])
            return pv, "b_psI"

        if DBG_STOP == 72:
            return
        pcur = 0
        for k in range(3):
            pv, pk = mm4(Uu[k], ("b_U", k), L[k], ("b_L", k))
            A(lambda e, pv=pv, k=k: e.copy(out=L[k + 1][:], in_=pv), [pk], [("b_L", k + 1)])
            if k < 2:
                pv, pk = mm4(L[k], ("b_L", k), Uu[k], ("b_U", k))
                V(lambda e, pv=pv, k=k: e.tensor_copy(out=Uu[k + 1][:], in_=pv), [pk], [("b_U", k + 1)])
            nxt = 1 - pcur
            pv, pk = mm4(L[k + 1], ("b_L", k + 1), PU[pcur], ("b_PU", pcur))
            V(lambda e, pv=pv, pcur=pcur, nxt=nxt: e.tensor_tensor(out=PU[nxt][:], in0=pv, in1=PU[pcur][:], op=ALU.add),
              [pk, ("b_PU", pcur)], [("b_PU", nxt)])
            if k < 2:
                pv, pk = mm4(Uu[k + 1], ("b_U", k + 1), PL[pcur], ("b_PL", pcur))
                V(lambda e, pv=pv, pcur=pcur, nxt=nxt: e.tensor_tensor(out=PL[nxt][:], in0=pv, in1=PL[pcur][:], op=ALU.add),
                  [pk, ("b_PL", pcur)], [("b_PL", nxt)])
            else:
                pv, pk = mm4(L[2], ("b_L", 2), Uu[2], ("b_U", 2))
                A(lambda e, pv=pv: e.copy(out=Y[0][:], in_=pv), [pk], [("b_Y", 0)])
                pv, pk = mm4(Y[0], ("b_Y", 0), PL[pcur], ("b_PL", pcur))
                V(lambda e, pv=pv, pcur=pcur, nxt=nxt: e.tensor_tensor(out=PL[nxt][:], in0=pv, in1=PL[pcur][:], op=ALU.add),
                  [pk, ("b_PL", pcur)], [("b_PL", nxt)])
            pcur = nxt
        if DBG_STOP == 73:
            return
        TdU, TdUk, TdL, TdLk = PU[pcur], ("b_PU", pcur), PL[pcur], ("b_PL", pcur)
        pv, pk = mm4(O1, "b_O1", TdU, TdUk)
        A(lambda e, pv=pv: e.copy(out=Y[0][:], in_=pv), [pk], [("b_Y", 0)])
        pv, pk = mm4(TdL, TdLk, Y[0], ("b_Y", 0))
        V(lambda e, pv=pv: e.tensor_tensor(out=T32U[:], in0=TdU[:], in1=pv, op=ALU.subtract), [pk, TdUk], ["b_T32U"])
        if DBG_STOP == 74:
            return
        pv, pk = mm4(N1, "b_N1", TdL, TdLk)
        A(lambda e, pv=pv: e.copy(out=Y[1][:], in_=pv), [pk], [("b_Y", 1)])
        pv, pk = mm4(TdU, TdUk, Y[1], ("b_Y", 1))
        V(lambda e, pv=pv: e.tensor_tensor(out=T32L[:], in0=TdL[:], in1=pv, op=ALU.subtract), [pk, TdLk], ["b_T32L"])
        if DBG_STOP == 75:
            return
        pv, pk = mm4(O2, "b_O2", T32U, "b_T32U")
        A(lambda e, pv=pv: e.copy(out=Y[0][:], in_=pv), [pk], [("b_Y", 0)])
        pv, pk = mm4(T32L, "b_T32L", Y[0], ("b_Y", 0))
        V(lambda e, pv=pv: e.tensor_tensor(out=TTb[:], in0=T32U[:], in1=pv, op=ALU.subtract), [pk, "b_T32U"], ["b_TTb"])
        if DBG_STOP <= 8:
            return
        tv = psT[0:64, 0:768].rearrange("p (a b) -> p a b", a=6)
        for hk in range(2):
            T(lambda e, hk=hk: e.transpose(out=tv[:, hk, :], in_=qkT[:, 2 + hk, c0:c0 + 64], identity=ident[:]),
              [("b_qkT", 2 + hk), "b_ident"], ["b_psT"])
        for h in range(4):
            T(lambda e, h=h: e.transpose(out=tv[:, 2 + h, :], in_=vT[:, h, c0:c0 + 64], identity=ident[:]), [("b_vT", h), "b_ident"], ["b_psT"])
        bc128 = lambda ap: ap.unsqueeze(2).to_broadcast([64, 4, 128])
        kpair = tv[:, 0:2, :].unsqueeze(2).to_broadcast([64, 2, 2, 128])
        k4 = lambda t_: t_[:].rearrange("p (a b) c -> p a b c", a=2)
        s4 = lambda ap: ap.rearrange("p (a b) -> p a b", a=2).unsqueeze(3).to_broadcast([64, 2, 2, 128])
        V(lambda e: e.tensor_tensor(out=vb[:], in0=tv[:, 2:6, :], in1=bc128(beta[:, ck, :]), op=ALU.mult), ["b_psT", "b_beta"], ["b_vb"])
        V(lambda e: e.tensor_tensor(out=k4(kbg), in0=kpair, in1=s4(beg[:]), op=ALU.mult), ["b_psT", "b_beg"], ["b_kbg"])
        V(lambda e: e.tensor_tensor(out=k4(kst), in0=kpair, in1=s4(ekl[:]), op=ALU.mult), ["b_psT", "b_ekl"], ["b_kst"])
        if DBG_STOP <= 9:
            return
        wTp = psB[:, 0:256]
        for h in range(4):
            T(lambda e, h=h: e.matmul(wTp[:, h * 64:(h + 1) * 64], lhsT=kbg[:, h, :], rhs=TTb[:, h * 64:(h + 1) * 64], start=True, stop=True),
              ["b_kbg", "b_TTb"], ["b_psB"])
        A(lambda e: e.activation(out=wTn[:], in_=wTp, func=AF.Copy, scale=-1.0), ["b_psB"], ["b_wTn"])
        if DBG_STOP <= 10:
            return
        vp = psV[0:64, :].rearrange("p (a b) -> p a b", a=4)
        for h in range(4):
            T(lambda e, h=h: e.matmul(vp[:, h, :], lhsT=TTb[:, h * 64:(h + 1) * 64], rhs=vb[:, h, :], start=True, stop=False),
              ["b_TTb", "b_vb"], ["b_psV"])
            T(lambda e, h=h: e.matmul(vp[:, h, :], lhsT=wTn[:, h * 64:(h + 1) * 64], rhs=Sb[:, h, :], start=False, stop=True),
              ["b_wTn", "b_Sb"], ["b_psV"])
        A(lambda e: e.copy(out=vnb[:], in_=vp), ["b_psV"], ["b_vnb"])
        if DBG_STOP <= 11:
            return
        if need_o:
            qpair = qkT[:, 0:2, c0:c0 + 64].unsqueeze(2).to_broadcast([128, 2, 2, 64])
            G(lambda e: e.tensor_copy(out=qg[:].rearrange("p (a b c) -> p a b c", a=2, b=2), in_=qpair), [("b_qkT", 0), ("b_qkT", 1)], ["b_qg"])
            G(lambda e: e.tensor_tensor(out=qg[:], in0=qg[:], in1=Eg[:], op=ALU.mult), ["b_qg", "b_Eg"], ["b_qg"])
            oTp = psB[:, 256:512]
            for h in range(4):
                T(lambda e, h=h: e.matmul(oTp[:, h * 64:(h + 1) * 64], lhsT=Sb[:, h, :], rhs=qg[:, h * 64:(h + 1) * 64], start=True, stop=False),
                  ["b_Sb", "b_qg"], ["b_psB"])
                T(lambda e, h=h: e.matmul(oTp[:, h * 64:(h + 1) * 64], lhsT=vnb[:, h, :], rhs=attnT[:, h * 64:(h + 1) * 64], start=False, stop=True),
                  ["b_vnb", "b_attnT"], ["b_psB"])
            A(lambda e: e.copy(out=o_sb[:, :, c0:c0 + 64], in_=oTp.rearrange("p (a b) -> p a b", a=4)), ["b_psB"],
              [("b_o", h) for h in range(4)])
        if DBG_STOP <= 12:
            return
        sp_ = psS[:].rearrange("p (a b) -> p a b", a=4)
        for h in range(4):
            T(lambda e, h=h: e.matmul(sp_[:, h, :], lhsT=kst[:, h, :], rhs=vnb[:, h, :], start=True, stop=True), ["b_kst", "b_vnb"], ["b_psS"])
        G(lambda e: e.tensor_tensor(out=S32[:], in0=S32[:], in1=egl[:].unsqueeze(2).to_broadcast([128, 4, 128]), op=ALU.mult),
          ["b_S32", "b_egl"], ["b_S32"])
        V(lambda e: e.tensor_tensor(out=S32[:], in0=S32[:], in1=sp_, op=ALU.add), ["b_S32", "b_psS"], ["b_S32"])
        A(lambda e: e.copy(out=Sb[:], in_=S32[:]), ["b_S32"], ["b_Sb"])

    tile(0, 0, 64, False)
    ntile = (nchunks - 1) // 8
    assert ntile * 8 + 1 == nchunks
    for ti in range(ntile):
        tile(ti + 1, 64 + ti * TM, TM, True)


def dn_weight_layout(r, dn_norm_w, dn_w_in, dn_conv_w, dn_a_log, dn_dt_bias, dn_o_norm_w):
    w = np.asarray(dn_w_in[0], np.float32)
    qc = list(range(2 * r * 128, (2 * r + 2) * 128))
    kc = [1024 + c for c in qc]
    vc = list(range(2048 + 4 * r * 128, 2048 + (4 * r + 4) * 128))
    zc = list(range(4096 + 4 * r * 128, 4096 + (4 * r + 4) * 128))
    bcol = list(range(6144 + 4 * r, 6144 + 4 * r + 4))
    acol = list(range(6160 + 4 * r, 6160 + 4 * r + 4))
    cols = qc + kc + vc + zc + bcol + acol
    out = {}
    out["wB"] = np.ascontiguousarray(w[:, cols])
    out["nwB"] = np.ascontiguousarray(np.asarray(dn_norm_w[0], np.float32).reshape(8, 128).T)
    cwf = np.asarray(dn_conv_w[0], np.float32)[:, qc + kc + vc]
    out["convw"] = np.ascontiguousarray(cwf.reshape(4, 8, 128).transpose(2, 1, 0))
    out["onw"] = np.ascontiguousarray(np.asarray(dn_o_norm_w[0], np.float32).reshape(128, 1))
    out["alog"] = np.ascontiguousarray(np.broadcast_to(np.asarray(dn_a_log[0], np.float32)[None, 4 * r:4 * r + 4], (64, 4)))
    out["dtb"] = np.ascontiguousarray(np.broadcast_to(np.asarray(dn_dt_bias[0], np.float32)[None, 4 * r:4 * r + 4], (64, 4)))
    out["cstB"] = dn_consts()
    return out


_NC_CACHE = {}


def _get_nc(name, builder):
    if name not in _NC_CACHE:
        _NC_CACHE[name] = builder()
    return _NC_CACHE[name]


def kernel(x, meta_tokens, attn_norm_w, attn_w_in, attn_q_norm_w, attn_k_norm_w, attn_sinks, attn_w_out,
           dn_norm_w, dn_w_in, dn_conv_w, dn_a_log, dn_dt_bias, dn_o_norm_w, dn_w_out):
    x = np.asarray(x, np.float32)
    meta = np.asarray(meta_tokens, np.float32)
    cores = list(range(8))
    SEG = 4096
    WA = attn_weight_layout(attn_norm_w, attn_w_in, attn_q_norm_w, attn_k_norm_w, attn_sinks, attn_w_out)
    xm = np.zeros((128, 1024), np.float32)
    xm[:16] = meta
    tabs0, tabs1 = attn_tables(True), attn_tables(False)
    maps = []
    for c in cores:
        b, s = c // 4, c % 4
        if s == 0:
            halo = np.concatenate([np.zeros((112, 1024), np.float32), meta], 0)
        else:
            halo = x[b, s * SEG - 128:s * SEG]
        xa = np.ascontiguousarray(np.concatenate([halo, x[b, s * SEG:(s + 1) * SEG]], 0))
        tb, mtb = tabs0 if s == 0 else tabs1
        maps.append(dict(xa=xa, xm=xm, tabs=tb, mtabs=mtb, **WA))
    resA = run_bass_kernel_spmd(_get_nc("A", lambda: build_phase_a(33)), maps, core_ids=cores).results
    maps = []
    for c in cores:
        b, r = c // 4, c % 4
        parts = [np.asarray(resA[b * 4 + s]["xn1T"]) for s in range(4)]
        xnT = np.ascontiguousarray(np.concatenate([parts[0]] + [p[:, 64:] for p in parts[1:]], axis=1))
        WB = dn_weight_layout(r, dn_norm_w, dn_w_in, dn_conv_w, dn_a_log, dn_dt_bias, dn_o_norm_w)
        maps.append(dict(xnT=xnT, **WB))
    resB = run_bass_kernel_spmd(_get_nc("B", lambda: build_phase_b(257)), maps, core_ids=cores).results
    wout = np.ascontiguousarray(np.asarray(dn_w_out[0], np.float32))
    maps = []
    for c in cores:
        b, s = c // 4, c % 4
        ogT = np.ascontiguousarray(np.concatenate(
            [np.asarray(resB[b * 4 + r]["ogT"])[:, s * SEG:(s + 1) * SEG] for r in range(4)], axis=0))
        maps.append(dict(ogT=ogT, h1=np.asarray(resA[c]["h1"]), wout=wout))
    resC = run_bass_kernel_spmd(_get_nc("C", lambda: build_phase_c(SEG)), maps, core_ids=cores).results
    out = np.empty((2, 16384, 1024), np.float32)
    for c in cores:
        b, s = c // 4, c % 4
        out[b, s * SEG:(s + 1) * SEG] = np.asarray(resC[c]["y"])
    return out
```

```python
import numpy as np
import ml_dtypes
import concourse.bass as bass
import concourse.mybir as mybir
from concourse.bass_utils import run_bass_kernel_spmd

F32 = mybir.dt.float32
BF16 = mybir.dt.bfloat16
AF = mybir.ActivationFunctionType
ALU = mybir.AluOpType
AX = mybir.AxisListType

NPBF = ml_dtypes.bfloat16


class Prog:
    COMPUTE = ("pe", "act", "dve", "pool")

    def __init__(self, nc, n_dma_sems=32, sem_stack=None, prefix="", barrier=False):
        self.nc = nc
        self.sem_stack = sem_stack
        self.prefix = prefix
        self.barrier = barrier
        self.ops = []
        self.last_w = {}
        self.readers = {}
        self.n_dma_sems = n_dma_sems
        self.dma_count = 0
        self.dma_last = [None] * n_dma_sems
        self.dma_uses = [0] * n_dma_sems

    def add(self, eng, fn, reads=(), writes=(), dma=False, inc=16):
        oid = len(self.ops)
        deps = set()
        for r in reads:
            if r in self.last_w:
                deps.add(self.last_w[r])
        for w in writes:
            if w in self.last_w:
                deps.add(self.last_w[w])
            deps |= self.readers.get(w, set())
        op = dict(eng=eng, fn=fn, deps=deps, dma=dma, signal=False, sigval=None)
        if dma:
            k = self.dma_count % self.n_dma_sems
            self.dma_count += 1
            if self.dma_last[k] is not None:
                deps.add(self.dma_last[k])
            self.dma_last[k] = oid
            self.dma_uses[k] += inc
            op["dsem"] = k
            op["dinc"] = inc
            op["dval"] = self.dma_uses[k]
        deps.discard(oid)
        self.ops.append(op)
        for r in reads:
            self.readers.setdefault(r, set()).add(oid)
        for w in writes:
            self.last_w[w] = oid
            self.readers[w] = set()
        return oid

    def emit(self):
        nc = self.nc
        ops = self.ops
        for op in ops:
            for d in op["deps"]:
                D = ops[d]
                if D["dma"]:
                    continue
                if D["eng"] == op["eng"] == "pe" and not op["dma"]:
                    continue
                D["signal"] = True
        if self.barrier:
            last = {}
            for i, op in enumerate(ops):
                if not op["dma"]:
                    last[op["eng"]] = i
            for i in last.values():
                ops[i]["signal"] = True
        cnt = {e: 0 for e in self.COMPUTE}
        for op in ops:
            if op["dma"]:
                continue
            if op["signal"]:
                cnt[op["eng"]] += 1
                op["sigval"] = cnt[op["eng"]]
        by_eng = {e: [] for e in ("pe", "act", "dve", "pool", "sp")}
        for op in ops:
            by_eng[op["eng"]].append(op)
        import contextlib
        with contextlib.ExitStack() as st:
            sst = self.sem_stack if self.sem_stack is not None else st
            csem = {e: sst.enter_context(nc.semaphore(self.prefix + "cs_" + e)) for e in self.COMPUTE}
            dsem = [sst.enter_context(nc.semaphore(self.prefix + "ds_%d" % k)) for k in range(self.n_dma_sems)]
            block = st.enter_context(nc.Block())

            def run(engname, eng):
                waited = {}
                for op in by_eng[engname]:
                    targets = []
                    for d in sorted(op["deps"]):
                        D = ops[d]
                        if D["dma"]:
                            targets.append((("d", D["dsem"]), dsem[D["dsem"]], D["dval"]))
                        else:
                            if D["eng"] == engname == "pe" and not op["dma"]:
                                continue
                            targets.append((("c", D["eng"]), csem[D["eng"]], D["sigval"]))
                    best = {}
                    for key, sem, val in targets:
                        if val > waited.get(key, 0) and val > best.get(key, (None, 0))[1]:
                            best[key] = (sem, val)
                    for key, (sem, val) in best.items():
                        eng.wait_ge(sem, val)
                        waited[key] = val
                    ins = op["fn"](eng)
                    if op["dma"]:
                        ins.then_inc(dsem[op["dsem"]], op["dinc"])
                    elif op["signal"]:
                        ins.then_inc(csem[engname], 1)
                if engname == "sp" or self.barrier:
                    for k in range(self.n_dma_sems):
                        if self.dma_uses[k]:
                            eng.wait_ge(dsem[k], self.dma_uses[k])
                if self.barrier:
                    for e2 in self.COMPUTE:
                        if cnt[e2]:
                            eng.wait_ge(csem[e2], cnt[e2])

            @block.tensor
            def _(e):
                run("pe", e)

            @block.scalar
            def _(e):
                run("act", e)

            @block.vector
            def _(e):
                run("dve", e)

            @block.gpsimd
            def _(e):
                run("pool", e)

            @block.sync
            def _(e):
                run("sp", e)


def build_phase_c(ntok=4096):
    nc = bass.Bass("TRN2", target_bir_lowering=False)
    ogT = nc.dram_tensor("ogT", [2048, ntok], BF16, kind="ExternalInput").ap()
    h1 = nc.dram_tensor("h1", [ntok, 1024], F32, kind="ExternalInput").ap()
    wout = nc.dram_tensor("wout", [2048, 1024], F32, kind="ExternalInput").ap()
    y = nc.dram_tensor("y", [ntok, 1024], F32, kind="ExternalOutput").ap()
    import contextlib
    with contextlib.ExitStack() as st:
        P = Prog(nc)
        emit_phase_c(nc, st, P, ogT, h1, wout, y, ntok)
        P.emit()
    return nc


def emit_phase_c(nc, st, P, ogT, h1, wout, y, ntok, ogsrc=None, ogdeps=()):
    TT = 512
    nt = ntok // TT
    w_sb = st.enter_context(nc.sbuf_tensor("c_w", [128, 16, 1024], BF16))
    wst = [st.enter_context(nc.sbuf_tensor("c_wst%d" % i, [128, 2, 1024], F32)) for i in range(2)]
    og_sb = [st.enter_context(nc.sbuf_tensor("c_og%d" % i, [128, 16, TT], BF16)) for i in range(2)]
    h_sb = [st.enter_context(nc.sbuf_tensor("c_h%d" % i, [128, 1024], F32)) for i in range(3)]
    y_sb = [st.enter_context(nc.sbuf_tensor("c_y%d" % i, [128, 1024], F32)) for i in range(3)]
    ps = [st.enter_context(nc.psum_tensor("c_ps%d" % i, [128, 512], F32)) for i in range(4)]
    woutv = wout.rearrange("(c p) n -> p c n", p=128)
    for j in range(8):
        s = j % 2
        P.add("sp", lambda e, j=j, s=s: e.dma_start(out=wst[s][:], in_=woutv[:, 2 * j:2 * j + 2, :]),
              writes=[("c_wst", s)], dma=True)
        eng = "act" if j % 2 == 0 else "dve"
        if eng == "act":
            P.add("act", lambda e, j=j, s=s: e.copy(out=w_sb[:, 2 * j:2 * j + 2, :], in_=wst[s][:]),
                  reads=[("c_wst", s)], writes=[("c_w", j)])
        else:
            P.add("dve", lambda e, j=j, s=s: e.tensor_copy(out=w_sb[:, 2 * j:2 * j + 2, :], in_=wst[s][:]),
                  reads=[("c_wst", s)], writes=[("c_w", j)])
    ogv = ogT.rearrange("(c p) t -> p c t", p=128) if ogsrc is None else None
    blk = 0
    for t in range(nt):
        so = t % 2
        if ogsrc is None:
            P.add("sp", lambda e, t=t, so=so: e.dma_start(out=og_sb[so][:], in_=ogv[:, :, t * TT:(t + 1) * TT]),
                  reads=list(ogdeps), writes=[("c_og", so)], dma=True)
        else:
            P.add("sp", lambda e, t=t, so=so: e.dma_start(out=og_sb[so][:], in_=ogsrc(e, t * TT, TT)),
                  reads=list(ogdeps), writes=[("c_og", so)], dma=True)
        for b in range(TT // 128):
            r0 = t * TT + b * 128
            sh = blk % 3
            P.add("sp", lambda e, r0=r0, sh=sh: e.dma_start(out=h_sb[sh][:], in_=h1[r0:r0 + 128, :]),
                  writes=[("c_h", sh)], dma=True)
            for g in range(2):
                pb = (blk * 2 + g) % 4
                for c in range(16):
                    P.add("pe", lambda e, pb=pb, so=so, b=b, c=c, g=g: e.matmul(
                        ps[pb][:], lhsT=og_sb[so][:, c, b * 128:(b + 1) * 128],
                        rhs=w_sb[:, c, g * 512:(g + 1) * 512], start=(c == 0), stop=(c == 15)),
                        reads=[("c_og", so), ("c_w", c // 2)], writes=[("c_ps", pb)])
                P.add("dve", lambda e, pb=pb, sh=sh, g=g: e.tensor_tensor(
                    out=y_sb[sh][:, g * 512:(g + 1) * 512], in0=ps[pb][:],
                    in1=h_sb[sh][:, g * 512:(g + 1) * 512], op=ALU.add),
                    reads=[("c_ps", pb), ("c_h", sh)], writes=[("c_y", sh, g)])
            P.add("sp", lambda e, r0=r0, sh=sh: e.dma_start(out=y[r0:r0 + 128, :], in_=y_sb[sh][:]),
                  reads=[("c_y", sh, 0), ("c_y", sh, 1)], writes=[("c_yout", blk)], dma=True)
            blk += 1


EPS = 1e-6


def make_identity(nc, st, P, name):
    identf = st.enter_context(nc.sbuf_tensor(name + "_f", [128, 128], F32))
    ident = st.enter_context(nc.sbuf_tensor(name, [128, 128], BF16))
    P.add("pool", lambda e: e.memset(identf[:], 1.0), writes=[name + "_f"])
    P.add("pool", lambda e: e.affine_select(out=identf[:], in_=identf[:], pattern=[[-1, 128]],
                                             compare_op=ALU.is_equal, fill=0.0, base=0,
                                             channel_multiplier=1),
          reads=[name + "_f"], writes=[name + "_f"])
    P.add("dve", lambda e: e.tensor_copy(out=ident[:], in_=identf[:]), reads=[name + "_f"], writes=[name])
    return ident, identf


def build_phase_a(nb=33):
    nc = bass.Bass("TRN2", target_bir_lowering=False)
    d = {}
    d["xa"] = nc.dram_tensor("xa", [nb * 128, 1024], F32, kind="ExternalInput").ap()
    d["xm"] = nc.dram_tensor("xm", [128, 1024], F32, kind="ExternalInput").ap()
    d["w_in"] = nc.dram_tensor("w_in", [1024, 2304], F32, kind="ExternalInput").ap()
    d["nw"] = nc.dram_tensor("nw", [128, 8], F32, kind="ExternalInput").ap()
    d["wq"] = nc.dram_tensor("wq", [128, 128], F32, kind="ExternalInput").ap()
    d["wk"] = nc.dram_tensor("wk", [128, 128], F32, kind="ExternalInput").ap()
    d["sinks"] = nc.dram_tensor("sinks", [128, 16], F32, kind="ExternalInput").ap()
    d["w_out"] = nc.dram_tensor("w_out", [1024, 1024], F32, kind="ExternalInput").ap()
    d["tabs"] = nc.dram_tensor("tabs", [128, 4, 2048], F32, kind="ExternalInput").ap()
    d["mtabs"] = nc.dram_tensor("mtabs", [16, 3, 2048], F32, kind="ExternalInput").ap()
    d["h1"] = nc.dram_tensor("h1", [(nb - 1) * 128, 1024], F32, kind="ExternalOutput").ap()
    d["xn1T"] = nc.dram_tensor("xn1T", [1024, 64 + (nb - 1) * 128], BF16, kind="ExternalOutput").ap()
    import contextlib
    with contextlib.ExitStack() as st:
        P = Prog(nc)
        emit_phase_a(nc, st, P, d, nb)
        P.emit()
    return nc


def emit_phase_a(nc, st, P, d, nb):
    sb = lambda name, shape, dt: st.enter_context(nc.sbuf_tensor(name, shape, dt))
    ident, identf = make_identity(nc, st, P, "a_ident")
    w_sb = sb("a_w", [128, 8, 2304], BF16)
    wo_sb = sb("a_wo", [128, 8, 1024], BF16)
    wst = [sb("a_wst%d" % i, [128, 2304], F32) for i in range(2)]
    nw = sb("a_nw", [128, 8], F32)
    wqk = sb("a_wqk", [128, 128], F32)
    wk_t = sb("a_wk", [128, 128], F32)
    esink = sb("a_esink", [128, 16], F32)
    tabs = sb("a_tabs", [128, 4, 2048], F32)
    mtabs = sb("a_mtabs", [16, 3, 2048], F32)
    x_sb = [sb("a_x%d" % i, [128, 1024], F32) for i in range(2)]
    junk = sb("a_junk", [128, 1024], F32)
    st_sb = sb("a_stat", [128, 8], F32)
    xn = sb("a_xn", [128, 1024], BF16)
    xnT = sb("a_xnT", [128, 8, 128], BF16)
    qss = sb("a_qss", [128, 16], F32)
    qr = sb("a_qr", [128, 16], F32)
    kss = sb("a_kss", [128, 4], F32)
    qn = sb("a_qn", [128, 1024], BF16)
    kn = sb("a_kn", [128, 128], F32)
    ksq = sb("a_ksq", [128, 128], F32)
    kq = sb("a_kq", [128, 128], BF16)
    gs = sb("a_gs", [128, 1024], BF16)
    QT = sb("a_QT", [128, 1024], BF16)
    KT = [sb("a_KT%d" % i, [128, 128], BF16) for i in range(2)]
    KTm = sb("a_KTm", [128, 128], BF16)
    vE = [sb("a_vE%d" % i, [128, 2, 65], BF16) for i in range(2)]
    vEm = sb("a_vEm", [128, 2, 65], BF16)
    E_sb = [sb("a_E%d" % i, [128, 512], F32) for i in range(2)]
    PTp = sb("a_PTp", [128, 16, 128], BF16)
    PTc = sb("a_PTc", [128, 16, 128], BF16)
    PTm = sb("a_PTm", [16, 16, 128], BF16)
    den = sb("a_den", [128, 16], F32)
    rden = sb("a_rden", [128, 16], F32)
    ogp = sb("a_ogp", [128, 1024], F32)
    og = sb("a_og", [128, 1024], BF16)
    ogT = sb("a_ogT", [128, 8, 128], BF16)
    h1 = [sb("a_h1%d" % i, [128, 1024], F32) for i in range(2)]
    xn1 = sb("a_xn1", [128, 1024], BF16)
    xn1T = [sb("a_xn1T%d" % i, [128, 8, 128], BF16) for i in range(2)]

    ps_t = st.enter_context(nc.psum_tensor("a_pst", [128, 8, 128], BF16))
    ps_u = [st.enter_context(nc.psum_tensor("a_psu%d" % i, [128, 512], F32)) for i in range(2)]
    ps_s = [st.enter_context(nc.psum_tensor("a_pss%d" % i, [128, 512], F32)) for i in range(2)]
    ps_o = st.enter_context(nc.psum_tensor("a_pso", [128, 3, 512], F32))

    P.add("sp", lambda e: e.dma_start(out=nw[:], in_=d["nw"]), writes=["a_nw"], dma=True)
    P.add("sp", lambda e: e.dma_start(out=wqk[:], in_=d["wq"]), writes=["a_wqk"], dma=True)
    P.add("sp", lambda e: e.dma_start(out=wk_t[:], in_=d["wk"]), writes=["a_wk"], dma=True)
    P.add("sp", lambda e: e.dma_start(out=esink[:], in_=d["sinks"]), writes=["a_esink"], dma=True)
    P.add("dve", lambda e: e.scalar_tensor_tensor(out=wqk[:], in0=wqk[:], scalar=8.0, in1=wk_t[:],
                                                  op0=ALU.mult, op1=ALU.mult),
          reads=["a_wqk", "a_wk"], writes=["a_wqk"])
    P.add("act", lambda e: e.activation(out=esink[:], in_=esink[:], func=AF.Exp), reads=["a_esink"], writes=["a_esink"])
    for i in range(2):
        P.add("pool", lambda e, i=i: e.memset(vE[i][:], 1.0), writes=[("a_vE", i)])
    P.add("pool", lambda e: e.memset(vEm[:], 1.0), writes=["a_vEm"])
    w_inv = d["w_in"].rearrange("(c p) n -> p c n", p=128)
    for c in range(8):
        s = c % 2
        P.add("sp", lambda e, c=c, s=s: e.dma_start(out=wst[s][:], in_=w_inv[:, c, :]), writes=[("a_wst", s)], dma=True)
        if c % 2 == 0:
            P.add("dve", lambda e, c=c, s=s: e.tensor_scalar(out=w_sb[:, c, :], in0=wst[s][:], scalar1=nw[:, c:c + 1],
                                                           scalar2=None, op0=ALU.mult),
                  reads=[("a_wst", s), "a_nw"], writes=[("a_w", c)])
        else:
            P.add("act", lambda e, c=c, s=s: e.activation(out=w_sb[:, c, :], in_=wst[s][:], func=AF.Copy, scale=nw[:, c:c + 1]),
                  reads=[("a_wst", s), "a_nw"], writes=[("a_w", c)])
    w_outv = d["w_out"].rearrange("(c p) n -> p c n", p=128)
    for c in range(4):
        s = c % 2
        P.add("sp", lambda e, c=c, s=s: e.dma_start(out=wst[s][:, 0:2048].rearrange("p (a n) -> p a n", a=2),
                                                   in_=w_outv[:, 2 * c:2 * c + 2, :]), writes=[("a_wst", s)], dma=True)
        P.add("dve" if c % 2 == 0 else "pool",
              lambda e, c=c, s=s: e.tensor_copy(out=wo_sb[:, 2 * c:2 * c + 2, :],
                                                in_=wst[s][:, 0:2048].rearrange("p (a n) -> p a n", a=2)),
              reads=[("a_wst", s)], writes=[("a_wo", c)])
    for j in range(4):
        P.add("sp", lambda e, j=j: e.dma_start(out=tabs[:, j, :], in_=d["tabs"][:, j, :]), writes=[("a_tabs", j)], dma=True)
    P.add("sp", lambda e: e.dma_start(out=mtabs[:], in_=d["mtabs"]), writes=["a_mtabs"], dma=True)
    W_ALL = [("a_w", c) for c in range(8)]
    WO_ALL = [("a_wo", c) for c in range(4)]

    def rstd_from_ss(ss_ap, ln_ap, out_ap, bias, key_in, key_out):
        P.add("act", lambda e: e.activation(out=ln_ap, in_=ss_ap, func=AF.Ln, bias=bias, scale=1.0),
              reads=[key_in], writes=[key_out + "_ln"])
        P.add("act", lambda e: e.activation(out=out_ap, in_=ln_ap, func=AF.Exp, scale=-0.5),
              reads=[key_out + "_ln"], writes=[key_out])

    def transposes8(src, src_key, dstT, dst_key, evac_eng):
        for c in range(8):
            P.add("pe", lambda e, c=c: e.transpose(out=ps_t[:, c, :], in_=src[:, c * 128:(c + 1) * 128], identity=ident[:]),
                  reads=[src_key, "a_ident"], writes=["a_pst"])
        if evac_eng == "act":
            P.add("act", lambda e: e.copy(out=dstT, in_=ps_t[:]), reads=["a_pst"], writes=[dst_key])
        else:
            P.add(evac_eng, lambda e: e.tensor_copy(out=dstT, in_=ps_t[:]), reads=["a_pst"], writes=[dst_key])

    def front(src_ap, xs, KT_dst, KT_key, vE_dst, vE_key, full):
        P.add("sp", lambda e: e.dma_start(out=x_sb[xs][:], in_=src_ap), writes=[("a_x", xs)], dma=True)
        P.add("act", lambda e: e.activation(out=junk[:], in_=x_sb[xs][:], func=AF.Square, accum_out=st_sb[:, 0:1]),
              reads=[("a_x", xs)], writes=["a_junk", "a_ss"])
        rstd_from_ss(st_sb[:, 0:1], st_sb[:, 1:2], st_sb[:, 2:3], 1024 * EPS, "a_ss", "a_rstd")
        P.add("dve", lambda e: e.tensor_scalar(out=xn[:], in0=x_sb[xs][:], scalar1=st_sb[:, 2:3], scalar2=32.0,
                                               op0=ALU.mult, op1=ALU.mult),
              reads=[("a_x", xs), "a_rstd"], writes=["a_xn"])
        transposes8(xn, "a_xn", xnT[:], "a_xnT", "act")
        groups = [(0, 512), (512, 512), (1024, 256), (1280, 512), (1792, 512)] if full else [(1024, 256)]
        for gi, (c0, cw) in enumerate(groups):
            pb = gi % 2
            for c in range(8):
                P.add("pe", lambda e, c=c, c0=c0, cw=cw, pb=pb: e.matmul(
                    ps_u[pb][:, 0:cw], lhsT=xnT[:, c, :], rhs=w_sb[:, c, c0:c0 + cw], start=(c == 0), stop=(c == 7)),
                    reads=["a_xnT"] + W_ALL, writes=[("a_psu", pb)])
            if c0 < 1024:
                h0 = c0 // 64
                P.add("act", lambda e, pb=pb: e.activation(out=junk[:, 0:512], in_=ps_u[pb][:], func=AF.Square),
                      reads=[("a_psu", pb)], writes=["a_junk"])
                P.add("dve", lambda e, h0=h0: e.tensor_reduce(out=qss[:, h0:h0 + 8], in_=junk[:, 0:512].rearrange("p (a b) -> p a b", a=8),
                                                            axis=AX.X, op=ALU.add),
                      reads=["a_junk"], writes=[("a_qss", h0)])
                rstd_from_ss(qss[:, h0:h0 + 8], qr[:, h0:h0 + 8], qr[:, h0:h0 + 8], 64 * EPS, ("a_qss", h0), "a_qr%d" % h0)
                P.add("dve", lambda e, pb=pb, h0=h0, c0=c0: e.tensor_tensor(
                    out=qn[:, c0:c0 + 512].rearrange("p (a b) -> p a b", a=8),
                    in0=ps_u[pb][:].rearrange("p (a b) -> p a b", a=8),
                    in1=qr[:, h0:h0 + 8].unsqueeze(2).to_broadcast([128, 8, 64]), op=ALU.mult),
                    reads=[("a_psu", pb), "a_qr%d" % h0], writes=[("a_qn", h0)])
            elif c0 == 1024:
                P.add("act", lambda e, pb=pb: e.activation(out=ksq[:], in_=ps_u[pb][:, 0:128], func=AF.Square),
                      reads=[("a_psu", pb)], writes=["a_junk2"])
                P.add("dve", lambda e: e.tensor_reduce(out=kss[:, 0:2], in_=ksq[:].rearrange("p (a b) -> p a b", a=2),
                                                       axis=AX.X, op=ALU.add),
                      reads=["a_junk2"], writes=["a_kss"])
                rstd_from_ss(kss[:, 0:2], kss[:, 2:4], kss[:, 2:4], 64 * EPS, "a_kss", "a_kr")
                P.add("dve", lambda e, pb=pb: e.tensor_tensor(
                    out=kn[:].rearrange("p (a b) -> p a b", a=2), in0=ps_u[pb][:, 0:128].rearrange("p (a b) -> p a b", a=2),
                    in1=kss[:, 2:4].unsqueeze(2).to_broadcast([128, 2, 64]), op=ALU.mult),
                    reads=[("a_psu", pb), "a_kr"], writes=["a_kn"])
                P.add("dve", lambda e: e.tensor_tensor(out=kq[:], in0=kn[:], in1=wqk[:], op=ALU.mult),
                      reads=["a_kn", "a_wqk"], writes=["a_kq"])
                P.add("act", lambda e, pb=pb: e.copy(out=vE_dst[:, :, 0:64], in_=ps_u[pb][:, 128:256].rearrange("p (a b) -> p a b", a=2)),
                      reads=[("a_psu", pb)], writes=[vE_key])
            else:
                g0 = c0 - 1280
                P.add("act", lambda e, pb=pb, g0=g0: e.activation(out=gs[:, g0:g0 + 512], in_=ps_u[pb][:], func=AF.Silu),
                      reads=[("a_psu", pb)], writes=[("a_gs", g0)])
        if full:
            for c in range(8):
                P.add("pe", lambda e, c=c: e.transpose(out=ps_t[:, c, :], in_=qn[:, c * 128:(c + 1) * 128], identity=ident[:]),
                      reads=[("a_qn", 0), ("a_qn", 8), "a_ident"], writes=["a_pst"])
            P.add("dve", lambda e: e.tensor_copy(out=QT[:].rearrange("p (a b) -> p a b", a=8), in_=ps_t[:]),
                  reads=["a_pst"], writes=["a_QT"])
        P.add("pe", lambda e: e.transpose(out=ps_t[:, 0, :], in_=kq[:], identity=ident[:]),
              reads=["a_kq", "a_ident"], writes=["a_pst"])
        P.add("act", lambda e: e.copy(out=KT_dst[:], in_=ps_t[:, 0, :]), reads=["a_pst"], writes=[KT_key])

    front(d["xm"], 0, KTm, "a_KTm", vEm, "a_vEm", full=False)

    for i in range(nb):
        cur = i % 2
        prv = 1 - cur
        front(d["xa"][i * 128:(i + 1) * 128, :], i % 2, KT[cur], ("a_KT", cur), vE[cur], ("a_vE", cur), full=True)
        if i == 0:
            chunks = [("cur", 0), ("meta", 0)]
        elif i == 1:
            chunks = [("prev", 1), ("cur", 3), ("meta", 1)]
        else:
            chunks = [("prev", 2), ("cur", 3), ("meta", 2)]
        n_e = 0
        for kind, tix in chunks:
            for g in range(2):
                for hf in range(2):
                    pb = n_e % 2
                    hh = g * 8 + hf * 4
                    if kind == "meta":
                        lhsT, lk, M = KTm[g * 64:(g + 1) * 64, 0:16], "a_KTm", 16
                    elif kind == "cur":
                        lhsT, lk, M = KT[cur][g * 64:(g + 1) * 64, :], ("a_KT", cur), 128
                    else:
                        lhsT, lk, M = KT[prv][g * 64:(g + 1) * 64, :], ("a_KT", prv), 128
                    P.add("pe", lambda e, pb=pb, lhsT=lhsT, g=g, hf=hf, M=M: e.matmul(
                        ps_s[pb][0:M, :], lhsT=lhsT, rhs=QT[g * 64:(g + 1) * 64, hf * 512:(hf + 1) * 512], start=True, stop=True),
                        reads=[lk, "a_QT"], writes=[("a_pss", pb)])
                    P.add("act", lambda e, pb=pb, M=M: e.activation(out=E_sb[pb][0:M, :], in_=ps_s[pb][0:M, :], func=AF.Exp),
                          reads=[("a_pss", pb)], writes=[("a_E", pb)])
                    if kind == "meta":
                        tab = mtabs[:, tix, hh * 128:(hh + 4) * 128]
                        dst = PTm[:, hh:hh + 4, :]
                        dk = ("a_PTm", hh)
                        tk = "a_mtabs"
                    else:
                        tab = tabs[:, tix, hh * 128:(hh + 4) * 128]
                        dst = (PTc if kind == "cur" else PTp)[:, hh:hh + 4, :]
                        dk = ("a_PTc" if kind == "cur" else "a_PTp", hh)
                        tk = ("a_tabs", tix)
                    P.add("dve" if n_e % 2 == 0 else "pool",
                          lambda e, pb=pb, M=M, tab=tab, dst=dst: e.tensor_tensor(
                              out=dst.rearrange("p a b -> p (a b)"), in0=E_sb[pb][0:M, :], in1=tab, op=ALU.mult),
                          reads=[("a_E", pb), tk], writes=[dk])
                    n_e += 1
        for h in range(16):
            g = h // 8
            hh = (h // 4) * 4
            bank, off = h // 7, (h % 7) * 65
            for ci, (kind, tix) in enumerate(chunks):
                if kind == "meta":
                    lhsT, lk = PTm[:, h, :], ("a_PTm", hh)
                    rhs, rk = vEm[0:16, g, :], "a_vEm"
                elif kind == "cur":
                    lhsT, lk = PTc[:, h, :], ("a_PTc", hh)
                    rhs, rk = vE[cur][:, g, :], ("a_vE", cur)
                else:
                    lhsT, lk = PTp[:, h, :], ("a_PTp", hh)
                    rhs, rk = vE[prv][:, g, :], ("a_vE", prv)
                P.add("pe", lambda e, bank=bank, off=off, lhsT=lhsT, rhs=rhs, ci=ci, n=len(chunks): e.matmul(
                    ps_o[:, bank, off:off + 65], lhsT=lhsT, rhs=rhs, start=(ci == 0), stop=(ci == n - 1)),
                    reads=[lk, rk], writes=[("a_pso", bank)])
        for bank, (h0, nh) in enumerate([(0, 7), (7, 7), (14, 2)]):
            ov = ps_o[:, bank, 0:nh * 65].rearrange("p (a b) -> p a b", b=65)
            P.add("dve", lambda e, ov=ov, h0=h0, nh=nh: e.tensor_tensor(
                out=den[:, h0:h0 + nh].unsqueeze(2), in0=ov[:, :, 64:65], in1=esink[:, h0:h0 + nh].unsqueeze(2), op=ALU.add),
                reads=[("a_pso", bank), "a_esink"], writes=[("a_den", bank)])
            P.add("dve", lambda e, h0=h0, nh=nh: e.reciprocal(out=rden[:, h0:h0 + nh], in_=den[:, h0:h0 + nh]),
                  reads=[("a_den", bank)], writes=[("a_rden", bank)])
            P.add("dve", lambda e, ov=ov, h0=h0, nh=nh: e.tensor_tensor(
                out=ogp[:, h0 * 64:(h0 + nh) * 64].rearrange("p (a b) -> p a b", b=64), in0=ov[:, :, 0:64],
                in1=rden[:, h0:h0 + nh].unsqueeze(2).to_broadcast([128, nh, 64]), op=ALU.mult),
                reads=[("a_pso", bank), ("a_rden", bank)], writes=[("a_ogp", bank)])
        P.add("pool", lambda e: e.tensor_tensor(out=og[:], in0=ogp[:], in1=gs[:], op=ALU.mult),
              reads=[("a_ogp", 0), ("a_ogp", 1), ("a_ogp", 2), ("a_gs", 0), ("a_gs", 512)], writes=["a_og"])
        transposes8(og, "a_og", ogT[:], "a_ogT", "act")
        hs = i % 2
        for g2 in range(2):
            pb = g2
            for c in range(8):
                P.add("pe", lambda e, c=c, g2=g2, pb=pb: e.matmul(
                    ps_u[pb][:], lhsT=ogT[:, c, :], rhs=wo_sb[:, c, g2 * 512:(g2 + 1) * 512], start=(c == 0), stop=(c == 7)),
                    reads=["a_ogT"] + WO_ALL, writes=[("a_psu", pb)])
            P.add("dve", lambda e, g2=g2, pb=pb, hs=hs, xs=i % 2: e.tensor_tensor(
                out=h1[hs][:, g2 * 512:(g2 + 1) * 512], in0=ps_u[pb][:], in1=x_sb[xs][:, g2 * 512:(g2 + 1) * 512], op=ALU.add),
                reads=[("a_psu", pb), ("a_x", i % 2)], writes=[("a_h1", hs, g2)])
        H1K = [("a_h1", hs, 0), ("a_h1", hs, 1)]
        if i >= 1:
            P.add("sp", lambda e, i=i, hs=hs: e.dma_start(out=d["h1"][(i - 1) * 128:i * 128, :], in_=h1[hs][:]),
                  reads=H1K, writes=[("a_h1out", i)], dma=True)
        P.add("act", lambda e, hs=hs: e.activation(out=junk[:], in_=h1[hs][:], func=AF.Square, accum_out=st_sb[:, 4:5]),
              reads=H1K, writes=["a_junk", "a_ss1"])
        rstd_from_ss(st_sb[:, 4:5], st_sb[:, 5:6], st_sb[:, 6:7], 1024 * EPS, "a_ss1", "a_rstd1")
        P.add("dve", lambda e, hs=hs: e.tensor_scalar(out=xn1[:], in0=h1[hs][:], scalar1=st_sb[:, 6:7], scalar2=32.0,
                                                      op0=ALU.mult, op1=ALU.mult),
              reads=H1K + ["a_rstd1"], writes=["a_xn1"])
        transposes8(xn1, "a_xn1", xn1T[hs][:], ("a_xn1T", hs), "act")
        xv = d["xn1T"].rearrange("(c p) t -> p c t", p=128)
        if i == 0:
            P.add("sp", lambda e, hs=hs: e.dma_start(out=xv[:, :, 0:64], in_=xn1T[hs][:, :, 64:128]),
                  reads=[("a_xn1T", hs)], writes=[("a_xn1out", i)], dma=True)
        else:
            P.add("sp", lambda e, hs=hs, i=i: e.dma_start(out=xv[:, :, 64 + (i - 1) * 128:64 + i * 128], in_=xn1T[hs][:]),
                  reads=[("a_xn1T", hs)], writes=[("a_xn1out", i)], dma=True)


def attn_tables(seg0):
    m = np.exp2(-8.0 * np.arange(1, 17) / 16.0)[None, :, None]
    j = np.arange(128)[:, None, None].astype(np.float64)
    i = np.arange(128)[None, None, :].astype(np.float64)
    t_prev = np.where(j > i, np.exp(-m * (128 + i - j)), 0.0)
    t_cur = np.where(j <= i, np.exp(-m * (i - j)), 0.0)
    jm = np.arange(16)[:, None, None].astype(np.float64)
    t_meta = np.exp(-m * 128.0) * np.ones((16, 16, 128))
    if seg0:
        t_cur0 = np.zeros_like(t_cur)
        t_prev1 = np.zeros_like(t_prev)
        pq = i - 112.0
        t_meta0 = np.where(pq >= jm, np.exp(-m * np.maximum(pq - jm, 0.0)), 0.0) * np.ones((16, 16, 128))
        t_meta1 = np.exp(-m * np.minimum(16.0 + i - jm, 128.0))
    else:
        t_cur0, t_prev1, t_meta0, t_meta1 = t_cur, t_prev, t_meta, t_meta
    tabs = np.stack([t_cur0, t_prev1, t_prev, t_cur], axis=1).reshape(128, 4, 2048).astype(np.float32)
    mtabs = np.stack([t_meta0, t_meta1, t_meta], axis=1).reshape(16, 3, 2048).astype(np.float32)
    return np.ascontiguousarray(tabs), np.ascontiguousarray(mtabs)


def attn_weight_layout(attn_norm_w, attn_w_in, attn_q_norm_w, attn_k_norm_w, attn_sinks, attn_w_out):
    w_in = np.asarray(attn_w_in[0], dtype=np.float32)
    perm = []
    for c in range(8):
        for half in range(2):
            h = c + 8 * half
            perm.extend(range(h * 64, (h + 1) * 64))
    perm = np.array(perm + list(range(1024, 2304)))
    out = {}
    out["w_in"] = np.ascontiguousarray(w_in[:, perm])
    out["nw"] = np.ascontiguousarray(np.asarray(attn_norm_w[0], np.float32).reshape(8, 128).T)
    out["wq"] = np.ascontiguousarray(np.broadcast_to(np.tile(np.asarray(attn_q_norm_w[0], np.float32), 2)[None, :], (128, 128)))
    out["wk"] = np.ascontiguousarray(np.broadcast_to(np.tile(np.asarray(attn_k_norm_w[0], np.float32), 2)[None, :], (128, 128)))
    out["sinks"] = np.ascontiguousarray(np.broadcast_to(np.asarray(attn_sinks[0], np.float32)[None, :], (128, 16)))
    out["w_out"] = np.ascontiguousarray(np.asarray(attn_w_out[0], np.float32))
    return out


NEG = -30000.0
DBG_STOP = 99
CB_U, CB_MUI, CB_MUS, CB_MLS, CB_BD, CB_B1L, CB_B1U, CB_B2L, CB_ONES, CB_I = 0, 64, 320, 576, 832, 1088, 1344, 1600, 1856, 1984
CB_COLS = 2048


def dn_consts():
    a = np.arange(64)[:, None]
    b = np.arange(64)[None, :]
    rep4 = lambda m: np.tile(m[:, None, :], (1, 4, 1)).reshape(64, 256)
    U = (a <= b).astype(np.float32)
    mui = np.where(a <= b, 0.0, NEG)
    mus = np.where(a < b, 0.0, NEG)
    mls = np.where(b < a, 0.0, NEG)
    bd = (a // 16 == b // 16).astype(np.float32)
    b1l = ((a // 32 == b // 32) & (a // 16 == b // 16 + 1)).astype(np.float32)
    b2l = ((a >= 32) & (b < 32)).astype(np.float32)
    c = np.zeros((64, CB_COLS), np.float32)
    c[:, CB_U:CB_U + 64] = U
    c[:, CB_MUI:CB_MUI + 256] = rep4(mui)
    c[:, CB_MUS:CB_MUS + 256] = rep4(mus)
    c[:, CB_MLS:CB_MLS + 256] = rep4(mls)
    c[:, CB_BD:CB_BD + 256] = rep4(bd)
    c[:, CB_B1L:CB_B1L + 256] = rep4(b1l)
    c[:, CB_B1U:CB_B1U + 256] = rep4(b1l.T)
    c[:, CB_B2L:CB_B2L + 256] = rep4(b2l)
    c[:, CB_ONES:CB_ONES + 128] = 1.0
    c[:, CB_I:CB_I + 64] = np.eye(64)
    return c


def build_phase_b(nchunks=257):
    nc = bass.Bass("TRN2", target_bir_lowering=False)
    TT = 64 * nchunks
    d = {}
    d["xnT"] = nc.dram_tensor("xnT", [1024, TT], BF16, kind="ExternalInput").ap()
    d["wB"] = nc.dram_tensor("wB", [1024, 1544], F32, kind="ExternalInput").ap()
    d["nwB"] = nc.dram_tensor("nwB", [128, 8], F32, kind="ExternalInput").ap()
    d["convw"] = nc.dram_tensor("convw", [128, 8, 4], F32, kind="ExternalInput").ap()
    d["onw"] = nc.dram_tensor("onw", [128, 1], F32, kind="ExternalInput").ap()
    d["alog"] = nc.dram_tensor("alog", [64, 4], F32, kind="ExternalInput").ap()
    d["dtb"] = nc.dram_tensor("dtb", [64, 4], F32, kind="ExternalInput").ap()
    d["cstB"] = nc.dram_tensor("cstB", [64, CB_COLS], F32, kind="ExternalInput").ap()
    d["ogT"] = nc.dram_tensor("ogT", [512, TT - 64], BF16, kind="ExternalOutput").ap()
    import contextlib
    with contextlib.ExitStack() as st:
        P = Prog(nc)
        emit_phase_b(nc, st, P, d, nchunks)
        P.emit()
    return nc


def emit_phase_b(nc, st, P, d, nchunks, xsrc=None, xdeps=()):
    sb = lambda name, shape, dt: st.enter_context(nc.sbuf_tensor(name, shape, dt))
    V = lambda fn, r, w: P.add("dve", fn, reads=r, writes=w)
    A = lambda fn, r, w: P.add("act", fn, reads=r, writes=w)
    G = lambda fn, r, w: P.add("pool", fn, reads=r, writes=w)
    T = lambda fn, r, w: P.add("pe", fn, reads=r, writes=w)
    D = lambda fn, r, w: P.add("sp", fn, reads=r, writes=w, dma=True)
    TM = 512
    ident, identf = make_identity(nc, st, P, "b_ident")
    w_sb = sb("b_w", [128, 8, 1544], BF16)
    wst = [sb("b_wst%d" % i, [128, 1544], F32) for i in range(2)]
    nw = sb("b_nw", [128, 8], F32)
    cw = sb("b_cw", [128, 8, 4], F32)
    onw = sb("b_onw", [128, 1], F32)
    negA = sb("b_negA", [64, 4], F32)
    dtb = sb("b_dtb", [64, 4], F32)
    cst = sb("b_cst", [64, CB_COLS], F32)
    ones_bf = sb("b_ones", [128, 128], BF16)
    xt = [sb("b_xt%d" % i, [128, 8, TM], BF16) for i in range(2)]
    u_sb = sb("b_u", [128, 8, TM + 3], F32)
    acc = [sb("b_acc%d" % i, [128, TM], F32) for i in range(2)]
    csil = sb("b_csil", [128, 4, TM], F32)
    ctmp = sb("b_ctmp", [128, TM], F32)
    sq = sb("b_sq", [128, TM], BF16)
    rs = sb("b_rs", [128, TM], F32)
    qkT = sb("b_qkT", [128, 4, TM], BF16)
    vT = sb("b_vT", [128, 4, TM], BF16)
    zs = sb("b_zs", [128, 4, TM], F32)
    o_sb = sb("b_o", [128, 4, TM], F32)
    ogt = sb("b_ogt", [128, 4, TM], BF16)
    S32 = sb("b_S32", [128, 4, 128], F32)
    Sb = sb("b_Sb", [128, 4, 128], BF16)
    ba = sb("b_ba", [64, 8, 8], F32)
    e1 = sb("b_e1", [64, 8, 8], F32)
    lnb = sb("b_lnb", [64, 8, 4], F32)
    beta = sb("b_beta", [64, 8, 4], F32)
    gg = sb("b_g", [64, 8, 4], F32)
    gb = sb("b_gb", [64, 4, 128], F32)
    lb = sb("b_lb", [64, 4, 64], F32)
    gc = sb("b_gc", [64, 4], F32)
    gcl = sb("b_gcl", [64, 4], F32)
    beg = sb("b_beg", [64, 4], F32)
    ekl = sb("b_ekl", [64, 4], F32)
    egl = sb("b_egl", [128, 4], F32)
    args = sb("b_args", [64, 3, 256], F32)
    Eall = sb("b_Eall", [64, 3, 256], F32)
    Eg = sb("b_Eg", [128, 256], F32)
    Nm = sb("b_N", [64, 256], F32)
    Mm = sb("b_M", [64, 256], F32)
    attnT = sb("b_attnT", [64, 256], BF16)
    L = [sb("b_L%d" % i, [64, 256], BF16) for i in range(4)]
    Uu = [sb("b_U%d" % i, [64, 256], BF16) for i in range(3)]
    O1 = sb("b_O1", [64, 256], BF16)
    N1 = sb("b_N1", [64, 256], BF16)
    O2 = sb("b_O2", [64, 256], BF16)
    PU = [sb("b_PU%d" % i, [64, 256], BF16) for i in range(2)]
    PL = [sb("b_PL%d" % i, [64, 256], BF16) for i in range(2)]
    PUb, PLb = PU, PL
    Y = [sb("b_Y%d" % i, [64, 256], BF16) for i in range(2)]
    T32U = sb("b_T32U", [64, 256], BF16)
    T32L = sb("b_T32L", [64, 256], BF16)
    T32Ub, T32Lb = T32U, T32L
    gbb = sb("b_gbb", [64, 4, 128], BF16)
    lbb = sb("b_lbb", [64, 4, 64], BF16)
    gbf = sb("b_gbf", [64, 8, 4], BF16)
    cstb = sb("b_cstb", [64, 256], BF16)
    TTb = sb("b_TTb", [64, 256], BF16)
    wTn = sb("b_wTn", [128, 256], BF16)
    vb = sb("b_vb", [64, 4, 128], BF16)
    kbg = sb("b_kbg", [64, 4, 128], BF16)
    kst = sb("b_kst", [64, 4, 128], BF16)
    qg = sb("b_qg", [128, 256], BF16)
    vnb = sb("b_vnb", [64, 4, 128], BF16)
    kcp = sb("b_kcp", [128, 2, 64], BF16)

    psA = st.enter_context(nc.psum_tensor("b_psA", [128, 512], F32))
    psB = st.enter_context(nc.psum_tensor("b_psB", [128, 512], F32))
    psG = st.enter_context(nc.psum_tensor("b_psG", [128, 512], F32))
    psM = st.enter_context(nc.psum_tensor("b_psM", [128, 512], F32))
    psI = st.enter_context(nc.psum_tensor("b_psI", [128, 512], F32))
    psT = st.enter_context(nc.psum_tensor("b_psT", [128, 1024], BF16))
    psV = st.enter_context(nc.psum_tensor("b_psV", [128, 512], F32))
    psS = st.enter_context(nc.psum_tensor("b_psS", [128, 512], F32))

    cUb = cstb[:, 0:64]
    cIb = cstb[:, 64:128]
    cOnesb = cstb[:, 128:256]
    cU = cst[:, CB_U:CB_U + 64]
    cI = cst[:, CB_I:CB_I + 64]
    cOnes = cst[:, CB_ONES:CB_ONES + 128]
    c4 = lambda o: cst[:, o:o + 256]

    for name, t_, src in [("b_nw", nw, "nwB"), ("b_cw", cw, "convw"), ("b_onw", onw, "onw"), ("b_negA", negA, "alog"),
                          ("b_dtb", dtb, "dtb"), ("b_cst", cst, "cstB")]:
        D(lambda e, t_=t_, src=src: e.dma_start(out=t_[:], in_=d[src]), [], [name])
    V(lambda e: e.tensor_copy(out=cstb[:, 0:64], in_=cst[:, CB_U:CB_U + 64]), ["b_cst"], ["b_cst"])
    V(lambda e: e.tensor_copy(out=cstb[:, 64:128], in_=cst[:, CB_I:CB_I + 64]), ["b_cst"], ["b_cst"])
    V(lambda e: e.tensor_copy(out=cstb[:, 128:256], in_=cst[:, CB_ONES:CB_ONES + 128]), ["b_cst"], ["b_cst"])
    A(lambda e: e.activation(out=negA[:], in_=negA[:], func=AF.Exp), ["b_negA"], ["b_negA"])
    V(lambda e: e.tensor_scalar(out=negA[:], in0=negA[:], scalar1=-1.0, scalar2=None, op0=ALU.mult), ["b_negA"], ["b_negA"])
    G(lambda e: e.memset(ones_bf[:], 1.0), [], ["b_ones"])
    G(lambda e: e.memset(S32[:], 0.0), [], ["b_S32"])
    G(lambda e: e.memset(Sb[:], 0.0), [], ["b_Sb"])
    G(lambda e: e.memset(u_sb[:], 0.0), [], ["b_u"])
    wv = d["wB"].rearrange("(c p) n -> p c n", p=128)
    for c in range(8):
        s = c % 2
        D(lambda e, c=c, s=s: e.dma_start(out=wst[s][:], in_=wv[:, c, :]), [], [("b_wst", s)])
        if c % 2 == 0:
            V(lambda e, c=c, s=s: e.tensor_scalar(out=w_sb[:, c, :], in0=wst[s][:], scalar1=nw[:, c:c + 1], scalar2=None, op0=ALU.mult),
              [("b_wst", s), "b_nw"], ["b_w"])
        else:
            A(lambda e, c=c, s=s: e.activation(out=w_sb[:, c, :], in_=wst[s][:], func=AF.Copy, scale=nw[:, c:c + 1]),
              [("b_wst", s), "b_nw"], ["b_w"])

    def rsqrt_act(out_ap, in_ap, scale, bias_ln, bias_exp, rkeys, wkey):
        A(lambda e: e.activation(out=out_ap, in_=in_ap, func=AF.Ln, bias=bias_ln, scale=scale), rkeys, [wkey])
        A(lambda e: e.activation(out=out_ap, in_=out_ap, func=AF.Exp, scale=-0.5, bias=bias_exp), [wkey], [wkey])

    def tile(ti, t0, TW, need_o):
        xs = ti % 2
        nck = TW // 64
        if xsrc is None:
            xv = d["xnT"].rearrange("(c p) t -> p c t", p=128)
            D(lambda e: e.dma_start(out=xt[xs][:, :, 0:TW], in_=xv[:, :, t0:t0 + TW]), [], [("b_xt", xs)])
        else:
            D(lambda e: e.dma_start(out=xt[xs][:, :, 0:TW], in_=xsrc(e, t0, TW)), list(xdeps(t0)), [("b_xt", xs)])
        for m in range(8):
            ps = psA if m % 2 == 0 else psB
            pk = "b_psA" if m % 2 == 0 else "b_psB"
            pkw = [pk]
            for c in range(8):
                T(lambda e, c=c, m=m, ps=ps: e.matmul(ps[:, 0:TW], lhsT=w_sb[:, c, m * 128:(m + 1) * 128], rhs=xt[xs][:, c, 0:TW],
                                                      start=(c == 0), stop=(c == 7)), ["b_w", ("b_xt", xs)], pkw)
            A(lambda e, m=m, ps=ps: e.copy(out=u_sb[:, m, 3:3 + TW], in_=ps[:, 0:TW]), [pk], [("b_u", m)])
            ac = acc[m % 2]
            ak = ("b_acc", m % 2)
            A(lambda e, m=m, ac=ac, ps=ps: e.activation(out=ac[:, 0:TW], in_=ps[:, 0:TW], func=AF.Copy, scale=cw[:, m, 3:4]),
              [pk, "b_cw"], [ak])
            for j in (2, 1, 0):
                V(lambda e, m=m, ac=ac, j=j: e.scalar_tensor_tensor(out=ac[:, 0:TW], in0=u_sb[:, m, j:j + TW], scalar=cw[:, m, j:j + 1],
                                                                   in1=ac[:, 0:TW], op0=ALU.mult, op1=ALU.add),
                  [("b_u", m), "b_cw", ak], [ak])
            eng = V
            if m < 4:
                A(lambda e, m=m, ac=ac: e.activation(out=csil[:, m, 0:TW], in_=ac[:, 0:TW], func=AF.Silu), [ak], [("b_csil", m)])
            else:
                A(lambda e, m=m, ac=ac: e.activation(out=vT[:, m - 4, 0:TW], in_=ac[:, 0:TW], func=AF.Silu), [ak], [("b_vT", m - 4)])
            eng(lambda e, m=m: e.tensor_copy(out=u_sb[:, m, 0:3], in_=u_sb[:, m, TW:TW + 3]), [("b_u", m)], [("b_u", m)])
        if DBG_STOP <= 1:
            return
        for m in range(4):
            A(lambda e, m=m: e.activation(out=sq[:, 0:TW], in_=csil[:, m, 0:TW], func=AF.Square), [("b_csil", m)], ["b_sq"])
            T(lambda e: e.matmul(psA[:, 0:TW], lhsT=ones_bf[:], rhs=sq[:, 0:TW], start=True, stop=True), ["b_sq", "b_ones"], ["b_psA"])
            rsqrt_act(rs[:, 0:TW], psA[:, 0:TW], 1.0, EPS, (-0.5 * float(np.log(128.0))) if m < 2 else 0.0, ["b_psA"], "b_rs")
            V(lambda e, m=m: e.tensor_tensor(out=qkT[:, m, 0:TW], in0=csil[:, m, 0:TW], in1=rs[:, 0:TW], op=ALU.mult),
              [("b_csil", m), "b_rs"], [("b_qkT", m)])
        if DBG_STOP <= 2:
            return
        if need_o:
            for h in range(4):
                ps = psA if h % 2 == 0 else psB
                pk = "b_psA" if h % 2 == 0 else "b_psB"
                pkw = [pk]
                for c in range(8):
                    T(lambda e, c=c, h=h, ps=ps: e.matmul(ps[:, 0:TW], lhsT=w_sb[:, c, 1024 + h * 128:1024 + (h + 1) * 128],
                                                          rhs=xt[xs][:, c, 0:TW], start=(c == 0), stop=(c == 7)),
                      ["b_w", ("b_xt", xs)], pkw)
                A(lambda e, h=h, ps=ps: e.activation(out=zs[:, h, 0:TW], in_=ps[:, 0:TW], func=AF.Silu), [pk], [("b_zs", h)])
        if DBG_STOP <= 3:
            return
        bav = psM[0:64, 64:128].rearrange("p (a b) -> p a b", b=8)
        for ck in range(nck):
            for c in range(8):
                T(lambda e, c=c, ck=ck: e.matmul(bav[:, ck, :], lhsT=xt[xs][:, c, ck * 64:(ck + 1) * 64], rhs=w_sb[:, c, 1536:1544],
                                                 start=(c == 0), stop=(c == 7)), ["b_w", ("b_xt", xs)], ["b_psM"])
        V(lambda e: e.tensor_copy(out=ba[:, 0:nck, :], in_=bav[:, 0:nck, :]), ["b_psM"], ["b_ba"])
        V(lambda e: e.tensor_tensor(out=ba[:, 0:nck, 4:8], in0=ba[:, 0:nck, 4:8], in1=dtb[:].unsqueeze(1).to_broadcast([64, nck, 4]), op=ALU.add),
          ["b_ba", "b_dtb"], ["b_ba"])
        A(lambda e: e.activation(out=e1[:, 0:nck, 0:4], in_=ba[:, 0:nck, 0:4], func=AF.Exp, scale=-1.0), ["b_ba"], ["b_e1"])
        A(lambda e: e.activation(out=e1[:, 0:nck, 4:8], in_=ba[:, 0:nck, 4:8], func=AF.Exp), ["b_ba", "b_e1"], ["b_e1"])
        A(lambda e: e.activation(out=e1[:, 0:nck, :], in_=e1[:, 0:nck, :], func=AF.Ln, bias=1.0, scale=1.0), ["b_e1"], ["b_e1"])
        V(lambda e: e.tensor_scalar(out=lnb[:, 0:nck, :], in0=e1[:, 0:nck, 0:4], scalar1=-1.0, scalar2=None, op0=ALU.mult), ["b_e1"], ["b_lnb"])
        A(lambda e: e.activation(out=beta[:, 0:nck, :], in_=lnb[:, 0:nck, :], func=AF.Exp), ["b_lnb"], ["b_beta"])
        V(lambda e: e.tensor_tensor(out=gg[:, 0:nck, :], in0=e1[:, 0:nck, 4:8], in1=negA[:].unsqueeze(1).to_broadcast([64, nck, 4]), op=ALU.mult),
          ["b_e1", "b_negA"], ["b_g"])
        V(lambda e: e.tensor_copy(out=gbf[:, 0:nck, :], in_=gg[:, 0:nck, :]), ["b_g"], ["b_gbf"])
        if DBG_STOP <= 4:
            return
        for ck in range(nck):
            chunk(ti, xs, ck, need_o)
        if need_o:
            for h in range(4):
                A(lambda e, h=h: e.activation(out=sq[:, 0:TW], in_=o_sb[:, h, 0:TW], func=AF.Square), [("b_o", h)], ["b_sq"])
                T(lambda e: e.matmul(psA[:, 0:TW], lhsT=ones_bf[:], rhs=sq[:, 0:TW], start=True, stop=True), ["b_sq", "b_ones"], ["b_psA"])
                rsqrt_act(rs[:, 0:TW], psA[:, 0:TW], 1.0 / 128.0, EPS, 0.0, ["b_psA"], "b_rs")
                V(lambda e, h=h: e.tensor_tensor(out=o_sb[:, h, 0:TW], in0=o_sb[:, h, 0:TW], in1=rs[:, 0:TW], op=ALU.mult),
                  [("b_o", h), "b_rs"], [("b_o", h)])
                V(lambda e, h=h: e.scalar_tensor_tensor(out=ogt[:, h, 0:TW], in0=o_sb[:, h, 0:TW], scalar=onw[:, 0:1], in1=zs[:, h, 0:TW],
                                                        op0=ALU.mult, op1=ALU.mult),
                  [("b_o", h), ("b_zs", h), "b_onw"], [("b_ogt", h)])
            ov = d["ogT"].rearrange("(h p) t -> p h t", p=128)
            D(lambda e: e.dma_start(out=ov[:, :, t0 - 64:t0 - 64 + TW], in_=ogt[:, :, 0:TW]), [("b_ogt", h) for h in range(4)], [("b_ogout", ti)])

    def chunk(ti, xs, ck, need_o):
        c0 = ck * 64
        g_ck = gg[:, ck, :]
        V(lambda e: e.tensor_copy(out=gbb[:], in_=g_ck.unsqueeze(2).to_broadcast([64, 4, 128])), ["b_g"], ["b_gb"])
        A(lambda e: e.copy(out=lbb[:], in_=lnb[:, ck, :].unsqueeze(2).to_broadcast([64, 4, 64])), ["b_lnb"], ["b_lb"])
        Gp = psG[:, 0:256]
        GBp = psG[0:64, 256:512]
        for h in range(4):
            T(lambda e, h=h: e.matmul(Gp[:, h * 64:(h + 1) * 64], lhsT=gbb[:, h, :], rhs=cUb, start=True, stop=True),
              ["b_gb", "b_cst"], ["b_psG"])
        for h in range(4):
            T(lambda e, h=h: e.matmul(GBp[:, h * 64:(h + 1) * 64], lhsT=gbb[:, h, 0:64], rhs=cUb, start=True, stop=False),
              ["b_gb", "b_cst"], ["b_psG"])
            T(lambda e, h=h: e.matmul(GBp[:, h * 64:(h + 1) * 64], lhsT=lbb[:, h, :], rhs=cIb, start=False, stop=True),
              ["b_lb", "b_cst"], ["b_psG"])
        gcol = psM[0:64, 0:4]
        glast = psM[:, 4:8]
        T(lambda e: e.matmul(gcol, lhsT=cUb, rhs=gbf[:, ck, :], start=True, stop=True), ["b_gbf", "b_cst"], ["b_psM"])
        T(lambda e: e.matmul(glast, lhsT=cOnesb, rhs=gbf[:, ck, :], start=True, stop=True), ["b_gbf", "b_cst"], ["b_psM"])
        V(lambda e: e.tensor_copy(out=gc[:], in_=gcol), ["b_psM"], ["b_gc"])
        V(lambda e: e.tensor_tensor(out=gcl[:], in0=gc[:], in1=lnb[:, ck, :], op=ALU.add), ["b_gc", "b_lnb"], ["b_gcl"])
        V(lambda e: e.tensor_tensor(out=ekl[:], in0=glast[0:64, :], in1=gc[:], op=ALU.subtract), ["b_psM", "b_gc"], ["b_ekl"])
        A(lambda e: e.activation(out=ekl[:], in_=ekl[:], func=AF.Exp), ["b_ekl"], ["b_ekl"])
        A(lambda e: e.activation(out=beg[:], in_=gcl[:], func=AF.Exp), ["b_gcl"], ["b_beg"])
        A(lambda e: e.activation(out=egl[:], in_=glast, func=AF.Exp), ["b_psM"], ["b_egl"])
        if DBG_STOP <= 5:
            return
        bc = lambda t_: t_[:].unsqueeze(2).to_broadcast([64, 4, 64])
        a3 = lambda i: args[:, i, :].rearrange("p (a b) -> p a b", a=4)
        p3 = lambda ap: ap.rearrange("p (a b) -> p a b", a=4)
        V(lambda e: e.tensor_tensor(out=a3(0), in0=p3(Gp[0:64, :]), in1=bc(gc), op=ALU.subtract), ["b_psG", "b_gc"], [("b_args", 0)])
        V(lambda e: e.tensor_tensor(out=a3(1), in0=p3(GBp), in1=bc(gc), op=ALU.subtract), ["b_psG", "b_gc"], [("b_args", 1)])
        V(lambda e: e.tensor_tensor(out=a3(2), in0=p3(Gp[0:64, :]), in1=bc(gcl), op=ALU.subtract), ["b_psG", "b_gcl"], [("b_args", 2)])
        G(lambda e: e.tensor_tensor(out=args[:, 0, :], in0=args[:, 0, :], in1=c4(CB_MUI), op=ALU.add), [("b_args", 0), "b_cst"], [("b_args", 0)])
        G(lambda e: e.tensor_tensor(out=args[:, 1, :], in0=args[:, 1, :], in1=c4(CB_MUS), op=ALU.add), [("b_args", 1), "b_cst"], [("b_args", 1)])
        V(lambda e: e.scalar_tensor_tensor(out=args[:, 2, :], in0=args[:, 2, :], scalar=-1.0, in1=c4(CB_MLS), op0=ALU.mult, op1=ALU.add),
          [("b_args", 2), "b_cst"], [("b_args", 2)])
        A(lambda e: e.activation(out=Eall[:], in_=args[:], func=AF.Exp), [("b_args", 0), ("b_args", 1), ("b_args", 2)], ["b_Eall"])
        if need_o:
            A(lambda e: e.activation(out=Eg[:], in_=Gp, func=AF.Exp), ["b_psG"], ["b_Eg"])
        if DBG_STOP <= 6:
            return
        KQ = psM[0:64, 128:384].rearrange("p (a b c) -> p a b c", a=2, b=2)
        A(lambda e: e.copy(out=kcp[:], in_=qkT[:, 2:4, c0:c0 + 64]), [("b_qkT", 2), ("b_qkT", 3)], ["b_kcp"])
        for hk in range(2):
            kch = qkT[:, 2 + hk, c0:c0 + 64]
            T(lambda e, hk=hk, kch=kch: e.matmul(KQ[:, 0, hk, :], lhsT=kch, rhs=kcp[:, hk, :], start=True, stop=True),
              [("b_qkT", 2 + hk), "b_kcp"], ["b_psM"])
            if need_o:
                T(lambda e, hk=hk, kch=kch: e.matmul(KQ[:, 1, hk, :], lhsT=kch, rhs=qkT[:, hk, c0:c0 + 64], start=True, stop=True),
                  [("b_qkT", 2 + hk), ("b_qkT", hk)], ["b_psM"])
        if DBG_STOP == 65:
            return
        pair = lambda ap: ap.unsqueeze(2).to_broadcast([64, 2, 2, 64])
        o4 = lambda t_: t_.rearrange("p (a b c) -> p a b c", a=2, b=2)
        for j in range(2):
            V(lambda e, j=j: e.tensor_tensor(out=o4(Nm[:])[:, :, j, :], in0=KQ[:, 0, :, :], in1=o4(Eall[:, 1, :])[:, :, j, :], op=ALU.mult),
              ["b_psM", "b_Eall"], ["b_N"])
            V(lambda e, j=j: e.tensor_tensor(out=o4(Mm[:])[:, :, j, :], in0=KQ[:, 0, :, :], in1=o4(Eall[:, 2, :])[:, :, j, :], op=ALU.mult),
              ["b_psM", "b_Eall"], ["b_M"])
            if need_o:
                V(lambda e, j=j: e.tensor_tensor(out=o4(attnT[:])[:, :, j, :], in0=KQ[:, 1, :, :], in1=o4(Eall[:, 0, :])[:, :, j, :], op=ALU.mult),
                  ["b_psM", "b_Eall"], ["b_attnT"])
        if DBG_STOP <= 7:
            return
        V(lambda e: e.tensor_tensor(out=L[0][:], in0=Mm[:], in1=c4(CB_BD), op=ALU.mult), ["b_M", "b_cst"], [("b_L", 0)])
        V(lambda e: e.tensor_tensor(out=Uu[0][:], in0=Nm[:], in1=c4(CB_BD), op=ALU.mult), ["b_N", "b_cst"], [("b_U", 0)])
        G(lambda e: e.tensor_tensor(out=O1[:], in0=Mm[:], in1=c4(CB_B1L), op=ALU.mult), ["b_M", "b_cst"], ["b_O1"])
        G(lambda e: e.tensor_tensor(out=N1[:], in0=Nm[:], in1=c4(CB_B1U), op=ALU.mult), ["b_N", "b_cst"], ["b_N1"])
        G(lambda e: e.tensor_tensor(out=O2[:], in0=Mm[:], in1=c4(CB_B2L), op=ALU.mult), ["b_M", "b_cst"], ["b_O2"])
        I4 = cI.unsqueeze(1).to_broadcast([64, 4, 64])
        V(lambda e: e.tensor_tensor(out=p3(PU[0][:]), in0=I4, in1=p3(Uu[0][:]), op=ALU.subtract), ["b_cst", ("b_U", 0)], [("b_PU", 0)])
        V(lambda e: e.tensor_tensor(out=p3(PL[0][:]), in0=I4, in1=p3(L[0][:]), op=ALU.subtract), ["b_cst", ("b_L", 0)], [("b_PL", 0)])
        if DBG_STOP == 71:
            return

        def mm4(lhs, lk, rhs, rk):
            pv = psI[0:64, 0:256]
            for h in range(4):
                T(lambda e, h=h: e.matmul(pv[:, h * 64:(h + 1) * 64], lhsT=lhs[:, h * 64:(h + 1) * 64], rhs=rhs[:, h * 64:(h + 1) * 64],
                                          start=True, stop=True), [lk, rk], ["b_psI"])
            return pv, "b_psI"

        if DBG_STOP == 72:
            return
        pcur = 0
        for k in range(3):
            pv, pk = mm4(Uu[k], ("b_U", k), L[k], ("b_L", k))
            A(lambda e, pv=pv, k=k: e.copy(out=L[k + 1][:], in_=pv), [pk], [("b_L", k + 1)])
            if k < 2:
                pv, pk = mm4(L[k], ("b_L", k), Uu[k], ("b_U", k))
                V(lambda e, pv=pv, k=k: e.tensor_copy(out=Uu[k + 1][:], in_=pv), [pk], [("b_U", k + 1)])
                ulhs, ulk = Uu[k + 1], ("b_U", k + 1)
            else:
                pv, pk = mm4(L[2], ("b_L", 2), Uu[2], ("b_U", 2))
                V(lambda e, pv=pv: e.tensor_copy(out=Y[0][:], in_=pv), [pk], [("b_Y", 0)])
                ulhs, ulk = Y[0], ("b_Y", 0)
            nxt = 1 - pcur
            pv, pk = mm4(L[k + 1], ("b_L", k + 1), PUb[pcur], ("b_PU", pcur))
            V(lambda e, pv=pv, pcur=pcur, nxt=nxt: e.tensor_tensor(out=PU[nxt][:], in0=pv, in1=PU[pcur][:], op=ALU.add),
              [pk, ("b_PU", pcur)], [("b_PU", nxt)])
            pv, pk = mm4(ulhs, ulk, PLb[pcur], ("b_PL", pcur))
            V(lambda e, pv=pv, pcur=pcur, nxt=nxt: e.tensor_tensor(out=PL[nxt][:], in0=pv, in1=PL[pcur][:], op=ALU.add),
              [pk, ("b_PL", pcur)], [("b_PL", nxt)])
            pcur = nxt
        if DBG_STOP == 73:
            return
        TdU, TdUk, TdL, TdLk = PU[pcur], ("b_PU", pcur), PL[pcur], ("b_PL", pcur)
        TdUb, TdUbk, TdLb, TdLbk = PUb[pcur], ("b_PU", pcur), PLb[pcur], ("b_PL", pcur)
        pv, pk = mm4(O1, "b_O1", TdUb, TdUbk)
        A(lambda e, pv=pv: e.copy(out=Y[0][:], in_=pv), [pk], [("b_Y", 0)])
        pv, pk = mm4(TdLb, TdLbk, Y[0], ("b_Y", 0))
        V(lambda e, pv=pv: e.tensor_tensor(out=T32U[:], in0=TdU[:], in1=pv, op=ALU.subtract), [pk, TdUk], ["b_T32U"])
        pv, pk = mm4(N1, "b_N1", TdLb, TdLbk)
        A(lambda e, pv=pv: e.copy(out=Y[1][:], in_=pv), [pk], [("b_Y", 1)])
        pv, pk = mm4(TdUb, TdUbk, Y[1], ("b_Y", 1))
        V(lambda e, pv=pv: e.tensor_tensor(out=T32Lb[:], in0=TdL[:], in1=pv, op=ALU.subtract), [pk, TdLk], ["b_T32L"])
        pv, pk = mm4(O2, "b_O2", T32Ub, "b_T32U")
        A(lambda e, pv=pv: e.copy(out=Y[0][:], in_=pv), [pk], [("b_Y", 0)])
        pv, pk = mm4(T32Lb, "b_T32L", Y[0], ("b_Y", 0))
        V(lambda e, pv=pv: e.tensor_tensor(out=TTb[:], in0=T32U[:], in1=pv, op=ALU.subtract), [pk, "b_T32U"], ["b_TTb"])
        if DBG_STOP <= 8:
            return
        tv = psT[0:64, 0:768].rearrange("p (a b) -> p a b", a=6)
        for hk in range(2):
            T(lambda e, hk=hk: e.transpose(out=tv[:, hk, :], in_=qkT[:, 2 + hk, c0:c0 + 64], identity=ident[:]),
              [("b_qkT", 2 + hk), "b_ident"], ["b_psT"])
        for h in range(4):
            T(lambda e, h=h: e.transpose(out=tv[:, 2 + h, :], in_=vT[:, h, c0:c0 + 64], identity=ident[:]), [("b_vT", h), "b_ident"], ["b_psT"])
        bc128 = lambda ap: ap.unsqueeze(2).to_broadcast([64, 4, 128])
        kpair = tv[:, 0:2, :].unsqueeze(2).to_broadcast([64, 2, 2, 128])
        k4 = lambda t_: t_[:].rearrange("p (a b) c -> p a b c", a=2)
        s4 = lambda ap: ap.rearrange("p (a b) -> p a b", a=2).unsqueeze(3).to_broadcast([64, 2, 2, 128])
        V(lambda e: e.tensor_tensor(out=vb[:], in0=tv[:, 2:6, :], in1=bc128(beta[:, ck, :]), op=ALU.mult), ["b_psT", "b_beta"], ["b_vb"])
        V(lambda e: e.tensor_tensor(out=k4(kbg), in0=kpair, in1=s4(beg[:]), op=ALU.mult), ["b_psT", "b_beg"], ["b_kbg"])
        V(lambda e: e.tensor_tensor(out=k4(kst), in0=kpair, in1=s4(ekl[:]), op=ALU.mult), ["b_psT", "b_ekl"], ["b_kst"])
        if DBG_STOP <= 9:
            return
        wTp = psB[:, 0:256]
        for h in range(4):
            T(lambda e, h=h: e.matmul(wTp[:, h * 64:(h + 1) * 64], lhsT=kbg[:, h, :], rhs=TTb[:, h * 64:(h + 1) * 64], start=True, stop=True),
              ["b_kbg", "b_TTb"], ["b_psB"])
        A(lambda e: e.activation(out=wTn[:], in_=wTp, func=AF.Copy, scale=-1.0), ["b_psB"], ["b_wTn"])
        if DBG_STOP <= 10:
            return
        vp = psV[0:64, :].rearrange("p (a b) -> p a b", a=4)
        for h in range(4):
            T(lambda e, h=h: e.matmul(vp[:, h, :], lhsT=TTb[:, h * 64:(h + 1) * 64], rhs=vb[:, h, :], start=True, stop=False),
              ["b_TTb", "b_vb"], ["b_psV"])
            T(lambda e, h=h: e.matmul(vp[:, h, :], lhsT=wTn[:, h * 64:(h + 1) * 64], rhs=Sb[:, h, :], start=False, stop=True),
              ["b_wTn", "b_Sb"], ["b_psV"])
        A(lambda e: e.copy(out=vnb[:], in_=vp), ["b_psV"], ["b_vnb"])
        if DBG_STOP <= 11:
            return
        if need_o:
            qpair = qkT[:, 0:2, c0:c0 + 64].unsqueeze(2).to_broadcast([128, 2, 2, 64])
            V(lambda e: e.tensor_tensor(out=qg[:].rearrange("p (a b c) -> p a b c", a=2, b=2),
                                        in0=Eg[:].rearrange("p (a b c) -> p a b c", a=2, b=2), in1=qpair, op=ALU.mult),
              [("b_qkT", 0), ("b_qkT", 1), "b_Eg"], ["b_qg"])
            oTp = psB[:, 256:512]
            for h in range(4):
                T(lambda e, h=h: e.matmul(oTp[:, h * 64:(h + 1) * 64], lhsT=Sb[:, h, :], rhs=qg[:, h * 64:(h + 1) * 64], start=True, stop=False),
                  ["b_Sb", "b_qg"], ["b_psB"])
                T(lambda e, h=h: e.matmul(oTp[:, h * 64:(h + 1) * 64], lhsT=vnb[:, h, :], rhs=attnT[:, h * 64:(h + 1) * 64], start=False, stop=True),
                  ["b_vnb", "b_attnT"], ["b_psB"])
            A(lambda e: e.copy(out=o_sb[:, :, c0:c0 + 64], in_=oTp.rearrange("p (a b) -> p a b", a=4)), ["b_psB"],
              [("b_o", h) for h in range(4)])
        if DBG_STOP <= 12:
            return
        sp_ = psS[:].rearrange("p (a b) -> p a b", a=4)
        for h in range(4):
            T(lambda e, h=h: e.matmul(sp_[:, h, :], lhsT=kst[:, h, :], rhs=vnb[:, h, :], start=True, stop=True), ["b_kst", "b_vnb"], ["b_psS"])
        for h in range(4):
            V(lambda e, h=h: e.scalar_tensor_tensor(out=S32[:, h, :], in0=S32[:, h, :], scalar=egl[:, h:h + 1], in1=sp_[:, h, :],
                                                    op0=ALU.mult, op1=ALU.add), ["b_S32", "b_egl", "b_psS"], ["b_S32"])
        A(lambda e: e.copy(out=Sb[:], in_=S32[:]), ["b_S32"], ["b_Sb"])

    tile(0, 0, 64, False)
    ntile = (nchunks - 1) // 8
    assert ntile * 8 + 1 == nchunks
    for ti in range(ntile):
        tile(ti + 1, 64 + ti * TM, TM, True)


def dn_weight_layout(r, dn_norm_w, dn_w_in, dn_conv_w, dn_a_log, dn_dt_bias, dn_o_norm_w):
    w = np.asarray(dn_w_in[0], np.float32)
    qc = list(range(2 * r * 128, (2 * r + 2) * 128))
    kc = [1024 + c for c in qc]
    vc = list(range(2048 + 4 * r * 128, 2048 + (4 * r + 4) * 128))
    zc = list(range(4096 + 4 * r * 128, 4096 + (4 * r + 4) * 128))
    bcol = list(range(6144 + 4 * r, 6144 + 4 * r + 4))
    acol = list(range(6160 + 4 * r, 6160 + 4 * r + 4))
    cols = qc + kc + vc + zc + bcol + acol
    out = {}
    out["wB"] = np.ascontiguousarray(w[:, cols])
    out["nwB"] = np.ascontiguousarray(np.asarray(dn_norm_w[0], np.float32).reshape(8, 128).T)
    cwf = np.asarray(dn_conv_w[0], np.float32)[:, qc + kc + vc]
    out["convw"] = np.ascontiguousarray(cwf.reshape(4, 8, 128).transpose(2, 1, 0))
    out["onw"] = np.ascontiguousarray(np.asarray(dn_o_norm_w[0], np.float32).reshape(128, 1))
    out["alog"] = np.ascontiguousarray(np.broadcast_to(np.asarray(dn_a_log[0], np.float32)[None, 4 * r:4 * r + 4], (64, 4)))
    out["dtb"] = np.ascontiguousarray(np.broadcast_to(np.asarray(dn_dt_bias[0], np.float32)[None, 4 * r:4 * r + 4], (64, 4)))
    out["cstB"] = dn_consts()
    return out


RG8 = [[0, 1, 2, 3, 4, 5, 6, 7]]


def build_fused():
    import contextlib
    nc = bass.Bass("TRN2", target_bir_lowering=False)
    ext = lambda name, shape, dt=F32: nc.dram_tensor(name, shape, dt, kind="ExternalInput").ap()
    dA = {}
    dA["xa"] = ext("xa", [33 * 128, 1024])
    dA["xm"] = ext("xm", [128, 1024])
    dA["w_in"] = ext("w_in", [1024, 2304])
    dA["nw"] = ext("nw", [128, 8])
    dA["wq"] = ext("wq", [128, 128])
    dA["wk"] = ext("wk", [128, 128])
    dA["sinks"] = ext("sinks", [128, 16])
    dA["w_out"] = ext("w_out", [1024, 1024])
    dA["tabs"] = ext("tabs", [128, 4, 2048])
    dA["mtabs"] = ext("mtabs", [16, 3, 2048])
    dB = {}
    dB["wB"] = ext("wB", [1024, 1544])
    dB["nwB"] = ext("nwB", [128, 8])
    dB["convw"] = ext("convw", [128, 8, 4])
    dB["onw"] = ext("onw", [128, 1])
    dB["alog"] = ext("alog", [64, 4])
    dB["dtb"] = ext("dtb", [64, 4])
    dB["cstB"] = ext("cstB", [64, CB_COLS])
    wout = ext("wout", [2048, 1024])
    y = nc.dram_tensor("y", [4096, 1024], F32, kind="ExternalOutput").ap()
    h1_loc = nc.dram_tensor("h1_loc", [4096, 1024], F32).ap()
    xn1T_loc = nc.dram_tensor("xn1T_loc", [1024, 4160], BF16).ap()
    xnT_all = nc.dram_tensor("xnT_all", [8 * 1024, 4160], BF16).ap()
    ogT_loc = nc.dram_tensor("ogT_loc", [512, 16384], BF16).ap()
    ogT_all = nc.dram_tensor("ogT_all", [8 * 512, 16384], BF16).ap()
    xnT_mine = nc.dram_tensor("xnT_mine", [1024, 16448], BF16).ap()
    ogT_mine = nc.dram_tensor("ogT_mine", [2048, 4096], BF16).ap()
    dA["h1"] = h1_loc
    dA["xn1T"] = xn1T_loc
    dB["ogT"] = ogT_loc
    with contextlib.ExitStack() as outer:
        with contextlib.ExitStack() as st:
            P = Prog(nc, n_dma_sems=16, sem_stack=outer, prefix="A", barrier=True)
            emit_phase_a(nc, st, P, dA, 33)
            P.emit()
        with contextlib.ExitStack() as st:
            P = Prog(nc, n_dma_sems=16, sem_stack=outer, prefix="B", barrier=True)
            P.add("pool", lambda e: e.collective_compute("AllGather", ALU.bypass, replica_groups=RG8, ins=[xn1T_loc], outs=[xnT_all]),
                  writes=["xnT_all"], dma=True, inc=1)

            xcache = {}

            def seg_copy(e, sg_):
                if "b" not in xcache:
                    xcache["b"] = e.snap(e.partition_id() // 4, min_val=0, max_val=1)
                src = xnT_all.rearrange("(r d) t -> r d t", d=1024)[bass.ds(xcache["b"] * 4 + sg_, 1), :, :]
                src = src.rearrange("o d t -> (o d) t")
                if sg_ == 0:
                    return e.dma_start(out=xnT_mine[:, 0:4160], in_=src)
                return e.dma_start(out=xnT_mine[:, 64 + sg_ * 4096:64 + (sg_ + 1) * 4096], in_=src[:, 64:4160])

            for sg_ in range(4):
                P.add("sp", lambda e, sg_=sg_: seg_copy(e, sg_), reads=["xnT_all"], writes=[("xnT_mine", sg_)], dma=True)

            def xsrc(e, t0, TW):
                return xnT_mine[:, t0:t0 + TW].rearrange("(c p) t -> p c t", p=128)

            xdep_fn = lambda t0: [("xnT_mine", 0 if t0 == 0 else (t0 - 64) // 4096)]
            emit_phase_b(nc, st, P, dB, 257, xsrc=xsrc, xdeps=xdep_fn)
            P.emit()
        with contextlib.ExitStack() as st:
            P = Prog(nc, n_dma_sems=16, sem_stack=outer, prefix="C", barrier=False)
            P.add("pool", lambda e: e.collective_compute("AllGather", ALU.bypass, replica_groups=RG8, ins=[ogT_loc], outs=[ogT_all]),
                  writes=["ogT_all"], dma=True, inc=1)

            ocache = {}

            def og_copy(e, r_):
                if "b" not in ocache:
                    pid = e.partition_id()
                    ocache["b"] = e.snap(pid // 4, min_val=0, max_val=1)
                    ocache["s"] = e.snap(pid % 4, min_val=0, max_val=3)
                v = ogT_all.rearrange("(r f) (s t) -> r f s t", f=512, s=4)
                src = v[bass.ds(ocache["b"] * 4 + r_, 1), :, bass.ds(ocache["s"], 1), :].rearrange("o f q t -> (o f) (q t)")
                return e.dma_start(out=ogT_mine[r_ * 512:(r_ + 1) * 512, :], in_=src)

            for r_ in range(4):
                P.add("sp", lambda e, r_=r_: og_copy(e, r_), reads=["ogT_all"], writes=["ogT_mine"], dma=True)
            emit_phase_c(nc, st, P, ogT_mine, h1_loc, wout, y, 4096, ogdeps=["ogT_mine"])
            P.emit()
    return nc


_NC_CACHE = {}


def _get_nc(name, builder):
    if name not in _NC_CACHE:
        _NC_CACHE[name] = builder()
    return _NC_CACHE[name]


def kernel(x, meta_tokens, attn_norm_w, attn_w_in, attn_q_norm_w, attn_k_norm_w, attn_sinks, attn_w_out,
           dn_norm_w, dn_w_in, dn_conv_w, dn_a_log, dn_dt_bias, dn_o_norm_w, dn_w_out):
    x = np.asarray(x, np.float32)
    meta = np.asarray(meta_tokens, np.float32)
    cores = list(range(8))
    SEG = 4096
    WA = attn_weight_layout(attn_norm_w, attn_w_in, attn_q_norm_w, attn_k_norm_w, attn_sinks, attn_w_out)
    xm = np.zeros((128, 1024), np.float32)
    xm[:16] = meta
    tabs0, tabs1 = attn_tables(True), attn_tables(False)
    wout = np.ascontiguousarray(np.asarray(dn_w_out[0], np.float32))
    WBs = [dn_weight_layout(r, dn_norm_w, dn_w_in, dn_conv_w, dn_a_log, dn_dt_bias, dn_o_norm_w) for r in range(4)]
    maps = []
    for c in cores:
        b, s = c // 4, c % 4
        if s == 0:
            halo = np.concatenate([np.zeros((112, 1024), np.float32), meta], 0)
        else:
            halo = x[b, s * SEG - 128:s * SEG]
        xa = np.ascontiguousarray(np.concatenate([halo, x[b, s * SEG:(s + 1) * SEG]], 0))
        tb, mtb = tabs0 if s == 0 else tabs1
        maps.append(dict(xa=xa, xm=xm, tabs=tb, mtabs=mtb, wout=wout, **WA, **WBs[s]))
    res = run_bass_kernel_spmd(_get_nc("F", build_fused), maps, core_ids=cores).results
    out = np.empty((2, 16384, 1024), np.float32)
    for c in cores:
        b, s = c // 4, c % 4
        out[b, s * SEG:(s + 1) * SEG] = np.asarray(res[c]["y"])
    return out
```

```python
import numpy as np
import ml_dtypes
import concourse.bass as bass
import concourse.mybir as mybir
from concourse.bass_utils import run_bass_kernel_spmd

F32 = mybir.dt.float32
BF16 = mybir.dt.bfloat16
AF = mybir.ActivationFunctionType
ALU = mybir.AluOpType
AX = mybir.AxisListType

NPBF = ml_dtypes.bfloat16


class Prog:
    COMPUTE = ("pe", "act", "dve", "pool")

    def __init__(self, nc, n_dma_sems=32, sem_stack=None, prefix="", barrier=False):
        self.nc = nc
        self.sem_stack = sem_stack
        self.prefix = prefix
        self.barrier = barrier
        self.ops = []
        self.last_w = {}
        self.readers = {}
        self.n_dma_sems = n_dma_sems
        self.dma_count = 0
        self.dma_last = [None] * n_dma_sems
        self.dma_uses = [0] * n_dma_sems

    def add(self, eng, fn, reads=(), writes=(), dma=False, inc=16):
        oid = len(self.ops)
        deps = set()
        isps = lambda k: (k if isinstance(k, str) else str(k[0])).startswith("b_ps")
        if any(isps(r) for r in reads):
            writes = list(writes) + [r for r in reads if isps(r)]
            reads = [r for r in reads if not isps(r)]
        for r in reads:
            if r in self.last_w:
                deps.add(self.last_w[r])
        for w in writes:
            if w in self.last_w:
                deps.add(self.last_w[w])
            deps |= self.readers.get(w, set())
        op = dict(eng=eng, fn=fn, deps=deps, dma=dma, signal=False, sigval=None)
        if dma:
            k = self.dma_count % self.n_dma_sems
            self.dma_count += 1
            if self.dma_last[k] is not None:
                deps.add(self.dma_last[k])
            self.dma_last[k] = oid
            self.dma_uses[k] += inc
            op["dsem"] = k
            op["dinc"] = inc
            op["dval"] = self.dma_uses[k]
        deps.discard(oid)
        self.ops.append(op)
        for r in reads:
            self.readers.setdefault(r, set()).add(oid)
        for w in writes:
            self.last_w[w] = oid
            self.readers[w] = set()
        return oid

    def emit(self):
        nc = self.nc
        ops = self.ops
        for op in ops:
            for d in op["deps"]:
                D = ops[d]
                if D["dma"]:
                    continue
                if D["eng"] == op["eng"] == "pe" and not op["dma"]:
                    continue
                D["signal"] = True
        if self.barrier:
            last = {}
            for i, op in enumerate(ops):
                if not op["dma"]:
                    last[op["eng"]] = i
            for i in last.values():
                ops[i]["signal"] = True
        cnt = {e: 0 for e in self.COMPUTE}
        for op in ops:
            if op["dma"]:
                continue
            if op["signal"]:
                cnt[op["eng"]] += 1
                op["sigval"] = cnt[op["eng"]]
        by_eng = {e: [] for e in ("pe", "act", "dve", "pool", "sp")}
        for op in ops:
            by_eng[op["eng"]].append(op)
        import contextlib
        with contextlib.ExitStack() as st:
            sst = self.sem_stack if self.sem_stack is not None else st
            csem = {e: sst.enter_context(nc.semaphore(self.prefix + "cs_" + e)) for e in self.COMPUTE}
            dsem = [sst.enter_context(nc.semaphore(self.prefix + "ds_%d" % k)) for k in range(self.n_dma_sems)]
            block = st.enter_context(nc.Block())

            def run(engname, eng):
                waited = {}
                for op in by_eng[engname]:
                    targets = []
                    for d in sorted(op["deps"]):
                        D = ops[d]
                        if D["dma"]:
                            targets.append((("d", D["dsem"]), dsem[D["dsem"]], D["dval"]))
                        else:
                            if D["eng"] == engname == "pe" and not op["dma"]:
                                continue
                            targets.append((("c", D["eng"]), csem[D["eng"]], D["sigval"]))
                    best = {}
                    for key, sem, val in targets:
                        if val > waited.get(key, 0) and val > best.get(key, (None, 0))[1]:
                            best[key] = (sem, val)
                    for key, (sem, val) in best.items():
                        eng.wait_ge(sem, val)
                        waited[key] = val
                    ins = op["fn"](eng)
                    if op["dma"]:
                        ins.then_inc(dsem[op["dsem"]], op["dinc"])
                    elif op["signal"]:
                        ins.then_inc(csem[engname], 1)
                if engname == "sp" or self.barrier:
                    for k in range(self.n_dma_sems):
                        if self.dma_uses[k]:
                            eng.wait_ge(dsem[k], self.dma_uses[k])
                if self.barrier:
                    for e2 in self.COMPUTE:
                        if cnt[e2]:
                            eng.wait_ge(csem[e2], cnt[e2])

            @block.tensor
            def _(e):
                run("pe", e)

            @block.scalar
            def _(e):
                run("act", e)

            @block.vector
            def _(e):
                run("dve", e)

            @block.gpsimd
            def _(e):
                run("pool", e)

            @block.sync
            def _(e):
                run("sp", e)


def build_phase_c(ntok=4096):
    nc = bass.Bass("TRN2", target_bir_lowering=False)
    ogT = nc.dram_tensor("ogT", [2048, ntok], BF16, kind="ExternalInput").ap()
    h1 = nc.dram_tensor("h1", [ntok, 1024], F32, kind="ExternalInput").ap()
    wout = nc.dram_tensor("wout", [2048, 1024], F32, kind="ExternalInput").ap()
    y = nc.dram_tensor("y", [ntok, 1024], F32, kind="ExternalOutput").ap()
    import contextlib
    with contextlib.ExitStack() as st:
        P = Prog(nc)
        emit_phase_c(nc, st, P, ogT, h1, wout, y, ntok)
        P.emit()
    return nc


def emit_phase_c(nc, st, P, ogT, h1, wout, y, ntok, ogsrc=None, ogdeps=()):
    TT = 512
    nt = ntok // TT
    w_sb = st.enter_context(nc.sbuf_tensor("c_w", [128, 16, 1024], BF16))
    wst = [st.enter_context(nc.sbuf_tensor("c_wst%d" % i, [128, 2, 1024], F32)) for i in range(2)]
    og_sb = [st.enter_context(nc.sbuf_tensor("c_og%d" % i, [128, 16, TT], BF16)) for i in range(2)]
    h_sb = [st.enter_context(nc.sbuf_tensor("c_h%d" % i, [128, 1024], F32)) for i in range(3)]
    y_sb = [st.enter_context(nc.sbuf_tensor("c_y%d" % i, [128, 1024], F32)) for i in range(3)]
    ps = [st.enter_context(nc.psum_tensor("c_ps%d" % i, [128, 512], F32)) for i in range(4)]
    woutv = wout.rearrange("(c p) n -> p c n", p=128)
    for j in range(8):
        s = j % 2
        P.add("sp", lambda e, j=j, s=s: e.dma_start(out=wst[s][:], in_=woutv[:, 2 * j:2 * j + 2, :]),
              writes=[("c_wst", s)], dma=True)
        eng = "act" if j % 2 == 0 else "dve"
        if eng == "act":
            P.add("act", lambda e, j=j, s=s: e.copy(out=w_sb[:, 2 * j:2 * j + 2, :], in_=wst[s][:]),
                  reads=[("c_wst", s)], writes=[("c_w", j)])
        else:
            P.add("dve", lambda e, j=j, s=s: e.tensor_copy(out=w_sb[:, 2 * j:2 * j + 2, :], in_=wst[s][:]),
                  reads=[("c_wst", s)], writes=[("c_w", j)])
    ogv = ogT.rearrange("(c p) t -> p c t", p=128) if ogsrc is None else None
    blk = 0
    for t in range(nt):
        so = t % 2
        if ogsrc is None:
            P.add("sp", lambda e, t=t, so=so: e.dma_start(out=og_sb[so][:], in_=ogv[:, :, t * TT:(t + 1) * TT]),
                  reads=list(ogdeps), writes=[("c_og", so)], dma=True)
        else:
            P.add("sp", lambda e, t=t, so=so: e.dma_start(out=og_sb[so][:], in_=ogsrc(e, t * TT, TT)),
                  reads=list(ogdeps), writes=[("c_og", so)], dma=True)
        for b in range(TT // 128):
            r0 = t * TT + b * 128
            sh = blk % 3
            P.add("sp", lambda e, r0=r0, sh=sh: e.dma_start(out=h_sb[sh][:], in_=h1[r0:r0 + 128, :]),
                  writes=[("c_h", sh)], dma=True)
            for g in range(2):
                pb = (blk * 2 + g) % 4
                for c in range(16):
                    P.add("pe", lambda e, pb=pb, so=so, b=b, c=c, g=g: e.matmul(
                        ps[pb][:], lhsT=og_sb[so][:, c, b * 128:(b + 1) * 128],
                        rhs=w_sb[:, c, g * 512:(g + 1) * 512], start=(c == 0), stop=(c == 15)),
                        reads=[("c_og", so), ("c_w", c // 2)], writes=[("c_ps", pb)])
                P.add("dve", lambda e, pb=pb, sh=sh, g=g: e.tensor_tensor(
                    out=y_sb[sh][:, g * 512:(g + 1) * 512], in0=ps[pb][:],
                    in1=h_sb[sh][:, g * 512:(g + 1) * 512], op=ALU.add),
                    reads=[("c_ps", pb), ("c_h", sh)], writes=[("c_y", sh, g)])
            P.add("sp", lambda e, r0=r0, sh=sh: e.dma_start(out=y[r0:r0 + 128, :], in_=y_sb[sh][:]),
                  reads=[("c_y", sh, 0), ("c_y", sh, 1)], writes=[("c_yout", blk)], dma=True)
            blk += 1


EPS = 1e-6


def make_identity(nc, st, P, name):
    identf = st.enter_context(nc.sbuf_tensor(name + "_f", [128, 128], F32))
    ident = st.enter_context(nc.sbuf_tensor(name, [128, 128], BF16))
    P.add("pool", lambda e: e.memset(identf[:], 1.0), writes=[name + "_f"])
    P.add("pool", lambda e: e.affine_select(out=identf[:], in_=identf[:], pattern=[[-1, 128]],
                                             compare_op=ALU.is_equal, fill=0.0, base=0,
                                             channel_multiplier=1),
          reads=[name + "_f"], writes=[name + "_f"])
    P.add("dve", lambda e: e.tensor_copy(out=ident[:], in_=identf[:]), reads=[name + "_f"], writes=[name])
    return ident, identf


def build_phase_a(nb=33):
    nc = bass.Bass("TRN2", target_bir_lowering=False)
    d = {}
    d["xa"] = nc.dram_tensor("xa", [nb * 128, 1024], F32, kind="ExternalInput").ap()
    d["xm"] = nc.dram_tensor("xm", [128, 1024], F32, kind="ExternalInput").ap()
    d["w_in"] = nc.dram_tensor("w_in", [1024, 2304], F32, kind="ExternalInput").ap()
    d["nw"] = nc.dram_tensor("nw", [128, 8], F32, kind="ExternalInput").ap()
    d["wq"] = nc.dram_tensor("wq", [128, 128], F32, kind="ExternalInput").ap()
    d["wk"] = nc.dram_tensor("wk", [128, 128], F32, kind="ExternalInput").ap()
    d["sinks"] = nc.dram_tensor("sinks", [128, 16], F32, kind="ExternalInput").ap()
    d["w_out"] = nc.dram_tensor("w_out", [1024, 1024], F32, kind="ExternalInput").ap()
    d["tabs"] = nc.dram_tensor("tabs", [128, 4, 2048], F32, kind="ExternalInput").ap()
    d["mtabs"] = nc.dram_tensor("mtabs", [16, 3, 2048], F32, kind="ExternalInput").ap()
    d["h1"] = nc.dram_tensor("h1", [(nb - 1) * 128, 1024], F32, kind="ExternalOutput").ap()
    d["xn1T"] = nc.dram_tensor("xn1T", [1024, 64 + (nb - 1) * 128], BF16, kind="ExternalOutput").ap()
    import contextlib
    with contextlib.ExitStack() as st:
        P = Prog(nc)
        emit_phase_a(nc, st, P, d, nb)
        P.emit()
    return nc


def emit_phase_a(nc, st, P, d, nb):
    sb = lambda name, shape, dt: st.enter_context(nc.sbuf_tensor(name, shape, dt))
    ident, identf = make_identity(nc, st, P, "a_ident")
    w_sb = sb("a_w", [128, 8, 2304], BF16)
    wo_sb = sb("a_wo", [128, 8, 1024], BF16)
    wst = [sb("a_wst%d" % i, [128, 2304], F32) for i in range(2)]
    nw = sb("a_nw", [128, 8], F32)
    wqk = sb("a_wqk", [128, 128], F32)
    wk_t = sb("a_wk", [128, 128], F32)
    esink = sb("a_esink", [128, 16], F32)
    tabs = sb("a_tabs", [128, 4, 2048], F32)
    mtabs = sb("a_mtabs", [16, 3, 2048], F32)
    x_sb = [sb("a_x%d" % i, [128, 1024], F32) for i in range(2)]
    junk = sb("a_junk", [128, 1024], F32)
    st_sb = sb("a_stat", [128, 8], F32)
    xn = sb("a_xn", [128, 1024], BF16)
    xnT = sb("a_xnT", [128, 8, 128], BF16)
    qss = sb("a_qss", [128, 16], F32)
    qr = sb("a_qr", [128, 16], F32)
    kss = sb("a_kss", [128, 4], F32)
    qn = sb("a_qn", [128, 1024], BF16)
    kn = sb("a_kn", [128, 128], F32)
    ksq = sb("a_ksq", [128, 128], F32)
    kq = sb("a_kq", [128, 128], BF16)
    gs = sb("a_gs", [128, 1024], BF16)
    QT = sb("a_QT", [128, 1024], BF16)
    KT = [sb("a_KT%d" % i, [128, 128], BF16) for i in range(2)]
    KTm = sb("a_KTm", [128, 128], BF16)
    vE = [sb("a_vE%d" % i, [128, 2, 65], BF16) for i in range(2)]
    vEm = sb("a_vEm", [128, 2, 65], BF16)
    E_sb = [sb("a_E%d" % i, [128, 512], F32) for i in range(2)]
    PTp = sb("a_PTp", [128, 16, 128], BF16)
    PTc = sb("a_PTc", [128, 16, 128], BF16)
    PTm = sb("a_PTm", [16, 16, 128], BF16)
    den = sb("a_den", [128, 16], F32)
    rden = sb("a_rden", [128, 16], F32)
    ogp = sb("a_ogp", [128, 1024], F32)
    og = sb("a_og", [128, 1024], BF16)
    ogT = sb("a_ogT", [128, 8, 128], BF16)
    h1 = [sb("a_h1%d" % i, [128, 1024], F32) for i in range(2)]
    xn1 = sb("a_xn1", [128, 1024], BF16)
    xn1T = [sb("a_xn1T%d" % i, [128, 8, 128], BF16) for i in range(2)]

    ps_t = st.enter_context(nc.psum_tensor("a_pst", [128, 8, 128], BF16))
    ps_u = [st.enter_context(nc.psum_tensor("a_psu%d" % i, [128, 512], F32)) for i in range(2)]
    ps_s = [st.enter_context(nc.psum_tensor("a_pss%d" % i, [128, 512], F32)) for i in range(2)]
    ps_o = st.enter_context(nc.psum_tensor("a_pso", [128, 3, 512], F32))

    P.add("sp", lambda e: e.dma_start(out=nw[:], in_=d["nw"]), writes=["a_nw"], dma=True)
    P.add("sp", lambda e: e.dma_start(out=wqk[:], in_=d["wq"]), writes=["a_wqk"], dma=True)
    P.add("sp", lambda e: e.dma_start(out=wk_t[:], in_=d["wk"]), writes=["a_wk"], dma=True)
    P.add("sp", lambda e: e.dma_start(out=esink[:], in_=d["sinks"]), writes=["a_esink"], dma=True)
    P.add("dve", lambda e: e.scalar_tensor_tensor(out=wqk[:], in0=wqk[:], scalar=8.0, in1=wk_t[:],
                                                  op0=ALU.mult, op1=ALU.mult),
          reads=["a_wqk", "a_wk"], writes=["a_wqk"])
    P.add("act", lambda e: e.activation(out=esink[:], in_=esink[:], func=AF.Exp), reads=["a_esink"], writes=["a_esink"])
    for i in range(2):
        P.add("pool", lambda e, i=i: e.memset(vE[i][:], 1.0), writes=[("a_vE", i)])
    P.add("pool", lambda e: e.memset(vEm[:], 1.0), writes=["a_vEm"])
    w_inv = d["w_in"].rearrange("(c p) n -> p c n", p=128)
    for c in range(8):
        s = c % 2
        P.add("sp", lambda e, c=c, s=s: e.dma_start(out=wst[s][:], in_=w_inv[:, c, :]), writes=[("a_wst", s)], dma=True)
        if c % 2 == 0:
            P.add("dve", lambda e, c=c, s=s: e.tensor_scalar(out=w_sb[:, c, :], in0=wst[s][:], scalar1=nw[:, c:c + 1],
                                                           scalar2=None, op0=ALU.mult),
                  reads=[("a_wst", s), "a_nw"], writes=[("a_w", c)])
        else:
            P.add("act", lambda e, c=c, s=s: e.activation(out=w_sb[:, c, :], in_=wst[s][:], func=AF.Copy, scale=nw[:, c:c + 1]),
                  reads=[("a_wst", s), "a_nw"], writes=[("a_w", c)])
    w_outv = d["w_out"].rearrange("(c p) n -> p c n", p=128)
    for c in range(4):
        s = c % 2
        P.add("sp", lambda e, c=c, s=s: e.dma_start(out=wst[s][:, 0:2048].rearrange("p (a n) -> p a n", a=2),
                                                   in_=w_outv[:, 2 * c:2 * c + 2, :]), writes=[("a_wst", s)], dma=True)
        P.add("dve" if c % 2 == 0 else "pool",
              lambda e, c=c, s=s: e.tensor_copy(out=wo_sb[:, 2 * c:2 * c + 2, :],
                                                in_=wst[s][:, 0:2048].rearrange("p (a n) -> p a n", a=2)),
              reads=[("a_wst", s)], writes=[("a_wo", c)])
    for j in range(4):
        P.add("sp", lambda e, j=j: e.dma_start(out=tabs[:, j, :], in_=d["tabs"][:, j, :]), writes=[("a_tabs", j)], dma=True)
    P.add("sp", lambda e: e.dma_start(out=mtabs[:], in_=d["mtabs"]), writes=["a_mtabs"], dma=True)
    W_ALL = [("a_w", c) for c in range(8)]
    WO_ALL = [("a_wo", c) for c in range(4)]

    def rstd_from_ss(ss_ap, ln_ap, out_ap, bias, key_in, key_out):
        P.add("act", lambda e: e.activation(out=ln_ap, in_=ss_ap, func=AF.Ln, bias=bias, scale=1.0),
              reads=[key_in], writes=[key_out + "_ln"])
        P.add("act", lambda e: e.activation(out=out_ap, in_=ln_ap, func=AF.Exp, scale=-0.5),
              reads=[key_out + "_ln"], writes=[key_out])

    def transposes8(src, src_key, dstT, dst_key, evac_eng):
        for c in range(8):
            P.add("pe", lambda e, c=c: e.transpose(out=ps_t[:, c, :], in_=src[:, c * 128:(c + 1) * 128], identity=ident[:]),
                  reads=[src_key, "a_ident"], writes=["a_pst"])
        if evac_eng == "act":
            P.add("act", lambda e: e.copy(out=dstT, in_=ps_t[:]), reads=["a_pst"], writes=[dst_key])
        else:
            P.add(evac_eng, lambda e: e.tensor_copy(out=dstT, in_=ps_t[:]), reads=["a_pst"], writes=[dst_key])

    def front(src_ap, xs, KT_dst, KT_key, vE_dst, vE_key, full):
        P.add("sp", lambda e: e.dma_start(out=x_sb[xs][:], in_=src_ap), writes=[("a_x", xs)], dma=True)
        P.add("act", lambda e: e.activation(out=junk[:], in_=x_sb[xs][:], func=AF.Square, accum_out=st_sb[:, 0:1]),
              reads=[("a_x", xs)], writes=["a_junk", "a_ss"])
        rstd_from_ss(st_sb[:, 0:1], st_sb[:, 1:2], st_sb[:, 2:3], 1024 * EPS, "a_ss", "a_rstd")
        P.add("dve", lambda e: e.tensor_scalar(out=xn[:], in0=x_sb[xs][:], scalar1=st_sb[:, 2:3], scalar2=32.0,
                                               op0=ALU.mult, op1=ALU.mult),
              reads=[("a_x", xs), "a_rstd"], writes=["a_xn"])
        transposes8(xn, "a_xn", xnT[:], "a_xnT", "act")
        groups = [(0, 512), (512, 512), (1024, 256), (1280, 512), (1792, 512)] if full else [(1024, 256)]
        for gi, (c0, cw) in enumerate(groups):
            pb = gi % 2
            for c in range(8):
                P.add("pe", lambda e, c=c, c0=c0, cw=cw, pb=pb: e.matmul(
                    ps_u[pb][:, 0:cw], lhsT=xnT[:, c, :], rhs=w_sb[:, c, c0:c0 + cw], start=(c == 0), stop=(c == 7)),
                    reads=["a_xnT"] + W_ALL, writes=[("a_psu", pb)])
            if c0 < 1024:
                h0 = c0 // 64
                P.add("act", lambda e, pb=pb: e.activation(out=junk[:, 0:512], in_=ps_u[pb][:], func=AF.Square),
                      reads=[("a_psu", pb)], writes=["a_junk"])
                P.add("dve", lambda e, h0=h0: e.tensor_reduce(out=qss[:, h0:h0 + 8], in_=junk[:, 0:512].rearrange("p (a b) -> p a b", a=8),
                                                            axis=AX.X, op=ALU.add),
                      reads=["a_junk"], writes=[("a_qss", h0)])
                rstd_from_ss(qss[:, h0:h0 + 8], qr[:, h0:h0 + 8], qr[:, h0:h0 + 8], 64 * EPS, ("a_qss", h0), "a_qr%d" % h0)
                P.add("dve", lambda e, pb=pb, h0=h0, c0=c0: e.tensor_tensor(
                    out=qn[:, c0:c0 + 512].rearrange("p (a b) -> p a b", a=8),
                    in0=ps_u[pb][:].rearrange("p (a b) -> p a b", a=8),
                    in1=qr[:, h0:h0 + 8].unsqueeze(2).to_broadcast([128, 8, 64]), op=ALU.mult),
                    reads=[("a_psu", pb), "a_qr%d" % h0], writes=[("a_qn", h0)])
            elif c0 == 1024:
                P.add("act", lambda e, pb=pb: e.activation(out=ksq[:], in_=ps_u[pb][:, 0:128], func=AF.Square),
                      reads=[("a_psu", pb)], writes=["a_junk2"])
                P.add("dve", lambda e: e.tensor_reduce(out=kss[:, 0:2], in_=ksq[:].rearrange("p (a b) -> p a b", a=2),
                                                       axis=AX.X, op=ALU.add),
                      reads=["a_junk2"], writes=["a_kss"])
                rstd_from_ss(kss[:, 0:2], kss[:, 2:4], kss[:, 2:4], 64 * EPS, "a_kss", "a_kr")
                P.add("dve", lambda e, pb=pb: e.tensor_tensor(
                    out=kn[:].rearrange("p (a b) -> p a b", a=2), in0=ps_u[pb][:, 0:128].rearrange("p (a b) -> p a b", a=2),
                    in1=kss[:, 2:4].unsqueeze(2).to_broadcast([128, 2, 64]), op=ALU.mult),
                    reads=[("a_psu", pb), "a_kr"], writes=["a_kn"])
                P.add("dve", lambda e: e.tensor_tensor(out=kq[:], in0=kn[:], in1=wqk[:], op=ALU.mult),
                      reads=["a_kn", "a_wqk"], writes=["a_kq"])
                P.add("act", lambda e, pb=pb: e.copy(out=vE_dst[:, :, 0:64], in_=ps_u[pb][:, 128:256].rearrange("p (a b) -> p a b", a=2)),
                      reads=[("a_psu", pb)], writes=[vE_key])
            else:
                g0 = c0 - 1280
                P.add("act", lambda e, pb=pb, g0=g0: e.activation(out=gs[:, g0:g0 + 512], in_=ps_u[pb][:], func=AF.Silu),
                      reads=[("a_psu", pb)], writes=[("a_gs", g0)])
        if full:
            for c in range(8):
                P.add("pe", lambda e, c=c: e.transpose(out=ps_t[:, c, :], in_=qn[:, c * 128:(c + 1) * 128], identity=ident[:]),
                      reads=[("a_qn", 0), ("a_qn", 8), "a_ident"], writes=["a_pst"])
            P.add("dve", lambda e: e.tensor_copy(out=QT[:].rearrange("p (a b) -> p a b", a=8), in_=ps_t[:]),
                  reads=["a_pst"], writes=["a_QT"])
        P.add("pe", lambda e: e.transpose(out=ps_t[:, 0, :], in_=kq[:], identity=ident[:]),
              reads=["a_kq", "a_ident"], writes=["a_pst"])
        P.add("act", lambda e: e.copy(out=KT_dst[:], in_=ps_t[:, 0, :]), reads=["a_pst"], writes=[KT_key])

    front(d["xm"], 0, KTm, "a_KTm", vEm, "a_vEm", full=False)

    for i in range(nb):
        cur = i % 2
        prv = 1 - cur
        front(d["xa"][i * 128:(i + 1) * 128, :], i % 2, KT[cur], ("a_KT", cur), vE[cur], ("a_vE", cur), full=True)
        if i == 0:
            chunks = [("cur", 0), ("meta", 0)]
        elif i == 1:
            chunks = [("prev", 1), ("cur", 3), ("meta", 1)]
        else:
            chunks = [("prev", 2), ("cur", 3), ("meta", 2)]
        n_e = 0
        for kind, tix in chunks:
            for g in range(2):
                for hf in range(2):
                    pb = n_e % 2
                    hh = g * 8 + hf * 4
                    if kind == "meta":
                        lhsT, lk, M = KTm[g * 64:(g + 1) * 64, 0:16], "a_KTm", 16
                    elif kind == "cur":
                        lhsT, lk, M = KT[cur][g * 64:(g + 1) * 64, :], ("a_KT", cur), 128
                    else:
                        lhsT, lk, M = KT[prv][g * 64:(g + 1) * 64, :], ("a_KT", prv), 128
                    P.add("pe", lambda e, pb=pb, lhsT=lhsT, g=g, hf=hf, M=M: e.matmul(
                        ps_s[pb][0:M, :], lhsT=lhsT, rhs=QT[g * 64:(g + 1) * 64, hf * 512:(hf + 1) * 512], start=True, stop=True),
                        reads=[lk, "a_QT"], writes=[("a_pss", pb)])
                    P.add("act", lambda e, pb=pb, M=M: e.activation(out=E_sb[pb][0:M, :], in_=ps_s[pb][0:M, :], func=AF.Exp),
                          reads=[("a_pss", pb)], writes=[("a_E", pb)])
                    if kind == "meta":
                        tab = mtabs[:, tix, hh * 128:(hh + 4) * 128]
                        dst = PTm[:, hh:hh + 4, :]
                        dk = ("a_PTm", hh)
                        tk = "a_mtabs"
                    else:
                        tab = tabs[:, tix, hh * 128:(hh + 4) * 128]
                        dst = (PTc if kind == "cur" else PTp)[:, hh:hh + 4, :]
                        dk = ("a_PTc" if kind == "cur" else "a_PTp", hh)
                        tk = ("a_tabs", tix)
                    P.add("dve" if n_e % 2 == 0 else "pool",
                          lambda e, pb=pb, M=M, tab=tab, dst=dst: e.tensor_tensor(
                              out=dst.rearrange("p a b -> p (a b)"), in0=E_sb[pb][0:M, :], in1=tab, op=ALU.mult),
                          reads=[("a_E", pb), tk], writes=[dk])
                    n_e += 1
        for h in range(16):
            g = h // 8
            hh = (h // 4) * 4
            bank, off = h // 7, (h % 7) * 65
            for ci, (kind, tix) in enumerate(chunks):
                if kind == "meta":
                    lhsT, lk = PTm[:, h, :], ("a_PTm", hh)
                    rhs, rk = vEm[0:16, g, :], "a_vEm"
                elif kind == "cur":
                    lhsT, lk = PTc[:, h, :], ("a_PTc", hh)
                    rhs, rk = vE[cur][:, g, :], ("a_vE", cur)
                else:
                    lhsT, lk = PTp[:, h, :], ("a_PTp", hh)
                    rhs, rk = vE[prv][:, g, :], ("a_vE", prv)
                P.add("pe", lambda e, bank=bank, off=off, lhsT=lhsT, rhs=rhs, ci=ci, n=len(chunks): e.matmul(
                    ps_o[:, bank, off:off + 65], lhsT=lhsT, rhs=rhs, start=(ci == 0), stop=(ci == n - 1)),
                    reads=[lk, rk], writes=[("a_pso", bank)])
        for bank, (h0, nh) in enumerate([(0, 7), (7, 7), (14, 2)]):
            ov = ps_o[:, bank, 0:nh * 65].rearrange("p (a b) -> p a b", b=65)
            P.add("dve", lambda e, ov=ov, h0=h0, nh=nh: e.tensor_tensor(
                out=den[:, h0:h0 + nh].unsqueeze(2), in0=ov[:, :, 64:65], in1=esink[:, h0:h0 + nh].unsqueeze(2), op=ALU.add),
                reads=[("a_pso", bank), "a_esink"], writes=[("a_den", bank)])
            P.add("dve", lambda e, h0=h0, nh=nh: e.reciprocal(out=rden[:, h0:h0 + nh], in_=den[:, h0:h0 + nh]),
                  reads=[("a_den", bank)], writes=[("a_rden", bank)])
            P.add("dve", lambda e, ov=ov, h0=h0, nh=nh: e.tensor_tensor(
                out=ogp[:, h0 * 64:(h0 + nh) * 64].rearrange("p (a b) -> p a b", b=64), in0=ov[:, :, 0:64],
                in1=rden[:, h0:h0 + nh].unsqueeze(2).to_broadcast([128, nh, 64]), op=ALU.mult),
                reads=[("a_pso", bank), ("a_rden", bank)], writes=[("a_ogp", bank)])
        P.add("pool", lambda e: e.tensor_tensor(out=og[:], in0=ogp[:], in1=gs[:], op=ALU.mult),
              reads=[("a_ogp", 0), ("a_ogp", 1), ("a_ogp", 2), ("a_gs", 0), ("a_gs", 512)], writes=["a_og"])
        transposes8(og, "a_og", ogT[:], "a_ogT", "act")
        hs = i % 2
        for g2 in range(2):
            pb = g2
            for c in range(8):
                P.add("pe", lambda e, c=c, g2=g2, pb=pb: e.matmul(
                    ps_u[pb][:], lhsT=ogT[:, c, :], rhs=wo_sb[:, c, g2 * 512:(g2 + 1) * 512], start=(c == 0), stop=(c == 7)),
                    reads=["a_ogT"] + WO_ALL, writes=[("a_psu", pb)])
            P.add("dve", lambda e, g2=g2, pb=pb, hs=hs, xs=i % 2: e.tensor_tensor(
                out=h1[hs][:, g2 * 512:(g2 + 1) * 512], in0=ps_u[pb][:], in1=x_sb[xs][:, g2 * 512:(g2 + 1) * 512], op=ALU.add),
                reads=[("a_psu", pb), ("a_x", i % 2)], writes=[("a_h1", hs, g2)])
        H1K = [("a_h1", hs, 0), ("a_h1", hs, 1)]
        if i >= 1:
            P.add("sp", lambda e, i=i, hs=hs: e.dma_start(out=d["h1"][(i - 1) * 128:i * 128, :], in_=h1[hs][:]),
                  reads=H1K, writes=[("a_h1out", i)], dma=True)
        P.add("act", lambda e, hs=hs: e.activation(out=junk[:], in_=h1[hs][:], func=AF.Square, accum_out=st_sb[:, 4:5]),
              reads=H1K, writes=["a_junk", "a_ss1"])
        rstd_from_ss(st_sb[:, 4:5], st_sb[:, 5:6], st_sb[:, 6:7], 1024 * EPS, "a_ss1", "a_rstd1")
        P.add("dve", lambda e, hs=hs: e.tensor_scalar(out=xn1[:], in0=h1[hs][:], scalar1=st_sb[:, 6:7], scalar2=32.0,
                                                      op0=ALU.mult, op1=ALU.mult),
              reads=H1K + ["a_rstd1"], writes=["a_xn1"])
        transposes8(xn1, "a_xn1", xn1T[hs][:], ("a_xn1T", hs), "act")
        xv = d["xn1T"].rearrange("(c p) t -> p c t", p=128)
        if i == 0:
            P.add("sp", lambda e, hs=hs: e.dma_start(out=xv[:, :, 0:64], in_=xn1T[hs][:, :, 64:128]),
                  reads=[("a_xn1T", hs)], writes=[("a_xn1out", i)], dma=True)
        else:
            P.add("sp", lambda e, hs=hs, i=i: e.dma_start(out=xv[:, :, 64 + (i - 1) * 128:64 + i * 128], in_=xn1T[hs][:]),
                  reads=[("a_xn1T", hs)], writes=[("a_xn1out", i)], dma=True)


def attn_tables(seg0):
    m = np.exp2(-8.0 * np.arange(1, 17) / 16.0)[None, :, None]
    j = np.arange(128)[:, None, None].astype(np.float64)
    i = np.arange(128)[None, None, :].astype(np.float64)
    t_prev = np.where(j > i, np.exp(-m * (128 + i - j)), 0.0)
    t_cur = np.where(j <= i, np.exp(-m * (i - j)), 0.0)
    jm = np.arange(16)[:, None, None].astype(np.float64)
    t_meta = np.exp(-m * 128.0) * np.ones((16, 16, 128))
    if seg0:
        t_cur0 = np.zeros_like(t_cur)
        t_prev1 = np.zeros_like(t_prev)
        pq = i - 112.0
        t_meta0 = np.where(pq >= jm, np.exp(-m * np.maximum(pq - jm, 0.0)), 0.0) * np.ones((16, 16, 128))
        t_meta1 = np.exp(-m * np.minimum(16.0 + i - jm, 128.0))
    else:
        t_cur0, t_prev1, t_meta0, t_meta1 = t_cur, t_prev, t_meta, t_meta
    tabs = np.stack([t_cur0, t_prev1, t_prev, t_cur], axis=1).reshape(128, 4, 2048).astype(np.float32)
    mtabs = np.stack([t_meta0, t_meta1, t_meta], axis=1).reshape(16, 3, 2048).astype(np.float32)
    return np.ascontiguousarray(tabs), np.ascontiguousarray(mtabs)


def attn_weight_layout(attn_norm_w, attn_w_in, attn_q_norm_w, attn_k_norm_w, attn_sinks, attn_w_out):
    w_in = np.asarray(attn_w_in[0], dtype=np.float32)
    perm = []
    for c in range(8):
        for half in range(2):
            h = c + 8 * half
            perm.extend(range(h * 64, (h + 1) * 64))
    perm = np.array(perm + list(range(1024, 2304)))
    out = {}
    out["w_in"] = np.ascontiguousarray(w_in[:, perm])
    out["nw"] = np.ascontiguousarray(np.asarray(attn_norm_w[0], np.float32).reshape(8, 128).T)
    out["wq"] = np.ascontiguousarray(np.broadcast_to(np.tile(np.asarray(attn_q_norm_w[0], np.float32), 2)[None, :], (128, 128)))
    out["wk"] = np.ascontiguousarray(np.broadcast_to(np.tile(np.asarray(attn_k_norm_w[0], np.float32), 2)[None, :], (128, 128)))
    out["sinks"] = np.ascontiguousarray(np.broadcast_to(np.asarray(attn_sinks[0], np.float32)[None, :], (128, 16)))
    out["w_out"] = np.ascontiguousarray(np.asarray(attn_w_out[0], np.float32))
    return out


NEG = -30000.0
DBG_STOP = 99
SEQ_MODE = False
CB_U, CB_MUI, CB_MUS, CB_MLS, CB_BD, CB_B1L, CB_B1U, CB_B2L, CB_ONES, CB_I = 0, 64, 320, 576, 832, 1088, 1344, 1600, 1856, 1984
CB_COLS = 2048


def dn_consts():
    a = np.arange(64)[:, None]
    b = np.arange(64)[None, :]
    rep4 = lambda m: np.tile(m[:, None, :], (1, 4, 1)).reshape(64, 256)
    U = (a <= b).astype(np.float32)
    mui = np.where(a <= b, 0.0, NEG)
    mus = np.where(a < b, 0.0, NEG)
    mls = np.where(b < a, 0.0, NEG)
    bd = (a // 16 == b // 16).astype(np.float32)
    b1l = ((a // 32 == b // 32) & (a // 16 == b // 16 + 1)).astype(np.float32)
    b2l = ((a >= 32) & (b < 32)).astype(np.float32)
    c = np.zeros((64, CB_COLS), np.float32)
    c[:, CB_U:CB_U + 64] = U
    c[:, CB_MUI:CB_MUI + 256] = rep4(mui)
    c[:, CB_MUS:CB_MUS + 256] = rep4(mus)
    c[:, CB_MLS:CB_MLS + 256] = rep4(mls)
    c[:, CB_BD:CB_BD + 256] = rep4(bd)
    c[:, CB_B1L:CB_B1L + 256] = rep4(b1l)
    c[:, CB_B1U:CB_B1U + 256] = rep4(b1l.T)
    c[:, CB_B2L:CB_B2L + 256] = rep4(b2l)
    c[:, CB_ONES:CB_ONES + 128] = 1.0
    c[:, CB_I:CB_I + 64] = np.eye(64)
    return c


def build_phase_b(nchunks=257):
    nc = bass.Bass("TRN2", target_bir_lowering=False)
    TT = 64 * nchunks
    d = {}
    d["xnT"] = nc.dram_tensor("xnT", [1024, TT], BF16, kind="ExternalInput").ap()
    d["wB"] = nc.dram_tensor("wB", [1024, 1544], F32, kind="ExternalInput").ap()
    d["nwB"] = nc.dram_tensor("nwB", [128, 8], F32, kind="ExternalInput").ap()
    d["convw"] = nc.dram_tensor("convw", [128, 8, 4], F32, kind="ExternalInput").ap()
    d["onw"] = nc.dram_tensor("onw", [128, 1], F32, kind="ExternalInput").ap()
    d["alog"] = nc.dram_tensor("alog", [64, 4], F32, kind="ExternalInput").ap()
    d["dtb"] = nc.dram_tensor("dtb", [64, 4], F32, kind="ExternalInput").ap()
    d["cstB"] = nc.dram_tensor("cstB", [64, CB_COLS], F32, kind="ExternalInput").ap()
    d["ogT"] = nc.dram_tensor("ogT", [512, TT - 64], BF16, kind="ExternalOutput").ap()
    import contextlib
    with contextlib.ExitStack() as st:
        P = Prog(nc)
        emit_phase_b(nc, st, P, d, nchunks)
        P.emit()
    return nc


def emit_phase_b(nc, st, P, d, nchunks, xsrc=None, xdeps=()):
    sb = lambda name, shape, dt: st.enter_context(nc.sbuf_tensor(name, shape, dt))
    V = lambda fn, r, w: P.add("dve", fn, reads=r, writes=w)
    A = lambda fn, r, w: P.add("act", fn, reads=r, writes=w)
    G = lambda fn, r, w: P.add("pool", fn, reads=r, writes=w)
    T = lambda fn, r, w: P.add("pe", fn, reads=r, writes=w)
    D = lambda fn, r, w: P.add("sp", fn, reads=r, writes=w, dma=True)
    TM = 512
    ident, identf = make_identity(nc, st, P, "b_ident")
    w_sb = sb("b_w", [128, 8, 1544], BF16)
    wst = [sb("b_wst%d" % i, [128, 1544], F32) for i in range(2)]
    nw = sb("b_nw", [128, 8], F32)
    cw = sb("b_cw", [128, 8, 4], F32)
    onw = sb("b_onw", [128, 1], F32)
    negA = sb("b_negA", [64, 4], F32)
    dtb = sb("b_dtb", [64, 4], F32)
    cst = sb("b_cst", [64, CB_COLS], F32)
    ones_bf = sb("b_ones", [128, 128], BF16)
    xt = [sb("b_xt%d" % i, [128, 8, TM], BF16) for i in range(2)]
    u_sb = sb("b_u", [128, 8, TM + 3], F32)
    acc = [sb("b_acc%d" % i, [128, TM], F32) for i in range(2)]
    csil = sb("b_csil", [128, 4, TM], F32)
    ctmp = sb("b_ctmp", [128, TM], F32)
    sq = sb("b_sq", [128, TM], BF16)
    rs = sb("b_rs", [128, TM], F32)
    qkT = sb("b_qkT", [128, 4, TM], BF16)
    vT = sb("b_vT", [128, 4, TM], BF16)
    zs = sb("b_zs", [128, 4, TM], F32)
    o_sb = sb("b_o", [128, 4, TM], F32)
    ogt = sb("b_ogt", [128, 4, TM], BF16)
    S32 = sb("b_S32", [128, 4, 128], F32)
    Sb = sb("b_Sb", [128, 4, 128], BF16)
    ba = sb("b_ba", [64, 8, 8], F32)
    e1 = sb("b_e1", [64, 8, 8], F32)
    lnb = sb("b_lnb", [64, 8, 4], F32)
    beta = sb("b_beta", [64, 8, 4], F32)
    gg = sb("b_g", [64, 8, 4], F32)
    two = lambda name, shape, dt: [sb("%s_%d" % (name, i), shape, dt) for i in range(2)]
    four = lambda name, shape, dt: [sb("%s_%d" % (name, i), shape, dt) for i in range(4)]
    gbb2 = two("b_gbb", [64, 4, 128], BF16)
    lbb2 = two("b_lbb", [64, 4, 64], BF16)
    gc2 = two("b_gc", [64, 4], F32)
    gcl2 = two("b_gcl", [64, 4], F32)
    beg2 = two("b_beg", [64, 4], F32)
    ekl2 = two("b_ekl", [64, 4], F32)
    args2 = two("b_args", [64, 3, 256], F32)
    Eg2 = two("b_Eg", [128, 256], F32)
    Nm2 = two("b_N", [64, 256], F32)
    Mm2 = two("b_M", [64, 256], F32)
    L2 = [[sb("b_L%d_%d" % (i, s_), [64, 256], BF16) for i in range(4)] for s_ in range(2)]
    Uu2 = [[sb("b_U%d_%d" % (i, s_), [64, 256], BF16) for i in range(3)] for s_ in range(2)]
    O12 = two("b_O1", [64, 256], BF16)
    N12 = two("b_N1", [64, 256], BF16)
    O22 = two("b_O2", [64, 256], BF16)
    PU2 = [[sb("b_PU%d_%d" % (i, s_), [64, 256], BF16) for i in range(2)] for s_ in range(2)]
    PL2 = [[sb("b_PL%d_%d" % (i, s_), [64, 256], BF16) for i in range(2)] for s_ in range(2)]
    Y2 = [[sb("b_Y%d_%d" % (i, s_), [64, 256], BF16) for i in range(2)] for s_ in range(2)]
    T32U2 = two("b_T32U", [64, 256], BF16)
    T32L2 = two("b_T32L", [64, 256], BF16)
    kcp2 = two("b_kcp", [128, 2, 64], BF16)
    kbg2 = two("b_kbg", [64, 4, 128], BF16)
    TTb4 = four("b_TTb", [64, 256], BF16)
    wTn4 = four("b_wTn", [128, 256], BF16)
    vb4 = four("b_vb", [64, 4, 128], BF16)
    kst4 = four("b_kst", [64, 4, 128], BF16)
    qg4 = four("b_qg", [128, 256], BF16)
    attnT4 = four("b_attnT", [64, 256], BF16)
    egl4 = four("b_egl", [128, 4], F32)
    vnb = sb("b_vnb", [64, 4, 128], BF16)
    gbf = sb("b_gbf", [64, 8, 4], BF16)
    cstb = sb("b_cstb", [64, 256], BF16)

    psA = st.enter_context(nc.psum_tensor("b_psA", [128, 512], F32))
    psB = st.enter_context(nc.psum_tensor("b_psB", [128, 512], F32))
    psG = st.enter_context(nc.psum_tensor("b_psG", [128, 512], F32))
    psM = st.enter_context(nc.psum_tensor("b_psM", [128, 512], F32))
    psI2 = [st.enter_context(nc.psum_tensor("b_psI%d" % i, [128, 512], F32)) for i in range(2)]
    psT = st.enter_context(nc.psum_tensor("b_psT", [128, 1024], BF16))
    psW = st.enter_context(nc.psum_tensor("b_psW", [128, 512], F32))

    cUb = cstb[:, 0:64]
    cIb = cstb[:, 64:128]
    cOnesb = cstb[:, 128:256]
    cU = cst[:, CB_U:CB_U + 64]
    cI = cst[:, CB_I:CB_I + 64]
    cOnes = cst[:, CB_ONES:CB_ONES + 128]
    c4 = lambda o: cst[:, o:o + 256]

    for name, t_, src in [("b_nw", nw, "nwB"), ("b_cw", cw, "convw"), ("b_onw", onw, "onw"), ("b_negA", negA, "alog"),
                          ("b_dtb", dtb, "dtb"), ("b_cst", cst, "cstB")]:
        D(lambda e, t_=t_, src=src: e.dma_start(out=t_[:], in_=d[src]), [], [name])
    V(lambda e: e.tensor_copy(out=cstb[:, 0:64], in_=cst[:, CB_U:CB_U + 64]), ["b_cst"], ["b_cst"])
    V(lambda e: e.tensor_copy(out=cstb[:, 64:128], in_=cst[:, CB_I:CB_I + 64]), ["b_cst"], ["b_cst"])
    V(lambda e: e.tensor_copy(out=cstb[:, 128:256], in_=cst[:, CB_ONES:CB_ONES + 128]), ["b_cst"], ["b_cst"])
    A(lambda e: e.activation(out=negA[:], in_=negA[:], func=AF.Exp), ["b_negA"], ["b_negA"])
    V(lambda e: e.tensor_scalar(out=negA[:], in0=negA[:], scalar1=-1.0, scalar2=None, op0=ALU.mult), ["b_negA"], ["b_negA"])
    G(lambda e: e.memset(ones_bf[:], 1.0), [], ["b_ones"])
    G(lambda e: e.memset(S32[:], 0.0), [], ["b_S32"])
    G(lambda e: e.memset(Sb[:], 0.0), [], ["b_Sb"])
    G(lambda e: e.memset(u_sb[:], 0.0), [], [("b_u", m_) for m_ in range(8)])
    wv = d["wB"].rearrange("(c p) n -> p c n", p=128)
    for c in range(8):
        s = c % 2
        D(lambda e, c=c, s=s: e.dma_start(out=wst[s][:], in_=wv[:, c, :]), [], [("b_wst", s)])
        if c % 2 == 0:
            V(lambda e, c=c, s=s: e.tensor_scalar(out=w_sb[:, c, :], in0=wst[s][:], scalar1=nw[:, c:c + 1], scalar2=None, op0=ALU.mult),
              [("b_wst", s), "b_nw"], ["b_w"])
        else:
            A(lambda e, c=c, s=s: e.activation(out=w_sb[:, c, :], in_=wst[s][:], func=AF.Copy, scale=nw[:, c:c + 1]),
              [("b_wst", s), "b_nw"], ["b_w"])

    def rsqrt_act(out_ap, in_ap, scale, bias_ln, bias_exp, rkeys, wkey):
        A(lambda e: e.activation(out=out_ap, in_=in_ap, func=AF.Ln, bias=bias_ln, scale=scale), rkeys, [wkey])
        A(lambda e: e.activation(out=out_ap, in_=out_ap, func=AF.Exp, scale=-0.5, bias=bias_exp), [wkey], [wkey])

    def tile(ti, t0, TW, need_o):
        xs = ti % 2
        nck = TW // 64
        if xsrc is None:
            xv = d["xnT"].rearrange("(c p) t -> p c t", p=128)
            D(lambda e: e.dma_start(out=xt[xs][:, :, 0:TW], in_=xv[:, :, t0:t0 + TW]), [], [("b_xt", xs)])
        else:
            D(lambda e: e.dma_start(out=xt[xs][:, :, 0:TW], in_=xsrc(e, t0, TW)), list(xdeps(t0)), [("b_xt", xs)])
        for m in range(8):
            ps = psA if m % 2 == 0 else psB
            pk = "b_psA" if m % 2 == 0 else "b_psB"
            pkw = [pk]
            for c in range(8):
                T(lambda e, c=c, m=m, ps=ps: e.matmul(ps[:, 0:TW], lhsT=w_sb[:, c, m * 128:(m + 1) * 128], rhs=xt[xs][:, c, 0:TW],
                                                      start=(c == 0), stop=(c == 7)), ["b_w", ("b_xt", xs)], pkw)
            A(lambda e, m=m, ps=ps: e.copy(out=u_sb[:, m, 3:3 + TW], in_=ps[:, 0:TW]), [pk], [("b_u", m)])
            ac = acc[m % 2]
            ak = ("b_acc", m % 2)
            A(lambda e, m=m, ac=ac, ps=ps: e.activation(out=ac[:, 0:TW], in_=ps[:, 0:TW], func=AF.Copy, scale=cw[:, m, 3:4]),
              [pk, "b_cw"], [ak])
            for j in (2, 1, 0):
                V(lambda e, m=m, ac=ac, j=j: e.scalar_tensor_tensor(out=ac[:, 0:TW], in0=u_sb[:, m, j:j + TW], scalar=cw[:, m, j:j + 1],
                                                                   in1=ac[:, 0:TW], op0=ALU.mult, op1=ALU.add),
                  [("b_u", m), "b_cw", ak], [ak])
            eng = V
            if m < 4:
                A(lambda e, m=m, ac=ac: e.activation(out=csil[:, m, 0:TW], in_=ac[:, 0:TW], func=AF.Silu), [ak], [("b_csil", m)])
            else:
                A(lambda e, m=m, ac=ac: e.activation(out=vT[:, m - 4, 0:TW], in_=ac[:, 0:TW], func=AF.Silu), [ak], [("b_vT", m - 4)])
            eng(lambda e, m=m: e.tensor_copy(out=u_sb[:, m, 0:3], in_=u_sb[:, m, TW:TW + 3]), [("b_u", m)], [("b_u", m)])
        if DBG_STOP <= 1:
            return
        for m in range(4):
            A(lambda e, m=m: e.activation(out=sq[:, 0:TW], in_=csil[:, m, 0:TW], func=AF.Square), [("b_csil", m)], ["b_sq"])
            T(lambda e: e.matmul(psA[:, 0:TW], lhsT=ones_bf[:], rhs=sq[:, 0:TW], start=True, stop=True), ["b_sq", "b_ones"], ["b_psA"])
            rsqrt_act(rs[:, 0:TW], psA[:, 0:TW], 1.0, EPS, (-0.5 * float(np.log(128.0))) if m < 2 else 0.0, ["b_psA"], "b_rs")
            V(lambda e, m=m: e.tensor_tensor(out=qkT[:, m, 0:TW], in0=csil[:, m, 0:TW], in1=rs[:, 0:TW], op=ALU.mult),
              [("b_csil", m), "b_rs"], [("b_qkT", m)])
        if DBG_STOP <= 2:
            return
        if need_o:
            for h in range(4):
                ps = psA if h % 2 == 0 else psB
                pk = "b_psA" if h % 2 == 0 else "b_psB"
                pkw = [pk]
                for c in range(8):
                    T(lambda e, c=c, h=h, ps=ps: e.matmul(ps[:, 0:TW], lhsT=w_sb[:, c, 1024 + h * 128:1024 + (h + 1) * 128],
                                                          rhs=xt[xs][:, c, 0:TW], start=(c == 0), stop=(c == 7)),
                      ["b_w", ("b_xt", xs)], pkw)
                A(lambda e, h=h, ps=ps: e.activation(out=zs[:, h, 0:TW], in_=ps[:, 0:TW], func=AF.Silu), [pk], [("b_zs", h)])
        if DBG_STOP <= 3:
            return
        bav = psM[0:64, 64:128].rearrange("p (a b) -> p a b", b=8)
        for ck in range(nck):
            for c in range(8):
                T(lambda e, c=c, ck=ck: e.matmul(bav[:, ck, :], lhsT=xt[xs][:, c, ck * 64:(ck + 1) * 64], rhs=w_sb[:, c, 1536:1544],
                                                 start=(c == 0), stop=(c == 7)), ["b_w", ("b_xt", xs)], ["b_psM"])
        V(lambda e: e.tensor_copy(out=ba[:, 0:nck, :], in_=bav[:, 0:nck, :]), ["b_psM"], ["b_ba"])
        V(lambda e: e.tensor_tensor(out=ba[:, 0:nck, 4:8], in0=ba[:, 0:nck, 4:8], in1=dtb[:].unsqueeze(1).to_broadcast([64, nck, 4]), op=ALU.add),
          ["b_ba", "b_dtb"], ["b_ba"])
        A(lambda e: e.activation(out=e1[:, 0:nck, 0:4], in_=ba[:, 0:nck, 0:4], func=AF.Exp, scale=-1.0), ["b_ba"], ["b_e1"])
        A(lambda e: e.activation(out=e1[:, 0:nck, 4:8], in_=ba[:, 0:nck, 4:8], func=AF.Exp), ["b_ba", "b_e1"], ["b_e1"])
        A(lambda e: e.activation(out=e1[:, 0:nck, :], in_=e1[:, 0:nck, :], func=AF.Ln, bias=1.0, scale=1.0), ["b_e1"], ["b_e1"])
        V(lambda e: e.tensor_scalar(out=lnb[:, 0:nck, :], in0=e1[:, 0:nck, 0:4], scalar1=-1.0, scalar2=None, op0=ALU.mult), ["b_e1"], ["b_lnb"])
        A(lambda e: e.activation(out=beta[:, 0:nck, :], in_=lnb[:, 0:nck, :], func=AF.Exp), ["b_lnb"], ["b_beta"])
        V(lambda e: e.tensor_tensor(out=gg[:, 0:nck, :], in0=e1[:, 0:nck, 4:8], in1=negA[:].unsqueeze(1).to_broadcast([64, nck, 4]), op=ALU.mult),
          ["b_e1", "b_negA"], ["b_g"])
        V(lambda e: e.tensor_copy(out=gbf[:, 0:nck, :], in_=gg[:, 0:nck, :]), ["b_g"], ["b_gbf"])
        if DBG_STOP <= 4:
            return
        chunks_of_tile(ti, xs, nck, need_o)
        if need_o:
            for h in range(4):
                A(lambda e, h=h: e.activation(out=sq[:, 0:TW], in_=o_sb[:, h, 0:TW], func=AF.Square), [("b_o", h)], ["b_sq"])
                T(lambda e: e.matmul(psA[:, 0:TW], lhsT=ones_bf[:], rhs=sq[:, 0:TW], start=True, stop=True), ["b_sq", "b_ones"], ["b_psA"])
                rsqrt_act(rs[:, 0:TW], psA[:, 0:TW], 1.0 / 128.0, EPS, 0.0, ["b_psA"], "b_rs")
                V(lambda e, h=h: e.tensor_tensor(out=o_sb[:, h, 0:TW], in0=o_sb[:, h, 0:TW], in1=rs[:, 0:TW], op=ALU.mult),
                  [("b_o", h), "b_rs"], [("b_o", h)])
                V(lambda e, h=h: e.scalar_tensor_tensor(out=ogt[:, h, 0:TW], in0=o_sb[:, h, 0:TW], scalar=onw[:, 0:1], in1=zs[:, h, 0:TW],
                                                        op0=ALU.mult, op1=ALU.mult),
                  [("b_o", h), ("b_zs", h), "b_onw"], [("b_ogt", h)])
            ov = d["ogT"].rearrange("(h p) t -> p h t", p=128)
            D(lambda e: e.dma_start(out=ov[:, :, t0 - 64:t0 - 64 + TW], in_=ogt[:, :, 0:TW]), [("b_ogt", h) for h in range(4)], [("b_ogout", ti)])

    def stage1(ti, xs, ck, need_o, sl, hs):
        c0 = ck * 64
        K = lambda name: (name, sl)
        H = lambda name: (name, hs)
        g_ck = gg[:, ck, :]
        gbb_, lbb_, gc_, gcl_, beg_, ekl_ = gbb2[sl], lbb2[sl], gc2[sl], gcl2[sl], beg2[sl], ekl2[sl]
        args_, Eg_, Nm_, Mm_ = args2[sl], Eg2[sl], Nm2[sl], Mm2[sl]
        L_, Uu_, O1_, N1_, O2_, PU_, PL_, Y_ = L2[sl], Uu2[sl], O12[sl], N12[sl], O22[sl], PU2[sl], PL2[sl], Y2[sl]
        T32U_, T32L_, kcp_, kbg_ = T32U2[sl], T32L2[sl], kcp2[sl], kbg2[sl]
        TTb_, wTn_, vb_, kst_, qg_, attnT_, egl_ = TTb4[hs], wTn4[hs], vb4[hs], kst4[hs], qg4[hs], attnT4[hs], egl4[hs]
        psI_ = psI2[sl]
        PIK = ("b_psI", sl)
        V(lambda e: e.tensor_copy(out=gbb_[:], in_=g_ck.unsqueeze(2).to_broadcast([64, 4, 128])), ["b_g"], [K("gb")])
        yield
        A(lambda e: e.copy(out=lbb_[:], in_=lnb[:, ck, :].unsqueeze(2).to_broadcast([64, 4, 64])), ["b_lnb"], [K("lb")])
        yield
        Gp = psG[:, 0:256]
        GBp = psG[0:64, 256:512]
        for h in range(4):
            T(lambda e, h=h: e.matmul(Gp[:, h * 64:(h + 1) * 64], lhsT=gbb_[:, h, :], rhs=cUb, start=True, stop=True),
              [K("gb"), "b_cst"], ["b_psG"])
        for h in range(4):
            T(lambda e, h=h: e.matmul(GBp[:, h * 64:(h + 1) * 64], lhsT=gbb_[:, h, 0:64], rhs=cUb, start=True, stop=False),
              [K("gb"), "b_cst"], ["b_psG"])
            T(lambda e, h=h: e.matmul(GBp[:, h * 64:(h + 1) * 64], lhsT=lbb_[:, h, :], rhs=cIb, start=False, stop=True),
              [K("lb"), "b_cst"], ["b_psG"])
        gcol = psM[0:64, 0:4]
        glast = psM[:, 4:8]
        T(lambda e: e.matmul(gcol, lhsT=cUb, rhs=gbf[:, ck, :], start=True, stop=True), ["b_gbf", "b_cst"], ["b_psM"])
        T(lambda e: e.matmul(glast, lhsT=cOnesb, rhs=gbf[:, ck, :], start=True, stop=True), ["b_gbf", "b_cst"], ["b_psM"])
        V(lambda e: e.tensor_copy(out=gc_[:], in_=gcol), ["b_psM"], [K("gc")])
        V(lambda e: e.tensor_tensor(out=ekl_[:], in0=glast[0:64, :], in1=gc_[:], op=ALU.subtract), ["b_psM", K("gc")], [K("ekl")])
        A(lambda e: e.activation(out=egl_[:], in_=glast, func=AF.Exp), ["b_psM"], [H("egl")])
        V(lambda e: e.tensor_tensor(out=gcl_[:], in0=gc_[:], in1=lnb[:, ck, :], op=ALU.add), [K("gc"), "b_lnb"], [K("gcl")])
        A(lambda e: e.activation(out=ekl_[:], in_=ekl_[:], func=AF.Exp), [K("ekl")], [K("ekl")])
        A(lambda e: e.activation(out=beg_[:], in_=gcl_[:], func=AF.Exp), [K("gcl")], [K("beg")])
        bc = lambda t_: t_[:].unsqueeze(2).to_broadcast([64, 4, 64])
        a3 = lambda i: args_[:, i, :].rearrange("p (a b) -> p a b", a=4)
        p3 = lambda ap: ap.rearrange("p (a b) -> p a b", a=4)
        V(lambda e: e.tensor_tensor(out=a3(0), in0=p3(Gp[0:64, :]), in1=bc(gc_), op=ALU.subtract), ["b_psG", K("gc")], [K("args0")])
        V(lambda e: e.tensor_tensor(out=a3(1), in0=p3(GBp), in1=bc(gc_), op=ALU.subtract), ["b_psG", K("gc")], [K("args1")])
        V(lambda e: e.tensor_tensor(out=a3(2), in0=p3(Gp[0:64, :]), in1=bc(gcl_), op=ALU.subtract), ["b_psG", K("gcl")], [K("args2")])
        if need_o:
            A(lambda e: e.activation(out=Eg_[:], in_=Gp, func=AF.Exp), ["b_psG"], [K("Eg")])
        G(lambda e: e.tensor_tensor(out=args_[:, 0, :], in0=args_[:, 0, :], in1=c4(CB_MUI), op=ALU.add), [K("args0"), "b_cst"], [K("args0")])
        G(lambda e: e.tensor_tensor(out=args_[:, 1, :], in0=args_[:, 1, :], in1=c4(CB_MUS), op=ALU.add), [K("args1"), "b_cst"], [K("args1")])
        V(lambda e: e.scalar_tensor_tensor(out=args_[:, 2, :], in0=args_[:, 2, :], scalar=-1.0, in1=c4(CB_MLS), op0=ALU.mult, op1=ALU.add),
          [K("args2"), "b_cst"], [K("args2")])
        yield
        A(lambda e: e.activation(out=args_[:], in_=args_[:], func=AF.Exp), [K("args0"), K("args1"), K("args2")],
          [K("args0"), K("args1"), K("args2")])
        yield
        KQ = psM[0:64, 128:384].rearrange("p (a b c) -> p a b c", a=2, b=2)
        A(lambda e: e.copy(out=kcp_[:], in_=qkT[:, 2:4, c0:c0 + 64]), [("b_qkT", 2), ("b_qkT", 3)], [K("kcp")])
        for hk in range(2):
            kch = qkT[:, 2 + hk, c0:c0 + 64]
            T(lambda e, hk=hk, kch=kch: e.matmul(KQ[:, 0, hk, :], lhsT=kch, rhs=kcp_[:, hk, :], start=True, stop=True),
              [("b_qkT", 2 + hk), K("kcp")], ["b_psM"])
            if need_o:
                T(lambda e, hk=hk, kch=kch: e.matmul(KQ[:, 1, hk, :], lhsT=kch, rhs=qkT[:, hk, c0:c0 + 64], start=True, stop=True),
                  [("b_qkT", 2 + hk), ("b_qkT", hk)], ["b_psM"])
        o4 = lambda t_: t_.rearrange("p (a b c) -> p a b c", a=2, b=2)
        EK = [K("args0"), K("args1"), K("args2")]
        for j in range(2):
            V(lambda e, j=j: e.tensor_tensor(out=o4(Nm_[:])[:, :, j, :], in0=KQ[:, 0, :, :], in1=o4(args_[:, 1, :])[:, :, j, :], op=ALU.mult),
              ["b_psM"] + EK, [K("N")])
            V(lambda e, j=j: e.tensor_tensor(out=o4(Mm_[:])[:, :, j, :], in0=KQ[:, 0, :, :], in1=o4(args_[:, 2, :])[:, :, j, :], op=ALU.mult),
              ["b_psM"] + EK, [K("M")])
            if need_o:
                V(lambda e, j=j: e.tensor_tensor(out=o4(attnT_[:])[:, :, j, :], in0=KQ[:, 1, :, :], in1=o4(args_[:, 0, :])[:, :, j, :], op=ALU.mult),
                  ["b_psM"] + EK, [H("attnT")])
        if DBG_STOP <= 7:
            return
        V(lambda e: e.tensor_tensor(out=L_[0][:], in0=Mm_[:], in1=c4(CB_BD), op=ALU.mult), [K("M"), "b_cst"], [K("L0")])
        V(lambda e: e.tensor_tensor(out=Uu_[0][:], in0=Nm_[:], in1=c4(CB_BD), op=ALU.mult), [K("N"), "b_cst"], [K("U0")])
        G(lambda e: e.tensor_tensor(out=O1_[:], in0=Mm_[:], in1=c4(CB_B1L), op=ALU.mult), [K("M"), "b_cst"], [K("O1")])
        G(lambda e: e.tensor_tensor(out=N1_[:], in0=Nm_[:], in1=c4(CB_B1U), op=ALU.mult), [K("N"), "b_cst"], [K("N1")])
        G(lambda e: e.tensor_tensor(out=O2_[:], in0=Mm_[:], in1=c4(CB_B2L), op=ALU.mult), [K("M"), "b_cst"], [K("O2")])
        yield
        I4 = cI.unsqueeze(1).to_broadcast([64, 4, 64])
        V(lambda e: e.tensor_tensor(out=p3(PU_[0][:]), in0=I4, in1=p3(Uu_[0][:]), op=ALU.subtract), ["b_cst", K("U0")], [K("PU0")])
        V(lambda e: e.tensor_tensor(out=p3(PL_[0][:]), in0=I4, in1=p3(L_[0][:]), op=ALU.subtract), ["b_cst", K("L0")], [K("PL0")])
        yield

        def mm4(lhs, lk, rhs, rk):
            pv = psI_[0:64, 0:256]
            for h in range(4):
                T(lambda e, h=h: e.matmul(pv[:, h * 64:(h + 1) * 64], lhsT=lhs[:, h * 64:(h + 1) * 64], rhs=rhs[:, h * 64:(h + 1) * 64],
                                          start=True, stop=True), [lk, rk], [PIK])
            return pv

        pcur = 0
        for k in range(3):
            pv = mm4(Uu_[k], K("U%d" % k), L_[k], K("L%d" % k))
            yield
            A(lambda e, pv=pv, k=k: e.copy(out=L_[k + 1][:], in_=pv), [PIK], [K("L%d" % (k + 1))])
            yield
            pv = mm4(L_[k], K("L%d" % k), Uu_[k], K("U%d" % k))
            yield
            if k < 2:
                V(lambda e, pv=pv, k=k: e.tensor_copy(out=Uu_[k + 1][:], in_=pv), [PIK], [K("U%d" % (k + 1))])
                ulhs, ulk = Uu_[k + 1], K("U%d" % (k + 1))
            else:
                V(lambda e, pv=pv: e.tensor_copy(out=Y_[0][:], in_=pv), [PIK], [K("Y0")])
                ulhs, ulk = Y_[0], K("Y0")
            yield
            nxt = 1 - pcur
            pv = mm4(L_[k + 1], K("L%d" % (k + 1)), PU_[pcur], K("PU%d" % pcur))
            yield
            V(lambda e, pv=pv, pcur=pcur, nxt=nxt: e.tensor_tensor(out=PU_[nxt][:], in0=pv, in1=PU_[pcur][:], op=ALU.add),
              [PIK, K("PU%d" % pcur)], [K("PU%d" % nxt)])
            yield
            pv = mm4(ulhs, ulk, PL_[pcur], K("PL%d" % pcur))
            yield
            V(lambda e, pv=pv, pcur=pcur, nxt=nxt: e.tensor_tensor(out=PL_[nxt][:], in0=pv, in1=PL_[pcur][:], op=ALU.add),
              [PIK, K("PL%d" % pcur)], [K("PL%d" % nxt)])
            yield
            pcur = nxt
        TdU, TdUk, TdL, TdLk = PU_[pcur], K("PU%d" % pcur), PL_[pcur], K("PL%d" % pcur)
        pv = mm4(O1_, K("O1"), TdU, TdUk)
        yield
        A(lambda e, pv=pv: e.copy(out=Y_[0][:], in_=pv), [PIK], [K("Y0")])
        yield
        pv = mm4(TdL, TdLk, Y_[0], K("Y0"))
        yield
        V(lambda e, pv=pv: e.tensor_tensor(out=T32U_[:], in0=TdU[:], in1=pv, op=ALU.subtract), [PIK, TdUk], [K("T32U")])
        yield
        pv = mm4(N1_, K("N1"), TdL, TdLk)
        yield
        A(lambda e, pv=pv: e.copy(out=Y_[1][:], in_=pv), [PIK], [K("Y1")])
        yield
        pv = mm4(TdU, TdUk, Y_[1], K("Y1"))
        yield
        V(lambda e, pv=pv: e.tensor_tensor(out=T32L_[:], in0=TdL[:], in1=pv, op=ALU.subtract), [PIK, TdLk], [K("T32L")])
        yield
        pv = mm4(O2_, K("O2"), T32U_, K("T32U"))
        yield
        A(lambda e, pv=pv: e.copy(out=Y_[0][:], in_=pv), [PIK], [K("Y0")])
        yield
        pv = mm4(T32L_, K("T32L"), Y_[0], K("Y0"))
        yield
        V(lambda e, pv=pv: e.tensor_tensor(out=TTb_[:], in0=T32U_[:], in1=pv, op=ALU.subtract), [PIK, K("T32U")], [H("TTb")])
        yield
        if DBG_STOP <= 8:
            return
        tv = psT[0:64, 0:768].rearrange("p (a b) -> p a b", a=6)
        for hk in range(2):
            T(lambda e, hk=hk: e.transpose(out=tv[:, hk, :], in_=qkT[:, 2 + hk, c0:c0 + 64], identity=ident[:]),
              [("b_qkT", 2 + hk), "b_ident"], ["b_psT"])
        for h in range(4):
            T(lambda e, h=h: e.transpose(out=tv[:, 2 + h, :], in_=vT[:, h, c0:c0 + 64], identity=ident[:]), [("b_vT", h), "b_ident"], ["b_psT"])
        bc128 = lambda ap: ap.unsqueeze(2).to_broadcast([64, 4, 128])
        kpair = tv[:, 0:2, :].unsqueeze(2).to_broadcast([64, 2, 2, 128])
        k4 = lambda t_: t_[:].rearrange("p (a b) c -> p a b c", a=2)
        s4 = lambda ap: ap.rearrange("p (a b) -> p a b", a=2).unsqueeze(3).to_broadcast([64, 2, 2, 128])
        V(lambda e: e.tensor_tensor(out=vb_[:], in0=tv[:, 2:6, :], in1=bc128(beta[:, ck, :]), op=ALU.mult), ["b_psT", "b_beta"], [H("vb")])
        V(lambda e: e.tensor_tensor(out=k4(kbg_), in0=kpair, in1=s4(beg_[:]), op=ALU.mult), ["b_psT", K("beg")], [K("kbg")])
        V(lambda e: e.tensor_tensor(out=k4(kst_), in0=kpair, in1=s4(ekl_[:]), op=ALU.mult), ["b_psT", K("ekl")], [H("kst")])
        wTp = psW[:, 0:256]
        for h in range(4):
            T(lambda e, h=h: e.matmul(wTp[:, h * 64:(h + 1) * 64], lhsT=kbg_[:, h, :], rhs=TTb_[:, h * 64:(h + 1) * 64], start=True, stop=True),
              [K("kbg"), H("TTb")], ["b_psW"])
        A(lambda e: e.activation(out=wTn_[:], in_=wTp, func=AF.Copy, scale=-1.0), ["b_psW"], [H("wTn")])
        if need_o:
            qpair = qkT[:, 0:2, c0:c0 + 64].unsqueeze(2).to_broadcast([128, 2, 2, 64])
            V(lambda e: e.tensor_tensor(out=qg_[:].rearrange("p (a b c) -> p a b c", a=2, b=2),
                                        in0=Eg_[:].rearrange("p (a b c) -> p a b c", a=2, b=2), in1=qpair, op=ALU.mult),
              [("b_qkT", 0), ("b_qkT", 1), K("Eg")], [H("qg")])
            yield

    def stage2(ti, xs, ck, need_o, hs):
        c0 = ck * 64
        H = lambda name: (name, hs)
        TTb_, wTn_, vb_, kst_, qg_, attnT_, egl_ = TTb4[hs], wTn4[hs], vb4[hs], kst4[hs], qg4[hs], attnT4[hs], egl4[hs]
        if DBG_STOP <= 10:
            return
        vp = psA[0:64, :].rearrange("p (a b) -> p a b", a=4)
        for h in range(4):
            T(lambda e, h=h: e.matmul(vp[:, h, :], lhsT=TTb_[:, h * 64:(h + 1) * 64], rhs=vb_[:, h, :], start=True, stop=False),
              [H("TTb"), H("vb")], ["b_psA"])
            T(lambda e, h=h: e.matmul(vp[:, h, :], lhsT=wTn_[:, h * 64:(h + 1) * 64], rhs=Sb[:, h, :], start=False, stop=True),
              [H("wTn"), "b_Sb"], ["b_psA"])
        yield
        A(lambda e: e.copy(out=vnb[:], in_=vp), ["b_psA"], ["b_vnb"])
        yield
        if need_o:
            oTp = psW[:, 256:512]
            for h in range(4):
                T(lambda e, h=h: e.matmul(oTp[:, h * 64:(h + 1) * 64], lhsT=Sb[:, h, :], rhs=qg_[:, h * 64:(h + 1) * 64], start=True, stop=False),
                  ["b_Sb", H("qg")], ["b_psW"])
                T(lambda e, h=h: e.matmul(oTp[:, h * 64:(h + 1) * 64], lhsT=vnb[:, h, :], rhs=attnT_[:, h * 64:(h + 1) * 64], start=False, stop=True),
                  ["b_vnb", H("attnT")], ["b_psW"])
            yield
            A(lambda e: e.copy(out=o_sb[:, :, c0:c0 + 64], in_=oTp.rearrange("p (a b) -> p a b", a=4)), ["b_psW"],
              [("b_o", h) for h in range(4)])
            yield
        sp_ = psB[:].rearrange("p (a b) -> p a b", a=4)
        for h in range(4):
            T(lambda e, h=h: e.matmul(sp_[:, h, :], lhsT=kst_[:, h, :], rhs=vnb[:, h, :], start=True, stop=True), [H("kst"), "b_vnb"], ["b_psB"])
        yield
        for h in range(4):
            V(lambda e, h=h: e.scalar_tensor_tensor(out=S32[:, h, :], in0=S32[:, h, :], scalar=egl_[:, h:h + 1], in1=sp_[:, h, :],
                                                    op0=ALU.mult, op1=ALU.add), ["b_S32", H("egl"), "b_psB"], ["b_S32"])
        yield
        A(lambda e: e.copy(out=Sb[:], in_=S32[:]), ["b_S32"], ["b_Sb"])
        yield

    def run_rr(gens):
        gens = [g for g in gens if g is not None]
        if SEQ_MODE:
            for g in gens:
                for _ in g:
                    pass
            return
        while gens:
            for g in list(gens):
                try:
                    next(g)
                except StopIteration:
                    gens.remove(g)

    def chain(*gs):
        for g in gs:
            yield from g

    hs_ctr = [0]

    def chunks_of_tile(ti, xs, nck, need_o):
        pending = None
        for p0 in range(0, nck, 2):
            cks = list(range(p0, min(p0 + 2, nck)))
            hss = []
            s1 = []
            for i, ck in enumerate(cks):
                hs = hs_ctr[0] % 4
                hs_ctr[0] += 1
                hss.append(hs)
                s1.append(stage1(ti, xs, ck, need_o, i, hs))
            run_rr([pending] + s1)
            pending = chain(*[stage2(ti, xs, ck, need_o, hs) for ck, hs in zip(cks, hss)])
        run_rr([pending])

    tile(0, 0, 64, False)
    ntile = (nchunks - 1) // 8
    assert ntile * 8 + 1 == nchunks
    for ti in range(ntile):
        tile(ti + 1, 64 + ti * TM, TM, True)


def dn_weight_layout(r, dn_norm_w, dn_w_in, dn_conv_w, dn_a_log, dn_dt_bias, dn_o_norm_w):
    w = np.asarray(dn_w_in[0], np.float32)
    qc = list(range(2 * r * 128, (2 * r + 2) * 128))
    kc = [1024 + c for c in qc]
    vc = list(range(2048 + 4 * r * 128, 2048 + (4 * r + 4) * 128))
    zc = list(range(4096 + 4 * r * 128, 4096 + (4 * r + 4) * 128))
    bcol = list(range(6144 + 4 * r, 6144 + 4 * r + 4))
    acol = list(range(6160 + 4 * r, 6160 + 4 * r + 4))
    cols = qc + kc + vc + zc + bcol + acol
    out = {}
    out["wB"] = np.ascontiguousarray(w[:, cols])
    out["nwB"] = np.ascontiguousarray(np.asarray(dn_norm_w[0], np.float32).reshape(8, 128).T)
    cwf = np.asarray(dn_conv_w[0], np.float32)[:, qc + kc + vc]
    out["convw"] = np.ascontiguousarray(cwf.reshape(4, 8, 128).transpose(2, 1, 0))
    out["onw"] = np.ascontiguousarray(np.asarray(dn_o_norm_w[0], np.float32).reshape(128, 1))
    out["alog"] = np.ascontiguousarray(np.broadcast_to(np.asarray(dn_a_log[0], np.float32)[None, 4 * r:4 * r + 4], (64, 4)))
    out["dtb"] = np.ascontiguousarray(np.broadcast_to(np.asarray(dn_dt_bias[0], np.float32)[None, 4 * r:4 * r + 4], (64, 4)))
    out["cstB"] = dn_consts()
    return out


RG8 = [[0, 1, 2, 3, 4, 5, 6, 7]]


def build_fused():
    import contextlib
    nc = bass.Bass("TRN2", target_bir_lowering=False)
    ext = lambda name, shape, dt=F32: nc.dram_tensor(name, shape, dt, kind="ExternalInput").ap()
    dA = {}
    dA["xa"] = ext("xa", [33 * 128, 1024])
    dA["xm"] = ext("xm", [128, 1024])
    dA["w_in"] = ext("w_in", [1024, 2304])
    dA["nw"] = ext("nw", [128, 8])
    dA["wq"] = ext("wq", [128, 128])
    dA["wk"] = ext("wk", [128, 128])
    dA["sinks"] = ext("sinks", [128, 16])
    dA["w_out"] = ext("w_out", [1024, 1024])
    dA["tabs"] = ext("tabs", [128, 4, 2048])
    dA["mtabs"] = ext("mtabs", [16, 3, 2048])
    dB = {}
    dB["wB"] = ext("wB", [1024, 1544])
    dB["nwB"] = ext("nwB", [128, 8])
    dB["convw"] = ext("convw", [128, 8, 4])
    dB["onw"] = ext("onw", [128, 1])
    dB["alog"] = ext("alog", [64, 4])
    dB["dtb"] = ext("dtb", [64, 4])
    dB["cstB"] = ext("cstB", [64, CB_COLS])
    wout = ext("wout", [2048, 1024])
    y = nc.dram_tensor("y", [4096, 1024], F32, kind="ExternalOutput").ap()
    h1_loc = nc.dram_tensor("h1_loc", [4096, 1024], F32).ap()
    xn1T_loc = nc.dram_tensor("xn1T_loc", [1024, 4160], BF16).ap()
    xnT_all = nc.dram_tensor("xnT_all", [8 * 1024, 4160], BF16).ap()
    ogT_loc = nc.dram_tensor("ogT_loc", [512, 16384], BF16).ap()
    ogT_all = nc.dram_tensor("ogT_all", [8 * 512, 16384], BF16).ap()
    xnT_mine = nc.dram_tensor("xnT_mine", [1024, 16448], BF16).ap()
    ogT_mine = nc.dram_tensor("ogT_mine", [2048, 4096], BF16).ap()
    dA["h1"] = h1_loc
    dA["xn1T"] = xn1T_loc
    dB["ogT"] = ogT_loc
    with contextlib.ExitStack() as outer:
        with contextlib.ExitStack() as st:
            P = Prog(nc, n_dma_sems=16, sem_stack=outer, prefix="A", barrier=True)
            emit_phase_a(nc, st, P, dA, 33)
            P.emit()
        with contextlib.ExitStack() as st:
            P = Prog(nc, n_dma_sems=16, sem_stack=outer, prefix="B", barrier=True)
            P.add("pool", lambda e: e.collective_compute("AllGather", ALU.bypass, replica_groups=RG8, ins=[xn1T_loc], outs=[xnT_all]),
                  writes=["xnT_all"], dma=True, inc=1)

            xcache = {}

            def seg_copy(e, sg_):
                if "b" not in xcache:
                    xcache["b"] = e.snap(e.partition_id() // 4, min_val=0, max_val=1)
                src = xnT_all.rearrange("(r d) t -> r d t", d=1024)[bass.ds(xcache["b"] * 4 + sg_, 1), :, :]
                src = src.rearrange("o d t -> (o d) t")
                if sg_ == 0:
                    return e.dma_start(out=xnT_mine[:, 0:4160], in_=src)
                return e.dma_start(out=xnT_mine[:, 64 + sg_ * 4096:64 + (sg_ + 1) * 4096], in_=src[:, 64:4160])

            for sg_ in range(4):
                P.add("sp", lambda e, sg_=sg_: seg_copy(e, sg_), reads=["xnT_all"], writes=[("xnT_mine", sg_)], dma=True)

            def xsrc(e, t0, TW):
                return xnT_mine[:, t0:t0 + TW].rearrange("(c p) t -> p c t", p=128)

            xdep_fn = lambda t0: [("xnT_mine", 0 if t0 == 0 else (t0 - 64) // 4096)]
            emit_phase_b(nc, st, P, dB, 257, xsrc=xsrc, xdeps=xdep_fn)
            P.emit()
        with contextlib.ExitStack() as st:
            P = Prog(nc, n_dma_sems=16, sem_stack=outer, prefix="C", barrier=False)
            P.add("pool", lambda e: e.collective_compute("AllGather", ALU.bypass, replica_groups=RG8, ins=[ogT_loc], outs=[ogT_all]),
                  writes=["ogT_all"], dma=True, inc=1)

            ocache = {}

            def og_copy(e, r_):
                if "b" not in ocache:
                    pid = e.partition_id()
                    ocache["b"] = e.snap(pid // 4, min_val=0, max_val=1)
                    ocache["s"] = e.snap(pid % 4, min_val=0, max_val=3)
                v = ogT_all.rearrange("(r f) (s t) -> r f s t", f=512, s=4)
                src = v[bass.ds(ocache["b"] * 4 + r_, 1), :, bass.ds(ocache["s"], 1), :].rearrange("o f q t -> (o f) (q t)")
                return e.dma_start(out=ogT_mine[r_ * 512:(r_ + 1) * 512, :], in_=src)

            for r_ in range(4):
                P.add("sp", lambda e, r_=r_: og_copy(e, r_), reads=["ogT_all"], writes=["ogT_mine"], dma=True)
            emit_phase_c(nc, st, P, ogT_mine, h1_loc, wout, y, 4096, ogdeps=["ogT_mine"])
            P.emit()
    return nc


_NC_CACHE = {}


def _get_nc(name, builder):
    if name not in _NC_CACHE:
        _NC_CACHE[name] = builder()
    return _NC_CACHE[name]


def kernel(x, meta_tokens, attn_norm_w, attn_w_in, attn_q_norm_w, attn_k_norm_w, attn_sinks, attn_w_out,
           dn_norm_w, dn_w_in, dn_conv_w, dn_a_log, dn_dt_bias, dn_o_norm_w, dn_w_out):
    x = np.asarray(x, np.float32)
    meta = np.asarray(meta_tokens, np.float32)
    cores = list(range(8))
    SEG = 4096
    WA = attn_weight_layout(attn_norm_w, attn_w_in, attn_q_norm_w, attn_k_norm_w, attn_sinks, attn_w_out)
    xm = np.zeros((128, 1024), np.float32)
    xm[:16] = meta
    tabs0, tabs1 = attn_tables(True), attn_tables(False)
    wout = np.ascontiguousarray(np.asarray(dn_w_out[0], np.float32))
    WBs = [dn_weight_layout(r, dn_norm_w, dn_w_in, dn_conv_w, dn_a_log, dn_dt_bias, dn_o_norm_w) for r in range(4)]
    maps = []
    for c in cores:
        b, s = c // 4, c % 4
        if s == 0:
            halo = np.concatenate([np.zeros((112, 1024), np.float32), meta], 0)
        else:
            halo = x[b, s * SEG - 128:s * SEG]
        xa = np.ascontiguousarray(np.concatenate([halo, x[b, s * SEG:(s + 1) * SEG]], 0))
        tb, mtb = tabs0 if s == 0 else tabs1
        maps.append(dict(xa=xa, xm=xm, tabs=tb, mtabs=mtb, wout=wout, **WA, **WBs[s]))
    res = run_bass_kernel_spmd(_get_nc("F", build_fused), maps, core_ids=cores).results
    out = np.empty((2, 16384, 1024), np.float32)
    for c in cores:
        b, s = c // 4, c % 4
        out[b, s * SEG:(s + 1) * SEG] = np.asarray(res[c]["y"])
    return out
```

```python
import numpy as np
import ml_dtypes
import concourse.bass as bass
import concourse.mybir as mybir
from concourse.bass_utils import run_bass_kernel_spmd

F32 = mybir.dt.float32
BF16 = mybir.dt.bfloat16
AF = mybir.ActivationFunctionType
ALU = mybir.AluOpType
AX = mybir.AxisListType

NPBF = ml_dtypes.bfloat16


class Prog:
    COMPUTE = ("pe", "act", "dve", "pool")

    def __init__(self, nc, n_dma_sems=32, sem_stack=None, prefix="", barrier=False):
        self.nc = nc
        self.sem_stack = sem_stack
        self.prefix = prefix
        self.barrier = barrier
        self.ops = []
        self.last_w = {}
        self.readers = {}
        self.n_dma_sems = n_dma_sems
        self.n_pool_sems = n_dma_sems
        self.dma_count = 0
        self.dma_last = [None] * n_dma_sems
        self.dma_uses = [0] * n_dma_sems

    def add(self, eng, fn, reads=(), writes=(), dma=False, inc=16, own_sem=False):
        oid = len(self.ops)
        deps = set()
        isps = lambda k: (k if isinstance(k, str) else str(k[0])).startswith("b_ps")
        if any(isps(r) for r in reads):
            writes = list(writes) + [r for r in reads if isps(r)]
            reads = [r for r in reads if not isps(r)]
        for r in reads:
            if r in self.last_w:
                deps.add(self.last_w[r])
        for w in writes:
            if w in self.last_w:
                deps.add(self.last_w[w])
            deps |= self.readers.get(w, set())
        op = dict(eng=eng, fn=fn, deps=deps, dma=dma, signal=False, sigval=None)
        if dma and own_sem:
            k = len(self.dma_last)
            self.dma_last.append(None)
            self.dma_uses.append(0)
        elif dma:
            k = self.dma_count % self.n_pool_sems
            self.dma_count += 1
        if dma:
            if self.dma_last[k] is not None:
                deps.add(self.dma_last[k])
            self.dma_last[k] = oid
            self.dma_uses[k] += inc
            op["dsem"] = k
            op["dinc"] = inc
            op["dval"] = self.dma_uses[k]
        deps.discard(oid)
        self.ops.append(op)
        for r in reads:
            self.readers.setdefault(r, set()).add(oid)
        for w in writes:
            self.last_w[w] = oid
            self.readers[w] = set()
        return oid

    def emit(self):
        nc = self.nc
        ops = self.ops
        for op in ops:
            for d in op["deps"]:
                D = ops[d]
                if D["dma"]:
                    continue
                if D["eng"] == op["eng"] == "pe" and not op["dma"]:
                    continue
                D["signal"] = True
        if self.barrier:
            last = {}
            for i, op in enumerate(ops):
                if not op["dma"]:
                    last[op["eng"]] = i
            for i in last.values():
                ops[i]["signal"] = True
        cnt = {e: 0 for e in self.COMPUTE}
        for op in ops:
            if op["dma"]:
                continue
            if op["signal"]:
                cnt[op["eng"]] += 1
                op["sigval"] = cnt[op["eng"]]
        self.n_dma_sems = len(self.dma_last)
        by_eng = {e: [] for e in ("pe", "act", "dve", "pool", "sp")}
        for op in ops:
            by_eng[op["eng"]].append(op)
        import contextlib
        with contextlib.ExitStack() as st:
            sst = self.sem_stack if self.sem_stack is not None else st
            csem = {e: sst.enter_context(nc.semaphore(self.prefix + "cs_" + e)) for e in self.COMPUTE}
            dsem = [sst.enter_context(nc.semaphore(self.prefix + "ds_%d" % k)) for k in range(self.n_dma_sems)]
            block = st.enter_context(nc.Block())

            def run(engname, eng):
                waited = {}
                for op in by_eng[engname]:
                    targets = []
                    for d in sorted(op["deps"]):
                        D = ops[d]
                        if D["dma"]:
                            targets.append((("d", D["dsem"]), dsem[D["dsem"]], D["dval"]))
                        else:
                            if D["eng"] == engname == "pe" and not op["dma"]:
                                continue
                            targets.append((("c", D["eng"]), csem[D["eng"]], D["sigval"]))
                    best = {}
                    for key, sem, val in targets:
                        if val > waited.get(key, 0) and val > best.get(key, (None, 0))[1]:
                            best[key] = (sem, val)
                    for key, (sem, val) in best.items():
                        eng.wait_ge(sem, val)
                        waited[key] = val
                    ins = op["fn"](eng)
                    if op["dma"]:
                        ins.then_inc(dsem[op["dsem"]], op["dinc"])
                    elif op["signal"]:
                        ins.then_inc(csem[engname], 1)
                if engname == "sp" or self.barrier:
                    for k in range(self.n_dma_sems):
                        if self.dma_uses[k]:
                            eng.wait_ge(dsem[k], self.dma_uses[k])
                if self.barrier:
                    for e2 in self.COMPUTE:
                        if cnt[e2]:
                            eng.wait_ge(csem[e2], cnt[e2])

            @block.tensor
            def _(e):
                run("pe", e)

            @block.scalar
            def _(e):
                run("act", e)

            @block.vector
            def _(e):
                run("dve", e)

            @block.gpsimd
            def _(e):
                run("pool", e)

            @block.sync
            def _(e):
                run("sp", e)


def build_phase_c(ntok=4096):
    nc = bass.Bass("TRN2", target_bir_lowering=False)
    ogT = nc.dram_tensor("ogT", [2048, ntok], BF16, kind="ExternalInput").ap()
    h1 = nc.dram_tensor("h1", [ntok, 1024], F32, kind="ExternalInput").ap()
    wout = nc.dram_tensor("wout", [2048, 1024], F32, kind="ExternalInput").ap()
    y = nc.dram_tensor("y", [ntok, 1024], F32, kind="ExternalOutput").ap()
    import contextlib
    with contextlib.ExitStack() as st:
        P = Prog(nc)
        emit_phase_c(nc, st, P, ogT, h1, wout, y, ntok)
        P.emit()
    return nc


def emit_phase_c(nc, st, P, ogT, h1, wout, y, ntok, ogsrc=None, ogdeps=()):
    TT = 512
    nt = ntok // TT
    w_sb = st.enter_context(nc.sbuf_tensor("c_w", [128, 16, 1024], BF16))
    wst = [st.enter_context(nc.sbuf_tensor("c_wst%d" % i, [128, 2, 1024], F32)) for i in range(2)]
    og_sb = [st.enter_context(nc.sbuf_tensor("c_og%d" % i, [128, 16, TT], BF16)) for i in range(2)]
    h_sb = [st.enter_context(nc.sbuf_tensor("c_h%d" % i, [128, 1024], F32)) for i in range(3)]
    y_sb = [st.enter_context(nc.sbuf_tensor("c_y%d" % i, [128, 1024], F32)) for i in range(3)]
    ps = [st.enter_context(nc.psum_tensor("c_ps%d" % i, [128, 512], F32)) for i in range(4)]
    woutv = wout.rearrange("(c p) n -> p c n", p=128)
    for j in range(8):
        s = j % 2
        P.add("sp", lambda e, j=j, s=s: e.dma_start(out=wst[s][:], in_=woutv[:, 2 * j:2 * j + 2, :]),
              writes=[("c_wst", s)], dma=True)
        eng = "act" if j % 2 == 0 else "dve"
        if eng == "act":
            P.add("act", lambda e, j=j, s=s: e.copy(out=w_sb[:, 2 * j:2 * j + 2, :], in_=wst[s][:]),
                  reads=[("c_wst", s)], writes=[("c_w", j)])
        else:
            P.add("dve", lambda e, j=j, s=s: e.tensor_copy(out=w_sb[:, 2 * j:2 * j + 2, :], in_=wst[s][:]),
                  reads=[("c_wst", s)], writes=[("c_w", j)])
    ogv = ogT.rearrange("(c p) t -> p c t", p=128) if ogsrc is None else None
    blk = 0
    for t in range(nt):
        so = t % 2
        if ogsrc is None:
            P.add("sp", lambda e, t=t, so=so: e.dma_start(out=og_sb[so][:], in_=ogv[:, :, t * TT:(t + 1) * TT]),
                  reads=list(ogdeps), writes=[("c_og", so)], dma=True)
        else:
            P.add("sp", lambda e, t=t, so=so: e.dma_start(out=og_sb[so][:], in_=ogsrc(e, t * TT, TT)),
                  reads=list(ogdeps), writes=[("c_og", so)], dma=True)
        for b in range(TT // 128):
            r0 = t * TT + b * 128
            sh = blk % 3
            P.add("sp", lambda e, r0=r0, sh=sh: e.dma_start(out=h_sb[sh][:], in_=h1[r0:r0 + 128, :]),
                  writes=[("c_h", sh)], dma=True)
            for g in range(2):
                pb = (blk * 2 + g) % 4
                for c in range(16):
                    P.add("pe", lambda e, pb=pb, so=so, b=b, c=c, g=g: e.matmul(
                        ps[pb][:], lhsT=og_sb[so][:, c, b * 128:(b + 1) * 128],
                        rhs=w_sb[:, c, g * 512:(g + 1) * 512], start=(c == 0), stop=(c == 15)),
                        reads=[("c_og", so), ("c_w", c // 2)], writes=[("c_ps", pb)])
                P.add("dve", lambda e, pb=pb, sh=sh, g=g: e.tensor_tensor(
                    out=y_sb[sh][:, g * 512:(g + 1) * 512], in0=ps[pb][:],
                    in1=h_sb[sh][:, g * 512:(g + 1) * 512], op=ALU.add),
                    reads=[("c_ps", pb), ("c_h", sh)], writes=[("c_y", sh, g)])
            P.add("sp", lambda e, r0=r0, sh=sh: e.dma_start(out=y[r0:r0 + 128, :], in_=y_sb[sh][:]),
                  reads=[("c_y", sh, 0), ("c_y", sh, 1)], writes=[("c_yout", blk)], dma=True)
            blk += 1


EPS = 1e-6


def make_identity(nc, st, P, name):
    identf = st.enter_context(nc.sbuf_tensor(name + "_f", [128, 128], F32))
    ident = st.enter_context(nc.sbuf_tensor(name, [128, 128], BF16))
    P.add("pool", lambda e: e.memset(identf[:], 1.0), writes=[name + "_f"])
    P.add("pool", lambda e: e.affine_select(out=identf[:], in_=identf[:], pattern=[[-1, 128]],
                                             compare_op=ALU.is_equal, fill=0.0, base=0,
                                             channel_multiplier=1),
          reads=[name + "_f"], writes=[name + "_f"])
    P.add("dve", lambda e: e.tensor_copy(out=ident[:], in_=identf[:]), reads=[name + "_f"], writes=[name])
    return ident, identf


def build_phase_a(nb=33):
    nc = bass.Bass("TRN2", target_bir_lowering=False)
    d = {}
    d["xa"] = nc.dram_tensor("xa", [nb * 128, 1024], F32, kind="ExternalInput").ap()
    d["xm"] = nc.dram_tensor("xm", [128, 1024], F32, kind="ExternalInput").ap()
    d["w_in"] = nc.dram_tensor("w_in", [1024, 2304], F32, kind="ExternalInput").ap()
    d["nw"] = nc.dram_tensor("nw", [128, 8], F32, kind="ExternalInput").ap()
    d["wq"] = nc.dram_tensor("wq", [128, 128], F32, kind="ExternalInput").ap()
    d["wk"] = nc.dram_tensor("wk", [128, 128], F32, kind="ExternalInput").ap()
    d["sinks"] = nc.dram_tensor("sinks", [128, 16], F32, kind="ExternalInput").ap()
    d["w_out"] = nc.dram_tensor("w_out", [1024, 1024], F32, kind="ExternalInput").ap()
    d["tabs"] = nc.dram_tensor("tabs", [128, 4, 2048], F32, kind="ExternalInput").ap()
    d["mtabs"] = nc.dram_tensor("mtabs", [16, 3, 2048], F32, kind="ExternalInput").ap()
    d["h1"] = nc.dram_tensor("h1", [(nb - 1) * 128, 1024], F32, kind="ExternalOutput").ap()
    d["xn1T"] = nc.dram_tensor("xn1T", [1024, 64 + (nb - 1) * 128], BF16, kind="ExternalOutput").ap()
    import contextlib
    with contextlib.ExitStack() as st:
        P = Prog(nc)
        emit_phase_a(nc, st, P, d, nb)
        P.emit()
    return nc


def emit_phase_a(nc, st, P, d, nb):
    sb = lambda name, shape, dt: st.enter_context(nc.sbuf_tensor(name, shape, dt))
    ident, identf = make_identity(nc, st, P, "a_ident")
    w_sb = sb("a_w", [128, 8, 2304], BF16)
    wo_sb = sb("a_wo", [128, 8, 1024], BF16)
    wst = [sb("a_wst%d" % i, [128, 2304], F32) for i in range(2)]
    nw = sb("a_nw", [128, 8], F32)
    wqk = sb("a_wqk", [128, 128], F32)
    wk_t = sb("a_wk", [128, 128], F32)
    esink = sb("a_esink", [128, 16], F32)
    tabs = sb("a_tabs", [128, 4, 2048], F32)
    mtabs = sb("a_mtabs", [16, 3, 2048], F32)
    x_sb = [sb("a_x%d" % i, [128, 1024], F32) for i in range(2)]
    junk = sb("a_junk", [128, 1024], F32)
    st_sb = sb("a_stat", [128, 8], F32)
    xn = sb("a_xn", [128, 1024], BF16)
    xnT = sb("a_xnT", [128, 8, 128], BF16)
    qss = sb("a_qss", [128, 16], F32)
    qr = sb("a_qr", [128, 16], F32)
    kss = sb("a_kss", [128, 4], F32)
    qn = sb("a_qn", [128, 1024], BF16)
    kn = sb("a_kn", [128, 128], F32)
    ksq = sb("a_ksq", [128, 128], F32)
    kq = sb("a_kq", [128, 128], BF16)
    gs = sb("a_gs", [128, 1024], BF16)
    QT = sb("a_QT", [128, 1024], BF16)
    KT = [sb("a_KT%d" % i, [128, 128], BF16) for i in range(2)]
    KTm = sb("a_KTm", [128, 128], BF16)
    vE = [sb("a_vE%d" % i, [128, 2, 65], BF16) for i in range(2)]
    vEm = sb("a_vEm", [128, 2, 65], BF16)
    E_sb = [sb("a_E%d" % i, [128, 512], F32) for i in range(2)]
    PTp = sb("a_PTp", [128, 16, 128], BF16)
    PTc = sb("a_PTc", [128, 16, 128], BF16)
    PTm = sb("a_PTm", [16, 16, 128], BF16)
    den = sb("a_den", [128, 16], F32)
    rden = sb("a_rden", [128, 16], F32)
    ogp = sb("a_ogp", [128, 1024], F32)
    og = sb("a_og", [128, 1024], BF16)
    ogT = sb("a_ogT", [128, 8, 128], BF16)
    h1 = [sb("a_h1%d" % i, [128, 1024], F32) for i in range(2)]
    xn1 = sb("a_xn1", [128, 1024], BF16)
    xn1T = [sb("a_xn1T%d" % i, [128, 8, 128], BF16) for i in range(2)]

    ps_t = st.enter_context(nc.psum_tensor("a_pst", [128, 8, 128], BF16))
    ps_u = [st.enter_context(nc.psum_tensor("a_psu%d" % i, [128, 512], F32)) for i in range(2)]
    ps_s = [st.enter_context(nc.psum_tensor("a_pss%d" % i, [128, 512], F32)) for i in range(2)]
    ps_o = st.enter_context(nc.psum_tensor("a_pso", [128, 3, 512], F32))

    P.add("sp", lambda e: e.dma_start(out=nw[:], in_=d["nw"]), writes=["a_nw"], dma=True)
    P.add("sp", lambda e: e.dma_start(out=wqk[:], in_=d["wq"]), writes=["a_wqk"], dma=True)
    P.add("sp", lambda e: e.dma_start(out=wk_t[:], in_=d["wk"]), writes=["a_wk"], dma=True)
    P.add("sp", lambda e: e.dma_start(out=esink[:], in_=d["sinks"]), writes=["a_esink"], dma=True)
    P.add("dve", lambda e: e.scalar_tensor_tensor(out=wqk[:], in0=wqk[:], scalar=8.0, in1=wk_t[:],
                                                  op0=ALU.mult, op1=ALU.mult),
          reads=["a_wqk", "a_wk"], writes=["a_wqk"])
    P.add("act", lambda e: e.activation(out=esink[:], in_=esink[:], func=AF.Exp), reads=["a_esink"], writes=["a_esink"])
    for i in range(2):
        P.add("pool", lambda e, i=i: e.memset(vE[i][:], 1.0), writes=[("a_vE", i)])
    P.add("pool", lambda e: e.memset(vEm[:], 1.0), writes=["a_vEm"])
    w_inv = d["w_in"].rearrange("(c p) n -> p c n", p=128)
    for c in range(8):
        s = c % 2
        P.add("sp", lambda e, c=c, s=s: e.dma_start(out=wst[s][:], in_=w_inv[:, c, :]), writes=[("a_wst", s)], dma=True)
        if c % 2 == 0:
            P.add("dve", lambda e, c=c, s=s: e.tensor_scalar(out=w_sb[:, c, :], in0=wst[s][:], scalar1=nw[:, c:c + 1],
                                                           scalar2=None, op0=ALU.mult),
                  reads=[("a_wst", s), "a_nw"], writes=[("a_w", c)])
        else:
            P.add("act", lambda e, c=c, s=s: e.activation(out=w_sb[:, c, :], in_=wst[s][:], func=AF.Copy, scale=nw[:, c:c + 1]),
                  reads=[("a_wst", s), "a_nw"], writes=[("a_w", c)])
    w_outv = d["w_out"].rearrange("(c p) n -> p c n", p=128)
    for c in range(4):
        s = c % 2
        P.add("sp", lambda e, c=c, s=s: e.dma_start(out=wst[s][:, 0:2048].rearrange("p (a n) -> p a n", a=2),
                                                   in_=w_outv[:, 2 * c:2 * c + 2, :]), writes=[("a_wst", s)], dma=True)
        P.add("dve" if c % 2 == 0 else "pool",
              lambda e, c=c, s=s: e.tensor_copy(out=wo_sb[:, 2 * c:2 * c + 2, :],
                                                in_=wst[s][:, 0:2048].rearrange("p (a n) -> p a n", a=2)),
              reads=[("a_wst", s)], writes=[("a_wo", c)])
    for j in range(4):
        P.add("sp", lambda e, j=j: e.dma_start(out=tabs[:, j, :], in_=d["tabs"][:, j, :]), writes=[("a_tabs", j)], dma=True)
    P.add("sp", lambda e: e.dma_start(out=mtabs[:], in_=d["mtabs"]), writes=["a_mtabs"], dma=True)
    W_ALL = [("a_w", c) for c in range(8)]
    WO_ALL = [("a_wo", c) for c in range(4)]

    def rstd_from_ss(ss_ap, ln_ap, out_ap, bias, key_in, key_out):
        P.add("act", lambda e: e.activation(out=ln_ap, in_=ss_ap, func=AF.Ln, bias=bias, scale=1.0),
              reads=[key_in], writes=[key_out + "_ln"])
        P.add("act", lambda e: e.activation(out=out_ap, in_=ln_ap, func=AF.Exp, scale=-0.5),
              reads=[key_out + "_ln"], writes=[key_out])

    def transposes8(src, src_key, dstT, dst_key, evac_eng):
        for c in range(8):
            P.add("pe", lambda e, c=c: e.transpose(out=ps_t[:, c, :], in_=src[:, c * 128:(c + 1) * 128], identity=ident[:]),
                  reads=[src_key, "a_ident"], writes=["a_pst"])
        if evac_eng == "act":
            P.add("act", lambda e: e.copy(out=dstT, in_=ps_t[:]), reads=["a_pst"], writes=[dst_key])
        else:
            P.add(evac_eng, lambda e: e.tensor_copy(out=dstT, in_=ps_t[:]), reads=["a_pst"], writes=[dst_key])

    def front(src_ap, xs, KT_dst, KT_key, vE_dst, vE_key, full):
        P.add("sp", lambda e: e.dma_start(out=x_sb[xs][:], in_=src_ap), writes=[("a_x", xs)], dma=True)
        P.add("act", lambda e: e.activation(out=junk[:], in_=x_sb[xs][:], func=AF.Square, accum_out=st_sb[:, 0:1]),
              reads=[("a_x", xs)], writes=["a_junk", "a_ss"])
        rstd_from_ss(st_sb[:, 0:1], st_sb[:, 1:2], st_sb[:, 2:3], 1024 * EPS, "a_ss", "a_rstd")
        P.add("dve", lambda e: e.tensor_scalar(out=xn[:], in0=x_sb[xs][:], scalar1=st_sb[:, 2:3], scalar2=32.0,
                                               op0=ALU.mult, op1=ALU.mult),
              reads=[("a_x", xs), "a_rstd"], writes=["a_xn"])
        transposes8(xn, "a_xn", xnT[:], "a_xnT", "act")
        groups = [(0, 512), (512, 512), (1024, 256), (1280, 512), (1792, 512)] if full else [(1024, 256)]
        for gi, (c0, cw) in enumerate(groups):
            pb = gi % 2
            for c in range(8):
                P.add("pe", lambda e, c=c, c0=c0, cw=cw, pb=pb: e.matmul(
                    ps_u[pb][:, 0:cw], lhsT=xnT[:, c, :], rhs=w_sb[:, c, c0:c0 + cw], start=(c == 0), stop=(c == 7)),
                    reads=["a_xnT"] + W_ALL, writes=[("a_psu", pb)])
            if c0 < 1024:
                h0 = c0 // 64
                P.add("act", lambda e, pb=pb: e.activation(out=junk[:, 0:512], in_=ps_u[pb][:], func=AF.Square),
                      reads=[("a_psu", pb)], writes=["a_junk"])
                P.add("dve", lambda e, h0=h0: e.tensor_reduce(out=qss[:, h0:h0 + 8], in_=junk[:, 0:512].rearrange("p (a b) -> p a b", a=8),
                                                            axis=AX.X, op=ALU.add),
                      reads=["a_junk"], writes=[("a_qss", h0)])
                rstd_from_ss(qss[:, h0:h0 + 8], qr[:, h0:h0 + 8], qr[:, h0:h0 + 8], 64 * EPS, ("a_qss", h0), "a_qr%d" % h0)
                P.add("dve", lambda e, pb=pb, h0=h0, c0=c0: e.tensor_tensor(
                    out=qn[:, c0:c0 + 512].rearrange("p (a b) -> p a b", a=8),
                    in0=ps_u[pb][:].rearrange("p (a b) -> p a b", a=8),
                    in1=qr[:, h0:h0 + 8].unsqueeze(2).to_broadcast([128, 8, 64]), op=ALU.mult),
                    reads=[("a_psu", pb), "a_qr%d" % h0], writes=[("a_qn", h0)])
            elif c0 == 1024:
                P.add("act", lambda e, pb=pb: e.activation(out=ksq[:], in_=ps_u[pb][:, 0:128], func=AF.Square),
                      reads=[("a_psu", pb)], writes=["a_junk2"])
                P.add("dve", lambda e: e.tensor_reduce(out=kss[:, 0:2], in_=ksq[:].rearrange("p (a b) -> p a b", a=2),
                                                       axis=AX.X, op=ALU.add),
                      reads=["a_junk2"], writes=["a_kss"])
                rstd_from_ss(kss[:, 0:2], kss[:, 2:4], kss[:, 2:4], 64 * EPS, "a_kss", "a_kr")
                P.add("dve", lambda e, pb=pb: e.tensor_tensor(
                    out=kn[:].rearrange("p (a b) -> p a b", a=2), in0=ps_u[pb][:, 0:128].rearrange("p (a b) -> p a b", a=2),
                    in1=kss[:, 2:4].unsqueeze(2).to_broadcast([128, 2, 64]), op=ALU.mult),
                    reads=[("a_psu", pb), "a_kr"], writes=["a_kn"])
                P.add("dve", lambda e: e.tensor_tensor(out=kq[:], in0=kn[:], in1=wqk[:], op=ALU.mult),
                      reads=["a_kn", "a_wqk"], writes=["a_kq"])
                P.add("act", lambda e, pb=pb: e.copy(out=vE_dst[:, :, 0:64], in_=ps_u[pb][:, 128:256].rearrange("p (a b) -> p a b", a=2)),
                      reads=[("a_psu", pb)], writes=[vE_key])
            else:
                g0 = c0 - 1280
                P.add("act", lambda e, pb=pb, g0=g0: e.activation(out=gs[:, g0:g0 + 512], in_=ps_u[pb][:], func=AF.Silu),
                      reads=[("a_psu", pb)], writes=[("a_gs", g0)])
        if full:
            for c in range(8):
                P.add("pe", lambda e, c=c: e.transpose(out=ps_t[:, c, :], in_=qn[:, c * 128:(c + 1) * 128], identity=ident[:]),
                      reads=[("a_qn", 0), ("a_qn", 8), "a_ident"], writes=["a_pst"])
            P.add("dve", lambda e: e.tensor_copy(out=QT[:].rearrange("p (a b) -> p a b", a=8), in_=ps_t[:]),
                  reads=["a_pst"], writes=["a_QT"])
        P.add("pe", lambda e: e.transpose(out=ps_t[:, 0, :], in_=kq[:], identity=ident[:]),
              reads=["a_kq", "a_ident"], writes=["a_pst"])
        P.add("act", lambda e: e.copy(out=KT_dst[:], in_=ps_t[:, 0, :]), reads=["a_pst"], writes=[KT_key])

    front(d["xm"], 0, KTm, "a_KTm", vEm, "a_vEm", full=False)

    for i in range(nb):
        cur = i % 2
        prv = 1 - cur
        front(d["xa"][i * 128:(i + 1) * 128, :], i % 2, KT[cur], ("a_KT", cur), vE[cur], ("a_vE", cur), full=True)
        if i == 0:
            chunks = [("cur", 0), ("meta", 0)]
        elif i == 1:
            chunks = [("prev", 1), ("cur", 3), ("meta", 1)]
        else:
            chunks = [("prev", 2), ("cur", 3), ("meta", 2)]
        n_e = 0
        for kind, tix in chunks:
            for g in range(2):
                for hf in range(2):
                    pb = n_e % 2
                    hh = g * 8 + hf * 4
                    if kind == "meta":
                        lhsT, lk, M = KTm[g * 64:(g + 1) * 64, 0:16], "a_KTm", 16
                    elif kind == "cur":
                        lhsT, lk, M = KT[cur][g * 64:(g + 1) * 64, :], ("a_KT", cur), 128
                    else:
                        lhsT, lk, M = KT[prv][g * 64:(g + 1) * 64, :], ("a_KT", prv), 128
                    P.add("pe", lambda e, pb=pb, lhsT=lhsT, g=g, hf=hf, M=M: e.matmul(
                        ps_s[pb][0:M, :], lhsT=lhsT, rhs=QT[g * 64:(g + 1) * 64, hf * 512:(hf + 1) * 512], start=True, stop=True),
                        reads=[lk, "a_QT"], writes=[("a_pss", pb)])
                    P.add("act", lambda e, pb=pb, M=M: e.activation(out=E_sb[pb][0:M, :], in_=ps_s[pb][0:M, :], func=AF.Exp),
                          reads=[("a_pss", pb)], writes=[("a_E", pb)])
                    if kind == "meta":
                        tab = mtabs[:, tix, hh * 128:(hh + 4) * 128]
                        dst = PTm[:, hh:hh + 4, :]
                        dk = ("a_PTm", hh)
                        tk = "a_mtabs"
                    else:
                        tab = tabs[:, tix, hh * 128:(hh + 4) * 128]
                        dst = (PTc if kind == "cur" else PTp)[:, hh:hh + 4, :]
                        dk = ("a_PTc" if kind == "cur" else "a_PTp", hh)
                        tk = ("a_tabs", tix)
                    P.add("dve" if n_e % 2 == 0 else "pool",
                          lambda e, pb=pb, M=M, tab=tab, dst=dst: e.tensor_tensor(
                              out=dst.rearrange("p a b -> p (a b)"), in0=E_sb[pb][0:M, :], in1=tab, op=ALU.mult),
                          reads=[("a_E", pb), tk], writes=[dk])
                    n_e += 1
        for h in range(16):
            g = h // 8
            hh = (h // 4) * 4
            bank, off = h // 7, (h % 7) * 65
            for ci, (kind, tix) in enumerate(chunks):
                if kind == "meta":
                    lhsT, lk = PTm[:, h, :], ("a_PTm", hh)
                    rhs, rk = vEm[0:16, g, :], "a_vEm"
                elif kind == "cur":
                    lhsT, lk = PTc[:, h, :], ("a_PTc", hh)
                    rhs, rk = vE[cur][:, g, :], ("a_vE", cur)
                else:
                    lhsT, lk = PTp[:, h, :], ("a_PTp", hh)
                    rhs, rk = vE[prv][:, g, :], ("a_vE", prv)
                P.add("pe", lambda e, bank=bank, off=off, lhsT=lhsT, rhs=rhs, ci=ci, n=len(chunks): e.matmul(
                    ps_o[:, bank, off:off + 65], lhsT=lhsT, rhs=rhs, start=(ci == 0), stop=(ci == n - 1)),
                    reads=[lk, rk], writes=[("a_pso", bank)])
        for bank, (h0, nh) in enumerate([(0, 7), (7, 7), (14, 2)]):
            ov = ps_o[:, bank, 0:nh * 65].rearrange("p (a b) -> p a b", b=65)
            P.add("dve", lambda e, ov=ov, h0=h0, nh=nh: e.tensor_tensor(
                out=den[:, h0:h0 + nh].unsqueeze(2), in0=ov[:, :, 64:65], in1=esink[:, h0:h0 + nh].unsqueeze(2), op=ALU.add),
                reads=[("a_pso", bank), "a_esink"], writes=[("a_den", bank)])
            P.add("dve", lambda e, h0=h0, nh=nh: e.reciprocal(out=rden[:, h0:h0 + nh], in_=den[:, h0:h0 + nh]),
                  reads=[("a_den", bank)], writes=[("a_rden", bank)])
            P.add("dve", lambda e, ov=ov, h0=h0, nh=nh: e.tensor_tensor(
                out=ogp[:, h0 * 64:(h0 + nh) * 64].rearrange("p (a b) -> p a b", b=64), in0=ov[:, :, 0:64],
                in1=rden[:, h0:h0 + nh].unsqueeze(2).to_broadcast([128, nh, 64]), op=ALU.mult),
                reads=[("a_pso", bank), ("a_rden", bank)], writes=[("a_ogp", bank)])
        P.add("pool", lambda e: e.tensor_tensor(out=og[:], in0=ogp[:], in1=gs[:], op=ALU.mult),
              reads=[("a_ogp", 0), ("a_ogp", 1), ("a_ogp", 2), ("a_gs", 0), ("a_gs", 512)], writes=["a_og"])
        transposes8(og, "a_og", ogT[:], "a_ogT", "act")
        hs = i % 2
        for g2 in range(2):
            pb = g2
            for c in range(8):
                P.add("pe", lambda e, c=c, g2=g2, pb=pb: e.matmul(
                    ps_u[pb][:], lhsT=ogT[:, c, :], rhs=wo_sb[:, c, g2 * 512:(g2 + 1) * 512], start=(c == 0), stop=(c == 7)),
                    reads=["a_ogT"] + WO_ALL, writes=[("a_psu", pb)])
            P.add("dve", lambda e, g2=g2, pb=pb, hs=hs, xs=i % 2: e.tensor_tensor(
                out=h1[hs][:, g2 * 512:(g2 + 1) * 512], in0=ps_u[pb][:], in1=x_sb[xs][:, g2 * 512:(g2 + 1) * 512], op=ALU.add),
                reads=[("a_psu", pb), ("a_x", i % 2)], writes=[("a_h1", hs, g2)])
        H1K = [("a_h1", hs, 0), ("a_h1", hs, 1)]
        if i >= 1:
            P.add("sp", lambda e, i=i, hs=hs: e.dma_start(out=d["h1"][(i - 1) * 128:i * 128, :], in_=h1[hs][:]),
                  reads=H1K, writes=[("a_h1out", i)], dma=True)
        P.add("act", lambda e, hs=hs: e.activation(out=junk[:], in_=h1[hs][:], func=AF.Square, accum_out=st_sb[:, 4:5]),
              reads=H1K, writes=["a_junk", "a_ss1"])
        rstd_from_ss(st_sb[:, 4:5], st_sb[:, 5:6], st_sb[:, 6:7], 1024 * EPS, "a_ss1", "a_rstd1")
        P.add("dve", lambda e, hs=hs: e.tensor_scalar(out=xn1[:], in0=h1[hs][:], scalar1=st_sb[:, 6:7], scalar2=32.0,
                                                      op0=ALU.mult, op1=ALU.mult),
              reads=H1K + ["a_rstd1"], writes=["a_xn1"])
        transposes8(xn1, "a_xn1", xn1T[hs][:], ("a_xn1T", hs), "act")
        xv = d["xn1T"].rearrange("(c p) t -> p c t", p=128)
        if i == 0:
            P.add("sp", lambda e, hs=hs: e.dma_start(out=xv[:, :, 0:64], in_=xn1T[hs][:, :, 64:128]),
                  reads=[("a_xn1T", hs)], writes=[("a_xn1out", i)], dma=True)
        else:
            P.add("sp", lambda e, hs=hs, i=i: e.dma_start(out=xv[:, :, 64 + (i - 1) * 128:64 + i * 128], in_=xn1T[hs][:]),
                  reads=[("a_xn1T", hs)], writes=[("a_xn1out", i)], dma=True)


def attn_tables(seg0):
    m = np.exp2(-8.0 * np.arange(1, 17) / 16.0)[None, :, None]
    j = np.arange(128)[:, None, None].astype(np.float64)
    i = np.arange(128)[None, None, :].astype(np.float64)
    t_prev = np.where(j > i, np.exp(-m * (128 + i - j)), 0.0)
    t_cur = np.where(j <= i, np.exp(-m * (i - j)), 0.0)
    jm = np.arange(16)[:, None, None].astype(np.float64)
    t_meta = np.exp(-m * 128.0) * np.ones((16, 16, 128))
    if seg0:
        t_cur0 = np.zeros_like(t_cur)
        t_prev1 = np.zeros_like(t_prev)
        pq = i - 112.0
        t_meta0 = np.where(pq >= jm, np.exp(-m * np.maximum(pq - jm, 0.0)), 0.0) * np.ones((16, 16, 128))
        t_meta1 = np.exp(-m * np.minimum(16.0 + i - jm, 128.0))
    else:
        t_cur0, t_prev1, t_meta0, t_meta1 = t_cur, t_prev, t_meta, t_meta
    tabs = np.stack([t_cur0, t_prev1, t_prev, t_cur], axis=1).reshape(128, 4, 2048).astype(np.float32)
    mtabs = np.stack([t_meta0, t_meta1, t_meta], axis=1).reshape(16, 3, 2048).astype(np.float32)
    return np.ascontiguousarray(tabs), np.ascontiguousarray(mtabs)


def attn_weight_layout(attn_norm_w, attn_w_in, attn_q_norm_w, attn_k_norm_w, attn_sinks, attn_w_out):
    w_in = np.asarray(attn_w_in[0], dtype=np.float32)
    perm = []
    for c in range(8):
        for half in range(2):
            h = c + 8 * half
            perm.extend(range(h * 64, (h + 1) * 64))
    perm = np.array(perm + list(range(1024, 2304)))
    out = {}
    out["w_in"] = np.ascontiguousarray(w_in[:, perm])
    out["nw"] = np.ascontiguousarray(np.asarray(attn_norm_w[0], np.float32).reshape(8, 128).T)
    out["wq"] = np.ascontiguousarray(np.broadcast_to(np.tile(np.asarray(attn_q_norm_w[0], np.float32), 2)[None, :], (128, 128)))
    out["wk"] = np.ascontiguousarray(np.broadcast_to(np.tile(np.asarray(attn_k_norm_w[0], np.float32), 2)[None, :], (128, 128)))
    out["sinks"] = np.ascontiguousarray(np.broadcast_to(np.asarray(attn_sinks[0], np.float32)[None, :], (128, 16)))
    out["w_out"] = np.ascontiguousarray(np.asarray(attn_w_out[0], np.float32))
    return out


NEG = -30000.0
DBG_STOP = 99
SEQ_MODE = False
CB_U, CB_MUI, CB_MUS, CB_MLS, CB_BD, CB_B1L, CB_B1U, CB_B2L, CB_ONES, CB_I = 0, 64, 320, 576, 832, 1088, 1344, 1600, 1856, 1984
CB_COLS = 2048


def dn_consts():
    a = np.arange(64)[:, None]
    b = np.arange(64)[None, :]
    rep4 = lambda m: np.tile(m[:, None, :], (1, 4, 1)).reshape(64, 256)
    U = (a <= b).astype(np.float32)
    mui = np.where(a <= b, 0.0, NEG)
    mus = np.where(a < b, 0.0, NEG)
    mls = np.where(b < a, 0.0, NEG)
    bd = (a // 16 == b // 16).astype(np.float32)
    b1l = ((a // 32 == b // 32) & (a // 16 == b // 16 + 1)).astype(np.float32)
    b2l = ((a >= 32) & (b < 32)).astype(np.float32)
    c = np.zeros((64, CB_COLS), np.float32)
    c[:, CB_U:CB_U + 64] = U
    c[:, CB_MUI:CB_MUI + 256] = rep4(mui)
    c[:, CB_MUS:CB_MUS + 256] = rep4(mus)
    c[:, CB_MLS:CB_MLS + 256] = rep4(mls)
    c[:, CB_BD:CB_BD + 256] = rep4(bd)
    c[:, CB_B1L:CB_B1L + 256] = rep4(b1l)
    c[:, CB_B1U:CB_B1U + 256] = rep4(b1l.T)
    c[:, CB_B2L:CB_B2L + 256] = rep4(b2l)
    c[:, CB_ONES:CB_ONES + 128] = 1.0
    c[:, CB_I:CB_I + 64] = np.eye(64)
    return c


def build_phase_b(nchunks=257):
    nc = bass.Bass("TRN2", target_bir_lowering=False)
    TT = 64 * nchunks
    d = {}
    d["xnT"] = nc.dram_tensor("xnT", [1024, TT], BF16, kind="ExternalInput").ap()
    d["wB"] = nc.dram_tensor("wB", [1024, 1544], F32, kind="ExternalInput").ap()
    d["nwB"] = nc.dram_tensor("nwB", [128, 8], F32, kind="ExternalInput").ap()
    d["convw"] = nc.dram_tensor("convw", [128, 8, 4], F32, kind="ExternalInput").ap()
    d["onw"] = nc.dram_tensor("onw", [128, 1], F32, kind="ExternalInput").ap()
    d["alog"] = nc.dram_tensor("alog", [64, 4], F32, kind="ExternalInput").ap()
    d["dtb"] = nc.dram_tensor("dtb", [64, 4], F32, kind="ExternalInput").ap()
    d["cstB"] = nc.dram_tensor("cstB", [64, CB_COLS], F32, kind="ExternalInput").ap()
    d["ogT"] = nc.dram_tensor("ogT", [512, TT - 64], BF16, kind="ExternalOutput").ap()
    import contextlib
    with contextlib.ExitStack() as st:
        P = Prog(nc)
        emit_phase_b(nc, st, P, d, nchunks)
        P.emit()
    return nc


def emit_phase_b(nc, st, P, d, nchunks, xsrc=None, xdeps=(), ogdst=None, after_tile=None):
    sb = lambda name, shape, dt: st.enter_context(nc.sbuf_tensor(name, shape, dt))
    V = lambda fn, r, w: P.add("dve", fn, reads=r, writes=w)
    A = lambda fn, r, w: P.add("act", fn, reads=r, writes=w)
    G = lambda fn, r, w: P.add("pool", fn, reads=r, writes=w)
    T = lambda fn, r, w: P.add("pe", fn, reads=r, writes=w)
    D = lambda fn, r, w: P.add("sp", fn, reads=r, writes=w, dma=True)
    TM = 512
    ident, identf = make_identity(nc, st, P, "b_ident")
    w_sb = sb("b_w", [128, 8, 1544], BF16)
    wst = [sb("b_wst%d" % i, [128, 1544], F32) for i in range(2)]
    nw = sb("b_nw", [128, 8], F32)
    cw = sb("b_cw", [128, 8, 4], F32)
    onw = sb("b_onw", [128, 1], F32)
    negA = sb("b_negA", [64, 4], F32)
    dtb = sb("b_dtb", [64, 4], F32)
    cst = sb("b_cst", [64, CB_COLS], F32)
    ones_bf = sb("b_ones", [128, 128], BF16)
    xt = [sb("b_xt%d" % i, [128, 8, TM], BF16) for i in range(2)]
    u_sb = sb("b_u", [128, 8, TM + 3], F32)
    acc = [sb("b_acc%d" % i, [128, TM], F32) for i in range(2)]
    csil = sb("b_csil", [128, 4, TM], F32)
    ctmp = sb("b_ctmp", [128, TM], F32)
    sq = sb("b_sq", [128, TM], BF16)
    rs = sb("b_rs", [128, TM], F32)
    qkT = sb("b_qkT", [128, 4, TM], BF16)
    vT = sb("b_vT", [128, 4, TM], BF16)
    zs = sb("b_zs", [128, 4, TM], F32)
    o_sb = sb("b_o", [128, 4, TM], F32)
    ogt = sb("b_ogt", [128, 4, TM], BF16)
    S32 = sb("b_S32", [128, 4, 128], F32)
    Sb = sb("b_Sb", [128, 4, 128], BF16)
    ba = sb("b_ba", [64, 8, 8], F32)
    e1 = sb("b_e1", [64, 8, 8], F32)
    lnb = sb("b_lnb", [64, 8, 4], F32)
    beta = sb("b_beta", [64, 8, 4], F32)
    gg = sb("b_g", [64, 8, 4], F32)
    two = lambda name, shape, dt: [sb("%s_%d" % (name, i), shape, dt) for i in range(2)]
    four = lambda name, shape, dt: [sb("%s_%d" % (name, i), shape, dt) for i in range(4)]
    gbb2 = two("b_gbb", [64, 4, 128], BF16)
    lbb2 = two("b_lbb", [64, 4, 64], BF16)
    gc2 = two("b_gc", [64, 4], F32)
    gcl2 = two("b_gcl", [64, 4], F32)
    beg2 = two("b_beg", [64, 4], F32)
    ekl2 = two("b_ekl", [64, 4], F32)
    args2 = two("b_args", [64, 3, 256], F32)
    Eg2 = two("b_Eg", [128, 256], F32)
    Nm2 = two("b_N", [64, 256], F32)
    Mm2 = two("b_M", [64, 256], F32)
    L2 = [[sb("b_L%d_%d" % (i, s_), [64, 256], BF16) for i in range(4)] for s_ in range(2)]
    Uu2 = [[sb("b_U%d_%d" % (i, s_), [64, 256], BF16) for i in range(3)] for s_ in range(2)]
    O12 = two("b_O1", [64, 256], BF16)
    N12 = two("b_N1", [64, 256], BF16)
    O22 = two("b_O2", [64, 256], BF16)
    PU2 = [[sb("b_PU%d_%d" % (i, s_), [64, 256], BF16) for i in range(2)] for s_ in range(2)]
    PL2 = [[sb("b_PL%d_%d" % (i, s_), [64, 256], BF16) for i in range(2)] for s_ in range(2)]
    Y2 = [[sb("b_Y%d_%d" % (i, s_), [64, 256], BF16) for i in range(2)] for s_ in range(2)]
    T32U2 = two("b_T32U", [64, 256], BF16)
    T32L2 = two("b_T32L", [64, 256], BF16)
    kcp2 = two("b_kcp", [128, 2, 64], BF16)
    kbg2 = two("b_kbg", [64, 4, 128], BF16)
    TTb4 = four("b_TTb", [64, 256], BF16)
    wTn4 = four("b_wTn", [128, 256], BF16)
    vb4 = four("b_vb", [64, 4, 128], BF16)
    kst4 = four("b_kst", [64, 4, 128], BF16)
    qg4 = four("b_qg", [128, 256], BF16)
    attnT4 = four("b_attnT", [64, 256], BF16)
    egl4 = four("b_egl", [128, 4], F32)
    vnb = sb("b_vnb", [64, 4, 128], BF16)
    gbf = sb("b_gbf", [64, 8, 4], BF16)
    cstb = sb("b_cstb", [64, 256], BF16)

    psA = st.enter_context(nc.psum_tensor("b_psA", [128, 512], F32))
    psB = st.enter_context(nc.psum_tensor("b_psB", [128, 512], F32))
    psG = st.enter_context(nc.psum_tensor("b_psG", [128, 512], F32))
    psM = st.enter_context(nc.psum_tensor("b_psM", [128, 512], F32))
    psI2 = [st.enter_context(nc.psum_tensor("b_psI%d" % i, [128, 512], F32)) for i in range(2)]
    psT = st.enter_context(nc.psum_tensor("b_psT", [128, 1024], BF16))
    psW = st.enter_context(nc.psum_tensor("b_psW", [128, 512], F32))

    cUb = cstb[:, 0:64]
    cIb = cstb[:, 64:128]
    cOnesb = cstb[:, 128:256]
    cU = cst[:, CB_U:CB_U + 64]
    cI = cst[:, CB_I:CB_I + 64]
    cOnes = cst[:, CB_ONES:CB_ONES + 128]
    c4 = lambda o: cst[:, o:o + 256]

    for name, t_, src in [("b_nw", nw, "nwB"), ("b_cw", cw, "convw"), ("b_onw", onw, "onw"), ("b_negA", negA, "alog"),
                          ("b_dtb", dtb, "dtb"), ("b_cst", cst, "cstB")]:
        D(lambda e, t_=t_, src=src: e.dma_start(out=t_[:], in_=d[src]), [], [name])
    V(lambda e: e.tensor_copy(out=cstb[:, 0:64], in_=cst[:, CB_U:CB_U + 64]), ["b_cst"], ["b_cst"])
    V(lambda e: e.tensor_copy(out=cstb[:, 64:128], in_=cst[:, CB_I:CB_I + 64]), ["b_cst"], ["b_cst"])
    V(lambda e: e.tensor_copy(out=cstb[:, 128:256], in_=cst[:, CB_ONES:CB_ONES + 128]), ["b_cst"], ["b_cst"])
    A(lambda e: e.activation(out=negA[:], in_=negA[:], func=AF.Exp), ["b_negA"], ["b_negA"])
    V(lambda e: e.tensor_scalar(out=negA[:], in0=negA[:], scalar1=-1.0, scalar2=None, op0=ALU.mult), ["b_negA"], ["b_negA"])
    G(lambda e: e.memset(ones_bf[:], 1.0), [], ["b_ones"])
    G(lambda e: e.memset(S32[:], 0.0), [], ["b_S32"])
    G(lambda e: e.memset(Sb[:], 0.0), [], ["b_Sb"])
    G(lambda e: e.memset(u_sb[:], 0.0), [], [("b_u", m_) for m_ in range(8)])
    wv = d["wB"].rearrange("(c p) n -> p c n", p=128)
    for c in range(8):
        s = c % 2
        D(lambda e, c=c, s=s: e.dma_start(out=wst[s][:], in_=wv[:, c, :]), [], [("b_wst", s)])
        if c % 2 == 0:
            V(lambda e, c=c, s=s: e.tensor_scalar(out=w_sb[:, c, :], in0=wst[s][:], scalar1=nw[:, c:c + 1], scalar2=None, op0=ALU.mult),
              [("b_wst", s), "b_nw"], ["b_w"])
        else:
            A(lambda e, c=c, s=s: e.activation(out=w_sb[:, c, :], in_=wst[s][:], func=AF.Copy, scale=nw[:, c:c + 1]),
              [("b_wst", s), "b_nw"], ["b_w"])

    def rsqrt_act(out_ap, in_ap, scale, bias_ln, bias_exp, rkeys, wkey):
        A(lambda e: e.activation(out=out_ap, in_=in_ap, func=AF.Ln, bias=bias_ln, scale=scale), rkeys, [wkey])
        A(lambda e: e.activation(out=out_ap, in_=out_ap, func=AF.Exp, scale=-0.5, bias=bias_exp), [wkey], [wkey])

    def tile(ti, t0, TW, need_o):
        xs = ti % 2
        nck = TW // 64
        if xsrc is None:
            xv = d["xnT"].rearrange("(c p) t -> p c t", p=128)
            D(lambda e: e.dma_start(out=xt[xs][:, :, 0:TW], in_=xv[:, :, t0:t0 + TW]), [], [("b_xt", xs)])
        else:
            D(lambda e: e.dma_start(out=xt[xs][:, :, 0:TW], in_=xsrc(e, t0, TW)), list(xdeps(t0)), [("b_xt", xs)])
        for m in range(8):
            ps = psA if m % 2 == 0 else psB
            pk = "b_psA" if m % 2 == 0 else "b_psB"
            pkw = [pk]
            for c in range(8):
                T(lambda e, c=c, m=m, ps=ps: e.matmul(ps[:, 0:TW], lhsT=w_sb[:, c, m * 128:(m + 1) * 128], rhs=xt[xs][:, c, 0:TW],
                                                      start=(c == 0), stop=(c == 7)), ["b_w", ("b_xt", xs)], pkw)
            A(lambda e, m=m, ps=ps: e.copy(out=u_sb[:, m, 3:3 + TW], in_=ps[:, 0:TW]), [pk], [("b_u", m)])
            ac = acc[m % 2]
            ak = ("b_acc", m % 2)
            A(lambda e, m=m, ac=ac, ps=ps: e.activation(out=ac[:, 0:TW], in_=ps[:, 0:TW], func=AF.Copy, scale=cw[:, m, 3:4]),
              [pk, "b_cw"], [ak])
            for j in (2, 1, 0):
                V(lambda e, m=m, ac=ac, j=j: e.scalar_tensor_tensor(out=ac[:, 0:TW], in0=u_sb[:, m, j:j + TW], scalar=cw[:, m, j:j + 1],
                                                                   in1=ac[:, 0:TW], op0=ALU.mult, op1=ALU.add),
                  [("b_u", m), "b_cw", ak], [ak])
            eng = V
            if m < 4:
                A(lambda e, m=m, ac=ac: e.activation(out=csil[:, m, 0:TW], in_=ac[:, 0:TW], func=AF.Silu), [ak], [("b_csil", m)])
            else:
                A(lambda e, m=m, ac=ac: e.activation(out=vT[:, m - 4, 0:TW], in_=ac[:, 0:TW], func=AF.Silu), [ak], [("b_vT", m - 4)])
            eng(lambda e, m=m: e.tensor_copy(out=u_sb[:, m, 0:3], in_=u_sb[:, m, TW:TW + 3]), [("b_u", m)], [("b_u", m)])
        if DBG_STOP <= 1:
            return
        for m in range(4):
            A(lambda e, m=m: e.activation(out=sq[:, 0:TW], in_=csil[:, m, 0:TW], func=AF.Square), [("b_csil", m)], ["b_sq"])
            T(lambda e: e.matmul(psA[:, 0:TW], lhsT=ones_bf[:], rhs=sq[:, 0:TW], start=True, stop=True), ["b_sq", "b_ones"], ["b_psA"])
            rsqrt_act(rs[:, 0:TW], psA[:, 0:TW], 1.0, EPS, (-0.5 * float(np.log(128.0))) if m < 2 else 0.0, ["b_psA"], "b_rs")
            V(lambda e, m=m: e.tensor_tensor(out=qkT[:, m, 0:TW], in0=csil[:, m, 0:TW], in1=rs[:, 0:TW], op=ALU.mult),
              [("b_csil", m), "b_rs"], [("b_qkT", m)])
        if DBG_STOP <= 2:
            return
        if need_o:
            for h in range(4):
                ps = psA if h % 2 == 0 else psB
                pk = "b_psA" if h % 2 == 0 else "b_psB"
                pkw = [pk]
                for c in range(8):
                    T(lambda e, c=c, h=h, ps=ps: e.matmul(ps[:, 0:TW], lhsT=w_sb[:, c, 1024 + h * 128:1024 + (h + 1) * 128],
                                                          rhs=xt[xs][:, c, 0:TW], start=(c == 0), stop=(c == 7)),
                      ["b_w", ("b_xt", xs)], pkw)
                A(lambda e, h=h, ps=ps: e.activation(out=zs[:, h, 0:TW], in_=ps[:, 0:TW], func=AF.Silu), [pk], [("b_zs", h)])
        if DBG_STOP <= 3:
            return
        bav = psM[0:64, 64:128].rearrange("p (a b) -> p a b", b=8)
        for ck in range(nck):
            for c in range(8):
                T(lambda e, c=c, ck=ck: e.matmul(bav[:, ck, :], lhsT=xt[xs][:, c, ck * 64:(ck + 1) * 64], rhs=w_sb[:, c, 1536:1544],
                                                 start=(c == 0), stop=(c == 7)), ["b_w", ("b_xt", xs)], ["b_psM"])
        V(lambda e: e.tensor_copy(out=ba[:, 0:nck, :], in_=bav[:, 0:nck, :]), ["b_psM"], ["b_ba"])
        V(lambda e: e.tensor_tensor(out=ba[:, 0:nck, 4:8], in0=ba[:, 0:nck, 4:8], in1=dtb[:].unsqueeze(1).to_broadcast([64, nck, 4]), op=ALU.add),
          ["b_ba", "b_dtb"], ["b_ba"])
        A(lambda e: e.activation(out=e1[:, 0:nck, 0:4], in_=ba[:, 0:nck, 0:4], func=AF.Exp, scale=-1.0), ["b_ba"], ["b_e1"])
        A(lambda e: e.activation(out=e1[:, 0:nck, 4:8], in_=ba[:, 0:nck, 4:8], func=AF.Exp), ["b_ba", "b_e1"], ["b_e1"])
        A(lambda e: e.activation(out=e1[:, 0:nck, :], in_=e1[:, 0:nck, :], func=AF.Ln, bias=1.0, scale=1.0), ["b_e1"], ["b_e1"])
        V(lambda e: e.tensor_scalar(out=lnb[:, 0:nck, :], in0=e1[:, 0:nck, 0:4], scalar1=-1.0, scalar2=None, op0=ALU.mult), ["b_e1"], ["b_lnb"])
        A(lambda e: e.activation(out=beta[:, 0:nck, :], in_=lnb[:, 0:nck, :], func=AF.Exp), ["b_lnb"], ["b_beta"])
        V(lambda e: e.tensor_tensor(out=gg[:, 0:nck, :], in0=e1[:, 0:nck, 4:8], in1=negA[:].unsqueeze(1).to_broadcast([64, nck, 4]), op=ALU.mult),
          ["b_e1", "b_negA"], ["b_g"])
        V(lambda e: e.tensor_copy(out=gbf[:, 0:nck, :], in_=gg[:, 0:nck, :]), ["b_g"], ["b_gbf"])
        if DBG_STOP <= 4:
            return
        chunks_of_tile(ti, xs, nck, need_o)
        if need_o:
            for h in range(4):
                A(lambda e, h=h: e.activation(out=sq[:, 0:TW], in_=o_sb[:, h, 0:TW], func=AF.Square), [("b_o", h)], ["b_sq"])
                T(lambda e: e.matmul(psA[:, 0:TW], lhsT=ones_bf[:], rhs=sq[:, 0:TW], start=True, stop=True), ["b_sq", "b_ones"], ["b_psA"])
                rsqrt_act(rs[:, 0:TW], psA[:, 0:TW], 1.0 / 128.0, EPS, 0.0, ["b_psA"], "b_rs")
                V(lambda e, h=h: e.tensor_tensor(out=o_sb[:, h, 0:TW], in0=o_sb[:, h, 0:TW], in1=rs[:, 0:TW], op=ALU.mult),
                  [("b_o", h), "b_rs"], [("b_o", h)])
                V(lambda e, h=h: e.scalar_tensor_tensor(out=ogt[:, h, 0:TW], in0=o_sb[:, h, 0:TW], scalar=onw[:, 0:1], in1=zs[:, h, 0:TW],
                                                        op0=ALU.mult, op1=ALU.mult),
                  [("b_o", h), ("b_zs", h), "b_onw"], [("b_ogt", h)])
            if ogdst is None:
                ov = d["ogT"].rearrange("(h p) t -> p h t", p=128)
                dst = ov[:, :, t0 - 64:t0 - 64 + TW]
            else:
                dst = ogdst(t0 - 64, TW)
            D(lambda e: e.dma_start(out=dst, in_=ogt[:, :, 0:TW]), [("b_ogt", h) for h in range(4)], [("b_ogout", ti)])
            if after_tile is not None:
                after_tile(ti)

    def stage1(ti, xs, ck, need_o, sl, hs):
        c0 = ck * 64
        K = lambda name: (name, sl)
        H = lambda name: (name, hs)
        g_ck = gg[:, ck, :]
        gbb_, lbb_, gc_, gcl_, beg_, ekl_ = gbb2[sl], lbb2[sl], gc2[sl], gcl2[sl], beg2[sl], ekl2[sl]
        args_, Eg_, Nm_, Mm_ = args2[sl], Eg2[sl], Nm2[sl], Mm2[sl]
        L_, Uu_, O1_, N1_, O2_, PU_, PL_, Y_ = L2[sl], Uu2[sl], O12[sl], N12[sl], O22[sl], PU2[sl], PL2[sl], Y2[sl]
        T32U_, T32L_, kcp_, kbg_ = T32U2[sl], T32L2[sl], kcp2[sl], kbg2[sl]
        TTb_, wTn_, vb_, kst_, qg_, attnT_, egl_ = TTb4[hs], wTn4[hs], vb4[hs], kst4[hs], qg4[hs], attnT4[hs], egl4[hs]
        psI_ = psI2[sl]
        PIK = ("b_psI", sl)
        V(lambda e: e.tensor_copy(out=gbb_[:], in_=g_ck.unsqueeze(2).to_broadcast([64, 4, 128])), ["b_g"], [K("gb")])
        yield
        A(lambda e: e.copy(out=lbb_[:], in_=lnb[:, ck, :].unsqueeze(2).to_broadcast([64, 4, 64])), ["b_lnb"], [K("lb")])
        yield
        Gp = psG[:, 0:256]
        GBp = psG[0:64, 256:512]
        for h in range(4):
            T(lambda e, h=h: e.matmul(Gp[:, h * 64:(h + 1) * 64], lhsT=gbb_[:, h, :], rhs=cUb, start=True, stop=True),
              [K("gb"), "b_cst"], ["b_psG"])
        for h in range(4):
            T(lambda e, h=h: e.matmul(GBp[:, h * 64:(h + 1) * 64], lhsT=gbb_[:, h, 0:64], rhs=cUb, start=True, stop=False),
              [K("gb"), "b_cst"], ["b_psG"])
            T(lambda e, h=h: e.matmul(GBp[:, h * 64:(h + 1) * 64], lhsT=lbb_[:, h, :], rhs=cIb, start=False, stop=True),
              [K("lb"), "b_cst"], ["b_psG"])
        gcol = psM[0:64, 0:4]
        glast = psM[:, 4:8]
        T(lambda e: e.matmul(gcol, lhsT=cUb, rhs=gbf[:, ck, :], start=True, stop=True), ["b_gbf", "b_cst"], ["b_psM"])
        T(lambda e: e.matmul(glast, lhsT=cOnesb, rhs=gbf[:, ck, :], start=True, stop=True), ["b_gbf", "b_cst"], ["b_psM"])
        V(lambda e: e.tensor_copy(out=gc_[:], in_=gcol), ["b_psM"], [K("gc")])
        V(lambda e: e.tensor_tensor(out=ekl_[:], in0=glast[0:64, :], in1=gc_[:], op=ALU.subtract), ["b_psM", K("gc")], [K("ekl")])
        A(lambda e: e.activation(out=egl_[:], in_=glast, func=AF.Exp), ["b_psM"], [H("egl")])
        V(lambda e: e.tensor_tensor(out=gcl_[:], in0=gc_[:], in1=lnb[:, ck, :], op=ALU.add), [K("gc"), "b_lnb"], [K("gcl")])
        A(lambda e: e.activation(out=ekl_[:], in_=ekl_[:], func=AF.Exp), [K("ekl")], [K("ekl")])
        A(lambda e: e.activation(out=beg_[:], in_=gcl_[:], func=AF.Exp), [K("gcl")], [K("beg")])
        bc = lambda t_: t_[:].unsqueeze(2).to_broadcast([64, 4, 64])
        a3 = lambda i: args_[:, i, :].rearrange("p (a b) -> p a b", a=4)
        p3 = lambda ap: ap.rearrange("p (a b) -> p a b", a=4)
        V(lambda e: e.tensor_tensor(out=a3(0), in0=p3(Gp[0:64, :]), in1=bc(gc_), op=ALU.subtract), ["b_psG", K("gc")], [K("args0")])
        V(lambda e: e.tensor_tensor(out=a3(1), in0=p3(GBp), in1=bc(gc_), op=ALU.subtract), ["b_psG", K("gc")], [K("args1")])
        V(lambda e: e.tensor_tensor(out=a3(2), in0=p3(Gp[0:64, :]), in1=bc(gcl_), op=ALU.subtract), ["b_psG", K("gcl")], [K("args2")])
        if need_o:
            A(lambda e: e.activation(out=Eg_[:], in_=Gp, func=AF.Exp), ["b_psG"], [K("Eg")])
        G(lambda e: e.tensor_tensor(out=args_[:, 0, :], in0=args_[:, 0, :], in1=c4(CB_MUI), op=ALU.add), [K("args0"), "b_cst"], [K("args0")])
        G(lambda e: e.tensor_tensor(out=args_[:, 1, :], in0=args_[:, 1, :], in1=c4(CB_MUS), op=ALU.add), [K("args1"), "b_cst"], [K("args1")])
        V(lambda e: e.scalar_tensor_tensor(out=args_[:, 2, :], in0=args_[:, 2, :], scalar=-1.0, in1=c4(CB_MLS), op0=ALU.mult, op1=ALU.add),
          [K("args2"), "b_cst"], [K("args2")])
        yield
        A(lambda e: e.activation(out=args_[:], in_=args_[:], func=AF.Exp), [K("args0"), K("args1"), K("args2")],
          [K("args0"), K("args1"), K("args2")])
        yield
        KQ = psM[0:64, 128:384].rearrange("p (a b c) -> p a b c", a=2, b=2)
        A(lambda e: e.copy(out=kcp_[:], in_=qkT[:, 2:4, c0:c0 + 64]), [("b_qkT", 2), ("b_qkT", 3)], [K("kcp")])
        for hk in range(2):
            kch = qkT[:, 2 + hk, c0:c0 + 64]
            T(lambda e, hk=hk, kch=kch: e.matmul(KQ[:, 0, hk, :], lhsT=kch, rhs=kcp_[:, hk, :], start=True, stop=True),
              [("b_qkT", 2 + hk), K("kcp")], ["b_psM"])
            if need_o:
                T(lambda e, hk=hk, kch=kch: e.matmul(KQ[:, 1, hk, :], lhsT=kch, rhs=qkT[:, hk, c0:c0 + 64], start=True, stop=True),
                  [("b_qkT", 2 + hk), ("b_qkT", hk)], ["b_psM"])
        o4 = lambda t_: t_.rearrange("p (a b c) -> p a b c", a=2, b=2)
        EK = [K("args0"), K("args1"), K("args2")]
        for j in range(2):
            V(lambda e, j=j: e.tensor_tensor(out=o4(Nm_[:])[:, :, j, :], in0=KQ[:, 0, :, :], in1=o4(args_[:, 1, :])[:, :, j, :], op=ALU.mult),
              ["b_psM"] + EK, [K("N")])
            V(lambda e, j=j: e.tensor_tensor(out=o4(Mm_[:])[:, :, j, :], in0=KQ[:, 0, :, :], in1=o4(args_[:, 2, :])[:, :, j, :], op=ALU.mult),
              ["b_psM"] + EK, [K("M")])
            if need_o:
                V(lambda e, j=j: e.tensor_tensor(out=o4(attnT_[:])[:, :, j, :], in0=KQ[:, 1, :, :], in1=o4(args_[:, 0, :])[:, :, j, :], op=ALU.mult),
                  ["b_psM"] + EK, [H("attnT")])
        if DBG_STOP <= 7:
            return
        V(lambda e: e.tensor_tensor(out=L_[0][:], in0=Mm_[:], in1=c4(CB_BD), op=ALU.mult), [K("M"), "b_cst"], [K("L0")])
        V(lambda e: e.tensor_tensor(out=Uu_[0][:], in0=Nm_[:], in1=c4(CB_BD), op=ALU.mult), [K("N"), "b_cst"], [K("U0")])
        G(lambda e: e.tensor_tensor(out=O1_[:], in0=Mm_[:], in1=c4(CB_B1L), op=ALU.mult), [K("M"), "b_cst"], [K("O1")])
        G(lambda e: e.tensor_tensor(out=N1_[:], in0=Nm_[:], in1=c4(CB_B1U), op=ALU.mult), [K("N"), "b_cst"], [K("N1")])
        G(lambda e: e.tensor_tensor(out=O2_[:], in0=Mm_[:], in1=c4(CB_B2L), op=ALU.mult), [K("M"), "b_cst"], [K("O2")])
        yield
        I4 = cI.unsqueeze(1).to_broadcast([64, 4, 64])
        V(lambda e: e.tensor_tensor(out=p3(PU_[0][:]), in0=I4, in1=p3(Uu_[0][:]), op=ALU.subtract), ["b_cst", K("U0")], [K("PU0")])
        V(lambda e: e.tensor_tensor(out=p3(PL_[0][:]), in0=I4, in1=p3(L_[0][:]), op=ALU.subtract), ["b_cst", K("L0")], [K("PL0")])
        yield

        def mm4(lhs, lk, rhs, rk):
            pv = psI_[0:64, 0:256]
            for h in range(4):
                T(lambda e, h=h: e.matmul(pv[:, h * 64:(h + 1) * 64], lhsT=lhs[:, h * 64:(h + 1) * 64], rhs=rhs[:, h * 64:(h + 1) * 64],
                                          start=True, stop=True), [lk, rk], [PIK])
            return pv

        pcur = 0
        for k in range(3):
            pv = mm4(Uu_[k], K("U%d" % k), L_[k], K("L%d" % k))
            yield
            A(lambda e, pv=pv, k=k: e.copy(out=L_[k + 1][:], in_=pv), [PIK], [K("L%d" % (k + 1))])
            yield
            pv = mm4(L_[k], K("L%d" % k), Uu_[k], K("U%d" % k))
            yield
            if k < 2:
                V(lambda e, pv=pv, k=k: e.tensor_copy(out=Uu_[k + 1][:], in_=pv), [PIK], [K("U%d" % (k + 1))])
                ulhs, ulk = Uu_[k + 1], K("U%d" % (k + 1))
            else:
                V(lambda e, pv=pv: e.tensor_copy(out=Y_[0][:], in_=pv), [PIK], [K("Y0")])
                ulhs, ulk = Y_[0], K("Y0")
            yield
            nxt = 1 - pcur
            pv = mm4(L_[k + 1], K("L%d" % (k + 1)), PU_[pcur], K("PU%d" % pcur))
            yield
            V(lambda e, pv=pv, pcur=pcur, nxt=nxt: e.tensor_tensor(out=PU_[nxt][:], in0=pv, in1=PU_[pcur][:], op=ALU.add),
              [PIK, K("PU%d" % pcur)], [K("PU%d" % nxt)])
            yield
            pv = mm4(ulhs, ulk, PL_[pcur], K("PL%d" % pcur))
            yield
            V(lambda e, pv=pv, pcur=pcur, nxt=nxt: e.tensor_tensor(out=PL_[nxt][:], in0=pv, in1=PL_[pcur][:], op=ALU.add),
              [PIK, K("PL%d" % pcur)], [K("PL%d" % nxt)])
            yield
            pcur = nxt
        TdU, TdUk, TdL, TdLk = PU_[pcur], K("PU%d" % pcur), PL_[pcur], K("PL%d" % pcur)
        pv = mm4(O1_, K("O1"), TdU, TdUk)
        yield
        A(lambda e, pv=pv: e.copy(out=Y_[0][:], in_=pv), [PIK], [K("Y0")])
        yield
        pv = mm4(TdL, TdLk, Y_[0], K("Y0"))
        yield
        V(lambda e, pv=pv: e.tensor_tensor(out=T32U_[:], in0=TdU[:], in1=pv, op=ALU.subtract), [PIK, TdUk], [K("T32U")])
        yield
        pv = mm4(N1_, K("N1"), TdL, TdLk)
        yield
        A(lambda e, pv=pv: e.copy(out=Y_[1][:], in_=pv), [PIK], [K("Y1")])
        yield
        pv = mm4(TdU, TdUk, Y_[1], K("Y1"))
        yield
        V(lambda e, pv=pv: e.tensor_tensor(out=T32L_[:], in0=TdL[:], in1=pv, op=ALU.subtract), [PIK, TdLk], [K("T32L")])
        yield
        pv = mm4(O2_, K("O2"), T32U_, K("T32U"))
        yield
        A(lambda e, pv=pv: e.copy(out=Y_[0][:], in_=pv), [PIK], [K("Y0")])
        yield
        pv = mm4(T32L_, K("T32L"), Y_[0], K("Y0"))
        yield
        V(lambda e, pv=pv: e.tensor_tensor(out=TTb_[:], in0=T32U_[:], in1=pv, op=ALU.subtract), [PIK, K("T32U")], [H("TTb")])
        yield
        if DBG_STOP <= 8:
            return
        tv = psT[0:64, 0:768].rearrange("p (a b) -> p a b", a=6)
        for hk in range(2):
            T(lambda e, hk=hk: e.transpose(out=tv[:, hk, :], in_=qkT[:, 2 + hk, c0:c0 + 64], identity=ident[:]),
              [("b_qkT", 2 + hk), "b_ident"], ["b_psT"])
        for h in range(4):
            T(lambda e, h=h: e.transpose(out=tv[:, 2 + h, :], in_=vT[:, h, c0:c0 + 64], identity=ident[:]), [("b_vT", h), "b_ident"], ["b_psT"])
        bc128 = lambda ap: ap.unsqueeze(2).to_broadcast([64, 4, 128])
        kpair = tv[:, 0:2, :].unsqueeze(2).to_broadcast([64, 2, 2, 128])
        k4 = lambda t_: t_[:].rearrange("p (a b) c -> p a b c", a=2)
        s4 = lambda ap: ap.rearrange("p (a b) -> p a b", a=2).unsqueeze(3).to_broadcast([64, 2, 2, 128])
        V(lambda e: e.tensor_tensor(out=vb_[:], in0=tv[:, 2:6, :], in1=bc128(beta[:, ck, :]), op=ALU.mult), ["b_psT", "b_beta"], [H("vb")])
        V(lambda e: e.tensor_tensor(out=k4(kbg_), in0=kpair, in1=s4(beg_[:]), op=ALU.mult), ["b_psT", K("beg")], [K("kbg")])
        V(lambda e: e.tensor_tensor(out=k4(kst_), in0=kpair, in1=s4(ekl_[:]), op=ALU.mult), ["b_psT", K("ekl")], [H("kst")])
        wTp = psW[:, 0:256]
        for h in range(4):
            T(lambda e, h=h: e.matmul(wTp[:, h * 64:(h + 1) * 64], lhsT=kbg_[:, h, :], rhs=TTb_[:, h * 64:(h + 1) * 64], start=True, stop=True),
              [K("kbg"), H("TTb")], ["b_psW"])
        A(lambda e: e.activation(out=wTn_[:], in_=wTp, func=AF.Copy, scale=-1.0), ["b_psW"], [H("wTn")])
        if need_o:
            qpair = qkT[:, 0:2, c0:c0 + 64].unsqueeze(2).to_broadcast([128, 2, 2, 64])
            V(lambda e: e.tensor_tensor(out=qg_[:].rearrange("p (a b c) -> p a b c", a=2, b=2),
                                        in0=Eg_[:].rearrange("p (a b c) -> p a b c", a=2, b=2), in1=qpair, op=ALU.mult),
              [("b_qkT", 0), ("b_qkT", 1), K("Eg")], [H("qg")])
            yield

    def stage2(ti, xs, ck, need_o, hs):
        c0 = ck * 64
        H = lambda name: (name, hs)
        TTb_, wTn_, vb_, kst_, qg_, attnT_, egl_ = TTb4[hs], wTn4[hs], vb4[hs], kst4[hs], qg4[hs], attnT4[hs], egl4[hs]
        if DBG_STOP <= 10:
            return
        vp = psA[0:64, :].rearrange("p (a b) -> p a b", a=4)
        for h in range(4):
            T(lambda e, h=h: e.matmul(vp[:, h, :], lhsT=TTb_[:, h * 64:(h + 1) * 64], rhs=vb_[:, h, :], start=True, stop=False),
              [H("TTb"), H("vb")], ["b_psA"])
            T(lambda e, h=h: e.matmul(vp[:, h, :], lhsT=wTn_[:, h * 64:(h + 1) * 64], rhs=Sb[:, h, :], start=False, stop=True),
              [H("wTn"), "b_Sb"], ["b_psA"])
        yield
        A(lambda e: e.copy(out=vnb[:], in_=vp), ["b_psA"], ["b_vnb"])
        yield
        if need_o:
            oTp = psW[:, 256:512]
            for h in range(4):
                T(lambda e, h=h: e.matmul(oTp[:, h * 64:(h + 1) * 64], lhsT=Sb[:, h, :], rhs=qg_[:, h * 64:(h + 1) * 64], start=True, stop=False),
                  ["b_Sb", H("qg")], ["b_psW"])
                T(lambda e, h=h: e.matmul(oTp[:, h * 64:(h + 1) * 64], lhsT=vnb[:, h, :], rhs=attnT_[:, h * 64:(h + 1) * 64], start=False, stop=True),
                  ["b_vnb", H("attnT")], ["b_psW"])
            yield
            A(lambda e: e.copy(out=o_sb[:, :, c0:c0 + 64], in_=oTp.rearrange("p (a b) -> p a b", a=4)), ["b_psW"],
              [("b_o", h) for h in range(4)])
            yield
        sp_ = psB[:].rearrange("p (a b) -> p a b", a=4)
        for h in range(4):
            T(lambda e, h=h: e.matmul(sp_[:, h, :], lhsT=kst_[:, h, :], rhs=vnb[:, h, :], start=True, stop=True), [H("kst"), "b_vnb"], ["b_psB"])
        yield
        for h in range(4):
            V(lambda e, h=h: e.scalar_tensor_tensor(out=S32[:, h, :], in0=S32[:, h, :], scalar=egl_[:, h:h + 1], in1=sp_[:, h, :],
                                                    op0=ALU.mult, op1=ALU.add), ["b_S32", H("egl"), "b_psB"], ["b_S32"])
        yield
        A(lambda e: e.copy(out=Sb[:], in_=S32[:]), ["b_S32"], ["b_Sb"])
        yield

    def run_rr(gens):
        gens = [g for g in gens if g is not None]
        if SEQ_MODE:
            for g in gens:
                for _ in g:
                    pass
            return
        while gens:
            for g in list(gens):
                try:
                    next(g)
                except StopIteration:
                    gens.remove(g)

    def chain(*gs):
        for g in gs:
            yield from g

    hs_ctr = [0]

    def chunks_of_tile(ti, xs, nck, need_o):
        pending = None
        for p0 in range(0, nck, 2):
            cks = list(range(p0, min(p0 + 2, nck)))
            hss = []
            s1 = []
            for i, ck in enumerate(cks):
                hs = hs_ctr[0] % 4
                hs_ctr[0] += 1
                hss.append(hs)
                s1.append(stage1(ti, xs, ck, need_o, i, hs))
            run_rr([pending] + s1)
            pending = chain(*[stage2(ti, xs, ck, need_o, hs) for ck, hs in zip(cks, hss)])
        run_rr([pending])

    tile(0, 0, 64, False)
    ntile = (nchunks - 1) // 8
    assert ntile * 8 + 1 == nchunks
    for ti in range(ntile):
        tile(ti + 1, 64 + ti * TM, TM, True)


def dn_weight_layout(r, dn_norm_w, dn_w_in, dn_conv_w, dn_a_log, dn_dt_bias, dn_o_norm_w):
    w = np.asarray(dn_w_in[0], np.float32)
    qc = list(range(2 * r * 128, (2 * r + 2) * 128))
    kc = [1024 + c for c in qc]
    vc = list(range(2048 + 4 * r * 128, 2048 + (4 * r + 4) * 128))
    zc = list(range(4096 + 4 * r * 128, 4096 + (4 * r + 4) * 128))
    bcol = list(range(6144 + 4 * r, 6144 + 4 * r + 4))
    acol = list(range(6160 + 4 * r, 6160 + 4 * r + 4))
    cols = qc + kc + vc + zc + bcol + acol
    out = {}
    out["wB"] = np.ascontiguousarray(w[:, cols])
    out["nwB"] = np.ascontiguousarray(np.asarray(dn_norm_w[0], np.float32).reshape(8, 128).T)
    cwf = np.asarray(dn_conv_w[0], np.float32)[:, qc + kc + vc]
    out["convw"] = np.ascontiguousarray(cwf.reshape(4, 8, 128).transpose(2, 1, 0))
    out["onw"] = np.ascontiguousarray(np.asarray(dn_o_norm_w[0], np.float32).reshape(128, 1))
    out["alog"] = np.ascontiguousarray(np.broadcast_to(np.asarray(dn_a_log[0], np.float32)[None, 4 * r:4 * r + 4], (64, 4)))
    out["dtb"] = np.ascontiguousarray(np.broadcast_to(np.asarray(dn_dt_bias[0], np.float32)[None, 4 * r:4 * r + 4], (64, 4)))
    out["cstB"] = dn_consts()
    return out


RG8 = [[0, 1, 2, 3, 4, 5, 6, 7]]


def build_fused():
    import contextlib
    nc = bass.Bass("TRN2", target_bir_lowering=False)
    ext = lambda name, shape, dt=F32: nc.dram_tensor(name, shape, dt, kind="ExternalInput").ap()
    dA = {}
    dA["xa"] = ext("xa", [33 * 128, 1024])
    dA["xm"] = ext("xm", [128, 1024])
    dA["w_in"] = ext("w_in", [1024, 2304])
    dA["nw"] = ext("nw", [128, 8])
    dA["wq"] = ext("wq", [128, 128])
    dA["wk"] = ext("wk", [128, 128])
    dA["sinks"] = ext("sinks", [128, 16])
    dA["w_out"] = ext("w_out", [1024, 1024])
    dA["tabs"] = ext("tabs", [128, 4, 2048])
    dA["mtabs"] = ext("mtabs", [16, 3, 2048])
    dB = {}
    dB["wB"] = ext("wB", [1024, 1544])
    dB["nwB"] = ext("nwB", [128, 8])
    dB["convw"] = ext("convw", [128, 8, 4])
    dB["onw"] = ext("onw", [128, 1])
    dB["alog"] = ext("alog", [64, 4])
    dB["dtb"] = ext("dtb", [64, 4])
    dB["cstB"] = ext("cstB", [64, CB_COLS])
    wout = ext("wout", [2048, 1024])
    y = nc.dram_tensor("y", [4096, 1024], F32, kind="ExternalOutput").ap()
    h1_loc = nc.dram_tensor("h1_loc", [4096, 1024], F32).ap()
    xn1T_loc = nc.dram_tensor("xn1T_loc", [1024, 4160], BF16).ap()
    xnT_all = nc.dram_tensor("xnT_all", [8 * 1024, 4160], BF16).ap()
    ogT_loc = [nc.dram_tensor("ogT_loc%d" % j, [512, 4096], BF16).ap() for j in range(4)]
    ogT_all3 = nc.dram_tensor("ogT_all", [4, 8 * 512, 4096], BF16).ap()
    ogT_all = [ogT_all3[j] for j in range(4)]
    xnT_mine = nc.dram_tensor("xnT_mine", [1024, 16448], BF16).ap()
    ogT_mine = nc.dram_tensor("ogT_mine", [2048, 4096], BF16).ap()
    dA["h1"] = h1_loc
    dA["xn1T"] = xn1T_loc
    dB["ogT"] = None
    with contextlib.ExitStack() as outer:
        with contextlib.ExitStack() as st:
            P = Prog(nc, n_dma_sems=16, sem_stack=outer, prefix="A", barrier=True)
            emit_phase_a(nc, st, P, dA, 33)
            P.emit()
        with contextlib.ExitStack() as st:
            P = Prog(nc, n_dma_sems=16, sem_stack=outer, prefix="B", barrier=True)
            P.add("pool", lambda e: e.collective_compute("AllGather", ALU.bypass, replica_groups=RG8, ins=[xn1T_loc], outs=[xnT_all]),
                  writes=["xnT_all"], dma=True, inc=1, own_sem=True)

            xcache = {}

            def seg_copy(e, sg_):
                if "b" not in xcache:
                    xcache["b"] = e.snap(e.partition_id() // 4, min_val=0, max_val=1)
                src = xnT_all.rearrange("(r d) t -> r d t", d=1024)[bass.ds(xcache["b"] * 4 + sg_, 1), :, :]
                src = src.rearrange("o d t -> (o d) t")
                if sg_ == 0:
                    return e.dma_start(out=xnT_mine[:, 0:4160], in_=src)
                return e.dma_start(out=xnT_mine[:, 64 + sg_ * 4096:64 + (sg_ + 1) * 4096], in_=src[:, 64:4160])

            for sg_ in range(4):
                P.add("sp", lambda e, sg_=sg_: seg_copy(e, sg_), reads=["xnT_all"], writes=[("xnT_mine", sg_)], dma=True)

            def xsrc(e, t0, TW):
                return xnT_mine[:, t0:t0 + TW].rearrange("(c p) t -> p c t", p=128)

            xdep_fn = lambda t0: [("xnT_mine", 0 if t0 == 0 else (t0 - 64) // 4096)]
            def ogdst(c0, TW):
                j, lc = c0 // 4096, c0 % 4096
                return ogT_loc[j].rearrange("(h p) t -> p h t", p=128)[:, :, lc:lc + TW]

            def after_tile(ti):
                if ti % 8 == 0:
                    j = ti // 8 - 1
                    P.add("pool", lambda e, j=j: e.collective_compute("AllGather", ALU.bypass, replica_groups=RG8,
                                                                      ins=[ogT_loc[j]], outs=[ogT_all[j]]),
                          reads=[("b_ogout", t_) for t_ in range(ti - 7, ti + 1)], writes=[("ogT_all", j)], dma=True, inc=1, own_sem=True)

            emit_phase_b(nc, st, P, dB, 257, xsrc=xsrc, xdeps=xdep_fn, ogdst=ogdst, after_tile=after_tile)
            P.emit()
        with contextlib.ExitStack() as st:
            P = Prog(nc, n_dma_sems=16, sem_stack=outer, prefix="C", barrier=False)

            ocache = {}

            def og_copy(e, r_):
                if "b" not in ocache:
                    pid = e.partition_id()
                    ocache["b"] = e.snap(pid // 4, min_val=0, max_val=1)
                    ocache["s"] = e.snap(pid % 4, min_val=0, max_val=3)
                v = ogT_all3.rearrange("j (r f) t -> j r f t", f=512)
                src = v[bass.ds(ocache["s"], 1), bass.ds(ocache["b"] * 4 + r_, 1), :, :].rearrange("q o f t -> (q o f) t")
                return e.dma_start(out=ogT_mine[r_ * 512:(r_ + 1) * 512, :], in_=src)

            for r_ in range(4):
                P.add("sp", lambda e, r_=r_: og_copy(e, r_), writes=["ogT_mine"], dma=True)
            emit_phase_c(nc, st, P, ogT_mine, h1_loc, wout, y, 4096, ogdeps=["ogT_mine"])
            P.emit()
    return nc


_NC_CACHE = {}


def _get_nc(name, builder):
    if name not in _NC_CACHE:
        _NC_CACHE[name] = builder()
    return _NC_CACHE[name]


def kernel(x, meta_tokens, attn_norm_w, attn_w_in, attn_q_norm_w, attn_k_norm_w, attn_sinks, attn_w_out,
           dn_norm_w, dn_w_in, dn_conv_w, dn_a_log, dn_dt_bias, dn_o_norm_w, dn_w_out):
    x = np.asarray(x, np.float32)
    meta = np.asarray(meta_tokens, np.float32)
    cores = list(range(8))
    SEG = 4096
    WA = attn_weight_layout(attn_norm_w, attn_w_in, attn_q_norm_w, attn_k_norm_w, attn_sinks, attn_w_out)
    xm = np.zeros((128, 1024), np.float32)
    xm[:16] = meta
    tabs0, tabs1 = attn_tables(True), attn_tables(False)
    wout = np.ascontiguousarray(np.asarray(dn_w_out[0], np.float32))
    WBs = [dn_weight_layout(r, dn_norm_w, dn_w_in, dn_conv_w, dn_a_log, dn_dt_bias, dn_o_norm_w) for r in range(4)]
    maps = []
    for c in cores:
        b, s = c // 4, c % 4
        if s == 0:
            halo = np.concatenate([np.zeros((112, 1024), np.float32), meta], 0)
        else:
            halo = x[b, s * SEG - 128:s * SEG]
        xa = np.ascontiguousarray(np.concatenate([halo, x[b, s * SEG:(s + 1) * SEG]], 0))
        tb, mtb = tabs0 if s == 0 else tabs1
        maps.append(dict(xa=xa, xm=xm, tabs=tb, mtabs=mtb, wout=wout, **WA, **WBs[s]))
    res = run_bass_kernel_spmd(_get_nc("F", build_fused), maps, core_ids=cores).results
    out = np.empty((2, 16384, 1024), np.float32)
    for c in cores:
        b, s = c // 4, c % 4
        out[b, s * SEG:(s + 1) * SEG] = np.asarray(res[c]["y"])
    return out
```

```python
import numpy as np
import ml_dtypes
import concourse.bass as bass
import concourse.mybir as mybir
from concourse.bass_utils import run_bass_kernel_spmd

F32 = mybir.dt.float32
BF16 = mybir.dt.bfloat16
AF = mybir.ActivationFunctionType
ALU = mybir.AluOpType
AX = mybir.AxisListType

NPBF = ml_dtypes.bfloat16


class Prog:
    COMPUTE = ("pe", "act", "dve", "pool")

    def __init__(self, nc, n_dma_sems=32, sem_stack=None, prefix="", barrier=False):
        self.nc = nc
        self.sem_stack = sem_stack
        self.prefix = prefix
        self.barrier = barrier
        self.ops = []
        self.last_w = {}
        self.readers = {}
        self.n_dma_sems = n_dma_sems
        self.n_pool_sems = n_dma_sems
        self.dma_count = 0
        self.dma_last = [None] * n_dma_sems
        self.dma_uses = [0] * n_dma_sems

    def add(self, eng, fn, reads=(), writes=(), dma=False, inc=16, own_sem=False):
        oid = len(self.ops)
        deps = set()
        isps = lambda k: (k if isinstance(k, str) else str(k[0])).startswith("b_ps")
        if any(isps(r) for r in reads):
            writes = list(writes) + [r for r in reads if isps(r)]
            reads = [r for r in reads if not isps(r)]
        for r in reads:
            if r in self.last_w:
                deps.add(self.last_w[r])
        for w in writes:
            if w in self.last_w:
                deps.add(self.last_w[w])
            deps |= self.readers.get(w, set())
        op = dict(eng=eng, fn=fn, deps=deps, dma=dma, signal=False, sigval=None)
        if dma and own_sem:
            k = len(self.dma_last)
            self.dma_last.append(None)
            self.dma_uses.append(0)
        elif dma:
            k = self.dma_count % self.n_pool_sems
            self.dma_count += 1
        if dma:
            if self.dma_last[k] is not None:
                deps.add(self.dma_last[k])
            self.dma_last[k] = oid
            self.dma_uses[k] += inc
            op["dsem"] = k
            op["dinc"] = inc
            op["dval"] = self.dma_uses[k]
        deps.discard(oid)
        self.ops.append(op)
        for r in reads:
            self.readers.setdefault(r, set()).add(oid)
        for w in writes:
            self.last_w[w] = oid
            self.readers[w] = set()
        return oid

    def emit(self):
        nc = self.nc
        ops = self.ops
        for op in ops:
            for d in op["deps"]:
                D = ops[d]
                if D["dma"]:
                    continue
                if D["eng"] == op["eng"] == "pe" and not op["dma"]:
                    continue
                D["signal"] = True
        if self.barrier:
            last = {}
            for i, op in enumerate(ops):
                if not op["dma"]:
                    last[op["eng"]] = i
            for i in last.values():
                ops[i]["signal"] = True
        cnt = {e: 0 for e in self.COMPUTE}
        for op in ops:
            if op["dma"]:
                continue
            if op["signal"]:
                cnt[op["eng"]] += 1
                op["sigval"] = cnt[op["eng"]]
        self.n_dma_sems = len(self.dma_last)
        by_eng = {e: [] for e in ("pe", "act", "dve", "pool", "sp")}
        for op in ops:
            by_eng[op["eng"]].append(op)
        import contextlib
        with contextlib.ExitStack() as st:
            sst = self.sem_stack if self.sem_stack is not None else st
            csem = {e: sst.enter_context(nc.semaphore(self.prefix + "cs_" + e)) for e in self.COMPUTE}
            dsem = [sst.enter_context(nc.semaphore(self.prefix + "ds_%d" % k)) for k in range(self.n_dma_sems)]
            block = st.enter_context(nc.Block())

            def run(engname, eng):
                waited = {}
                for op in by_eng[engname]:
                    targets = []
                    for d in sorted(op["deps"]):
                        D = ops[d]
                        if D["dma"]:
                            targets.append((("d", D["dsem"]), dsem[D["dsem"]], D["dval"]))
                        else:
                            if D["eng"] == engname == "pe" and not op["dma"]:
                                continue
                            targets.append((("c", D["eng"]), csem[D["eng"]], D["sigval"]))
                    best = {}
                    for key, sem, val in targets:
                        if val > waited.get(key, 0) and val > best.get(key, (None, 0))[1]:
                            best[key] = (sem, val)
                    for key, (sem, val) in best.items():
                        eng.wait_ge(sem, val)
                        waited[key] = val
                    ins = op["fn"](eng)
                    if op["dma"]:
                        ins.then_inc(dsem[op["dsem"]], op["dinc"])
                    elif op["signal"]:
                        ins.then_inc(csem[engname], 1)
                if engname == "sp" or self.barrier:
                    for k in range(self.n_dma_sems):
                        if self.dma_uses[k]:
                            eng.wait_ge(dsem[k], self.dma_uses[k])
                if self.barrier:
                    for e2 in self.COMPUTE:
                        if cnt[e2]:
                            eng.wait_ge(csem[e2], cnt[e2])

            @block.tensor
            def _(e):
                run("pe", e)

            @block.scalar
            def _(e):
                run("act", e)

            @block.vector
            def _(e):
                run("dve", e)

            @block.gpsimd
            def _(e):
                run("pool", e)

            @block.sync
            def _(e):
                run("sp", e)


def build_phase_c(ntok=4096):
    nc = bass.Bass("TRN2", target_bir_lowering=False)
    ogT = nc.dram_tensor("ogT", [2048, ntok], BF16, kind="ExternalInput").ap()
    h1 = nc.dram_tensor("h1", [ntok, 1024], F32, kind="ExternalInput").ap()
    wout = nc.dram_tensor("wout", [2048, 1024], F32, kind="ExternalInput").ap()
    y = nc.dram_tensor("y", [ntok, 1024], F32, kind="ExternalOutput").ap()
    import contextlib
    with contextlib.ExitStack() as st:
        P = Prog(nc)
        emit_phase_c(nc, st, P, ogT, h1, wout, y, ntok)
        P.emit()
    return nc


def emit_phase_c(nc, st, P, ogT, h1, wout, y, ntok, ogsrc=None, ogdeps=()):
    TT = 512
    nt = ntok // TT
    w_sb = st.enter_context(nc.sbuf_tensor("c_w", [128, 16, 1024], BF16))
    wst = [st.enter_context(nc.sbuf_tensor("c_wst%d" % i, [128, 2, 1024], F32)) for i in range(2)]
    og_sb = [st.enter_context(nc.sbuf_tensor("c_og%d" % i, [128, 16, TT], BF16)) for i in range(2)]
    h_sb = [st.enter_context(nc.sbuf_tensor("c_h%d" % i, [128, 1024], F32)) for i in range(3)]
    y_sb = [st.enter_context(nc.sbuf_tensor("c_y%d" % i, [128, 1024], F32)) for i in range(3)]
    ps = [st.enter_context(nc.psum_tensor("c_ps%d" % i, [128, 512], F32)) for i in range(4)]
    woutv = wout.rearrange("(c p) n -> p c n", p=128)
    for j in range(8):
        s = j % 2
        P.add("sp", lambda e, j=j, s=s: e.dma_start(out=wst[s][:], in_=woutv[:, 2 * j:2 * j + 2, :]),
              writes=[("c_wst", s)], dma=True)
        eng = "act" if j % 2 == 0 else "dve"
        if eng == "act":
            P.add("act", lambda e, j=j, s=s: e.copy(out=w_sb[:, 2 * j:2 * j + 2, :], in_=wst[s][:]),
                  reads=[("c_wst", s)], writes=[("c_w", j)])
        else:
            P.add("dve", lambda e, j=j, s=s: e.tensor_copy(out=w_sb[:, 2 * j:2 * j + 2, :], in_=wst[s][:]),
                  reads=[("c_wst", s)], writes=[("c_w", j)])
    ogv = ogT.rearrange("(c p) t -> p c t", p=128) if ogsrc is None else None
    blk = 0
    for t in range(nt):
        so = t % 2
        if ogsrc is None:
            P.add("sp", lambda e, t=t, so=so: e.dma_start(out=og_sb[so][:], in_=ogv[:, :, t * TT:(t + 1) * TT]),
                  reads=list(ogdeps), writes=[("c_og", so)], dma=True)
        else:
            P.add("sp", lambda e, t=t, so=so: e.dma_start(out=og_sb[so][:], in_=ogsrc(e, t * TT, TT)),
                  reads=list(ogdeps), writes=[("c_og", so)], dma=True)
        for b in range(TT // 128):
            r0 = t * TT + b * 128
            sh = blk % 3
            P.add("sp", lambda e, r0=r0, sh=sh: e.dma_start(out=h_sb[sh][:], in_=h1[r0:r0 + 128, :]),
                  writes=[("c_h", sh)], dma=True)
            for g in range(2):
                pb = (blk * 2 + g) % 4
                for c in range(16):
                    P.add("pe", lambda e, pb=pb, so=so, b=b, c=c, g=g: e.matmul(
                        ps[pb][:], lhsT=og_sb[so][:, c, b * 128:(b + 1) * 128],
                        rhs=w_sb[:, c, g * 512:(g + 1) * 512], start=(c == 0), stop=(c == 15)),
                        reads=[("c_og", so), ("c_w", c // 2)], writes=[("c_ps", pb)])
                P.add("dve", lambda e, pb=pb, sh=sh, g=g: e.tensor_tensor(
                    out=y_sb[sh][:, g * 512:(g + 1) * 512], in0=ps[pb][:],
                    in1=h_sb[sh][:, g * 512:(g + 1) * 512], op=ALU.add),
                    reads=[("c_ps", pb), ("c_h", sh)], writes=[("c_y", sh, g)])
            P.add("sp", lambda e, r0=r0, sh=sh: e.dma_start(out=y[r0:r0 + 128, :], in_=y_sb[sh][:]),
                  reads=[("c_y", sh, 0), ("c_y", sh, 1)], writes=[("c_yout", blk)], dma=True)
            blk += 1


EPS = 1e-6


def make_identity(nc, st, P, name):
    identf = st.enter_context(nc.sbuf_tensor(name + "_f", [128, 128], F32))
    ident = st.enter_context(nc.sbuf_tensor(name, [128, 128], BF16))
    P.add("pool", lambda e: e.memset(identf[:], 1.0), writes=[name + "_f"])
    P.add("pool", lambda e: e.affine_select(out=identf[:], in_=identf[:], pattern=[[-1, 128]],
                                             compare_op=ALU.is_equal, fill=0.0, base=0,
                                             channel_multiplier=1),
          reads=[name + "_f"], writes=[name + "_f"])
    P.add("dve", lambda e: e.tensor_copy(out=ident[:], in_=identf[:]), reads=[name + "_f"], writes=[name])
    return ident, identf


def build_phase_a(nb=33):
    nc = bass.Bass("TRN2", target_bir_lowering=False)
    d = {}
    d["xa"] = nc.dram_tensor("xa", [nb * 128, 1024], F32, kind="ExternalInput").ap()
    d["xm"] = nc.dram_tensor("xm", [128, 1024], F32, kind="ExternalInput").ap()
    d["w_in"] = nc.dram_tensor("w_in", [1024, 2304], F32, kind="ExternalInput").ap()
    d["nw"] = nc.dram_tensor("nw", [128, 8], F32, kind="ExternalInput").ap()
    d["wq"] = nc.dram_tensor("wq", [128, 128], F32, kind="ExternalInput").ap()
    d["wk"] = nc.dram_tensor("wk", [128, 128], F32, kind="ExternalInput").ap()
    d["sinks"] = nc.dram_tensor("sinks", [128, 16], F32, kind="ExternalInput").ap()
    d["w_out"] = nc.dram_tensor("w_out", [1024, 1024], F32, kind="ExternalInput").ap()
    d["tabs"] = nc.dram_tensor("tabs", [128, 4, 2048], F32, kind="ExternalInput").ap()
    d["mtabs"] = nc.dram_tensor("mtabs", [16, 3, 2048], F32, kind="ExternalInput").ap()
    d["h1"] = nc.dram_tensor("h1", [(nb - 1) * 128, 1024], F32, kind="ExternalOutput").ap()
    d["xn1T"] = nc.dram_tensor("xn1T", [1024, 64 + (nb - 1) * 128], BF16, kind="ExternalOutput").ap()
    import contextlib
    with contextlib.ExitStack() as st:
        P = Prog(nc)
        emit_phase_a(nc, st, P, d, nb)
        P.emit()
    return nc


def emit_phase_a(nc, st, P, d, nb):
    sb = lambda name, shape, dt: st.enter_context(nc.sbuf_tensor(name, shape, dt))
    ident, identf = make_identity(nc, st, P, "a_ident")
    w_sb = sb("a_w", [128, 8, 2304], BF16)
    wo_sb = sb("a_wo", [128, 8, 1024], BF16)
    wst = [sb("a_wst%d" % i, [128, 2304], F32) for i in range(2)]
    nw = sb("a_nw", [128, 8], F32)
    wqk = sb("a_wqk", [128, 128], F32)
    wk_t = sb("a_wk", [128, 128], F32)
    esink = sb("a_esink", [128, 16], F32)
    tabs = sb("a_tabs", [128, 4, 2048], F32)
    mtabs = sb("a_mtabs", [16, 3, 2048], F32)
    x_sb = [sb("a_x%d" % i, [128, 1024], F32) for i in range(2)]
    junk = sb("a_junk", [128, 1024], F32)
    st_sb = sb("a_stat", [128, 8], F32)
    xn = sb("a_xn", [128, 1024], BF16)
    xnT = sb("a_xnT", [128, 8, 128], BF16)
    qss = sb("a_qss", [128, 16], F32)
    qr = sb("a_qr", [128, 16], F32)
    kss = sb("a_kss", [128, 4], F32)
    qn = sb("a_qn", [128, 1024], BF16)
    kn = sb("a_kn", [128, 128], F32)
    ksq = sb("a_ksq", [128, 128], F32)
    kq = sb("a_kq", [128, 128], BF16)
    gs = sb("a_gs", [128, 1024], BF16)
    QT = sb("a_QT", [128, 1024], BF16)
    KT = [sb("a_KT%d" % i, [128, 128], BF16) for i in range(2)]
    KTm = sb("a_KTm", [128, 128], BF16)
    vE = [sb("a_vE%d" % i, [128, 2, 65], BF16) for i in range(2)]
    vEm = sb("a_vEm", [128, 2, 65], BF16)
    E_sb = [sb("a_E%d" % i, [128, 512], F32) for i in range(2)]
    PTp = sb("a_PTp", [128, 16, 128], BF16)
    PTc = sb("a_PTc", [128, 16, 128], BF16)
    PTm = sb("a_PTm", [16, 16, 128], BF16)
    den = sb("a_den", [128, 16], F32)
    rden = sb("a_rden", [128, 16], F32)
    ogp = sb("a_ogp", [128, 1024], F32)
    og = sb("a_og", [128, 1024], BF16)
    ogT = sb("a_ogT", [128, 8, 128], BF16)
    h1 = [sb("a_h1%d" % i, [128, 1024], F32) for i in range(2)]
    xn1 = sb("a_xn1", [128, 1024], BF16)
    xn1T = [sb("a_xn1T%d" % i, [128, 8, 128], BF16) for i in range(2)]

    ps_t = st.enter_context(nc.psum_tensor("a_pst", [128, 8, 128], BF16))
    ps_u = [st.enter_context(nc.psum_tensor("a_psu%d" % i, [128, 512], F32)) for i in range(2)]
    ps_s = [st.enter_context(nc.psum_tensor("a_pss%d" % i, [128, 512], F32)) for i in range(2)]
    ps_o = st.enter_context(nc.psum_tensor("a_pso", [128, 3, 512], F32))

    P.add("sp", lambda e: e.dma_start(out=nw[:], in_=d["nw"]), writes=["a_nw"], dma=True)
    P.add("sp", lambda e: e.dma_start(out=wqk[:], in_=d["wq"]), writes=["a_wqk"], dma=True)
    P.add("sp", lambda e: e.dma_start(out=wk_t[:], in_=d["wk"]), writes=["a_wk"], dma=True)
    P.add("sp", lambda e: e.dma_start(out=esink[:], in_=d["sinks"]), writes=["a_esink"], dma=True)
    P.add("dve", lambda e: e.scalar_tensor_tensor(out=wqk[:], in0=wqk[:], scalar=8.0, in1=wk_t[:],
                                                  op0=ALU.mult, op1=ALU.mult),
          reads=["a_wqk", "a_wk"], writes=["a_wqk"])
    P.add("act", lambda e: e.activation(out=esink[:], in_=esink[:], func=AF.Exp), reads=["a_esink"], writes=["a_esink"])
    for i in range(2):
        P.add("pool", lambda e, i=i: e.memset(vE[i][:], 1.0), writes=[("a_vE", i)])
    P.add("pool", lambda e: e.memset(vEm[:], 1.0), writes=["a_vEm"])
    w_inv = d["w_in"].rearrange("(c p) n -> p c n", p=128)
    for c in range(8):
        s = c % 2
        P.add("sp", lambda e, c=c, s=s: e.dma_start(out=wst[s][:], in_=w_inv[:, c, :]), writes=[("a_wst", s)], dma=True)
        if c % 2 == 0:
            P.add("dve", lambda e, c=c, s=s: e.tensor_scalar(out=w_sb[:, c, :], in0=wst[s][:], scalar1=nw[:, c:c + 1],
                                                           scalar2=None, op0=ALU.mult),
                  reads=[("a_wst", s), "a_nw"], writes=[("a_w", c)])
        else:
            P.add("act", lambda e, c=c, s=s: e.activation(out=w_sb[:, c, :], in_=wst[s][:], func=AF.Copy, scale=nw[:, c:c + 1]),
                  reads=[("a_wst", s), "a_nw"], writes=[("a_w", c)])
    w_outv = d["w_out"].rearrange("(c p) n -> p c n", p=128)
    for c in range(4):
        s = c % 2
        P.add("sp", lambda e, c=c, s=s: e.dma_start(out=wst[s][:, 0:2048].rearrange("p (a n) -> p a n", a=2),
                                                   in_=w_outv[:, 2 * c:2 * c + 2, :]), writes=[("a_wst", s)], dma=True)
        P.add("dve" if c % 2 == 0 else "pool",
              lambda e, c=c, s=s: e.tensor_copy(out=wo_sb[:, 2 * c:2 * c + 2, :],
                                                in_=wst[s][:, 0:2048].rearrange("p (a n) -> p a n", a=2)),
              reads=[("a_wst", s)], writes=[("a_wo", c)])
    for j in range(4):
        P.add("sp", lambda e, j=j: e.dma_start(out=tabs[:, j, :], in_=d["tabs"][:, j, :]), writes=[("a_tabs", j)], dma=True)
    P.add("sp", lambda e: e.dma_start(out=mtabs[:], in_=d["mtabs"]), writes=["a_mtabs"], dma=True)
    W_ALL = [("a_w", c) for c in range(8)]
    WO_ALL = [("a_wo", c) for c in range(4)]

    def rstd_from_ss(ss_ap, ln_ap, out_ap, bias, key_in, key_out):
        P.add("act", lambda e: e.activation(out=ln_ap, in_=ss_ap, func=AF.Ln, bias=bias, scale=1.0),
              reads=[key_in], writes=[key_out + "_ln"])
        P.add("act", lambda e: e.activation(out=out_ap, in_=ln_ap, func=AF.Exp, scale=-0.5),
              reads=[key_out + "_ln"], writes=[key_out])

    def transposes8(src, src_key, dstT, dst_key, evac_eng):
        for c in range(8):
            P.add("pe", lambda e, c=c: e.transpose(out=ps_t[:, c, :], in_=src[:, c * 128:(c + 1) * 128], identity=ident[:]),
                  reads=[src_key, "a_ident"], writes=["a_pst"])
        if evac_eng == "act":
            P.add("act", lambda e: e.copy(out=dstT, in_=ps_t[:]), reads=["a_pst"], writes=[dst_key])
        else:
            P.add(evac_eng, lambda e: e.tensor_copy(out=dstT, in_=ps_t[:]), reads=["a_pst"], writes=[dst_key])

    def front(src_ap, xs, KT_dst, KT_key, vE_dst, vE_key, full):
        P.add("sp", lambda e: e.dma_start(out=x_sb[xs][:], in_=src_ap), writes=[("a_x", xs)], dma=True)
        P.add("act", lambda e: e.activation(out=junk[:], in_=x_sb[xs][:], func=AF.Square, accum_out=st_sb[:, 0:1]),
              reads=[("a_x", xs)], writes=["a_junk", "a_ss"])
        rstd_from_ss(st_sb[:, 0:1], st_sb[:, 1:2], st_sb[:, 2:3], 1024 * EPS, "a_ss", "a_rstd")
        P.add("dve", lambda e: e.tensor_scalar(out=xn[:], in0=x_sb[xs][:], scalar1=st_sb[:, 2:3], scalar2=32.0,
                                               op0=ALU.mult, op1=ALU.mult),
              reads=[("a_x", xs), "a_rstd"], writes=["a_xn"])
        transposes8(xn, "a_xn", xnT[:], "a_xnT", "act")
        groups = [(0, 512), (512, 512), (1024, 256), (1280, 512), (1792, 512)] if full else [(1024, 256)]
        for gi, (c0, cw) in enumerate(groups):
            pb = gi % 2
            for c in range(8):
                P.add("pe", lambda e, c=c, c0=c0, cw=cw, pb=pb: e.matmul(
                    ps_u[pb][:, 0:cw], lhsT=xnT[:, c, :], rhs=w_sb[:, c, c0:c0 + cw], start=(c == 0), stop=(c == 7)),
                    reads=["a_xnT"] + W_ALL, writes=[("a_psu", pb)])
            if c0 < 1024:
                h0 = c0 // 64
                P.add("act", lambda e, pb=pb: e.activation(out=junk[:, 0:512], in_=ps_u[pb][:], func=AF.Square),
                      reads=[("a_psu", pb)], writes=["a_junk"])
                P.add("dve", lambda e, h0=h0: e.tensor_reduce(out=qss[:, h0:h0 + 8], in_=junk[:, 0:512].rearrange("p (a b) -> p a b", a=8),
                                                            axis=AX.X, op=ALU.add),
                      reads=["a_junk"], writes=[("a_qss", h0)])
                rstd_from_ss(qss[:, h0:h0 + 8], qr[:, h0:h0 + 8], qr[:, h0:h0 + 8], 64 * EPS, ("a_qss", h0), "a_qr%d" % h0)
                P.add("dve", lambda e, pb=pb, h0=h0, c0=c0: e.tensor_tensor(
                    out=qn[:, c0:c0 + 512].rearrange("p (a b) -> p a b", a=8),
                    in0=ps_u[pb][:].rearrange("p (a b) -> p a b", a=8),
                    in1=qr[:, h0:h0 + 8].unsqueeze(2).to_broadcast([128, 8, 64]), op=ALU.mult),
                    reads=[("a_psu", pb), "a_qr%d" % h0], writes=[("a_qn", h0)])
            elif c0 == 1024:
                P.add("act", lambda e, pb=pb: e.activation(out=ksq[:], in_=ps_u[pb][:, 0:128], func=AF.Square),
                      reads=[("a_psu", pb)], writes=["a_junk2"])
                P.add("dve", lambda e: e.tensor_reduce(out=kss[:, 0:2], in_=ksq[:].rearrange("p (a b) -> p a b", a=2),
                                                       axis=AX.X, op=ALU.add),
                      reads=["a_junk2"], writes=["a_kss"])
                rstd_from_ss(kss[:, 0:2], kss[:, 2:4], kss[:, 2:4], 64 * EPS, "a_kss", "a_kr")
                P.add("dve", lambda e, pb=pb: e.tensor_tensor(
                    out=kn[:].rearrange("p (a b) -> p a b", a=2), in0=ps_u[pb][:, 0:128].rearrange("p (a b) -> p a b", a=2),
                    in1=kss[:, 2:4].unsqueeze(2).to_broadcast([128, 2, 64]), op=ALU.mult),
                    reads=[("a_psu", pb), "a_kr"], writes=["a_kn"])
                P.add("dve", lambda e: e.tensor_tensor(out=kq[:], in0=kn[:], in1=wqk[:], op=ALU.mult),
                      reads=["a_kn", "a_wqk"], writes=["a_kq"])
                P.add("act", lambda e, pb=pb: e.copy(out=vE_dst[:, :, 0:64], in_=ps_u[pb][:, 128:256].rearrange("p (a b) -> p a b", a=2)),
                      reads=[("a_psu", pb)], writes=[vE_key])
            else:
                g0 = c0 - 1280
                P.add("act", lambda e, pb=pb, g0=g0: e.activation(out=gs[:, g0:g0 + 512], in_=ps_u[pb][:], func=AF.Silu),
                      reads=[("a_psu", pb)], writes=[("a_gs", g0)])
        if full:
            for c in range(8):
                P.add("pe", lambda e, c=c: e.transpose(out=ps_t[:, c, :], in_=qn[:, c * 128:(c + 1) * 128], identity=ident[:]),
                      reads=[("a_qn", 0), ("a_qn", 8), "a_ident"], writes=["a_pst"])
            P.add("dve", lambda e: e.tensor_copy(out=QT[:].rearrange("p (a b) -> p a b", a=8), in_=ps_t[:]),
                  reads=["a_pst"], writes=["a_QT"])
        P.add("pe", lambda e: e.transpose(out=ps_t[:, 0, :], in_=kq[:], identity=ident[:]),
              reads=["a_kq", "a_ident"], writes=["a_pst"])
        P.add("act", lambda e: e.copy(out=KT_dst[:], in_=ps_t[:, 0, :]), reads=["a_pst"], writes=[KT_key])

    front(d["xm"], 0, KTm, "a_KTm", vEm, "a_vEm", full=False)

    for i in range(nb):
        cur = i % 2
        prv = 1 - cur
        front(d["xa"][i * 128:(i + 1) * 128, :], i % 2, KT[cur], ("a_KT", cur), vE[cur], ("a_vE", cur), full=True)
        if i == 0:
            chunks = [("cur", 0), ("meta", 0)]
        elif i == 1:
            chunks = [("prev", 1), ("cur", 3), ("meta", 1)]
        else:
            chunks = [("prev", 2), ("cur", 3), ("meta", 2)]
        n_e = 0
        for kind, tix in chunks:
            for g in range(2):
                for hf in range(2):
                    pb = n_e % 2
                    hh = g * 8 + hf * 4
                    if kind == "meta":
                        lhsT, lk, M = KTm[g * 64:(g + 1) * 64, 0:16], "a_KTm", 16
                    elif kind == "cur":
                        lhsT, lk, M = KT[cur][g * 64:(g + 1) * 64, :], ("a_KT", cur), 128
                    else:
                        lhsT, lk, M = KT[prv][g * 64:(g + 1) * 64, :], ("a_KT", prv), 128
                    P.add("pe", lambda e, pb=pb, lhsT=lhsT, g=g, hf=hf, M=M: e.matmul(
                        ps_s[pb][0:M, :], lhsT=lhsT, rhs=QT[g * 64:(g + 1) * 64, hf * 512:(hf + 1) * 512], start=True, stop=True),
                        reads=[lk, "a_QT"], writes=[("a_pss", pb)])
                    P.add("act", lambda e, pb=pb, M=M: e.activation(out=E_sb[pb][0:M, :], in_=ps_s[pb][0:M, :], func=AF.Exp),
                          reads=[("a_pss", pb)], writes=[("a_E", pb)])
                    if kind == "meta":
                        tab = mtabs[:, tix, hh * 128:(hh + 4) * 128]
                        dst = PTm[:, hh:hh + 4, :]
                        dk = ("a_PTm", hh)
                        tk = "a_mtabs"
                    else:
                        tab = tabs[:, tix, hh * 128:(hh + 4) * 128]
                        dst = (PTc if kind == "cur" else PTp)[:, hh:hh + 4, :]
                        dk = ("a_PTc" if kind == "cur" else "a_PTp", hh)
                        tk = ("a_tabs", tix)
                    P.add("dve" if n_e % 2 == 0 else "pool",
                          lambda e, pb=pb, M=M, tab=tab, dst=dst: e.tensor_tensor(
                              out=dst.rearrange("p a b -> p (a b)"), in0=E_sb[pb][0:M, :], in1=tab, op=ALU.mult),
                          reads=[("a_E", pb), tk], writes=[dk])
                    n_e += 1
        for h in range(16):
            g = h // 8
            hh = (h // 4) * 4
            bank, off = h // 7, (h % 7) * 65
            for ci, (kind, tix) in enumerate(chunks):
                if kind == "meta":
                    lhsT, lk = PTm[:, h, :], ("a_PTm", hh)
                    rhs, rk = vEm[0:16, g, :], "a_vEm"
                elif kind == "cur":
                    lhsT, lk = PTc[:, h, :], ("a_PTc", hh)
                    rhs, rk = vE[cur][:, g, :], ("a_vE", cur)
                else:
                    lhsT, lk = PTp[:, h, :], ("a_PTp", hh)
                    rhs, rk = vE[prv][:, g, :], ("a_vE", prv)
                P.add("pe", lambda e, bank=bank, off=off, lhsT=lhsT, rhs=rhs, ci=ci, n=len(chunks): e.matmul(
                    ps_o[:, bank, off:off + 65], lhsT=lhsT, rhs=rhs, start=(ci == 0), stop=(ci == n - 1)),
                    reads=[lk, rk], writes=[("a_pso", bank)])
        for bank, (h0, nh) in enumerate([(0, 7), (7, 7), (14, 2)]):
            ov = ps_o[:, bank, 0:nh * 65].rearrange("p (a b) -> p a b", b=65)
            P.add("dve", lambda e, ov=ov, h0=h0, nh=nh: e.tensor_tensor(
                out=den[:, h0:h0 + nh].unsqueeze(2), in0=ov[:, :, 64:65], in1=esink[:, h0:h0 + nh].unsqueeze(2), op=ALU.add),
                reads=[("a_pso", bank), "a_esink"], writes=[("a_den", bank)])
            P.add("dve", lambda e, h0=h0, nh=nh: e.reciprocal(out=rden[:, h0:h0 + nh], in_=den[:, h0:h0 + nh]),
                  reads=[("a_den", bank)], writes=[("a_rden", bank)])
            P.add("dve", lambda e, ov=ov, h0=h0, nh=nh: e.tensor_tensor(
                out=ogp[:, h0 * 64:(h0 + nh) * 64].rearrange("p (a b) -> p a b", b=64), in0=ov[:, :, 0:64],
                in1=rden[:, h0:h0 + nh].unsqueeze(2).to_broadcast([128, nh, 64]), op=ALU.mult),
                reads=[("a_pso", bank), ("a_rden", bank)], writes=[("a_ogp", bank)])
        P.add("pool", lambda e: e.tensor_tensor(out=og[:], in0=ogp[:], in1=gs[:], op=ALU.mult),
              reads=[("a_ogp", 0), ("a_ogp", 1), ("a_ogp", 2), ("a_gs", 0), ("a_gs", 512)], writes=["a_og"])
        transposes8(og, "a_og", ogT[:], "a_ogT", "act")
        hs = i % 2
        for g2 in range(2):
            pb = g2
            for c in range(8):
                P.add("pe", lambda e, c=c, g2=g2, pb=pb: e.matmul(
                    ps_u[pb][:], lhsT=ogT[:, c, :], rhs=wo_sb[:, c, g2 * 512:(g2 + 1) * 512], start=(c == 0), stop=(c == 7)),
                    reads=["a_ogT"] + WO_ALL, writes=[("a_psu", pb)])
            P.add("dve", lambda e, g2=g2, pb=pb, hs=hs, xs=i % 2: e.tensor_tensor(
                out=h1[hs][:, g2 * 512:(g2 + 1) * 512], in0=ps_u[pb][:], in1=x_sb[xs][:, g2 * 512:(g2 + 1) * 512], op=ALU.add),
                reads=[("a_psu", pb), ("a_x", i % 2)], writes=[("a_h1", hs, g2)])
        H1K = [("a_h1", hs, 0), ("a_h1", hs, 1)]
        if i >= 1:
            P.add("sp", lambda e, i=i, hs=hs: e.dma_start(out=d["h1"][(i - 1) * 128:i * 128, :], in_=h1[hs][:]),
                  reads=H1K, writes=[("a_h1out", i)], dma=True)
        P.add("act", lambda e, hs=hs: e.activation(out=junk[:], in_=h1[hs][:], func=AF.Square, accum_out=st_sb[:, 4:5]),
              reads=H1K, writes=["a_junk", "a_ss1"])
        rstd_from_ss(st_sb[:, 4:5], st_sb[:, 5:6], st_sb[:, 6:7], 1024 * EPS, "a_ss1", "a_rstd1")
        P.add("dve", lambda e, hs=hs: e.tensor_scalar(out=xn1[:], in0=h1[hs][:], scalar1=st_sb[:, 6:7], scalar2=32.0,
                                                      op0=ALU.mult, op1=ALU.mult),
              reads=H1K + ["a_rstd1"], writes=["a_xn1"])
        transposes8(xn1, "a_xn1", xn1T[hs][:], ("a_xn1T", hs), "act")
        xv = d["xn1T"].rearrange("(c p) t -> p c t", p=128)
        if i == 0:
            P.add("sp", lambda e, hs=hs: e.dma_start(out=xv[:, :, 0:64], in_=xn1T[hs][:, :, 64:128]),
                  reads=[("a_xn1T", hs)], writes=[("a_xn1out", i)], dma=True)
        else:
            P.add("sp", lambda e, hs=hs, i=i: e.dma_start(out=xv[:, :, 64 + (i - 1) * 128:64 + i * 128], in_=xn1T[hs][:]),
                  reads=[("a_xn1T", hs)], writes=[("a_xn1out", i)], dma=True)


def attn_tables(seg0):
    m = np.exp2(-8.0 * np.arange(1, 17) / 16.0)[None, :, None]
    j = np.arange(128)[:, None, None].astype(np.float64)
    i = np.arange(128)[None, None, :].astype(np.float64)
    t_prev = np.where(j > i, np.exp(-m * (128 + i - j)), 0.0)
    t_cur = np.where(j <= i, np.exp(-m * (i - j)), 0.0)
    jm = np.arange(16)[:, None, None].astype(np.float64)
    t_meta = np.exp(-m * 128.0) * np.ones((16, 16, 128))
    if seg0:
        t_cur0 = np.zeros_like(t_cur)
        t_prev1 = np.zeros_like(t_prev)
        pq = i - 112.0
        t_meta0 = np.where(pq >= jm, np.exp(-m * np.maximum(pq - jm, 0.0)), 0.0) * np.ones((16, 16, 128))
        t_meta1 = np.exp(-m * np.minimum(16.0 + i - jm, 128.0))
    else:
        t_cur0, t_prev1, t_meta0, t_meta1 = t_cur, t_prev, t_meta, t_meta
    tabs = np.stack([t_cur0, t_prev1, t_prev, t_cur], axis=1).reshape(128, 4, 2048).astype(np.float32)
    mtabs = np.stack([t_meta0, t_meta1, t_meta], axis=1).reshape(16, 3, 2048).astype(np.float32)
    return np.ascontiguousarray(tabs), np.ascontiguousarray(mtabs)


def attn_weight_layout(attn_norm_w, attn_w_in, attn_q_norm_w, attn_k_norm_w, attn_sinks, attn_w_out):
    w_in = np.asarray(attn_w_in[0], dtype=np.float32)
    perm = []
    for c in range(8):
        for half in range(2):
            h = c + 8 * half
            perm.extend(range(h * 64, (h + 1) * 64))
    perm = np.array(perm + list(range(1024, 2304)))
    out = {}
    out["w_in"] = np.ascontiguousarray(w_in[:, perm])
    out["nw"] = np.ascontiguousarray(np.asarray(attn_norm_w[0], np.float32).reshape(8, 128).T)
    out["wq"] = np.ascontiguousarray(np.broadcast_to(np.tile(np.asarray(attn_q_norm_w[0], np.float32), 2)[None, :], (128, 128)))
    out["wk"] = np.ascontiguousarray(np.broadcast_to(np.tile(np.asarray(attn_k_norm_w[0], np.float32), 2)[None, :], (128, 128)))
    out["sinks"] = np.ascontiguousarray(np.broadcast_to(np.asarray(attn_sinks[0], np.float32)[None, :], (128, 16)))
    out["w_out"] = np.ascontiguousarray(np.asarray(attn_w_out[0], np.float32))
    return out


NEG = -30000.0
DBG_STOP = 99
SEQ_MODE = False
CB_U, CB_MUI, CB_MUS, CB_MLS, CB_BD, CB_B1L, CB_B1U, CB_B2L, CB_ONES, CB_I = 0, 64, 320, 576, 832, 1088, 1344, 1600, 1856, 1984
CB_COLS = 2048


def dn_consts():
    a = np.arange(64)[:, None]
    b = np.arange(64)[None, :]
    rep4 = lambda m: np.tile(m[:, None, :], (1, 4, 1)).reshape(64, 256)
    U = (a <= b).astype(np.float32)
    mui = np.where(a <= b, 0.0, NEG)
    mus = np.where(a < b, 0.0, NEG)
    mls = np.where(b < a, 0.0, NEG)
    bd = (a // 16 == b // 16).astype(np.float32)
    b1l = ((a // 32 == b // 32) & (a // 16 == b // 16 + 1)).astype(np.float32)
    b2l = ((a >= 32) & (b < 32)).astype(np.float32)
    c = np.zeros((64, CB_COLS), np.float32)
    c[:, CB_U:CB_U + 64] = U
    c[:, CB_MUI:CB_MUI + 256] = rep4(mui)
    c[:, CB_MUS:CB_MUS + 256] = rep4(mus)
    c[:, CB_MLS:CB_MLS + 256] = rep4(mls)
    c[:, CB_BD:CB_BD + 256] = rep4(bd)
    c[:, CB_B1L:CB_B1L + 256] = rep4(b1l)
    c[:, CB_B1U:CB_B1U + 256] = rep4(b1l.T)
    c[:, CB_B2L:CB_B2L + 256] = rep4(b2l)
    c[:, CB_ONES:CB_ONES + 128] = 1.0
    c[:, CB_I:CB_I + 64] = np.eye(64)
    return c


def build_phase_b(nchunks=257):
    nc = bass.Bass("TRN2", target_bir_lowering=False)
    TT = 64 * nchunks
    d = {}
    d["xnT"] = nc.dram_tensor("xnT", [1024, TT], BF16, kind="ExternalInput").ap()
    d["wB"] = nc.dram_tensor("wB", [1024, 1544], F32, kind="ExternalInput").ap()
    d["nwB"] = nc.dram_tensor("nwB", [128, 8], F32, kind="ExternalInput").ap()
    d["convw"] = nc.dram_tensor("convw", [128, 8, 4], F32, kind="ExternalInput").ap()
    d["onw"] = nc.dram_tensor("onw", [128, 1], F32, kind="ExternalInput").ap()
    d["alog"] = nc.dram_tensor("alog", [64, 4], F32, kind="ExternalInput").ap()
    d["dtb"] = nc.dram_tensor("dtb", [64, 4], F32, kind="ExternalInput").ap()
    d["cstB"] = nc.dram_tensor("cstB", [64, CB_COLS], F32, kind="ExternalInput").ap()
    d["ogT"] = nc.dram_tensor("ogT", [512, TT - 64], BF16, kind="ExternalOutput").ap()
    import contextlib
    with contextlib.ExitStack() as st:
        P = Prog(nc)
        emit_phase_b(nc, st, P, d, nchunks)
        P.emit()
    return nc


def emit_phase_b(nc, st, P, d, nchunks, xsrc=None, xdeps=(), ogdst=None, after_tile=None):
    sb = lambda name, shape, dt: st.enter_context(nc.sbuf_tensor(name, shape, dt))
    V = lambda fn, r, w: P.add("dve", fn, reads=r, writes=w)
    A = lambda fn, r, w: P.add("act", fn, reads=r, writes=w)
    G = lambda fn, r, w: P.add("pool", fn, reads=r, writes=w)
    T = lambda fn, r, w: P.add("pe", fn, reads=r, writes=w)
    D = lambda fn, r, w: P.add("sp", fn, reads=r, writes=w, dma=True)
    TM = 512
    ident, identf = make_identity(nc, st, P, "b_ident")
    w_sb = sb("b_w", [128, 8, 1544], BF16)
    wst = [sb("b_wst%d" % i, [128, 1544], F32) for i in range(2)]
    nw = sb("b_nw", [128, 8], F32)
    cw = sb("b_cw", [128, 8, 4], F32)
    onw = sb("b_onw", [128, 1], F32)
    negA = sb("b_negA", [64, 4], F32)
    dtb = sb("b_dtb", [64, 4], F32)
    cst = sb("b_cst", [64, CB_COLS], F32)
    ones_bf = sb("b_ones", [128, 128], BF16)
    xt = [sb("b_xt%d" % i, [128, 8, TM], BF16) for i in range(2)]
    u_sb = sb("b_u", [128, 8, TM + 3], F32)
    acc = [sb("b_acc%d" % i, [128, TM], F32) for i in range(2)]
    csil = sb("b_csil", [128, 4, TM], F32)
    ctmp = sb("b_ctmp", [128, TM], F32)
    sq = sb("b_sq", [128, TM], BF16)
    rs = sb("b_rs", [128, TM], F32)
    qkT = sb("b_qkT", [128, 4, TM], BF16)
    vT = sb("b_vT", [128, 4, TM], BF16)
    zs2 = [sb("b_zs%d" % i, [128, 4, TM], F32) for i in range(2)]
    o2 = [sb("b_o%d" % i, [128, 4, TM], F32) for i in range(2)]
    ogt2 = [sb("b_ogt%d" % i, [128, 4, TM], BF16) for i in range(2)]
    S32 = sb("b_S32", [128, 4, 128], F32)
    Sb = sb("b_Sb", [128, 4, 128], BF16)
    ba = sb("b_ba", [64, 8, 8], F32)
    e1 = sb("b_e1", [64, 8, 8], F32)
    lnb = sb("b_lnb", [64, 8, 4], F32)
    beta = sb("b_beta", [64, 8, 4], F32)
    gg = sb("b_g", [64, 8, 4], F32)
    two = lambda name, shape, dt: [sb("%s_%d" % (name, i), shape, dt) for i in range(2)]
    four = lambda name, shape, dt: [sb("%s_%d" % (name, i), shape, dt) for i in range(4)]
    gbb2 = two("b_gbb", [64, 4, 128], BF16)
    lbb2 = two("b_lbb", [64, 4, 64], BF16)
    gc2 = two("b_gc", [64, 4], F32)
    gcl2 = two("b_gcl", [64, 4], F32)
    beg2 = two("b_beg", [64, 4], F32)
    ekl2 = two("b_ekl", [64, 4], F32)
    args2 = two("b_args", [64, 3, 256], F32)
    Eg2 = two("b_Eg", [128, 256], F32)
    Nm2 = two("b_N", [64, 256], F32)
    Mm2 = two("b_M", [64, 256], F32)
    L2 = [[sb("b_L%d_%d" % (i, s_), [64, 256], BF16) for i in range(4)] for s_ in range(2)]
    Uu2 = [[sb("b_U%d_%d" % (i, s_), [64, 256], BF16) for i in range(3)] for s_ in range(2)]
    O12 = two("b_O1", [64, 256], BF16)
    N12 = two("b_N1", [64, 256], BF16)
    O22 = two("b_O2", [64, 256], BF16)
    PU2 = [[sb("b_PU%d_%d" % (i, s_), [64, 256], BF16) for i in range(2)] for s_ in range(2)]
    PL2 = [[sb("b_PL%d_%d" % (i, s_), [64, 256], BF16) for i in range(2)] for s_ in range(2)]
    Y2 = [[sb("b_Y%d_%d" % (i, s_), [64, 256], BF16) for i in range(2)] for s_ in range(2)]
    T32U2 = two("b_T32U", [64, 256], BF16)
    T32L2 = two("b_T32L", [64, 256], BF16)
    kcp2 = two("b_kcp", [128, 2, 64], BF16)
    kbg2 = two("b_kbg", [64, 4, 128], BF16)
    TTb4 = four("b_TTb", [64, 256], BF16)
    wTn4 = four("b_wTn", [128, 256], BF16)
    vb4 = four("b_vb", [64, 4, 128], BF16)
    kst4 = four("b_kst", [64, 4, 128], BF16)
    qg4 = four("b_qg", [128, 256], BF16)
    attnT4 = four("b_attnT", [64, 256], BF16)
    egl4 = four("b_egl", [128, 4], F32)
    vnb = sb("b_vnb", [64, 4, 128], BF16)
    gbf = sb("b_gbf", [64, 8, 4], BF16)
    cstb = sb("b_cstb", [64, 256], BF16)

    psA = st.enter_context(nc.psum_tensor("b_psA", [128, 512], F32))
    psB = st.enter_context(nc.psum_tensor("b_psB", [128, 512], F32))
    psG = st.enter_context(nc.psum_tensor("b_psG", [128, 512], F32))
    psM = st.enter_context(nc.psum_tensor("b_psM", [128, 512], F32))
    psI2 = [st.enter_context(nc.psum_tensor("b_psI%d" % i, [128, 512], F32)) for i in range(2)]
    psT = st.enter_context(nc.psum_tensor("b_psT", [128, 1024], BF16))
    psW = st.enter_context(nc.psum_tensor("b_psW", [128, 512], F32))

    cUb = cstb[:, 0:64]
    cIb = cstb[:, 64:128]
    cOnesb = cstb[:, 128:256]
    cU = cst[:, CB_U:CB_U + 64]
    cI = cst[:, CB_I:CB_I + 64]
    cOnes = cst[:, CB_ONES:CB_ONES + 128]
    c4 = lambda o: cst[:, o:o + 256]

    for name, t_, src in [("b_nw", nw, "nwB"), ("b_cw", cw, "convw"), ("b_onw", onw, "onw"), ("b_negA", negA, "alog"),
                          ("b_dtb", dtb, "dtb"), ("b_cst", cst, "cstB")]:
        D(lambda e, t_=t_, src=src: e.dma_start(out=t_[:], in_=d[src]), [], [name])
    V(lambda e: e.tensor_copy(out=cstb[:, 0:64], in_=cst[:, CB_U:CB_U + 64]), ["b_cst"], ["b_cst"])
    V(lambda e: e.tensor_copy(out=cstb[:, 64:128], in_=cst[:, CB_I:CB_I + 64]), ["b_cst"], ["b_cst"])
    V(lambda e: e.tensor_copy(out=cstb[:, 128:256], in_=cst[:, CB_ONES:CB_ONES + 128]), ["b_cst"], ["b_cst"])
    A(lambda e: e.activation(out=negA[:], in_=negA[:], func=AF.Exp), ["b_negA"], ["b_negA"])
    V(lambda e: e.tensor_scalar(out=negA[:], in0=negA[:], scalar1=-1.0, scalar2=None, op0=ALU.mult), ["b_negA"], ["b_negA"])
    G(lambda e: e.memset(ones_bf[:], 1.0), [], ["b_ones"])
    G(lambda e: e.memset(S32[:], 0.0), [], ["b_S32"])
    G(lambda e: e.memset(Sb[:], 0.0), [], ["b_Sb"])
    G(lambda e: e.memset(u_sb[:], 0.0), [], [("b_u", m_) for m_ in range(8)])
    wv = d["wB"].rearrange("(c p) n -> p c n", p=128)
    for c in range(8):
        s = c % 2
        D(lambda e, c=c, s=s: e.dma_start(out=wst[s][:], in_=wv[:, c, :]), [], [("b_wst", s)])
        if c % 2 == 0:
            V(lambda e, c=c, s=s: e.tensor_scalar(out=w_sb[:, c, :], in0=wst[s][:], scalar1=nw[:, c:c + 1], scalar2=None, op0=ALU.mult),
              [("b_wst", s), "b_nw"], ["b_w"])
        else:
            A(lambda e, c=c, s=s: e.activation(out=w_sb[:, c, :], in_=wst[s][:], func=AF.Copy, scale=nw[:, c:c + 1]),
              [("b_wst", s), "b_nw"], ["b_w"])

    def rsqrt_act(out_ap, in_ap, scale, bias_ln, bias_exp, rkeys, wkey):
        A(lambda e: e.activation(out=out_ap, in_=in_ap, func=AF.Ln, bias=bias_ln, scale=scale), rkeys, [wkey])
        A(lambda e: e.activation(out=out_ap, in_=out_ap, func=AF.Exp, scale=-0.5, bias=bias_exp), [wkey], [wkey])

    def tile(ti, t0, TW, need_o):
        xs = ti % 2
        nck = TW // 64
        if xsrc is None:
            xv = d["xnT"].rearrange("(c p) t -> p c t", p=128)
            D(lambda e: e.dma_start(out=xt[xs][:, :, 0:TW], in_=xv[:, :, t0:t0 + TW]), [], [("b_xt", xs)])
        else:
            D(lambda e: e.dma_start(out=xt[xs][:, :, 0:TW], in_=xsrc(e, t0, TW)), list(xdeps(t0)), [("b_xt", xs)])
        for m in range(8):
            ps = psA if m % 2 == 0 else psB
            pk = "b_psA" if m % 2 == 0 else "b_psB"
            pkw = [pk]
            for c in range(8):
                T(lambda e, c=c, m=m, ps=ps: e.matmul(ps[:, 0:TW], lhsT=w_sb[:, c, m * 128:(m + 1) * 128], rhs=xt[xs][:, c, 0:TW],
                                                      start=(c == 0), stop=(c == 7)), ["b_w", ("b_xt", xs)], pkw)
            A(lambda e, m=m, ps=ps: e.copy(out=u_sb[:, m, 3:3 + TW], in_=ps[:, 0:TW]), [pk], [("b_u", m)])
            ac = acc[m % 2]
            ak = ("b_acc", m % 2)
            A(lambda e, m=m, ac=ac, ps=ps: e.activation(out=ac[:, 0:TW], in_=ps[:, 0:TW], func=AF.Copy, scale=cw[:, m, 3:4]),
              [pk, "b_cw"], [ak])
            for j in (2, 1, 0):
                V(lambda e, m=m, ac=ac, j=j: e.scalar_tensor_tensor(out=ac[:, 0:TW], in0=u_sb[:, m, j:j + TW], scalar=cw[:, m, j:j + 1],
                                                                   in1=ac[:, 0:TW], op0=ALU.mult, op1=ALU.add),
                  [("b_u", m), "b_cw", ak], [ak])
            eng = V
            if m < 4:
                A(lambda e, m=m, ac=ac: e.activation(out=csil[:, m, 0:TW], in_=ac[:, 0:TW], func=AF.Silu), [ak], [("b_csil", m)])
            else:
                A(lambda e, m=m, ac=ac: e.activation(out=vT[:, m - 4, 0:TW], in_=ac[:, 0:TW], func=AF.Silu), [ak], [("b_vT", m - 4)])
            eng(lambda e, m=m: e.tensor_copy(out=u_sb[:, m, 0:3], in_=u_sb[:, m, TW:TW + 3]), [("b_u", m)], [("b_u", m)])
        if DBG_STOP <= 1:
            return
        for m in range(4):
            A(lambda e, m=m: e.activation(out=sq[:, 0:TW], in_=csil[:, m, 0:TW], func=AF.Square), [("b_csil", m)], ["b_sq"])
            T(lambda e: e.matmul(psA[:, 0:TW], lhsT=ones_bf[:], rhs=sq[:, 0:TW], start=True, stop=True), ["b_sq", "b_ones"], ["b_psA"])
            rsqrt_act(rs[:, 0:TW], psA[:, 0:TW], 1.0, EPS, (-0.5 * float(np.log(128.0))) if m < 2 else 0.0, ["b_psA"], "b_rs")
            V(lambda e, m=m: e.tensor_tensor(out=qkT[:, m, 0:TW], in0=csil[:, m, 0:TW], in1=rs[:, 0:TW], op=ALU.mult),
              [("b_csil", m), "b_rs"], [("b_qkT", m)])
        if DBG_STOP <= 2:
            return
        if need_o:
            for h in range(4):
                ps = psA if h % 2 == 0 else psB
                pk = "b_psA" if h % 2 == 0 else "b_psB"
                pkw = [pk]
                for c in range(8):
                    T(lambda e, c=c, h=h, ps=ps: e.matmul(ps[:, 0:TW], lhsT=w_sb[:, c, 1024 + h * 128:1024 + (h + 1) * 128],
                                                          rhs=xt[xs][:, c, 0:TW], start=(c == 0), stop=(c == 7)),
                      ["b_w", ("b_xt", xs)], pkw)
                A(lambda e, h=h, ps=ps: e.activation(out=zs2[ti % 2][:, h, 0:TW], in_=ps[:, 0:TW], func=AF.Silu), [pk], [("b_zs", ti % 2, h)])
        if DBG_STOP <= 3:
            return
        bav = psM[0:64, 64:128].rearrange("p (a b) -> p a b", b=8)
        for ck in range(nck):
            for c in range(8):
                T(lambda e, c=c, ck=ck: e.matmul(bav[:, ck, :], lhsT=xt[xs][:, c, ck * 64:(ck + 1) * 64], rhs=w_sb[:, c, 1536:1544],
                                                 start=(c == 0), stop=(c == 7)), ["b_w", ("b_xt", xs)], ["b_psM"])
        V(lambda e: e.tensor_copy(out=ba[:, 0:nck, :], in_=bav[:, 0:nck, :]), ["b_psM"], ["b_ba"])
        V(lambda e: e.tensor_tensor(out=ba[:, 0:nck, 4:8], in0=ba[:, 0:nck, 4:8], in1=dtb[:].unsqueeze(1).to_broadcast([64, nck, 4]), op=ALU.add),
          ["b_ba", "b_dtb"], ["b_ba"])
        A(lambda e: e.activation(out=e1[:, 0:nck, 0:4], in_=ba[:, 0:nck, 0:4], func=AF.Exp, scale=-1.0), ["b_ba"], ["b_e1"])
        A(lambda e: e.activation(out=e1[:, 0:nck, 4:8], in_=ba[:, 0:nck, 4:8], func=AF.Exp), ["b_ba", "b_e1"], ["b_e1"])
        A(lambda e: e.activation(out=e1[:, 0:nck, :], in_=e1[:, 0:nck, :], func=AF.Ln, bias=1.0, scale=1.0), ["b_e1"], ["b_e1"])
        V(lambda e: e.tensor_scalar(out=lnb[:, 0:nck, :], in0=e1[:, 0:nck, 0:4], scalar1=-1.0, scalar2=None, op0=ALU.mult), ["b_e1"], ["b_lnb"])
        A(lambda e: e.activation(out=beta[:, 0:nck, :], in_=lnb[:, 0:nck, :], func=AF.Exp), ["b_lnb"], ["b_beta"])
        V(lambda e: e.tensor_tensor(out=gg[:, 0:nck, :], in0=e1[:, 0:nck, 4:8], in1=negA[:].unsqueeze(1).to_broadcast([64, nck, 4]), op=ALU.mult),
          ["b_e1", "b_negA"], ["b_g"])
        V(lambda e: e.tensor_copy(out=gbf[:, 0:nck, :], in_=gg[:, 0:nck, :]), ["b_g"], ["b_gbf"])
        if DBG_STOP <= 4:
            return
        chunks_of_tile(ti, xs, nck, need_o, (onorm(ti, t0, TW) if need_o else None))

    def onorm(ti, t0, TW):
        par = ti % 2
        o_sb, zs, ogt = o2[par], zs2[par], ogt2[par]
        for h in range(4):
            A(lambda e, h=h: e.activation(out=sq[:, 0:TW], in_=o_sb[:, h, 0:TW], func=AF.Square), [("b_o", par, h)], ["b_sq"])
            yield
            T(lambda e: e.matmul(psA[:, 0:TW], lhsT=ones_bf[:], rhs=sq[:, 0:TW], start=True, stop=True), ["b_sq", "b_ones"], ["b_psA"])
            yield
            rsqrt_act(rs[:, 0:TW], psA[:, 0:TW], 1.0 / 128.0, EPS, 0.0, ["b_psA"], "b_rs")
            yield
            V(lambda e, h=h: e.tensor_tensor(out=o_sb[:, h, 0:TW], in0=o_sb[:, h, 0:TW], in1=rs[:, 0:TW], op=ALU.mult),
              [("b_o", par, h), "b_rs"], [("b_o", par, h)])
            yield
            V(lambda e, h=h: e.scalar_tensor_tensor(out=ogt[:, h, 0:TW], in0=o_sb[:, h, 0:TW], scalar=onw[:, 0:1], in1=zs[:, h, 0:TW],
                                                    op0=ALU.mult, op1=ALU.mult),
              [("b_o", par, h), ("b_zs", par, h), "b_onw"], [("b_ogt", par, h)])
            yield
        if ogdst is None:
            ov = d["ogT"].rearrange("(h p) t -> p h t", p=128)
            dst = ov[:, :, t0 - 64:t0 - 64 + TW]
        else:
            dst = ogdst(t0 - 64, TW)
        D(lambda e: e.dma_start(out=dst, in_=ogt[:, :, 0:TW]), [("b_ogt", par, h) for h in range(4)], [("b_ogout", ti)])
        if after_tile is not None:
            after_tile(ti)
        yield

    def stage1(ti, xs, ck, need_o, sl, hs):
        c0 = ck * 64
        K = lambda name: (name, sl)
        H = lambda name: (name, hs)
        g_ck = gg[:, ck, :]
        gbb_, lbb_, gc_, gcl_, beg_, ekl_ = gbb2[sl], lbb2[sl], gc2[sl], gcl2[sl], beg2[sl], ekl2[sl]
        args_, Eg_, Nm_, Mm_ = args2[sl], Eg2[sl], Nm2[sl], Mm2[sl]
        L_, Uu_, O1_, N1_, O2_, PU_, PL_, Y_ = L2[sl], Uu2[sl], O12[sl], N12[sl], O22[sl], PU2[sl], PL2[sl], Y2[sl]
        T32U_, T32L_, kcp_, kbg_ = T32U2[sl], T32L2[sl], kcp2[sl], kbg2[sl]
        TTb_, wTn_, vb_, kst_, qg_, attnT_, egl_ = TTb4[hs], wTn4[hs], vb4[hs], kst4[hs], qg4[hs], attnT4[hs], egl4[hs]
        psI_ = psI2[sl]
        PIK = ("b_psI", sl)
        V(lambda e: e.tensor_copy(out=gbb_[:], in_=g_ck.unsqueeze(2).to_broadcast([64, 4, 128])), ["b_g"], [K("gb")])
        yield
        A(lambda e: e.copy(out=lbb_[:], in_=lnb[:, ck, :].unsqueeze(2).to_broadcast([64, 4, 64])), ["b_lnb"], [K("lb")])
        yield
        Gp = psG[:, 0:256]
        GBp = psG[0:64, 256:512]
        for h in range(4):
            T(lambda e, h=h: e.matmul(Gp[:, h * 64:(h + 1) * 64], lhsT=gbb_[:, h, :], rhs=cUb, start=True, stop=True),
              [K("gb"), "b_cst"], ["b_psG"])
        for h in range(4):
            T(lambda e, h=h: e.matmul(GBp[:, h * 64:(h + 1) * 64], lhsT=gbb_[:, h, 0:64], rhs=cUb, start=True, stop=False),
              [K("gb"), "b_cst"], ["b_psG"])
            T(lambda e, h=h: e.matmul(GBp[:, h * 64:(h + 1) * 64], lhsT=lbb_[:, h, :], rhs=cIb, start=False, stop=True),
              [K("lb"), "b_cst"], ["b_psG"])
        gcol = psM[0:64, 0:4]
        glast = psM[:, 4:8]
        T(lambda e: e.matmul(gcol, lhsT=cUb, rhs=gbf[:, ck, :], start=True, stop=True), ["b_gbf", "b_cst"], ["b_psM"])
        T(lambda e: e.matmul(glast, lhsT=cOnesb, rhs=gbf[:, ck, :], start=True, stop=True), ["b_gbf", "b_cst"], ["b_psM"])
        V(lambda e: e.tensor_copy(out=gc_[:], in_=gcol), ["b_psM"], [K("gc")])
        V(lambda e: e.tensor_tensor(out=ekl_[:], in0=glast[0:64, :], in1=gc_[:], op=ALU.subtract), ["b_psM", K("gc")], [K("ekl")])
        A(lambda e: e.activation(out=egl_[:], in_=glast, func=AF.Exp), ["b_psM"], [H("egl")])
        V(lambda e: e.tensor_tensor(out=gcl_[:], in0=gc_[:], in1=lnb[:, ck, :], op=ALU.add), [K("gc"), "b_lnb"], [K("gcl")])
        A(lambda e: e.activation(out=ekl_[:], in_=ekl_[:], func=AF.Exp), [K("ekl")], [K("ekl")])
        A(lambda e: e.activation(out=beg_[:], in_=gcl_[:], func=AF.Exp), [K("gcl")], [K("beg")])
        bc = lambda t_: t_[:].unsqueeze(2).to_broadcast([64, 4, 64])
        a3 = lambda i: args_[:, i, :].rearrange("p (a b) -> p a b", a=4)
        p3 = lambda ap: ap.rearrange("p (a b) -> p a b", a=4)
        V(lambda e: e.tensor_tensor(out=a3(0), in0=p3(Gp[0:64, :]), in1=bc(gc_), op=ALU.subtract), ["b_psG", K("gc")], [K("args0")])
        V(lambda e: e.tensor_tensor(out=a3(1), in0=p3(GBp), in1=bc(gc_), op=ALU.subtract), ["b_psG", K("gc")], [K("args1")])
        V(lambda e: e.tensor_tensor(out=a3(2), in0=p3(Gp[0:64, :]), in1=bc(gcl_), op=ALU.subtract), ["b_psG", K("gcl")], [K("args2")])
        if need_o:
            A(lambda e: e.activation(out=Eg_[:], in_=Gp, func=AF.Exp), ["b_psG"], [K("Eg")])
        G(lambda e: e.tensor_tensor(out=args_[:, 0, :], in0=args_[:, 0, :], in1=c4(CB_MUI), op=ALU.add), [K("args0"), "b_cst"], [K("args0")])
        G(lambda e: e.tensor_tensor(out=args_[:, 1, :], in0=args_[:, 1, :], in1=c4(CB_MUS), op=ALU.add), [K("args1"), "b_cst"], [K("args1")])
        V(lambda e: e.scalar_tensor_tensor(out=args_[:, 2, :], in0=args_[:, 2, :], scalar=-1.0, in1=c4(CB_MLS), op0=ALU.mult, op1=ALU.add),
          [K("args2"), "b_cst"], [K("args2")])
        yield
        A(lambda e: e.activation(out=args_[:], in_=args_[:], func=AF.Exp), [K("args0"), K("args1"), K("args2")],
          [K("args0"), K("args1"), K("args2")])
        yield
        KQ = psM[0:64, 128:384].rearrange("p (a b c) -> p a b c", a=2, b=2)
        A(lambda e: e.copy(out=kcp_[:], in_=qkT[:, 2:4, c0:c0 + 64]), [("b_qkT", 2), ("b_qkT", 3)], [K("kcp")])
        for hk in range(2):
            kch = qkT[:, 2 + hk, c0:c0 + 64]
            T(lambda e, hk=hk, kch=kch: e.matmul(KQ[:, 0, hk, :], lhsT=kch, rhs=kcp_[:, hk, :], start=True, stop=True),
              [("b_qkT", 2 + hk), K("kcp")], ["b_psM"])
            if need_o:
                T(lambda e, hk=hk, kch=kch: e.matmul(KQ[:, 1, hk, :], lhsT=kch, rhs=qkT[:, hk, c0:c0 + 64], start=True, stop=True),
                  [("b_qkT", 2 + hk), ("b_qkT", hk)], ["b_psM"])
        o4 = lambda t_: t_.rearrange("p (a b c) -> p a b c", a=2, b=2)
        EK = [K("args0"), K("args1"), K("args2")]
        for j in range(2):
            V(lambda e, j=j: e.tensor_tensor(out=o4(Nm_[:])[:, :, j, :], in0=KQ[:, 0, :, :], in1=o4(args_[:, 1, :])[:, :, j, :], op=ALU.mult),
              ["b_psM"] + EK, [K("N")])
            V(lambda e, j=j: e.tensor_tensor(out=o4(Mm_[:])[:, :, j, :], in0=KQ[:, 0, :, :], in1=o4(args_[:, 2, :])[:, :, j, :], op=ALU.mult),
              ["b_psM"] + EK, [K("M")])
            if need_o:
                V(lambda e, j=j: e.tensor_tensor(out=o4(attnT_[:])[:, :, j, :], in0=KQ[:, 1, :, :], in1=o4(args_[:, 0, :])[:, :, j, :], op=ALU.mult),
                  ["b_psM"] + EK, [H("attnT")])
        if DBG_STOP <= 7:
            return
        V(lambda e: e.tensor_tensor(out=L_[0][:], in0=Mm_[:], in1=c4(CB_BD), op=ALU.mult), [K("M"), "b_cst"], [K("L0")])
        V(lambda e: e.tensor_tensor(out=Uu_[0][:], in0=Nm_[:], in1=c4(CB_BD), op=ALU.mult), [K("N"), "b_cst"], [K("U0")])
        G(lambda e: e.tensor_tensor(out=O1_[:], in0=Mm_[:], in1=c4(CB_B1L), op=ALU.mult), [K("M"), "b_cst"], [K("O1")])
        G(lambda e: e.tensor_tensor(out=N1_[:], in0=Nm_[:], in1=c4(CB_B1U), op=ALU.mult), [K("N"), "b_cst"], [K("N1")])
        G(lambda e: e.tensor_tensor(out=O2_[:], in0=Mm_[:], in1=c4(CB_B2L), op=ALU.mult), [K("M"), "b_cst"], [K("O2")])
        yield
        I4 = cI.unsqueeze(1).to_broadcast([64, 4, 64])
        V(lambda e: e.tensor_tensor(out=p3(PU_[0][:]), in0=I4, in1=p3(Uu_[0][:]), op=ALU.subtract), ["b_cst", K("U0")], [K("PU0")])
        V(lambda e: e.tensor_tensor(out=p3(PL_[0][:]), in0=I4, in1=p3(L_[0][:]), op=ALU.subtract), ["b_cst", K("L0")], [K("PL0")])
        yield

        def mm4(lhs, lk, rhs, rk):
            pv = psI_[0:64, 0:256]
            for h in range(4):
                T(lambda e, h=h: e.matmul(pv[:, h * 64:(h + 1) * 64], lhsT=lhs[:, h * 64:(h + 1) * 64], rhs=rhs[:, h * 64:(h + 1) * 64],
                                          start=True, stop=True), [lk, rk], [PIK])
            return pv

        pcur = 0
        for k in range(3):
            pv = mm4(Uu_[k], K("U%d" % k), L_[k], K("L%d" % k))
            yield
            A(lambda e, pv=pv, k=k: e.copy(out=L_[k + 1][:], in_=pv), [PIK], [K("L%d" % (k + 1))])
            yield
            pv = mm4(L_[k], K("L%d" % k), Uu_[k], K("U%d" % k))
            yield
            if k < 2:
                V(lambda e, pv=pv, k=k: e.tensor_copy(out=Uu_[k + 1][:], in_=pv), [PIK], [K("U%d" % (k + 1))])
                ulhs, ulk = Uu_[k + 1], K("U%d" % (k + 1))
            else:
                V(lambda e, pv=pv: e.tensor_copy(out=Y_[0][:], in_=pv), [PIK], [K("Y0")])
                ulhs, ulk = Y_[0], K("Y0")
            yield
            nxt = 1 - pcur
            pv = mm4(L_[k + 1], K("L%d" % (k + 1)), PU_[pcur], K("PU%d" % pcur))
            yield
            V(lambda e, pv=pv, pcur=pcur, nxt=nxt: e.tensor_tensor(out=PU_[nxt][:], in0=pv, in1=PU_[pcur][:], op=ALU.add),
              [PIK, K("PU%d" % pcur)], [K("PU%d" % nxt)])
            yield
            pv = mm4(ulhs, ulk, PL_[pcur], K("PL%d" % pcur))
            yield
            V(lambda e, pv=pv, pcur=pcur, nxt=nxt: e.tensor_tensor(out=PL_[nxt][:], in0=pv, in1=PL_[pcur][:], op=ALU.add),
              [PIK, K("PL%d" % pcur)], [K("PL%d" % nxt)])
            yield
            pcur = nxt
        TdU, TdUk, TdL, TdLk = PU_[pcur], K("PU%d" % pcur), PL_[pcur], K("PL%d" % pcur)
        pv = mm4(O1_, K("O1"), TdU, TdUk)
        yield
        A(lambda e, pv=pv: e.copy(out=Y_[0][:], in_=pv), [PIK], [K("Y0")])
        yield
        pv = mm4(TdL, TdLk, Y_[0], K("Y0"))
        yield
        V(lambda e, pv=pv: e.tensor_tensor(out=T32U_[:], in0=TdU[:], in1=pv, op=ALU.subtract), [PIK, TdUk], [K("T32U")])
        yield
        pv = mm4(N1_, K("N1"), TdL, TdLk)
        yield
        A(lambda e, pv=pv: e.copy(out=Y_[1][:], in_=pv), [PIK], [K("Y1")])
        yield
        pv = mm4(TdU, TdUk, Y_[1], K("Y1"))
        yield
        V(lambda e, pv=pv: e.tensor_tensor(out=T32L_[:], in0=TdL[:], in1=pv, op=ALU.subtract), [PIK, TdLk], [K("T32L")])
        yield
        pv = mm4(O2_, K("O2"), T32U_, K("T32U"))
        yield
        A(lambda e, pv=pv: e.copy(out=Y_[0][:], in_=pv), [PIK], [K("Y0")])
        yield
        pv = mm4(T32L_, K("T32L"), Y_[0], K("Y0"))
        yield
        V(lambda e, pv=pv: e.tensor_tensor(out=TTb_[:], in0=T32U_[:], in1=pv, op=ALU.subtract), [PIK, K("T32U")], [H("TTb")])
        yield
        if DBG_STOP <= 8:
            return
        tv = psT[0:64, 0:768].rearrange("p (a b) -> p a b", a=6)
        for hk in range(2):
            T(lambda e, hk=hk: e.transpose(out=tv[:, hk, :], in_=qkT[:, 2 + hk, c0:c0 + 64], identity=ident[:]),
              [("b_qkT", 2 + hk), "b_ident"], ["b_psT"])
        for h in range(4):
            T(lambda e, h=h: e.transpose(out=tv[:, 2 + h, :], in_=vT[:, h, c0:c0 + 64], identity=ident[:]), [("b_vT", h), "b_ident"], ["b_psT"])
        bc128 = lambda ap: ap.unsqueeze(2).to_broadcast([64, 4, 128])
        kpair = tv[:, 0:2, :].unsqueeze(2).to_broadcast([64, 2, 2, 128])
        k4 = lambda t_: t_[:].rearrange("p (a b) c -> p a b c", a=2)
        s4 = lambda ap: ap.rearrange("p (a b) -> p a b", a=2).unsqueeze(3).to_broadcast([64, 2, 2, 128])
        V(lambda e: e.tensor_tensor(out=vb_[:], in0=tv[:, 2:6, :], in1=bc128(beta[:, ck, :]), op=ALU.mult), ["b_psT", "b_beta"], [H("vb")])
        V(lambda e: e.tensor_tensor(out=k4(kbg_), in0=kpair, in1=s4(beg_[:]), op=ALU.mult), ["b_psT", K("beg")], [K("kbg")])
        V(lambda e: e.tensor_tensor(out=k4(kst_), in0=kpair, in1=s4(ekl_[:]), op=ALU.mult), ["b_psT", K("ekl")], [H("kst")])
        wTp = psW[:, 0:256]
        for h in range(4):
            T(lambda e, h=h: e.matmul(wTp[:, h * 64:(h + 1) * 64], lhsT=kbg_[:, h, :], rhs=TTb_[:, h * 64:(h + 1) * 64], start=True, stop=True),
              [K("kbg"), H("TTb")], ["b_psW"])
        A(lambda e: e.activation(out=wTn_[:], in_=wTp, func=AF.Copy, scale=-1.0), ["b_psW"], [H("wTn")])
        if need_o:
            qpair = qkT[:, 0:2, c0:c0 + 64].unsqueeze(2).to_broadcast([128, 2, 2, 64])
            V(lambda e: e.tensor_tensor(out=qg_[:].rearrange("p (a b c) -> p a b c", a=2, b=2),
                                        in0=Eg_[:].rearrange("p (a b c) -> p a b c", a=2, b=2), in1=qpair, op=ALU.mult),
              [("b_qkT", 0), ("b_qkT", 1), K("Eg")], [H("qg")])
            yield

    def stage2(ti, xs, ck, need_o, hs):
        c0 = ck * 64
        H = lambda name: (name, hs)
        TTb_, wTn_, vb_, kst_, qg_, attnT_, egl_ = TTb4[hs], wTn4[hs], vb4[hs], kst4[hs], qg4[hs], attnT4[hs], egl4[hs]
        if DBG_STOP <= 10:
            return
        vp = psA[0:64, :].rearrange("p (a b) -> p a b", a=4)
        for h in range(4):
            T(lambda e, h=h: e.matmul(vp[:, h, :], lhsT=TTb_[:, h * 64:(h + 1) * 64], rhs=vb_[:, h, :], start=True, stop=False),
              [H("TTb"), H("vb")], ["b_psA"])
            T(lambda e, h=h: e.matmul(vp[:, h, :], lhsT=wTn_[:, h * 64:(h + 1) * 64], rhs=Sb[:, h, :], start=False, stop=True),
              [H("wTn"), "b_Sb"], ["b_psA"])
        yield
        A(lambda e: e.copy(out=vnb[:], in_=vp), ["b_psA"], ["b_vnb"])
        yield
        if need_o:
            oTp = psW[:, 256:512]
            for h in range(4):
                T(lambda e, h=h: e.matmul(oTp[:, h * 64:(h + 1) * 64], lhsT=Sb[:, h, :], rhs=qg_[:, h * 64:(h + 1) * 64], start=True, stop=False),
                  ["b_Sb", H("qg")], ["b_psW"])
                T(lambda e, h=h: e.matmul(oTp[:, h * 64:(h + 1) * 64], lhsT=vnb[:, h, :], rhs=attnT_[:, h * 64:(h + 1) * 64], start=False, stop=True),
                  ["b_vnb", H("attnT")], ["b_psW"])
            yield
            A(lambda e: e.copy(out=o2[ti % 2][:, :, c0:c0 + 64], in_=oTp.rearrange("p (a b) -> p a b", a=4)), ["b_psW"],
              [("b_o", ti % 2, h) for h in range(4)])
            yield
        sp_ = psB[:].rearrange("p (a b) -> p a b", a=4)
        for h in range(4):
            T(lambda e, h=h: e.matmul(sp_[:, h, :], lhsT=kst_[:, h, :], rhs=vnb[:, h, :], start=True, stop=True), [H("kst"), "b_vnb"], ["b_psB"])
        yield
        for h in range(4):
            V(lambda e, h=h: e.scalar_tensor_tensor(out=S32[:, h, :], in0=S32[:, h, :], scalar=egl_[:, h:h + 1], in1=sp_[:, h, :],
                                                    op0=ALU.mult, op1=ALU.add), ["b_S32", H("egl"), "b_psB"], ["b_S32"])
        yield
        A(lambda e: e.copy(out=Sb[:], in_=S32[:]), ["b_S32"], ["b_Sb"])
        yield

    def run_rr(gens):
        gens = [g for g in gens if g is not None]
        if SEQ_MODE:
            for g in gens:
                for _ in g:
                    pass
            return
        while gens:
            for g in list(gens):
                try:
                    next(g)
                except StopIteration:
                    gens.remove(g)

    def chain(*gs):
        for g in gs:
            yield from g

    hs_ctr = [0]

    carry = [None]

    def chunks_of_tile(ti, xs, nck, need_o, tail):
        pending = carry[0]
        for p0 in range(0, nck, 2):
            cks = list(range(p0, min(p0 + 2, nck)))
            hss = []
            s1 = []
            for i, ck in enumerate(cks):
                hs = hs_ctr[0] % 4
                hs_ctr[0] += 1
                hss.append(hs)
                s1.append(stage1(ti, xs, ck, need_o, i, hs))
            run_rr([pending] + s1)
            pending = chain(*[stage2(ti, xs, ck, need_o, hs) for ck, hs in zip(cks, hss)])
        carry[0] = chain(pending, tail) if tail is not None else pending

    tile(0, 0, 64, False)
    ntile = (nchunks - 1) // 8
    assert ntile * 8 + 1 == nchunks
    for ti in range(ntile):
        tile(ti + 1, 64 + ti * TM, TM, True)
    run_rr([carry[0]])


def dn_weight_layout(r, dn_norm_w, dn_w_in, dn_conv_w, dn_a_log, dn_dt_bias, dn_o_norm_w):
    w = np.asarray(dn_w_in[0], np.float32)
    qc = list(range(2 * r * 128, (2 * r + 2) * 128))
    kc = [1024 + c for c in qc]
    vc = list(range(2048 + 4 * r * 128, 2048 + (4 * r + 4) * 128))
    zc = list(range(4096 + 4 * r * 128, 4096 + (4 * r + 4) * 128))
    bcol = list(range(6144 + 4 * r, 6144 + 4 * r + 4))
    acol = list(range(6160 + 4 * r, 6160 + 4 * r + 4))
    cols = qc + kc + vc + zc + bcol + acol
    out = {}
    out["wB"] = np.ascontiguousarray(w[:, cols])
    out["nwB"] = np.ascontiguousarray(np.asarray(dn_norm_w[0], np.float32).reshape(8, 128).T)
    cwf = np.asarray(dn_conv_w[0], np.float32)[:, qc + kc + vc]
    out["convw"] = np.ascontiguousarray(cwf.reshape(4, 8, 128).transpose(2, 1, 0))
    out["onw"] = np.ascontiguousarray(np.asarray(dn_o_norm_w[0], np.float32).reshape(128, 1))
    out["alog"] = np.ascontiguousarray(np.broadcast_to(np.asarray(dn_a_log[0], np.float32)[None, 4 * r:4 * r + 4], (64, 4)))
    out["dtb"] = np.ascontiguousarray(np.broadcast_to(np.asarray(dn_dt_bias[0], np.float32)[None, 4 * r:4 * r + 4], (64, 4)))
    out["cstB"] = dn_consts()
    return out


RG8 = [[0, 1, 2, 3, 4, 5, 6, 7]]


def build_fused():
    import contextlib
    nc = bass.Bass("TRN2", target_bir_lowering=False)
    ext = lambda name, shape, dt=F32: nc.dram_tensor(name, shape, dt, kind="ExternalInput").ap()
    dA = {}
    dA["xa"] = ext("xa", [33 * 128, 1024])
    dA["xm"] = ext("xm", [128, 1024])
    dA["w_in"] = ext("w_in", [1024, 2304])
    dA["nw"] = ext("nw", [128, 8])
    dA["wq"] = ext("wq", [128, 128])
    dA["wk"] = ext("wk", [128, 128])
    dA["sinks"] = ext("sinks", [128, 16])
    dA["w_out"] = ext("w_out", [1024, 1024])
    dA["tabs"] = ext("tabs", [128, 4, 2048])
    dA["mtabs"] = ext("mtabs", [16, 3, 2048])
    dB = {}
    dB["wB"] = ext("wB", [1024, 1544])
    dB["nwB"] = ext("nwB", [128, 8])
    dB["convw"] = ext("convw", [128, 8, 4])
    dB["onw"] = ext("onw", [128, 1])
    dB["alog"] = ext("alog", [64, 4])
    dB["dtb"] = ext("dtb", [64, 4])
    dB["cstB"] = ext("cstB", [64, CB_COLS])
    wout = ext("wout", [2048, 1024])
    y = nc.dram_tensor("y", [4096, 1024], F32, kind="ExternalOutput").ap()
    h1_loc = nc.dram_tensor("h1_loc", [4096, 1024], F32).ap()
    xn1T_loc = nc.dram_tensor("xn1T_loc", [1024, 4160], BF16).ap()
    xnT_all = nc.dram_tensor("xnT_all", [8 * 1024, 4160], BF16).ap()
    ogT_loc = [nc.dram_tensor("ogT_loc%d" % j, [512, 4096], BF16).ap() for j in range(4)]
    ogT_all3 = nc.dram_tensor("ogT_all", [4, 8 * 512, 4096], BF16).ap()
    ogT_all = [ogT_all3[j] for j in range(4)]
    xnT_mine = nc.dram_tensor("xnT_mine", [1024, 16448], BF16).ap()
    ogT_mine = nc.dram_tensor("ogT_mine", [2048, 4096], BF16).ap()
    dA["h1"] = h1_loc
    dA["xn1T"] = xn1T_loc
    dB["ogT"] = None
    with contextlib.ExitStack() as outer:
        with contextlib.ExitStack() as st:
            P = Prog(nc, n_dma_sems=16, sem_stack=outer, prefix="A", barrier=True)
            emit_phase_a(nc, st, P, dA, 33)
            P.emit()
        with contextlib.ExitStack() as st:
            P = Prog(nc, n_dma_sems=16, sem_stack=outer, prefix="B", barrier=True)
            P.add("pool", lambda e: e.collective_compute("AllGather", ALU.bypass, replica_groups=RG8, ins=[xn1T_loc], outs=[xnT_all]),
                  writes=["xnT_all"], dma=True, inc=1, own_sem=True)

            xcache = {}

            def seg_copy(e, sg_):
                if "b" not in xcache:
                    xcache["b"] = e.snap(e.partition_id() // 4, min_val=0, max_val=1)
                src = xnT_all.rearrange("(r d) t -> r d t", d=1024)[bass.ds(xcache["b"] * 4 + sg_, 1), :, :]
                src = src.rearrange("o d t -> (o d) t")
                if sg_ == 0:
                    return e.dma_start(out=xnT_mine[:, 0:4160], in_=src)
                return e.dma_start(out=xnT_mine[:, 64 + sg_ * 4096:64 + (sg_ + 1) * 4096], in_=src[:, 64:4160])

            for sg_ in range(4):
                P.add("sp", lambda e, sg_=sg_: seg_copy(e, sg_), reads=["xnT_all"], writes=[("xnT_mine", sg_)], dma=True)

            def xsrc(e, t0, TW):
                return xnT_mine[:, t0:t0 + TW].rearrange("(c p) t -> p c t", p=128)

            xdep_fn = lambda t0: [("xnT_mine", 0 if t0 == 0 else (t0 - 64) // 4096)]
            def ogdst(c0, TW):
                j, lc = c0 // 4096, c0 % 4096
                return ogT_loc[j].rearrange("(h p) t -> p h t", p=128)[:, :, lc:lc + TW]

            def after_tile(ti):
                if ti % 8 == 0:
                    j = ti // 8 - 1
                    P.add("pool", lambda e, j=j: e.collective_compute("AllGather", ALU.bypass, replica_groups=RG8,
                                                                      ins=[ogT_loc[j]], outs=[ogT_all[j]]),
                          reads=[("b_ogout", t_) for t_ in range(ti - 7, ti + 1)], writes=[("ogT_all", j)], dma=True, inc=1, own_sem=True)

            emit_phase_b(nc, st, P, dB, 257, xsrc=xsrc, xdeps=xdep_fn, ogdst=ogdst, after_tile=after_tile)
            P.emit()
        with contextlib.ExitStack() as st:
            P = Prog(nc, n_dma_sems=16, sem_stack=outer, prefix="C", barrier=False)

            ocache = {}

            def og_copy(e, r_):
                if "b" not in ocache:
                    pid = e.partition_id()
                    ocache["b"] = e.snap(pid // 4, min_val=0, max_val=1)
                    ocache["s"] = e.snap(pid % 4, min_val=0, max_val=3)
                v = ogT_all3.rearrange("j (r f) t -> j r f t", f=512)
                src = v[bass.ds(ocache["s"], 1), bass.ds(ocache["b"] * 4 + r_, 1), :, :].rearrange("q o f t -> (q o f) t")
                return e.dma_start(out=ogT_mine[r_ * 512:(r_ + 1) * 512, :], in_=src)

            for r_ in range(4):
                P.add("sp", lambda e, r_=r_: og_copy(e, r_), writes=["ogT_mine"], dma=True)
            emit_phase_c(nc, st, P, ogT_mine, h1_loc, wout, y, 4096, ogdeps=["ogT_mine"])
            P.emit()
    return nc


_NC_CACHE = {}


def _get_nc(name, builder):
    if name not in _NC_CACHE:
        _NC_CACHE[name] = builder()
    return _NC_CACHE[name]


def kernel(x, meta_tokens, attn_norm_w, attn_w_in, attn_q_norm_w, attn_k_norm_w, attn_sinks, attn_w_out,
           dn_norm_w, dn_w_in, dn_conv_w, dn_a_log, dn_dt_bias, dn_o_norm_w, dn_w_out):
    x = np.asarray(x, np.float32)
    meta = np.asarray(meta_tokens, np.float32)
    cores = list(range(8))
    SEG = 4096
    WA = attn_weight_layout(attn_norm_w, attn_w_in, attn_q_norm_w, attn_k_norm_w, attn_sinks, attn_w_out)
    xm = np.zeros((128, 1024), np.float32)
    xm[:16] = meta
    tabs0, tabs1 = attn_tables(True), attn_tables(False)
    wout = np.ascontiguousarray(np.asarray(dn_w_out[0], np.float32))
    WBs = [dn_weight_layout(r, dn_norm_w, dn_w_in, dn_conv_w, dn_a_log, dn_dt_bias, dn_o_norm_w) for r in range(4)]
    maps = []
    for c in cores:
        b, s = c // 4, c % 4
        if s == 0:
            halo = np.concatenate([np.zeros((112, 1024), np.float32), meta], 0)
        else:
            halo = x[b, s * SEG - 128:s * SEG]
        xa = np.ascontiguousarray(np.concatenate([halo, x[b, s * SEG:(s + 1) * SEG]], 0))
        tb, mtb = tabs0 if s == 0 else tabs1
        maps.append(dict(xa=xa, xm=xm, tabs=tb, mtabs=mtb, wout=wout, **WA, **WBs[s]))
    res = run_bass_kernel_spmd(_get_nc("F", build_fused), maps, core_ids=cores).results
    out = np.empty((2, 16384, 1024), np.float32)
    for c in cores:
        b, s = c // 4, c % 4
        out[b, s * SEG:(s + 1) * SEG] = np.asarray(res[c]["y"])
    return out
```
